# Optimizing a Trainium2 kernel written in Bass

```python
import math
import jax, jax.numpy as jnp
from jax import lax
import numpy as np

D_MODEL = 1024
BATCH = 2
SEQ = 8192
DEPTH = 2
DEC_BATCH = 32
DEC_SEQ = 8
PAST_LEN = 16384
PAGE_SIZE = 128

N_A_LAYERS = DEPTH // 2
D_MIX = D_MODEL
D_MEM = D_MODEL // 4
D_MAIN = D_MIX - D_MEM
MEM_HEADS = 4
MEM_HEAD_DIM = D_MEM // MEM_HEADS
MEM_TOKENS = 256
S5_GROUP_CH = 16
S5_GROUPS = D_MAIN // S5_GROUP_CH
S5_STATE = 64
S5_CHUNK = 128
FOX_HEAD_DIM = 64
FOX_HEADS = D_MAIN // FOX_HEAD_DIM
Q_BLOCK = 128
D_FF = 4 * D_MODEL
RMS_EPS = 1e-6
NEG_INF = -1e30

kernel_name = "yoco_s5_fox_hybrid_step"


def rmsnorm(x, g):
    x32 = x.astype(jnp.float32)
    y = x32 * lax.rsqrt(jnp.mean(x32 * x32, axis=-1, keepdims=True) + RMS_EPS)
    return (y * g.astype(jnp.float32)).astype(x.dtype)


def sqrelu_mlp(x, w_up, w_down):
    h = jax.nn.relu(x @ w_up)
    return (h * h) @ w_down


def memory_kv(mem, w):
    n, m, _ = mem.shape
    kv = mem @ w
    k = kv[..., :D_MEM].reshape(n, m, MEM_HEADS, MEM_HEAD_DIM)
    v = kv[..., D_MEM:].reshape(n, m, MEM_HEADS, MEM_HEAD_DIM)
    return k, v


def memory_attend(q, mk, mv):
    s = jnp.einsum("blhd,bmhd->bhlm", q, mk).astype(jnp.float32) * (MEM_HEAD_DIM ** -0.5)
    p = jax.nn.softmax(s, axis=-1)
    return jnp.einsum("bhlm,bmhd->blhd", p.astype(mv.dtype), mv)


def s5_discretize(a_re, a_im, log_step, b_re, b_im):
    dt = jnp.exp(log_step.astype(jnp.float32))[:, None]
    a_re = a_re.astype(jnp.float32)
    a_im = a_im.astype(jnp.float32)
    mag = jnp.exp(a_re * dt)
    ab_re = mag * jnp.cos(a_im * dt)
    ab_im = mag * jnp.sin(a_im * dt)
    den = a_re * a_re + a_im * a_im
    nr = ab_re - 1.0
    ni = ab_im
    cr = (nr * a_re + ni * a_im) / den
    ci = (ni * a_re - nr * a_im) / den
    b_re = b_re.astype(jnp.float32)
    b_im = b_im.astype(jnp.float32)
    bb_re = cr[..., None] * b_re - ci[..., None] * b_im
    bb_im = cr[..., None] * b_im + ci[..., None] * b_re
    return ab_re, ab_im, bb_re, bb_im


def _ssm_combine(c1, c2):
    a1r, a1i, b1r, b1i = c1
    a2r, a2i, b2r, b2i = c2
    return (a2r * a1r - a2i * a1i,
            a2r * a1i + a2i * a1r,
            a2r * b1r - a2i * b1i + b2r,
            a2r * b1i + a2i * b1r + b2i)


def s5_mixer(u, h0_re, h0_im, a_re, a_im, log_step, b_re, b_im, c_re, c_im, d, w_glu, b_glu):
    n, l, _ = u.shape
    u32 = u.astype(jnp.float32).reshape(n, l, S5_GROUPS, S5_GROUP_CH)
    ab_re, ab_im, bb_re, bb_im = s5_discretize(a_re, a_im, log_step, b_re, b_im)
    c_re = c_re.astype(jnp.float32)
    c_im = c_im.astype(jnp.float32)
    d = d.astype(jnp.float32)
    t = S5_CHUNK if l % S5_CHUNK == 0 else l
    nc = l // t
    u_blocks = u32.reshape(n, nc, t, S5_GROUPS, S5_GROUP_CH).swapaxes(0, 1)

    def step(carry, u_t):
        hr, hi = carry
        bur = jnp.einsum("ntgc,gpc->ntgp", u_t, bb_re)
        bui = jnp.einsum("ntgc,gpc->ntgp", u_t, bb_im)
        bur = bur.at[:, 0].add(ab_re * hr - ab_im * hi)
        bui = bui.at[:, 0].add(ab_re * hi + ab_im * hr)
        ar = jnp.broadcast_to(ab_re, bur.shape)
        ai = jnp.broadcast_to(ab_im, bur.shape)
        _, _, sr, si = lax.associative_scan(_ssm_combine, (ar, ai, bur, bui), axis=1)
        y = (jnp.einsum("ntgp,gcp->ntgc", sr, c_re) - jnp.einsum("ntgp,gcp->ntgc", si, c_im)
             + d * u_t)
        return (sr[:, -1], si[:, -1]), y

    (hr, hi), y = lax.scan(step, (h0_re.astype(jnp.float32), h0_im.astype(jnp.float32)), u_blocks)
    y = y.swapaxes(0, 1).reshape(n, l, D_MAIN)
    z = jax.nn.gelu(y)
    out = z * jax.nn.sigmoid(z @ w_glu.astype(jnp.float32) + b_glu.astype(jnp.float32))
    return out.astype(u.dtype), hr, hi


def shared_kv(h, norm_kv, w_kv, w_f, b_f):
    n, l, _ = h.shape
    hn = rmsnorm(h, norm_kv)
    kv = hn @ w_kv
    k = kv[..., :D_MAIN].reshape(n, l, FOX_HEADS, FOX_HEAD_DIM)
    v = kv[..., D_MAIN:].reshape(n, l, FOX_HEADS, FOX_HEAD_DIM)
    logf = jax.nn.log_sigmoid((hn @ w_f + b_f).astype(jnp.float32))
    return k, v, logf


def fox_attend(q, f_q, q_pos, groups):
    n, lq, h, dh = q.shape
    qb = Q_BLOCK if lq % Q_BLOCK == 0 else lq
    nb = lq // qb
    scale = dh ** -0.5
    groups_t = tuple((k, v, jnp.swapaxes(f_k, 1, 2), k_pos) for (k, v, f_k, k_pos) in groups)
    q_blk = q.reshape(n, nb, qb, h, dh).swapaxes(0, 1)
    fq_blk = jnp.swapaxes(f_q, 1, 2).reshape(n, h, nb, qb).transpose(2, 0, 1, 3)
    pos_blk = q_pos.reshape(nb, qb)

    def one_block(args):
        q_b, fq_b, pos_b = args
        logits = []
        for k, _, fk, k_pos in groups_t:
            s = jnp.einsum("bqhd,bkhd->bhqk", q_b, k).astype(jnp.float32) * scale
            s = s + fq_b[..., :, None] - fk[:, :, None, :]
            logits.append(jnp.where(k_pos[None, None, None, :] <= pos_b[None, None, :, None], s, NEG_INF))
        probs = jax.nn.softmax(jnp.concatenate(logits, axis=-1), axis=-1)
        out = None
        start = 0
        for k, v, _, _ in groups_t:
            lk = k.shape[1]
            o = jnp.einsum("bhqk,bkhd->bqhd", probs[..., start:start + lk].astype(v.dtype), v)
            out = o if out is None else out + o
            start += lk
        return out

    out = lax.map(one_block, (q_blk, fq_blk, pos_blk))
    return out.swapaxes(0, 1).reshape(n, lq, h, dh)


def trunk(x, pos0, h0_re, h0_im, mem_k, mem_v, past, p):
    n, l, _ = x.shape
    q_pos = pos0 + jnp.arange(l, dtype=jnp.int32)
    h = x
    s5_re, s5_im = [], []
    k_new = v_new = logf_new = None
    f_q = None
    groups = None
    for i in range(DEPTH):
        z = rmsnorm(h, p["norm_mix"][i]) @ p["w_in"][i]
        q_mem = z[..., D_MAIN:].reshape(n, l, MEM_HEADS, MEM_HEAD_DIM)
        y_mem = memory_attend(q_mem, mem_k[i], mem_v[i]).reshape(n, l, D_MEM)
        if i < N_A_LAYERS:
            y_main, hr, hi = s5_mixer(z[..., :D_MAIN], h0_re[i], h0_im[i],
                                      p["s5_a_re"][i], p["s5_a_im"][i], p["s5_log_step"][i],
                                      p["s5_b_re"][i], p["s5_b_im"][i], p["s5_c_re"][i], p["s5_c_im"][i],
                                      p["s5_d"][i], p["s5_w_glu"][i], p["s5_b_glu"][i])
            s5_re.append(hr)
            s5_im.append(hi)
        else:
            if i == N_A_LAYERS:
                k_new, v_new, logf_new = shared_kv(h, p["norm_kv"], p["w_kv"], p["w_f"], p["b_f"])
                if past is None:
                    f_q = jnp.cumsum(logf_new, axis=1)
                    groups = ((k_new, v_new, f_q, q_pos),)
                else:
                    k_past, v_past, logf_past = past
                    past_len = k_past.shape[1]
                    f_all = jnp.cumsum(jnp.concatenate([logf_past.astype(jnp.float32), logf_new], axis=1), axis=1)
                    f_q = f_all[:, past_len:]
                    groups = ((k_past, v_past, f_all[:, :past_len], jnp.arange(past_len, dtype=jnp.int32)),
                              (k_new, v_new, f_q, q_pos))
            q = z[..., :D_MAIN].reshape(n, l, FOX_HEADS, FOX_HEAD_DIM)
            y_main = fox_attend(q, f_q, q_pos, groups).reshape(n, l, D_MAIN)
        h = h + jnp.concatenate([y_main, y_mem], axis=-1) @ p["w_out"][i]
        h = h + sqrelu_mlp(rmsnorm(h, p["norm_mlp"][i]), p["w_up"][i], p["w_down"][i])
    y = rmsnorm(h, p["norm_final"])
    return y, jnp.stack(s5_re), jnp.stack(s5_im), k_new, v_new, logf_new


def setup_inputs(seed: int = 0) -> dict:
    key = jax.random.key(seed)
    ks = jax.random.split(key, 33)
    f32 = jnp.float32

    def normal(i, shape, scale=1.0):
        return scale * jax.random.normal(ks[i], shape, f32)

    n_pages = PAST_LEN // PAGE_SIZE
    n_used = DEC_BATCH * n_pages
    n_phys = (n_used * 5) // 4
    page_table = jax.random.permutation(ks[9], n_phys)[:n_used].reshape(DEC_BATCH, n_pages).astype(jnp.int32)
    s5_shape = (N_A_LAYERS, S5_GROUPS, S5_STATE)
    return {
        "x_prompt": normal(0, (BATCH, SEQ, D_MODEL)),
        "x_sample": normal(1, (DEC_BATCH, DEC_SEQ, D_MODEL)),
        "state_s5_re": normal(2, (N_A_LAYERS, DEC_BATCH, S5_GROUPS, S5_STATE), 0.5),
        "state_s5_im": normal(3, (N_A_LAYERS, DEC_BATCH, S5_GROUPS, S5_STATE), 0.5),
        "cache_mem_k": normal(4, (DEPTH, DEC_BATCH, MEM_TOKENS, MEM_HEADS, MEM_HEAD_DIM)),
        "cache_mem_v": normal(5, (DEPTH, DEC_BATCH, MEM_TOKENS, MEM_HEADS, MEM_HEAD_DIM)),
        "cache_k": normal(6, (n_phys, PAGE_SIZE, FOX_HEADS, FOX_HEAD_DIM)),
        "cache_v": normal(7, (n_phys, PAGE_SIZE, FOX_HEADS, FOX_HEAD_DIM)),
        "cache_logf": jax.nn.log_sigmoid(2.5 + normal(8, (n_phys, PAGE_SIZE, FOX_HEADS))),
        "page_table": page_table,
        "mem_prompt": normal(10, (BATCH, MEM_TOKENS, D_MODEL)),
        "norm_mix": 1.0 + normal(11, (DEPTH, D_MODEL), 0.1),
        "norm_mlp": 1.0 + normal(12, (DEPTH, D_MODEL), 0.1),
        "w_in": normal(13, (DEPTH, D_MODEL, D_MIX), D_MODEL ** -0.5),
        "w_out": normal(14, (DEPTH, D_MIX, D_MODEL), D_MIX ** -0.5),
        "w_up": normal(15, (DEPTH, D_MODEL, D_FF), D_MODEL ** -0.5),
        "w_down": normal(16, (DEPTH, D_FF, D_MODEL), D_FF ** -0.5),
        "w_mem_kv": normal(17, (DEPTH, D_MODEL, 2 * D_MEM), D_MODEL ** -0.5),
        "s5_a_re": -0.5 + normal(18, s5_shape, 0.01),
        "s5_a_im": math.pi * jnp.arange(S5_STATE, dtype=f32) + normal(19, s5_shape, 0.01),
        "s5_log_step": jax.random.uniform(ks[20], (N_A_LAYERS, S5_GROUPS), f32,
                                          minval=math.log(1e-3), maxval=math.log(1e-1)),
        "s5_b_re": normal(21, (N_A_LAYERS, S5_GROUPS, S5_STATE, S5_GROUP_CH), (2 * S5_GROUP_CH) ** -0.5),
        "s5_b_im": normal(22, (N_A_LAYERS, S5_GROUPS, S5_STATE, S5_GROUP_CH), (2 * S5_GROUP_CH) ** -0.5),
        "s5_c_re": normal(23, (N_A_LAYERS, S5_GROUPS, S5_GROUP_CH, S5_STATE), S5_STATE ** -0.5),
        "s5_c_im": normal(24, (N_A_LAYERS, S5_GROUPS, S5_GROUP_CH, S5_STATE), S5_STATE ** -0.5),
        "s5_d": normal(25, (N_A_LAYERS, S5_GROUPS, S5_GROUP_CH)),
        "s5_w_glu": normal(26, (N_A_LAYERS, D_MAIN, D_MAIN), D_MAIN ** -0.5),
        "s5_b_glu": normal(27, (N_A_LAYERS, D_MAIN), 0.01),
        "norm_kv": 1.0 + normal(28, (D_MODEL,), 0.1),
        "w_kv": normal(29, (D_MODEL, 2 * D_MAIN), D_MODEL ** -0.5),
        "w_f": normal(30, (D_MODEL, FOX_HEADS), D_MODEL ** -0.5),
        "b_f": jax.random.uniform(ks[31], (FOX_HEADS,), f32, minval=1.0, maxval=4.0),
        "norm_final": 1.0 + normal(32, (D_MODEL,), 0.1),
    }


def reference(x_prompt, x_sample, state_s5_re, state_s5_im, cache_mem_k, cache_mem_v,
              cache_k, cache_v, cache_logf, page_table, mem_prompt,
              norm_mix, norm_mlp, w_in, w_out, w_up, w_down, w_mem_kv,
              s5_a_re, s5_a_im, s5_log_step, s5_b_re, s5_b_im, s5_c_re, s5_c_im, s5_d,
              s5_w_glu, s5_b_glu, norm_kv, w_kv, w_f, b_f, norm_final):
    p = dict(norm_mix=norm_mix, norm_mlp=norm_mlp, w_in=w_in, w_out=w_out, w_up=w_up, w_down=w_down,
             s5_a_re=s5_a_re, s5_a_im=s5_a_im, s5_log_step=s5_log_step, s5_b_re=s5_b_re, s5_b_im=s5_b_im,
             s5_c_re=s5_c_re, s5_c_im=s5_c_im, s5_d=s5_d, s5_w_glu=s5_w_glu, s5_b_glu=s5_b_glu,
             norm_kv=norm_kv, w_kv=w_kv, w_f=w_f, b_f=b_f, norm_final=norm_final)

    n_p = x_prompt.shape[0]
    mem_kv_p = [memory_kv(mem_prompt, w_mem_kv[i]) for i in range(DEPTH)]
    p_mem_k = jnp.stack([kv[0] for kv in mem_kv_p])
    p_mem_v = jnp.stack([kv[1] for kv in mem_kv_p])
    zeros = jnp.zeros((N_A_LAYERS, n_p, S5_GROUPS, S5_STATE), jnp.float32)
    y_prompt, p_s5_re, p_s5_im, p_k, p_v, p_logf = trunk(
        x_prompt, 0, zeros, zeros, p_mem_k, p_mem_v, None, p)

    n_s = x_sample.shape[0]
    n_pages = page_table.shape[1]
    past_len = n_pages * cache_k.shape[1]
    k_past = cache_k[page_table].reshape(n_s, past_len, FOX_HEADS, FOX_HEAD_DIM)
    v_past = cache_v[page_table].reshape(n_s, past_len, FOX_HEADS, FOX_HEAD_DIM)
    logf_past = cache_logf[page_table].reshape(n_s, past_len, FOX_HEADS)
    y_sample, s_s5_re, s_s5_im, s_k, s_v, s_logf = trunk(
        x_sample, past_len, state_s5_re, state_s5_im, cache_mem_k, cache_mem_v,
        (k_past, v_past, logf_past), p)

    return (y_prompt, y_sample, p_s5_re, p_s5_im, p_mem_k, p_mem_v, p_k, p_v, p_logf,
            s_s5_re, s_s5_im, s_k, s_v, s_logf)
```

```python
import contextlib
import math
import numpy as np
import ml_dtypes
import concourse.bass as bass
import concourse.mybir as mybir
from concourse.bass_utils import run_bass_kernel_spmd

F32 = mybir.dt.float32
BF16 = mybir.dt.bfloat16
I32 = mybir.dt.int32
ALU = mybir.AluOpType
AF = mybir.ActivationFunctionType

NCORES = 8
D = 1024
MEM_T = 256
NT = 512
NH = 12
DMAIN = 768
TWO_PI = 2.0 * math.pi

U_IN = [0, 80]
U_OUT = [8, 88]
U_UP = [16, 96]
U_DOWN = [48, 128]
U_GLU = 160
U_KV = 166
N_UNITS = 178


class Sched:
    def __init__(self, nc, stack):
        self.nc = nc
        self.stack = stack
        self.engs = ["pe", "act", "dve", "pool", "sp"]
        self.ops = {e: [] for e in self.engs}
        self.sem = {e: stack.enter_context(nc.semaphore("sem_" + e)) for e in self.engs}
        self.cnt = {e: 0 for e in self.engs}
        self.epoch = {e: 0 for e in self.engs}
        self.old = []
        self.dsem = {}
        self.lastw = {}
        self.reads = {}
        self.waited = {e: {} for e in self.engs}

    def _collect(self, eng, r, w):
        waits = {}

        def add(t):
            sid, sem, val = t
            if sid not in waits or waits[sid][1] < val:
                waits[sid] = (sem, val)

        for k in list(r) + list(w):
            lw = self.lastw.get(k)
            if lw is not None:
                add(lw)
        for k in w:
            for rd in self.reads.get(k, []):
                add(rd)
        final = []
        for sid, (sem, val) in waits.items():
            if eng == "pe" and sid.startswith("pe#"):
                continue
            if self.waited[eng].get(sid, 0) >= val:
                continue
            self.waited[eng][sid] = val
            final.append((sem, val))
        return final

    def _record(self, t, r, w):
        for k in w:
            self.lastw[k] = t
            self.reads[k] = []
        for k in r:
            lst = self.reads.setdefault(k, [])
            lst.append(t)
            if len(lst) > 24:
                best = {}
                for sid, sem, val in lst:
                    if sid not in best or best[sid][2] < val:
                        best[sid] = (sid, sem, val)
                self.reads[k] = list(best.values())

    def op(self, eng, fn, r=(), w=()):
        final = self._collect(eng, r, w)
        if self.cnt[eng] >= 30000:
            self.old.append((self.sem[eng], self.cnt[eng]))
            self.epoch[eng] += 1
            self.sem[eng] = self.stack.enter_context(self.nc.semaphore(f"sem_{eng}_{self.epoch[eng]}"))
            self.cnt[eng] = 0
        self.cnt[eng] += 1
        v = self.cnt[eng]
        self.ops[eng].append((final, fn, self.sem[eng], 1))
        self._record((f"{eng}#{self.epoch[eng]}", self.sem[eng], v), r, w)

    def dma(self, q, fn, key, r=(), w=()):
        final = self._collect(q, r, w)
        if key not in self.dsem:
            self.dsem[key] = [self.stack.enter_context(self.nc.semaphore("d_" + key)), 0]
        ds = self.dsem[key]
        ds[1] += 16
        self.ops[q].append((final, fn, ds[0], 16))
        self._record(("d_" + key, ds[0], ds[1]), r, w)

    def barrier(self):
        allw = [(sem, val) for (sem, val) in self.dsem.values()] + list(self.old)
        allw += [(self.sem[e], self.cnt[e]) for e in self.engs if self.cnt[e] > 0]
        for e in self.engs:
            self.ops[e].append((list(allw), None, None, 0))

    def finish_waits(self, eng="sp"):
        final = [(sem, val) for (sem, val) in self.dsem.values()] + list(self.old)
        for e in self.engs:
            if e != eng and self.cnt[e] > 0:
                final.append((self.sem[e], self.cnt[e]))
        self.ops[eng].append((final, None, None, 0))

    def emit(self, block):
        table = {"pe": block.tensor, "act": block.scalar, "dve": block.vector,
                 "pool": block.gpsimd, "sp": block.sync}
        for e in self.engs:
            ops = self.ops[e]

            def body(engine, ops=ops):
                for waits, fn, sem, inc in ops:
                    for s, v in waits:
                        engine.wait_ge(s, v)
                    if fn is not None:
                        ins = fn(engine)
                        ins.then_inc(sem, inc)

            table[e](body)


def build_program(T, NPOOL=5120):
    nc = bass.Bass("TRN2", target_bir_lowering=False)
    NTILES = T // NT
    NBLK = T // 128

    def din(name, shape, dt=F32):
        return nc.dram_tensor(name, list(shape), dt, kind="ExternalInput").ap()

    def dout(name, shape, dt=F32):
        return nc.dram_tensor(name, list(shape), dt, kind="ExternalOutput").ap()

    def dscr(name, shape, dt):
        return nc.dram_tensor(name, list(shape), dt, kind="Internal").ap()

    x_in = din("x", [T, D])
    mem_prompt = din("mem_prompt", [MEM_T, D])
    w_mem_kv = din("w_mem_kv", [2, D, 512])
    wall = din("wall", [N_UNITS, 128, 1024])
    gall_d = din("gall", [128, 6, 8])
    bglu_d = din("bglu", [128, 6])
    s5d_d = din("s5d", [128, 6])
    wf_d = din("wf", [128, 8, 12])
    bf_d = din("bfb", [128, 12])
    a_b_d = din("a_b", [3, 128, 3072])
    a_s_d = din("a_s", [128, 3, 24])
    bpad_d = din("bpad", [2, 128, 24, 128])
    cpad_d = din("cpad", [2, 128, 24, 128])
    ident_f = din("ident_f", [128, 128])
    tri_d = din("tri_f", [128, 128])
    tcount_d = din("tcount", [128, 128])
    masks_d = din("masks", [128, 4, 512], BF16)
    selden_d = din("selden", [128, 2, 128])

    NPH = NPOOL
    xs_in = din("xs", [32, D])
    h0s_d = din("h0s", [128, 2, 24, 4])
    cmk_d = din("cmk", [2, 4, MEM_T, 256])
    cmv_d = din("cmv", [2, 4, MEM_T, 256])
    cache_k = din("cache_k", [NPH, 128, NH, 64])
    cache_v = din("cache_v", [NPH, 128, NH, 64])
    cache_logf = din("cache_logf", [NPH, 128, NH])
    ptab_d = din("ptab", [1, 512], I32)
    ptabT_d = din("ptabT", [128, 4], I32)
    iota_d = din("iota_i", [128, 1], I32)
    stri_d = din("stri_f", [128, 128])
    btri_d = din("btri32", [32, 32])
    esel_d = din("esel", [128, 4, 32])
    qmask_d = din("qmask", [32, 4, NH, 8])
    cmask_d = din("cmask", [8, NH, 8])
    ys_out = dout("ys", [32, D])
    o_sk = dout("o_sk", [32, DMAIN])
    o_sv = dout("o_sv", [32, DMAIN])
    o_slogf = dout("o_slogf", [32, NH])
    o_s5s = dout("o_s5s", [128, 2, 24, 4])
    NL1 = NTILES // 4
    hidx_d = din("hidx", [128, 4], I32)
    ohbb_d = din("ohbb", [128, 2, 4])
    y_out = dout("y", [NL1 * NT, D])
    o_mem_kv = dout("o_mem_kv", [2, MEM_T, 512])
    o_k = dout("o_k", [T, DMAIN])
    o_v = dout("o_v", [T, DMAIN])
    o_logf = dout("o_logf", [T, NH])
    o_s5 = dout("o_s5", [128, 2, 24])

    dbg_out = dout("dbg", [16, 128, NT]) if DEBUG else None
    wscr = dscr("wscr", [N_UNITS, 128, 1024], BF16)
    H1 = dscr("H1", [NTILES * 128, 8 * NT], F32)
    Fq = dscr("Fq", [NTILES * 128, 4 * NH], F32)
    KTs = dscr("KTs", [NH, 64, T], BF16)
    Vs = dscr("Vs", [NH, 128, NBLK, 128], BF16)

    with contextlib.ExitStack() as st:
        S = Sched(nc, st)

        def sb(name, shape, dt, stack=None):
            return (stack or st).enter_context(nc.sbuf_tensor("sb_" + name, list(shape), dt))

        def ps(name, shape, dt=F32):
            return st.enter_context(nc.psum_tensor(name, list(shape), dt))

        PS = [ps(f"ps{i}", [128, 512], F32) for i in range(4)]
        PSW = [ps(f"psw{i}", [128, 1024], F32) for i in range(2)]
        psrot = [0]

        def next_ps():
            i = psrot[0] % 4
            psrot[0] += 1
            return PS[i], f"ps{i}"

        evrot = [0]

        def evac_eng():
            evrot[0] += 1
            return "dve" if evrot[0] % 2 else "act"

        def copy_op(eng, out, in_, r, w):
            if eng == "act":
                S.op("act", lambda e: e.activation(out=out, in_=in_, func=AF.Copy), r=r, w=w)
            else:
                S.op(eng, lambda e: e.tensor_copy(out=out, in_=in_), r=r, w=w)

        identf = sb("identf", [128, 128], F32)
        identb = sb("identb", [128, 128], BF16)
        onesb = sb("onesb", [128, 128], BF16)
        onesf = sb("onesf", [128, 128], F32)
        trif = sb("trif", [128, 128], F32)
        masks = sb("masks", [128, 4, 512], BF16)
        selden = sb("selden_sb", [128, 2, 128], F32)
        gall = sb("gall_sb", [128, 6, 8], F32)
        bglu = sb("bglu_sb", [128, 6], F32)
        s5d = sb("s5d_sb", [128, 6], F32)
        wfb = sb("wfb", [128, 8, 12], BF16)
        bfb = sb("bfb_sb", [128, 12], F32)
        Ctab = sb("Ctab", [128, 24, 128], F32)
        Stab = sb("Stab", [128, 24, 128], F32)
        r_s = sb("r_s", [128, 24], F32)
        Bpad = sb("Bpad", [128, 2, 24, 128], BF16)
        Cpad = sb("Cpad", [128, 2, 24, 128], BF16)
        hprev = sb("hprev", [128, 2, 24], F32)
        KmT = sb("KmT", [128, 2, 2, MEM_T], BF16)
        Vmp = sb("Vmp", [128, 2, 2, 4, 128], BF16)
        onespad = sb("onespad", [128, 2, 128], BF16)
        Fneg = sb("Fneg", [128, max(NBLK, 4), NH], F32)
        idx_all = sb("idx_all", [128, 512], I32)
        ptT = sb("ptT", [128, 4], I32)
        h0s = sb("h0s_sb", [128, 2, 24, 4], F32)
        KmTs = sb("KmTs", [128, 4, 2, 2, MEM_T], BF16)
        Vmps = sb("Vmps", [128, 4, 2, 2, 4, 128], BF16)
        strif = sb("strif", [128, 128], F32)
        btri = sb("btri_sb", [32, 32], F32)
        esel = sb("esel_sb", [128, 4, 32], F32)
        qmask = sb("qmask_sb", [32, 4, NH, 8], F32)
        cmask = sb("cmask_sb", [8, NH, 8], F32)
        carry = sb("carry", [128, NH], F32)
        ZF = sb("ZF", [128, 4, NH, 65], BF16)

        def ld(dst, src, key):
            S.dma("sp", lambda e: e.dma_start(out=dst, in_=src), key, w=[key])

        ld(identf[:], ident_f[:, :], "identf")
        ld(trif[:], tri_d[:, :], "trif")
        ld(masks[:], masks_d[:, :, :], "masks")
        ld(selden[:], selden_d[:, :, :], "selden")
        ld(gall[:], gall_d[:, :, :], "gall")
        ld(bglu[:], bglu_d[:, :], "bglu")
        ld(s5d[:], s5d_d[:, :], "s5d")
        ld(bfb[:], bf_d[:, :], "bfb")
        ld(h0s[:], h0s_d[:, :, :, :], "h0s")
        ld(strif[:], stri_d[:, :], "strif")
        ld(btri[:], btri_d[:, :], "btri")
        ld(esel[:], esel_d[:, :, :], "esel")
        ld(qmask[:], qmask_d[:, :, :, :], "qmask")
        ld(cmask[:], cmask_d[:, :, :], "cmask")
        ld(ptT[:], ptabT_d[:, :], "ptT")
        hidx = sb("hidx_sb", [128, 4], I32)
        ohbb = sb("ohbb_sb", [128, 2, 4], F32)
        fq_sb = sb("fq_sb", [128, 4, NH], F32)
        FnB = sb("FnB", [128, 16, NH], F32)
        ld(hidx[:], hidx_d[:, :], "hidx")
        ld(ohbb[:], ohbb_d[:, :, :], "ohbb")
        S.op("dve", lambda e: e.tensor_copy(out=identb[:], in_=identf[:]), r=["identf"], w=["identb"])
        S.op("pool", lambda e: e.memset(onesb[:], 1.0), w=["onesb"])
        S.op("pool", lambda e: e.memset(onesf[:], 1.0), w=["onesf"])
        S.op("pool", lambda e: e.memset(onespad[:], 0.0), w=["onespad"])
        S.op("pool", lambda e: e.memset(onespad[:, 0, 0:64], 1.0), w=["onespad"])
        S.op("pool", lambda e: e.memset(onespad[:, 1, 64:128], 1.0), w=["onespad"])
        S.op("pool", lambda e: e.memset(hprev[:], 0.0), w=["hprev"])
        S.op("pool", lambda e: e.memset(carry[:], 0.0), w=["carry"])
        S.op("pool", lambda e: e.memset(ZF[:], 0.0), w=["ZF0", "ZF1", "ZF2", "ZF3"])
        S.op("pool", lambda e: e.memset(Vmp[:], 0.0), w=["Vmp"])

        with contextlib.ExitStack() as st0:
            wst = [sb(f"wst{i}", [128, 1024], F32, st0) for i in range(3)]
            wbf = [sb(f"wbf{i}", [128, 1024], BF16, st0) for i in range(3)]
            engs3 = ["act", "dve", "pool"]
            for u in range(N_UNITS):
                i = u % 3
                S.dma("sp", lambda e, u=u, i=i: e.dma_start(out=wst[i][:], in_=wall[u]), f"wst{i}", w=[f"wst{i}"])
                copy_op(engs3[i], wbf[i][:], wst[i][:], [f"wst{i}"], [f"wbf{i}"])
                S.dma("sp", lambda e, u=u, i=i: e.dma_start(out=wscr[u], in_=wbf[i][:]), f"wscr_w{i}",
                      r=[f"wbf{i}"], w=[f"wscr{u}"])
            wfst = sb("wfst", [128, 8, 12], F32, st0)
            ld(wfst[:], wf_d[:, :, :], "wfst")
            S.op("dve", lambda e: e.tensor_copy(out=wfb[:], in_=wfst[:]), r=["wfst"], w=["wfb"])

            mem_tm = sb("mem_tm", [128, 2, D], F32, st0)
            memT = sb("memT", [128, 8, MEM_T], BF16, st0)
            for t in range(2):
                S.dma("sp", lambda e, t=t: e.dma_start(out=mem_tm[:, t, :], in_=mem_prompt[t * 128:(t + 1) * 128, :]),
                      "mem_tm", w=[f"mem_tm{t}"])
            for t in range(2):
                for c4 in range(2):
                    p, pk = next_ps()
                    for cc in range(4):
                        c = c4 * 4 + cc
                        S.op("pe", lambda e, p=p, t=t, c=c, cc=cc: e.transpose(
                            out=p[:, cc * 128:(cc + 1) * 128], in_=mem_tm[:, t, c * 128:(c + 1) * 128],
                            identity=identf[:]), r=[f"mem_tm{t}", "identf"], w=[pk])
                    S.op("dve", lambda e, p=p, t=t, c4=c4: e.tensor_copy(
                        out=memT[:, c4 * 4:(c4 + 1) * 4, t * 128:(t + 1) * 128],
                        in_=p[:, :].rearrange("p (c t) -> p c t", c=4)), r=[pk], w=["memT"])
            wmst = sb("wmst", [128, 8, 512], F32, st0)
            wmbf = sb("wmbf", [128, 8, 512], BF16, st0)
            okv = sb("okv", [128, 2, 512], F32, st0)
            for i in range(2):
                S.dma("sp", lambda e, i=i: e.dma_start(
                    out=wmst[:], in_=w_mem_kv[i].rearrange("(c p) n -> p c n", p=128)), "wmst", w=["wmst"])
                S.op("act", lambda e: e.activation(out=wmbf[:], in_=wmst[:], func=AF.Copy), r=["wmst"], w=["wmbf"])
                for t in range(2):
                    p, pk = next_ps()
                    for c in range(8):
                        S.op("pe", lambda e, p=p, t=t, c=c: e.matmul(
                            p[:, :], lhsT=memT[:, c, t * 128:(t + 1) * 128], rhs=wmbf[:, c, :],
                            start=(c == 0), stop=(c == 7)), r=["memT", "wmbf"], w=[pk])
                    S.op("dve", lambda e, p=p, t=t: e.tensor_copy(out=okv[:, t, :], in_=p[:, :]),
                         r=[pk], w=[f"okv{t}"])
                    S.dma("sp", lambda e, i=i, t=t: e.dma_start(
                        out=o_mem_kv[i, t * 128:(t + 1) * 128, :], in_=okv[:, t, :]), "okv_out", r=[f"okv{t}"])
                    for h in range(4):
                        hb = (h % 2) * 64
                        S.op("pool", lambda e, i=i, t=t, h=h, hb=hb: e.tensor_copy(
                            out=Vmp[:, i, t, h, hb:hb + 64], in_=okv[:, t, 256 + h * 64:256 + (h + 1) * 64]),
                            r=[f"okv{t}"], w=["Vmp"])
                for c in range(2):
                    p, pk = next_ps()
                    for k in range(8):
                        S.op("pe", lambda e, p=p, c=c, k=k: e.matmul(
                            p[:, 0:MEM_T], lhsT=wmbf[:, k, c * 128:(c + 1) * 128], rhs=memT[:, k, :],
                            start=(k == 0), stop=(k == 7)), r=["memT", "wmbf"], w=[pk])
                    S.op("act", lambda e, p=p, i=i, c=c: e.activation(out=KmT[:, i, c, :], in_=p[:, 0:MEM_T], func=AF.Copy),
                         r=[pk], w=["KmT"])

            S.op("pool", lambda e: e.memset(Vmps[:], 0.0), w=["Vmps"])
            cm_k = sb("cm_k", [128, 2, 256], F32, st0)
            cm_v = sb("cm_v", [128, 2, 256], F32, st0)
            for sq in range(4):
                for i in range(2):
                    S.dma("sp", lambda e, sq=sq, i=i: e.dma_start(
                        out=cm_k[:], in_=cmk_d[i, sq].rearrange("(t p) f -> p t f", p=128)), "cm_k", w=["cm_k"])
                    S.dma("sp", lambda e, sq=sq, i=i: e.dma_start(
                        out=cm_v[:], in_=cmv_d[i, sq].rearrange("(t p) f -> p t f", p=128)), "cm_v", w=["cm_v"])
                    p, pk = next_ps()
                    for mt in range(2):
                        for c in range(2):
                            S.op("pe", lambda e, p=p, mt=mt, c=c: e.transpose(
                                out=p[:, (c * 2 + mt) * 128:(c * 2 + mt + 1) * 128], in_=cm_k[:, mt, c * 128:(c + 1) * 128],
                                identity=identf[:]), r=["cm_k", "identf"], w=[pk])
                    S.op("act", lambda e, p=p, sq=sq, i=i: e.activation(
                        out=KmTs[:, sq, i, :, :].rearrange("p c m -> p (c m)"), in_=p[:, :], func=AF.Copy), r=[pk], w=["KmTs"])
                    for mt in range(2):
                        for hd4 in range(4):
                            hb = (hd4 % 2) * 64
                            S.op("pool", lambda e, sq=sq, i=i, mt=mt, hd4=hd4, hb=hb: e.tensor_copy(
                                out=Vmps[:, sq, i, mt, hd4, hb:hb + 64], in_=cm_v[:, mt, hd4 * 64:(hd4 + 1) * 64]),
                                r=["cm_v"], w=["Vmps"])
            ptb = sb("ptb", [128, 512], I32, st0)
            ptf = sb("ptf", [128, 512], F32, st0)
            iot = sb("iot", [128, 1], I32, st0)
            iotf = sb("iotf", [128, 1], F32, st0)
            S.dma("sp", lambda e: e.dma_start(out=ptb[:], in_=ptab_d[0:1, :].partition_broadcast(128)), "ptb", w=["ptb"])
            ld(iot[:], iota_d[:, :], "iot")
            S.op("dve", lambda e: e.tensor_copy(out=ptf[:], in_=ptb[:]), r=["ptb"], w=["ptf"])
            S.op("dve", lambda e: e.tensor_copy(out=iotf[:], in_=iot[:]), r=["iot"], w=["iotf"])
            S.op("dve", lambda e: e.tensor_scalar(out=idx_all[:], in0=ptf[:], scalar1=128.0, scalar2=iotf[:, 0:1],
                                                  op0=ALU.mult, op1=ALU.add), r=["ptf", "iotf"], w=["idx_all"])
            S.barrier()
        with contextlib.ExitStack() as st0:
            TB = [sb(f"tb{i}", [128, 3072], F32, st0) for i in range(8)]
            TBi = TB[7][:].bitcast(I32)
            for i in range(3):
                S.dma("sp", lambda e, i=i: e.dma_start(out=TB[i][:], in_=a_b_d[i]), f"tb{i}", w=[f"tb{i}"])
            are, aim, dtb = TB[0], TB[1], TB[2]

            def dv(fn, r, w, eng="dve"):
                S.op(eng, fn, r=r, w=w)

            def sin_reduced(out, in_, ki, kf, keys_r, key_w, ikey, fkey):
                dv(lambda e: e.tensor_scalar(out=ki, in0=in_, scalar1=1.0 / TWO_PI, scalar2=None, op0=ALU.mult),
                   keys_r, [ikey])
                dv(lambda e: e.tensor_copy(out=kf, in_=ki), [ikey], [fkey])
                dv(lambda e: e.scalar_tensor_tensor(out=out, in0=kf, scalar=-TWO_PI, in1=in_, op0=ALU.mult, op1=ALU.add),
                   [fkey] + keys_r, [key_w])
                dv(lambda e: e.tensor_scalar(out=out, in0=out, scalar1=3.141592, scalar2=-3.141592, op0=ALU.min, op1=ALU.max),
                   [key_w], [key_w])
                S.op("act", lambda e: e.activation(out=out, in_=out, func=AF.Sin), r=[key_w], w=[key_w])

            S.op("act", lambda e: e.activation(out=dtb[:], in_=dtb[:], func=AF.Exp), r=["tb2"], w=["tb2"])
            dv(lambda e: e.tensor_tensor(out=TB[3][:], in0=are[:], in1=dtb[:], op=ALU.mult), ["tb0", "tb2"], ["tb3"])
            S.op("act", lambda e: e.activation(out=TB[3][:], in_=TB[3][:], func=AF.Exp), r=["tb3"], w=["tb3"])
            dv(lambda e: e.tensor_tensor(out=TB[4][:], in0=aim[:], in1=dtb[:], op=ALU.mult), ["tb1", "tb2"], ["tb4"])
            sin_reduced(TB[5][:], TB[4][:], TBi, TB[7][:], ["tb4"], "tb5", "tb7", "tb7")
            dv(lambda e: e.tensor_scalar(out=TB[4][:], in0=TB[4][:], scalar1=math.pi / 2, scalar2=None, op0=ALU.add),
               ["tb4"], ["tb4"])
            sin_reduced(TB[6][:], TB[4][:], TBi, TB[7][:], ["tb4"], "tb6", "tb7", "tb7")
            dv(lambda e: e.tensor_tensor(out=TB[6][:], in0=TB[6][:], in1=TB[3][:], op=ALU.mult), ["tb6", "tb3"], ["tb6"])
            dv(lambda e: e.tensor_scalar(out=TB[6][:], in0=TB[6][:], scalar1=-1.0, scalar2=None, op0=ALU.add), ["tb6"], ["tb6"])
            dv(lambda e: e.tensor_tensor(out=TB[5][:], in0=TB[5][:], in1=TB[3][:], op=ALU.mult), ["tb5", "tb3"], ["tb5"])
            dv(lambda e: e.tensor_tensor(out=TB[3][:], in0=are[:], in1=are[:], op=ALU.mult), ["tb0"], ["tb3"])
            dv(lambda e: e.tensor_tensor(out=TB[4][:], in0=aim[:], in1=aim[:], op=ALU.mult), ["tb1"], ["tb4"])
            dv(lambda e: e.tensor_tensor(out=TB[3][:], in0=TB[3][:], in1=TB[4][:], op=ALU.add), ["tb3", "tb4"], ["tb3"])
            dv(lambda e: e.reciprocal(out=TB[3][:], in_=TB[3][:]), ["tb3"], ["tb3"])
            dv(lambda e: e.tensor_tensor(out=TB[4][:], in0=TB[6][:], in1=are[:], op=ALU.mult), ["tb6", "tb0"], ["tb4"])
            dv(lambda e: e.tensor_tensor(out=TB[7][:], in0=TB[5][:], in1=aim[:], op=ALU.mult), ["tb5", "tb1"], ["tb7"])
            dv(lambda e: e.tensor_tensor(out=TB[4][:], in0=TB[4][:], in1=TB[7][:], op=ALU.add), ["tb4", "tb7"], ["tb4"])
            dv(lambda e: e.tensor_tensor(out=TB[4][:], in0=TB[4][:], in1=TB[3][:], op=ALU.mult), ["tb4", "tb3"], ["tb4"])
            dv(lambda e: e.tensor_tensor(out=TB[7][:], in0=TB[5][:], in1=are[:], op=ALU.mult), ["tb5", "tb0"], ["tb7"])
            dv(lambda e: e.tensor_tensor(out=TB[2][:], in0=TB[6][:], in1=aim[:], op=ALU.mult), ["tb6", "tb1"], ["tb2"])
            dv(lambda e: e.tensor_tensor(out=TB[7][:], in0=TB[7][:], in1=TB[2][:], op=ALU.subtract), ["tb7", "tb2"], ["tb7"])
            dv(lambda e: e.tensor_tensor(out=TB[7][:], in0=TB[7][:], in1=TB[3][:], op=ALU.mult), ["tb7", "tb3"], ["tb7"])
            crb, cib = TB[4], TB[7]
            bre, bim = TB[0], TB[1]
            S.dma("sp", lambda e: e.dma_start(out=bre[:], in_=bpad_d[0].rearrange("p j n -> p (j n)")), "tb0", w=["tb0"])
            S.dma("sp", lambda e: e.dma_start(out=bim[:], in_=bpad_d[1].rearrange("p j n -> p (j n)")), "tb1", w=["tb1"])
            dv(lambda e: e.tensor_tensor(out=TB[2][:], in0=crb[:], in1=bre[:], op=ALU.mult), ["tb4", "tb0"], ["tb2"])
            dv(lambda e: e.tensor_tensor(out=TB[3][:], in0=cib[:], in1=bim[:], op=ALU.mult), ["tb7", "tb1"], ["tb3"])
            dv(lambda e: e.tensor_tensor(out=Bpad[:, 0, :, :].rearrange("p j n -> p (j n)"), in0=TB[2][:], in1=TB[3][:],
                                         op=ALU.subtract), ["tb2", "tb3"], ["Bpad0"])
            dv(lambda e: e.tensor_tensor(out=TB[5][:], in0=crb[:], in1=bim[:], op=ALU.mult), ["tb4", "tb1"], ["tb5"])
            dv(lambda e: e.tensor_tensor(out=TB[6][:], in0=cib[:], in1=bre[:], op=ALU.mult), ["tb7", "tb0"], ["tb6"])
            dv(lambda e: e.tensor_tensor(out=Bpad[:, 1, :, :].rearrange("p j n -> p (j n)"), in0=TB[5][:], in1=TB[6][:],
                                         op=ALU.add), ["tb5", "tb6"], ["Bpad1"])
            S.dma("sp", lambda e: e.dma_start(out=TB[2][:], in_=cpad_d[0].rearrange("p j n -> p (j n)")), "tb2",
                  r=["tb2"], w=["tb2"])
            S.dma("sp", lambda e: e.dma_start(out=TB[3][:], in_=cpad_d[1].rearrange("p j n -> p (j n)")), "tb3",
                  r=["tb3"], w=["tb3"])
            dv(lambda e: e.tensor_copy(out=Cpad[:, 0, :, :].rearrange("p j n -> p (j n)"), in_=TB[2][:]), ["tb2"], ["Cpad0"])
            dv(lambda e: e.tensor_scalar(out=Cpad[:, 1, :, :].rearrange("p j n -> p (j n)"), in0=TB[3][:], scalar1=-1.0,
                                         scalar2=None, op0=ALU.mult), ["tb3"], ["Cpad1"])

            a_s = sb("a_s_sb", [128, 3, 24], F32, st0)
            th_s = sb("th_s", [128, 24], F32, st0)
            tcount = sb("tcount_sb", [128, 128], F32, st0)
            ld(a_s[:], a_s_d[:, :, :], "a_s")
            ld(tcount[:], tcount_d[:, :], "tcount")
            S.op("act", lambda e: e.activation(out=a_s[:, 2, :], in_=a_s[:, 2, :], func=AF.Exp), r=["a_s"], w=["a_s"])
            dv(lambda e: e.tensor_tensor(out=r_s[:], in0=a_s[:, 0, :], in1=a_s[:, 2, :], op=ALU.mult), ["a_s"], ["r_s"])
            S.op("act", lambda e: e.activation(out=r_s[:], in_=r_s[:], func=AF.Exp), r=["r_s"], w=["r_s"])
            dv(lambda e: e.tensor_tensor(out=th_s[:], in0=a_s[:, 1, :], in1=a_s[:, 2, :], op=ALU.mult), ["a_s"], ["th_s"])
            ang = TB[0]
            ang3 = ang[:].rearrange("p (j t) -> p j t", j=24)
            for j in range(24):
                dv(lambda e, j=j: e.tensor_scalar(out=ang3[:, j, :], in0=tcount[:], scalar1=th_s[:, j:j + 1], scalar2=None,
                                                  op0=ALU.mult), ["tcount", "th_s", "tb0"], ["tb0"])
            sin_reduced(Stab[:, :, :].rearrange("p j t -> p (j t)"), ang[:], TBi, TB[7][:], ["tb0"], "Stab", "tb7", "tb7")
            dv(lambda e: e.tensor_scalar(out=ang[:], in0=ang[:], scalar1=math.pi / 2, scalar2=None, op0=ALU.add),
               ["tb0"], ["tb0"])
            sin_reduced(Ctab[:, :, :].rearrange("p j t -> p (j t)"), ang[:], TBi, TB[7][:], ["tb0"], "Ctab", "tb7", "tb7")
            S.barrier()
        carry_s = sb("carry_s", [128, 4, NH], F32)
        s5so = sb("s5so", [128, 2, 24, 4], F32)
        Qblk = sb("Qblk", [128, 6, 16], BF16)
        KTnew = sb("KTnew", [128, 6, 32], BF16)
        vnbp = sb("vnbp", [32, DMAIN], BF16)
        FnegN = sb("FnegN", [8, 4, NH], F32)
        Ftb = sb("Ftb", [128, NH, 8], F32)
        h = sb("h", [128, 8, NT], F32)
        xn = sb("xn", [128, 8, NT], BF16)
        z = sb("z", [128, 8, NT], BF16)
        ymix = sb("ymix", [128, 8, NT], BF16)
        hid = sb("hid", [128, 32, NT], BF16)
        wbufs = [sb(f"wb{i}", [128, 8, 128], BF16) for i in range(6)]
        stat = sb("stat", [128, NT], F32)
        tmpf = [sb(f"tmpf{i}", [128, NT], F32) for i in range(3)]
        pT = [sb(f"pT{i}", [128, NT], BF16) for i in range(3)]
        lf = sb("lf", [128, 4, NH], F32)
        hflat = hid[:].rearrange("p a b -> p (a b)")
        hidf = hflat.bitcast(F32)

        def hchunks(a, b_, parts=128):
            return hid[0:parts, a:b_, :].rearrange("p a b -> p (a b)")

        xtm = hidf[:, 0:4096].rearrange("p (a b) -> p a b", a=4)
        yfm = hidf[:, 4096:8192].rearrange("p (k n) -> p k n", k=8)
        W6 = [hidf[:, i * 768:(i + 1) * 768].rearrange("p (j t) -> p j t", j=6) for i in range(4)]
        hre = hchunks(12, 18).rearrange("p (j t) -> p j t", j=24)
        him = hchunks(18, 24).rearrange("p (j t) -> p j t", j=24)
        QT = hid[0:65, 0:12, :]
        Vb = [hchunks(12, 16).rearrange("p (b d) -> p b d", b=16), hchunks(24, 28).rearrange("p (b d) -> p b d", b=16)]
        KTb = [hchunks(16, 20, 65), hchunks(20, 24, 65)]
        kvt = hidf[:, 0:1536]
        vaug = hchunks(6, 18).rearrange("p (s h d) -> p s h d", s=4, h=NH)
        ktt = hid[0:64, 18, :]
        Kpg = [hidf[:, 0:768], hidf[:, 768:1536]]
        Vpg = [hidf[:, 1536:2304], hidf[:, 2304:3072]]
        KTp = [hflat[:, 6144:6912].rearrange("p (c k) -> p c k", c=6), hflat[:, 6912:7680].rearrange("p (c k) -> p c k", c=6)]
        Vpb = [hflat[:, 7680:8448], hflat[:, 8448:9216]]
        lfpg = hidf[:, 4608:6144].rearrange("p (s h) -> p s h", h=NH)
        FnT = hidf[:, 6144:7680].rearrange("p (h n) -> p h n", h=NH)
        VnewS = hid[0:8, 30, :].rearrange("p (a b) -> p a b", a=1)[:, 0, :]
        VnewS = hflat[0:8, 15360:16128]
        HIDKEYS = [f"hid{m}" for m in range(32)]
        GROUPS = {
            "satt": ["Kpg0", "Kpg1", "Vpg0", "Vpg1", "KTp0", "KTp1", "Vpb0", "Vpb1", "lfpg", "FnT", "VnewS"],
            "xtm": ["xtm"], "sq": ["sq"], "yfm": ["yfm"], "hid": HIDKEYS,
            "s5": ["w6_0", "w6_1", "w6_2", "w6_3", "hre", "him"],
            "att": ["QT", "KTb0", "KTb1", "Vb0", "Vb1"],
            "kv": [f"kvt{n}" for n in range(12)] + [f"vaug{n}" for n in range(4)] + ["ktt"],
        }

        def enter(*groups):
            tgt = [k for g in groups for k in GROUPS[g]]
            best = {}
            for g, keys in GROUPS.items():
                if g in groups:
                    continue
                for k in keys:
                    lst = list(S.reads.get(k, []))
                    if S.lastw.get(k) is not None:
                        lst.append(S.lastw[k])
                    for sid, sem, val in lst:
                        if sid not in best or best[sid][2] < val:
                            best[sid] = (sid, sem, val)
            for k in tgt:
                S.reads.setdefault(k, []).extend(best.values())

        wrot = [0]

        def getw(u):
            i = wrot[0] % 6
            wrot[0] += 1
            S.dma("sp", lambda e, u=u, i=i: e.dma_start(out=wbufs[i][:].rearrange("p k n -> p (k n)"), in_=wscr[u]),
                  f"wb{i}", r=[f"wscr{u}"], w=[f"wb{i}"])
            return wbufs[i], f"wb{i}"

        def rmsnorm(N, gidx, out_fn, out_keys, hkeys):
            enter("sq")
            sq = hid[:, 0:8, 0:N]
            for k in range(8):
                S.op("act", lambda e, k=k: e.activation(out=sq[:, k, :], in_=h[:, k, 0:N], func=AF.Square),
                     r=[hkeys[k]], w=["sq"])
            p, pk = next_ps()
            for k in range(8):
                S.op("pe", lambda e, p=p, k=k: e.matmul(p[:, 0:N], lhsT=onesb[:], rhs=sq[:, k, :],
                                                         start=(k == 0), stop=(k == 7)), r=["sq", "onesb"], w=[pk])
            S.op("act", lambda e, p=p: e.activation(out=stat[:, 0:N], in_=p[:, 0:N], func=AF.Sqrt, bias=1e-6,
                                                    scale=1.0 / D), r=[pk], w=["stat"])
            S.op("dve", lambda e: e.reciprocal(out=stat[:, 0:N], in_=stat[:, 0:N]), r=["stat"], w=["stat"])
            for k in range(8):
                S.op("dve", lambda e, k=k: e.scalar_tensor_tensor(
                    out=out_fn(k), in0=h[:, k, 0:N], scalar=gall[:, gidx, k:k + 1], in1=stat[:, 0:N],
                    op0=ALU.mult, op1=ALU.mult), r=[hkeys[k], "stat", "gall"], w=[out_keys[k]])

        dbgbuf = sb("dbgbuf", [128, NT], F32) if DEBUG else None
        dbg_names = []

        def dump(name, ap, keys, n=NT):
            if not DEBUG or len(dbg_names) >= 16:
                return
            i = len(dbg_names)
            dbg_names.append(name)
            S.op("dve", lambda e: e.tensor_copy(out=dbgbuf[:, 0:n], in_=ap), r=keys, w=["dbgbuf"])
            S.dma("sp", lambda e: e.dma_start(out=dbg_out[i, :, 0:n], in_=dbgbuf[:, 0:n]), "dbg", r=["dbgbuf"])

        DBG_NAMES.clear()
        DBG_NAMES.append(dbg_names)
        HK = [f"h{k}" for k in range(8)]
        XK = [f"xn{k}" for k in range(8)]
        ZK = [f"z{k}" for k in range(8)]
        YK = [f"ym{k}" for k in range(8)]

        def proj_fm(ubase, Kc, m_list, src_fn, src_keys, N, evac):
            nkq = (Kc + 7) // 8
            for m in m_list:
                p, pk = next_ps()
                for kq in range(nkq):
                    wb, wk = getw(ubase + m * nkq + kq)
                    kn = min(8, Kc - kq * 8)
                    for kk in range(kn):
                        k = kq * 8 + kk
                        S.op("pe", lambda e, p=p, wb=wb, kk=kk, k=k: e.matmul(
                            p[:, 0:N], lhsT=wb[:, kk, :], rhs=src_fn(k), start=(k == 0), stop=(k == Kc - 1)),
                            r=[wk, src_keys[k]], w=[pk])
                evac(m, p, pk)

        def mem_attention(N, q0, km_fn, vm_fn, kkeys):
            for hc in range(2):
                pn, pnk = PSW[0][:, 0:512], "psw0"
                pd, pdk = PSW[1][:, 0:512], "psw1"
                first = True
                cnt = 0
                for hh in range(2):
                    hd = hc * 2 + hh
                    hb = hh * 64
                    for mt in range(2):
                        p, pk = next_ps()
                        S.op("pe", lambda e, p=p, hb=hb, mt=mt, hc=hc: e.matmul(
                            p[:, 0:N], lhsT=km_fn(hb, hc, mt),
                            rhs=z[hb:hb + 64, 6 + hc, q0:q0 + N], start=True, stop=True), r=kkeys + [ZK[6 + hc]], w=[pk])
                        pt = pT[cnt % 3]
                        ptk = f"pT{cnt % 3}"
                        S.op("act", lambda e, p=p, pt=pt: e.activation(out=pt[:, 0:N], in_=p[:, 0:N], func=AF.Exp,
                                                                       scale=0.125), r=[pk], w=[ptk])
                        last = (hh == 1 and mt == 1)
                        S.op("pe", lambda e, pn=pn, pt=pt, mt=mt, hd=hd, first=first, last=last: e.matmul(
                            pn[:, 0:N], lhsT=vm_fn(mt, hd), rhs=pt[:, 0:N], start=first, stop=last),
                            r=kkeys + [ptk], w=[pnk])
                        S.op("pe", lambda e, pd=pd, pt=pt, hh=hh, first=first, last=last: e.matmul(
                            pd[:, 0:N], lhsT=onespad[:, hh, :], rhs=pt[:, 0:N], start=first, stop=last),
                            r=["onespad", ptk], w=[pdk])
                        first = False
                        cnt += 1
                S.op("dve", lambda e, pd=pd: e.reciprocal(out=tmpf[0][:, 0:N], in_=pd[:, 0:N]), r=[pdk], w=["tmpf0"])
                S.op("dve", lambda e, pn=pn, hc=hc: e.tensor_tensor(out=ymix[:, 6 + hc, q0:q0 + N], in0=pn[:, 0:N],
                                                                    in1=tmpf[0][:, 0:N], op=ALU.mult),
                     r=[pnk, "tmpf0"], w=[YK[6 + hc]])

        s5_dumped = []

        def s5_layer(N, L, nseg, init_fn, fin_fn, skeys):
            enter("s5")
            CH = nseg * L
            zgk = [f"zg{c}" for c in range(6)]
            for ch in range(N // CH):
                c0 = ch * CH
                for grp in range(4):
                    j0 = grp * 6
                    bre, bim = PSW[0], PSW[1]
                    for jj in range(6):
                        j = j0 + jj
                        S.op("pe", lambda e, j=j, jj=jj, c0=c0: e.matmul(
                            bre[:, jj * CH:(jj + 1) * CH], lhsT=Bpad[:, 0, j, :], rhs=z[:, j // 4, c0:c0 + CH],
                            start=True, stop=True), r=["Bpad0", ZK[j // 4]], w=["psw0"])
                        S.op("pe", lambda e, j=j, jj=jj, c0=c0: e.matmul(
                            bim[:, jj * CH:(jj + 1) * CH], lhsT=Bpad[:, 1, j, :], rhs=z[:, j // 4, c0:c0 + CH],
                            start=True, stop=True), r=["Bpad1", ZK[j // 4]], w=["psw1"])
                    br3 = bre[:, 0:6 * CH].rearrange("p (j t) -> p j t", j=6)
                    bi3 = bim[:, 0:6 * CH].rearrange("p (j t) -> p j t", j=6)
                    w0, w1, w2, w3 = [w[:, :, 0:CH] for w in W6]
                    Cv = Ctab[:, j0:j0 + 6, 0:L]
                    Sv = Stab[:, j0:j0 + 6, 0:L]
                    SG = [(sg * L, (sg + 1) * L) for sg in range(nseg)]
                    for (s0, s1) in SG:
                        S.op("dve", lambda e, Cv=Cv, s0=s0, s1=s1: e.tensor_tensor(out=w0[:, :, s0:s1], in0=br3[:, :, s0:s1], in1=Cv, op=ALU.mult), r=["psw0", "Ctab"], w=["w6_0"])
                        S.op("dve", lambda e, Sv=Sv, s0=s0, s1=s1: e.tensor_tensor(out=w1[:, :, s0:s1], in0=bi3[:, :, s0:s1], in1=Sv, op=ALU.mult), r=["psw1", "Stab"], w=["w6_1"])
                        S.op("dve", lambda e, Cv=Cv, s0=s0, s1=s1: e.tensor_tensor(out=w2[:, :, s0:s1], in0=bi3[:, :, s0:s1], in1=Cv, op=ALU.mult), r=["psw1", "Ctab"], w=["w6_2"])
                        S.op("dve", lambda e, Sv=Sv, s0=s0, s1=s1: e.tensor_tensor(out=w3[:, :, s0:s1], in0=br3[:, :, s0:s1], in1=Sv, op=ALU.mult), r=["psw0", "Stab"], w=["w6_3"])
                    S.op("pool", lambda e: e.tensor_tensor(out=w0, in0=w0, in1=w1, op=ALU.add), r=["w6_0", "w6_1"], w=["w6_0"])
                    S.op("pool", lambda e: e.tensor_tensor(out=w2, in0=w2, in1=w3, op=ALU.subtract), r=["w6_2", "w6_3"], w=["w6_2"])
                    if False:
                        dump("bu_re", bre[:, 0:128], ["psw0"], 128)
                        dump("ctab", Ctab[:, 0, :], ["Ctab"], 128)
                        dump("stab", Stab[:, 0, :], ["Stab"], 128)
                        dump("gr", w0[:, 0, :], ["w6_0"], 128)
                        dump("bu_im", bim[:, 0:128], ["psw1"], 128)
                        dump("biS", w1[:, 0, :], ["w6_1"], 128)
                        dump("gi", w2[:, 0, :], ["w6_2"], 128)
                        dump("r_s", r_s[:, :], ["r_s"], 24)
                    for jj in range(6):
                        j = j0 + jj
                        for sg, (s0, s1) in enumerate(SG):
                            S.op("dve", lambda e, j=j, jj=jj, s0=s0, s1=s1, sg=sg: e.tensor_tensor_scan(
                                out=w0[:, jj, s0:s1], data0=r_s[:, j:j + 1].broadcast_to([128, L]), data1=w0[:, jj, s0:s1],
                                initial=init_fn(0, j, sg), op0=ALU.mult, op1=ALU.add), r=["w6_0", "r_s"] + skeys, w=["w6_0"])
                            S.op("dve", lambda e, j=j, jj=jj, s0=s0, s1=s1, sg=sg: e.tensor_tensor_scan(
                                out=w2[:, jj, s0:s1], data0=r_s[:, j:j + 1].broadcast_to([128, L]), data1=w2[:, jj, s0:s1],
                                initial=init_fn(1, j, sg), op0=ALU.mult, op1=ALU.add), r=["w6_2", "r_s"] + skeys, w=["w6_2"])
                    if False:
                        dump("sr", w0[:, 0, :], ["w6_0"], 128)
                    hre_o, him_o = hre[:, j0:j0 + 6, 0:CH], him[:, j0:j0 + 6, 0:CH]
                    for (s0, s1) in SG:
                        S.op("dve", lambda e, Cv=Cv, s0=s0, s1=s1: e.tensor_tensor(out=w1[:, :, s0:s1], in0=w0[:, :, s0:s1], in1=Cv, op=ALU.mult), r=["w6_0", "Ctab"], w=["w6_1"])
                        S.op("pool", lambda e, Sv=Sv, s0=s0, s1=s1: e.tensor_tensor(out=w3[:, :, s0:s1], in0=w2[:, :, s0:s1], in1=Sv, op=ALU.mult), r=["w6_2", "Stab"], w=["w6_3"])
                    S.op("pool", lambda e, o=hre_o: e.tensor_tensor(out=o, in0=w1, in1=w3, op=ALU.subtract),
                         r=["w6_1", "w6_3"], w=["hre"])
                    for sg, (s0, s1) in enumerate(SG):
                        S.op("dve", lambda e, o=fin_fn(0, j0, sg), s1=s1: e.tensor_tensor(out=o, in0=w1[:, :, s1 - 1], in1=w3[:, :, s1 - 1], op=ALU.subtract),
                             r=["w6_1", "w6_3"], w=skeys)
                    for (s0, s1) in SG:
                        S.op("dve", lambda e, Sv=Sv, s0=s0, s1=s1: e.tensor_tensor(out=w1[:, :, s0:s1], in0=w0[:, :, s0:s1], in1=Sv, op=ALU.mult), r=["w6_0", "Stab"], w=["w6_1"])
                        S.op("pool", lambda e, Cv=Cv, s0=s0, s1=s1: e.tensor_tensor(out=w3[:, :, s0:s1], in0=w2[:, :, s0:s1], in1=Cv, op=ALU.mult), r=["w6_2", "Ctab"], w=["w6_3"])
                    S.op("pool", lambda e, o=him_o: e.tensor_tensor(out=o, in0=w1, in1=w3, op=ALU.add),
                         r=["w6_1", "w6_3"], w=["him"])
                    for sg, (s0, s1) in enumerate(SG):
                        S.op("dve", lambda e, o=fin_fn(1, j0, sg), s1=s1: e.tensor_tensor(out=o, in0=w1[:, :, s1 - 1], in1=w3[:, :, s1 - 1], op=ALU.add),
                             r=["w6_1", "w6_3"], w=skeys)
                if False:
                    dump("hre", hre[:, 0, :], ["hre"], 128)
                    s5_dumped.append(1)
                for c in range(6):
                    p, pk = next_ps()
                    n = 0
                    for j in range(4 * c, 4 * c + 4):
                        S.op("pe", lambda e, p=p, j=j, n=n: e.matmul(p[:, 0:CH], lhsT=Cpad[:, 0, j, :], rhs=hre[:, j, 0:CH],
                                                                    start=(n == 0), stop=False), r=["Cpad0", "hre"], w=[pk])
                        S.op("pe", lambda e, p=p, j=j, n=n: e.matmul(p[:, 0:CH], lhsT=Cpad[:, 1, j, :], rhs=him[:, j, 0:CH],
                                                                    start=False, stop=(n == 3)), r=["Cpad1", "him"], w=[pk])
                        n += 1
                    t0, t1 = tmpf[0][:, 0:CH], tmpf[1][:, 0:CH]
                    S.op("dve", lambda e, p=p, c=c, c0=c0: e.scalar_tensor_tensor(
                        out=t0, in0=z[:, c, c0:c0 + CH], scalar=s5d[:, c:c + 1], in1=p[:, 0:CH], op0=ALU.mult, op1=ALU.add),
                        r=[pk, ZK[c], "s5d"], w=["tmpf0"])
                    S.op("pool", lambda e: e.tensor_tensor(out=t1, in0=t0, in1=t0, op=ALU.mult), r=["tmpf0"], w=["tmpf1"])
                    S.op("pool", lambda e: e.tensor_scalar(out=t1, in0=t1, scalar1=0.044715, scalar2=1.0, op0=ALU.mult,
                                                           op1=ALU.add), r=["tmpf1"], w=["tmpf1"])
                    S.op("pool", lambda e: e.tensor_tensor(out=t1, in0=t1, in1=t0, op=ALU.mult), r=["tmpf1", "tmpf0"], w=["tmpf1"])
                    S.op("act", lambda e: e.activation(out=t1, in_=t1, func=AF.Sigmoid, scale=2.0 * math.sqrt(2.0 / math.pi)),
                         r=["tmpf1"], w=["tmpf1"])
                    S.op("dve", lambda e, c=c, c0=c0: e.tensor_tensor(out=xn[:, c, c0:c0 + CH], in0=t0, in1=t1, op=ALU.mult),
                         r=["tmpf0", "tmpf1"], w=[XK[c]])
            def ev(m, p, pk):
                S.op("act", lambda e: e.activation(out=tmpf[2][:, 0:N], in_=p[:, 0:N], func=AF.Sigmoid, bias=bglu[:, m:m + 1],
                                                   scale=1.0), r=[pk, "bglu"], w=["tmpf2"])
                S.op("dve", lambda e: e.tensor_tensor(out=ymix[:, m, 0:N], in0=xn[:, m, 0:N], in1=tmpf[2][:, 0:N], op=ALU.mult),
                     r=["tmpf2", XK[m]], w=[YK[m]])
            proj_fm(U_GLU, 6, range(6), lambda k: xn[:, k, 0:N], XK, N, ev)

        def fox_layer(i1):
            N = NT
            enter("att")
            S.dma("pool", lambda e: e.indirect_dma_start(
                out=fq_sb[:].rearrange("p s h -> p (s h)"), out_offset=None, in_=Fq,
                in_offset=bass.IndirectOffsetOnAxis(ap=hidx[:, i1:i1 + 1], axis=0)), "fq_g",
                r=["hidx"] + [f"Fq{tt}" for tt in range(NTILES)], w=["fq_sb"])
            for sbk in range(4):
                S.op("dve", lambda e, sbk=sbk: e.tensor_copy(out=ZF[:, sbk, :, 64], in_=fq_sb[:, sbk, :]), r=["fq_sb"],
                     w=[f"ZF{sbk}"])
            for jj in range(4):
                S.op("dve", lambda e, jj=jj: e.tensor_scalar(
                    out=FnB[:, jj * 4:(jj + 1) * 4, :], in0=Fneg[:, 16 * i1 + jj * 4:16 * i1 + (jj + 1) * 4, :],
                    scalar1=ohbb[:, 1, jj:jj + 1], scalar2=None, op0=ALU.add), r=["Fneg", "ohbb"], w=["FnB"])
            for i in range(2):
                S.op("pool", lambda e, i=i: e.memset(KTb[i][64:65, :], 1.0), w=[f"KTb{i}"])
            for hd in range(NH):
                c, hb = hd // 2, (hd % 2) * 64
                if hb == 0:
                    S.op("act", lambda e, hd=hd, c=c: e.activation(out=QT[0:64, hd, :], in_=z[0:64, c, :], func=AF.Copy,
                                                                   scale=0.125), r=[ZK[c]], w=["QT"])
                else:
                    S.dma("sp", lambda e, hd=hd, c=c: e.dma_start(out=QT[0:64, hd, :], in_=z[64:128, c, :]), "QTmv",
                          r=[ZK[c]], w=["QT"])
                    S.op("act", lambda e, hd=hd: e.activation(out=QT[0:64, hd, :], in_=QT[0:64, hd, :], func=AF.Copy,
                                                              scale=0.125), r=["QT"], w=["QT"])
            for hd in range(NH):
                p, pk = next_ps()
                for sbk in range(4):
                    S.op("pe", lambda e, p=p, sbk=sbk, hd=hd: e.matmul(
                        p[0:65, sbk * 128:(sbk + 1) * 128], lhsT=ZF[:, sbk, hd, :], rhs=identb[:], start=True, stop=True),
                        r=[f"ZF{sbk}", "identb"], w=[pk])
                S.op("dve", lambda e, p=p, hd=hd: e.tensor_copy(out=QT[64:65, hd, :], in_=p[64:65, :]), r=[pk], w=["QT"])
            nkb = 16 * i1 + 16
            for hd in range(NH):
                po, pok = PSW[0][:, 0:512], "psw0"
                npieces = (nkb + 15) // 16
                cnt = 0
                hh = hd % 2
                hb = hh * 64
                for pc in range(npieces):
                    kb0 = pc * 16
                    nb = min(16, nkb - kb0)
                    bi = (hd * npieces + pc) % 2
                    tiles_needed = sorted(set((kb0 + i) // 4 for i in range(nb)))
                    S.dma("sp", lambda e, hd=hd, kb0=kb0, nb=nb, bi=bi: e.dma_start(
                        out=KTb[bi][0:64, 0:nb * 128], in_=KTs[hd, :, kb0 * 128:(kb0 + nb) * 128]), f"KTb{bi}",
                        r=[f"KTs{tt}" for tt in tiles_needed], w=[f"KTb{bi}"])
                    S.dma("act", lambda e, hd=hd, kb0=kb0, nb=nb, bi=bi: e.dma_start(
                        out=Vb[bi][:, 0:nb, :], in_=Vs[hd, :, kb0:kb0 + nb, :]), f"Vb{bi}",
                        r=[f"Vs{tt}" for tt in tiles_needed], w=[f"Vb{bi}"])
                    for i in range(nb):
                        kb = kb0 + i
                        zone = kb - 16 * i1
                        q0 = 0
                        p, pk = next_ps()
                        S.op("pe", lambda e, p=p, bi=bi, i=i, hd=hd: e.matmul(
                            p[:, 0:N], lhsT=KTb[bi][:, i * 128:(i + 1) * 128], rhs=QT[:, hd, 0:N], start=True, stop=True),
                            r=[f"KTb{bi}", "QT"], w=[pk])
                        pt = pT[cnt % 3]
                        ptk = f"pT{cnt % 3}"
                        cnt += 1
                        if zone >= 0:
                            jj, d = zone // 4, zone % 4
                            S.op("dve", lambda e, p=p, d=d, jj=jj: e.scalar_tensor_tensor(
                                out=tmpf[2][:, 0:N], in0=masks[:, d, 0:N], scalar=ohbb[:, 0, jj:jj + 1], in1=p[:, 0:N],
                                op0=ALU.mult, op1=ALU.add), r=[pk, "masks", "ohbb"], w=["tmpf2"])
                            S.op("act", lambda e, pt=pt, zone=zone, hd=hd: e.activation(
                                out=pt[:, 0:N], in_=tmpf[2][:, 0:N], func=AF.Exp, bias=FnB[:, zone, hd:hd + 1], scale=1.0),
                                r=["tmpf2", "FnB"], w=[ptk])
                        else:
                            S.op("act", lambda e, p=p, pt=pt, kb=kb, hd=hd: e.activation(
                                out=pt[:, 0:N], in_=p[:, 0:N], func=AF.Exp, bias=Fneg[:, kb, hd:hd + 1], scale=1.0),
                                r=[pk, "Fneg"], w=[ptk])
                        S.op("pe", lambda e, po=po, pt=pt, bi=bi, i=i, kb=kb, q0=q0: e.matmul(
                            po[:, q0:N], lhsT=Vb[bi][:, i, :], rhs=pt[:, q0:N], start=(kb == 0), stop=(kb == nkb - 1)),
                            r=[ptk, f"Vb{bi}"], w=[pok])
                osb = tmpf[0]
                S.op("dve", lambda e, po=po: e.tensor_copy(out=osb[:, 0:N], in_=po[:, 0:N]), r=[pok], w=["tmpf0"])
                pd, pdk = PSW[1][:, 0:512], "psw1"
                S.op("pe", lambda e, pd=pd, hh=hh: e.matmul(pd[:, 0:N], lhsT=selden[:, hh, :], rhs=osb[:, 0:N], start=True, stop=True),
                     r=["selden", "tmpf0"], w=[pdk])
                S.op("dve", lambda e, pd=pd, hb=hb: e.reciprocal(out=tmpf[1][hb:hb + 64, 0:N], in_=pd[hb:hb + 64, 0:N]),
                     r=[pdk], w=["tmpf1"])
                S.op("pool", lambda e, hb=hb, hd=hd: e.tensor_tensor(
                    out=ymix[hb:hb + 64, hd // 2, 0:N], in0=osb[hb:hb + 64, 0:N], in1=tmpf[1][hb:hb + 64, 0:N], op=ALU.mult),
                    r=["tmpf0", "tmpf1"], w=[YK[hd // 2]])

        def kv_stage(t, N):
            rmsnorm(N, 4, lambda k: xn[:, k, 0:N], XK, HK)
            enter("kv")
            for hd in range(NH):
                c, hb = hd // 2, (hd % 2) * 64
                p, pk = next_ps()
                wb, wk = getw(U_KV + c)
                for k in range(8):
                    S.op("pe", lambda e, p=p, wb=wb, k=k, hb=hb: e.matmul(
                        p[0:64, 0:N], lhsT=wb[:, k, hb:hb + 64], rhs=xn[:, k, 0:N], start=(k == 0), stop=(k == 7)),
                        r=[wk, XK[k]], w=[pk])
                copy_op(evac_eng(), ktt[:, 0:N], p[0:64, 0:N], [pk], ["ktt"])
                S.dma("sp", lambda e, hd=hd: e.dma_start(out=KTs[hd, :, t * NT:t * NT + N], in_=ktt[:, 0:N]), "ktt_out",
                      r=["ktt"], w=[f"KTs{t}"])
            for sbk in range(N // 128):
                tok0 = t * NT + sbk * 128
                for nch in range(12):
                    p, pk = next_ps()
                    wb, wk = getw(U_KV + nch)
                    for k in range(8):
                        S.op("pe", lambda e, p=p, wb=wb, k=k, sbk=sbk: e.matmul(
                            p[:, 0:128], lhsT=xn[:, k, sbk * 128:(sbk + 1) * 128], rhs=wb[:, k, :], start=(k == 0),
                            stop=(k == 7)), r=[wk, XK[k]], w=[pk])
                    copy_op(evac_eng(), kvt[:, nch * 128:(nch + 1) * 128], p[:, 0:128], [pk], [f"kvt{nch}"])
                kkeys = [f"kvt{n}" for n in range(6)]
                vkeys = [f"kvt{n}" for n in range(6, 12)]
                S.dma("sp", lambda e, tok0=tok0: e.dma_start(out=o_k[tok0:tok0 + 128, :], in_=kvt[:, 0:768]), "ok_out", r=kkeys)
                S.dma("sp", lambda e, tok0=tok0: e.dma_start(out=o_v[tok0:tok0 + 128, :], in_=kvt[:, 768:1536]), "ov_out", r=vkeys)
                va6 = vaug[:, sbk, :, :].rearrange("p (c two) d -> p c two d", two=2)
                kv6 = kvt[:, 768:1536].rearrange("p (c two d) -> p c two d", two=2, d=64)
                S.op("pool", lambda e, sbk=sbk: e.memset(vaug[:, sbk, :, :], 0.0), r=vkeys, w=[f"vaug{sbk}"])
                S.op("pool", lambda e, va6=va6, kv6=kv6: e.tensor_copy(out=va6[:, :, 0, 0:64], in_=kv6[:, :, 0, :]), r=vkeys, w=[f"vaug{sbk}"])
                S.op("pool", lambda e, va6=va6, kv6=kv6: e.tensor_copy(out=va6[:, :, 1, 64:128], in_=kv6[:, :, 1, :]), r=vkeys, w=[f"vaug{sbk}"])
                S.op("pool", lambda e, va6=va6: e.memset(va6[:, :, 0, 64:65], 1.0), w=[f"vaug{sbk}"])
                S.op("pool", lambda e, va6=va6: e.memset(va6[:, :, 1, 0:1], 1.0), w=[f"vaug{sbk}"])
                blk = tok0 // 128
                S.dma("sp", lambda e, sbk=sbk, blk=blk: e.dma_start(
                    out=Vs[:, :, blk, :].rearrange("h p d -> p h d"), in_=vaug[:, sbk, :, :]), "vs_out",
                    r=[f"vaug{sbk}"], w=[f"Vs{t}"])
                p, pk = next_ps()
                for k in range(8):
                    S.op("pe", lambda e, p=p, k=k, sbk=sbk: e.matmul(
                        p[:, 0:NH], lhsT=xn[:, k, sbk * 128:(sbk + 1) * 128], rhs=wfb[:, k, :], start=(k == 0), stop=(k == 7)),
                        r=["wfb", XK[k]], w=[pk])
                lfs = lf[:, sbk, :]
                lk = f"lf{sbk}"
                S.op("dve", lambda e, p=p, lfs=lfs: e.tensor_tensor(out=lfs, in0=p[:, 0:NH], in1=bfb[:], op=ALU.add),
                     r=[pk, "bfb"], w=[lk])
                S.op("act", lambda e, lfs=lfs: e.activation(out=lfs, in_=lfs, func=AF.Exp, scale=-1.0), r=[lk], w=[lk])
                S.op("act", lambda e, lfs=lfs: e.activation(out=lfs, in_=lfs, func=AF.Ln, bias=1.0, scale=1.0), r=[lk], w=[lk])
                S.op("dve", lambda e, lfs=lfs: e.tensor_scalar(out=lfs, in0=lfs, scalar1=-1.0, scalar2=None, op0=ALU.mult),
                     r=[lk], w=[lk])
                S.dma("sp", lambda e, tok0=tok0, lfs=lfs: e.dma_start(out=o_logf[tok0:tok0 + 128, :], in_=lfs), "olf_out", r=[lk])
                p, pk = next_ps()
                S.op("pe", lambda e, p=p, lfs=lfs: e.matmul(p[:, 0:NH], lhsT=trif[:], rhs=lfs, start=True, stop=True),
                     r=["trif", lk], w=[pk])
                p2, pk2 = next_ps()
                S.op("pe", lambda e, p2=p2, lfs=lfs: e.matmul(p2[:, 0:NH], lhsT=onesf[:], rhs=lfs, start=True, stop=True),
                     r=["onesf", lk], w=[pk2])
                S.op("dve", lambda e, p=p: e.tensor_tensor(out=tmpf[0][:, 0:NH], in0=p[:, 0:NH], in1=carry[:], op=ALU.add),
                     r=[pk, "carry"], w=["tmpf0"])
                S.op("dve", lambda e, blk=blk: e.tensor_scalar(out=Fneg[:, blk, :], in0=tmpf[0][:, 0:NH], scalar1=-1.0,
                                                                scalar2=None, op0=ALU.mult), r=["tmpf0"], w=["Fneg"])
                S.op("dve", lambda e, sbk=sbk: e.tensor_copy(out=fq_sb[:, sbk, :], in_=tmpf[0][:, 0:NH]), r=["tmpf0"],
                     w=["fq_sb"])
                S.op("dve", lambda e, p2=p2: e.tensor_tensor(out=carry[:], in0=carry[:], in1=p2[:, 0:NH], op=ALU.add),
                     r=[pk2, "carry"], w=["carry"])

        def dense_tail(layer, N):
            def ev_out(m, p, pk):
                S.op("dve", lambda e: e.tensor_tensor(out=h[:, m, 0:N], in0=p[:, 0:N], in1=h[:, m, 0:N], op=ALU.add),
                     r=[pk, HK[m]], w=[HK[m]])
            proj_fm(U_OUT[layer], 8, range(8), lambda k: ymix[:, k, 0:N], YK, N, ev_out)
            rmsnorm(N, 2 + layer, lambda k: xn[:, k, 0:N], XK, HK)
            enter("hid")

            def ev_up(m, p, pk):
                i = m % 2
                S.op("act", lambda e: e.activation(out=tmpf[i][:, 0:N], in_=p[:, 0:N], func=AF.Relu), r=[pk], w=[f"tmpf{i}"])
                S.op("pool", lambda e: e.tensor_tensor(out=hid[:, m, 0:N], in0=tmpf[i][:, 0:N], in1=tmpf[i][:, 0:N], op=ALU.mult),
                     r=[f"tmpf{i}"], w=[HIDKEYS[m]])
            proj_fm(U_UP[layer], 8, range(32), lambda k: xn[:, k, 0:N], XK, N, ev_up)
            proj_fm(U_DOWN[layer], 32, range(8), lambda k: hid[:, k, 0:N], HIDKEYS, N, ev_out)

        def in_proj(layer, N):
            rmsnorm(N, layer, lambda k: xn[:, k, 0:N], XK, HK)

            def ev_z(m, p, pk):
                copy_op(evac_eng(), z[:, m, 0:N], p[:, 0:N], [pk], [ZK[m]])
            proj_fm(U_IN[layer], 8, range(8), lambda k: xn[:, k, 0:N], XK, N, ev_z)
            mem_attention(N, 0, lambda hb, hc, mt: KmT[hb:hb + 64, layer, hc, mt * 128:(mt + 1) * 128],
                          lambda mt, hd: Vmp[:, layer, mt, hd, :], ["KmT", "Vmp"])

        for t in range(NTILES):
            enter("xtm")
            for sbk in range(4):
                S.dma("sp", lambda e, sbk=sbk, t=t: e.dma_start(out=xtm[:, sbk, :], in_=x_in[t * NT + sbk * 128:t * NT + (sbk + 1) * 128, :]),
                      "xtm", w=["xtm"])
            for c in range(8):
                p, pk = next_ps()
                for sbk in range(4):
                    S.op("pe", lambda e, p=p, sbk=sbk, c=c: e.transpose(
                        out=p[:, sbk * 128:(sbk + 1) * 128], in_=xtm[:, sbk, c * 128:(c + 1) * 128], identity=identf[:]),
                        r=["xtm", "identf"], w=[pk])
                copy_op(evac_eng(), h[:, c, :], p[:, :], [pk], [HK[c]])
            in_proj(0, NT)
            s5_layer(NT, 128, 1, lambda ri, j, sg: hprev[:, ri, j:j + 1], lambda ri, j0, sg: hprev[:, ri, j0:j0 + 6], ["hprev"])
            dense_tail(0, NT)
            S.dma("sp", lambda e, t=t: e.dma_start(out=H1[t * 128:(t + 1) * 128, :], in_=h[:].rearrange("p k n -> p (k n)")),
                  "h1_out", r=HK, w=[f"H1_{t}"])
            kv_stage(t, NT)
            S.dma("sp", lambda e, t=t: e.dma_start(out=Fq[t * 128:(t + 1) * 128, :], in_=fq_sb[:].rearrange("p s h -> p (s h)")),
                  "fq_out", r=["fq_sb"], w=[f"Fq{t}"])

        for i1 in range(NL1):
            S.dma("pool", lambda e, i1=i1: e.indirect_dma_start(
                out=h[:].rearrange("p k n -> p (k n)"), out_offset=None, in_=H1,
                in_offset=bass.IndirectOffsetOnAxis(ap=hidx[:, i1:i1 + 1], axis=0)), "h1_g",
                r=["hidx"] + [f"H1_{tt}" for tt in range(NTILES)], w=HK)
            in_proj(1, NT)
            fox_layer(i1)
            dense_tail(1, NT)
            enter("yfm")
            rmsnorm(NT, 5, lambda k: yfm[:, k, 0:NT], ["yfm"] * 8, HK)
            enter("xtm")
            for sbk in range(4):
                for c2 in range(2):
                    p, pk = next_ps()
                    for cc in range(4):
                        c = c2 * 4 + cc
                        S.op("pe", lambda e, p=p, sbk=sbk, c=c, cc=cc: e.transpose(
                            out=p[:, cc * 128:(cc + 1) * 128], in_=yfm[:, c, sbk * 128:(sbk + 1) * 128], identity=identf[:]),
                            r=["yfm", "identf"], w=[pk])
                    copy_op(evac_eng(), xtm[:, sbk, c2 * 512:(c2 + 1) * 512], p[:, :], [pk], ["xtm"])
                S.dma("sp", lambda e, sbk=sbk, i1=i1: e.dma_start(
                    out=y_out[i1 * NT + sbk * 128:i1 * NT + (sbk + 1) * 128, :], in_=xtm[:, sbk, :]), "y_out", r=["xtm"])

        NS = 32
        enter("xtm")
        S.dma("sp", lambda e: e.dma_start(out=xtm[0:NS, 0, :], in_=xs_in[:, :]), "xtm", w=["xtm"])
        for c2 in range(2):
            p, pk = next_ps()
            for cc in range(4):
                c = c2 * 4 + cc
                S.op("pe", lambda e, p=p, c=c, cc=cc: e.transpose(
                    out=p[:, cc * NS:(cc + 1) * NS], in_=xtm[0:NS, 0, c * 128:(c + 1) * 128], identity=identf[0:NS, 0:NS]),
                    r=["xtm", "identf"], w=[pk])
            copy_op(evac_eng(), h[:, c2 * 4:(c2 + 1) * 4, 0:NS], p[:, 0:4 * NS].rearrange("p (c n) -> p c n", c=4), [pk],
                    HK[c2 * 4:(c2 + 1) * 4])

        def sample_kv():
            N = NS
            rmsnorm(N, 4, lambda k: xn[:, k, 0:N], XK, HK)
            enter("kv")
            for nch in range(12):
                p, pk = next_ps()
                wb, wk = getw(U_KV + nch)
                for k in range(8):
                    S.op("pe", lambda e, p=p, wb=wb, k=k: e.matmul(
                        p[0:N, 0:128], lhsT=xn[:, k, 0:N], rhs=wb[:, k, :], start=(k == 0), stop=(k == 7)),
                        r=[wk, XK[k]], w=[pk])
                copy_op(evac_eng(), kvt[0:N, nch * 128:(nch + 1) * 128], p[0:N, 0:128], [pk], [f"kvt{nch}"])
                if nch < 6:
                    p2, pk2 = next_ps()
                    for k in range(8):
                        S.op("pe", lambda e, p2=p2, wb=wb, k=k: e.matmul(
                            p2[:, 0:N], lhsT=wb[:, k, :], rhs=xn[:, k, 0:N], start=(k == 0), stop=(k == 7)),
                            r=[wk, XK[k]], w=[pk2])
                    copy_op(evac_eng(), KTnew[:, nch, :], p2[:, 0:N], [pk2], ["KTnew"])
            kkeys = [f"kvt{n}" for n in range(6)]
            vkeys = [f"kvt{n}" for n in range(6, 12)]
            S.dma("sp", lambda e: e.dma_start(out=o_sk[:, :], in_=kvt[0:N, 0:768]), "ok_out", r=kkeys)
            S.dma("sp", lambda e: e.dma_start(out=o_sv[:, :], in_=kvt[0:N, 768:1536]), "ov_out", r=vkeys)
            S.op("pool", lambda e: e.tensor_copy(out=vnbp[:, :], in_=kvt[0:N, 768:1536]), r=vkeys, w=["vnbp"])
            p, pk = next_ps()
            for k in range(8):
                S.op("pe", lambda e, p=p, k=k: e.matmul(p[0:N, 0:NH], lhsT=xn[:, k, 0:N], rhs=wfb[:, k, :],
                                                         start=(k == 0), stop=(k == 7)), r=["wfb", XK[k]], w=[pk])
            lfs = lf[0:N, 0, :]
            S.op("dve", lambda e, p=p: e.tensor_tensor(out=lfs, in0=p[0:N, 0:NH], in1=bfb[0:N, :], op=ALU.add),
                 r=[pk, "bfb"], w=["lf0"])
            S.op("act", lambda e: e.activation(out=lfs, in_=lfs, func=AF.Exp, scale=-1.0), r=["lf0"], w=["lf0"])
            S.op("act", lambda e: e.activation(out=lfs, in_=lfs, func=AF.Ln, bias=1.0, scale=1.0), r=["lf0"], w=["lf0"])
            S.op("dve", lambda e: e.tensor_scalar(out=lfs, in0=lfs, scalar1=-1.0, scalar2=None, op0=ALU.mult), r=["lf0"], w=["lf0"])
            S.dma("sp", lambda e: e.dma_start(out=o_slogf[:, :], in_=lfs), "olf_out", r=["lf0"])

        def sample_past_logf(sq):
            cl2 = cache_logf.rearrange("n s h -> n (s h)")
            if True:
                S.dma("pool", lambda e, sq=sq: e.indirect_dma_start(
                    out=lfpg.rearrange("p s h -> p (s h)"), out_offset=None, in_=cl2,
                    in_offset=bass.IndirectOffsetOnAxis(ap=ptT[:, sq:sq + 1], axis=0)), "lfpg", r=["ptT"], w=["lfpg"])
                for hd in range(NH):
                    S.op("dve", lambda e, hd=hd: e.tensor_tensor_scan(
                        out=lfpg[:, :, hd], data0=onesf[:, :], data1=lfpg[:, :, hd], initial=0.0, op0=ALU.mult, op1=ALU.add),
                        r=["lfpg", "onesf"], w=["lfpg"])
                S.op("dve", lambda e: e.tensor_copy(out=tmpf[0][:, 0:NH], in_=lfpg[:, 127, :]), r=["lfpg"], w=["tmpf0"])
                p, pk = next_ps()
                S.op("pe", lambda e, p=p: e.matmul(p[:, 0:NH], lhsT=strif[:], rhs=tmpf[0][:, 0:NH], start=True, stop=True),
                     r=["strif", "tmpf0"], w=[pk])
                p2, pk2 = next_ps()
                S.op("pe", lambda e, p2=p2: e.matmul(p2[:, 0:NH], lhsT=onesf[:], rhs=tmpf[0][:, 0:NH], start=True, stop=True),
                     r=["onesf", "tmpf0"], w=[pk2])
                S.op("dve", lambda e, p2=p2, sq=sq: e.tensor_copy(out=carry_s[:, sq, :], in_=p2[:, 0:NH]), r=[pk2], w=["carry_s"])
                S.op("dve", lambda e, p=p: e.tensor_copy(out=tmpf[1][:, 0:NH], in_=p[:, 0:NH]), r=[pk], w=["tmpf1"])
                S.op("dve", lambda e: e.tensor_tensor(
                    out=lfpg, in0=lfpg, in1=tmpf[1][:, 0:NH].unsqueeze(1).broadcast_to([128, 128, NH]), op=ALU.add),
                    r=["lfpg", "tmpf1"], w=["lfpg"])
                for h4 in range(3):
                    p, pk = next_ps()
                    for hh in range(4):
                        hd = h4 * 4 + hh
                        S.op("pe", lambda e, p=p, hh=hh, hd=hd: e.transpose(
                            out=p[:, hh * 128:(hh + 1) * 128], in_=lfpg[:, :, hd], identity=identf[:]),
                            r=["lfpg", "identf"], w=[pk])
                    S.op("dve", lambda e, p=p, h4=h4, sq=sq: e.tensor_scalar(
                        out=FnT[:, h4 * 4:(h4 + 1) * 4, :], in0=p[:, :].rearrange("p (h n) -> p h n", h=4),
                        scalar1=-1.0, scalar2=None, op0=ALU.mult), r=[pk], w=["FnT"])

        def sample_fnew(sq):
            N = NS
            p, pk = next_ps()
            S.op("pe", lambda e, p=p: e.matmul(p[0:N, 0:NH], lhsT=btri[:, :], rhs=lf[0:N, 0, :], start=True, stop=False),
                 r=["btri", "lf0"], w=[pk])
            S.op("pe", lambda e, p=p, sq=sq: e.matmul(p[0:N, 0:NH], lhsT=esel[:, sq, :], rhs=carry_s[:, sq, :],
                                                       start=False, stop=True), r=["esel", "carry_s"], w=[pk])
            fnew = tmpf[0][0:N, 0:NH]
            S.op("dve", lambda e, p=p: e.tensor_copy(out=fnew, in_=p[0:N, 0:NH]), r=[pk], w=["tmpf0"])
            S.op("dve", lambda e: e.tensor_scalar(out=tmpf[1][0:N, 0:NH], in0=fnew, scalar1=-1.0, scalar2=None, op0=ALU.mult),
                 r=["tmpf0"], w=["tmpf1"])
            S.dma("sp", lambda e, sq=sq: e.dma_start(out=FnegN[:, sq, :], in_=tmpf[1][sq * 8:(sq + 1) * 8, 0:NH]), "fnegn",
                  r=["tmpf1"], w=["FnegN"])

        def sample_fox():
            N = NS
            enter("satt")
            ck2 = cache_k.rearrange("n s h d -> (n s) (h d)")
            cv2 = cache_v.rearrange("n s h d -> (n s) (h d)")
            accA = PSW[1][:, 0:384].rearrange("p (c n) -> p c n", c=4)
            accB = PSW[1][:, 512:512 + 288].rearrange("p (c n) -> p c n", c=3)
            S.op("pool", lambda e: e.memset(Qblk[:], 0.0), w=["Qblk"])
            for sq in range(4):
                q0 = sq * 8
                sample_past_logf(sq)
                sample_fnew(sq)
                S.dma("sp", lambda e, sq=sq: e.dma_start(out=VnewS, in_=vnbp[sq * 8:(sq + 1) * 8, :]), "vnew",
                      r=["vnbp"], w=["VnewS"])
                for hh in range(2):
                    hb = hh * 64
                    S.op("act", lambda e, hb=hb, hh=hh, q0=q0: e.activation(
                        out=Qblk[hb:hb + 64, :, hh * 8:(hh + 1) * 8], in_=z[hb:hb + 64, 0:6, q0:q0 + 8], func=AF.Copy, scale=0.125),
                        r=ZK[0:6], w=["Qblk"])
                xq = tmpf[2][0:N, 0:96].rearrange("p (h q) -> p h q", q=8)
                S.op("dve", lambda e, sq=sq: e.tensor_tensor(
                    out=xq, in0=qmask[:, sq, :, :], in1=tmpf[0][0:N, 0:NH].unsqueeze(2).broadcast_to([N, NH, 8]), op=ALU.mult),
                    r=["qmask", "tmpf0"], w=["tmpf2"])
                p, pk = next_ps()
                S.op("pe", lambda e, p=p: e.matmul(p[:, 0:96], lhsT=onesf[0:N, :], rhs=tmpf[2][0:N, 0:96], start=True, stop=True),
                     r=["onesf", "tmpf2"], w=[pk])
                S.op("dve", lambda e, p=p: e.tensor_copy(out=Ftb[:].rearrange("p h q -> p (h q)"), in_=p[:, 0:96]), r=[pk], w=["Ftb"])
                for n in range(129):
                    new = (n == 128)
                    bi = n % 2
                    if not new:
                        col = sq * 128 + n
                        S.dma("pool", lambda e, bi=bi, col=col: e.indirect_dma_start(
                            out=Kpg[bi], out_offset=None, in_=ck2,
                            in_offset=bass.IndirectOffsetOnAxis(ap=idx_all[:, col:col + 1], axis=0)), f"Kpg{bi}",
                            r=["idx_all"], w=[f"Kpg{bi}"])
                        S.dma("pool", lambda e, bi=bi, col=col: e.indirect_dma_start(
                            out=Vpg[bi], out_offset=None, in_=cv2,
                            in_offset=bass.IndirectOffsetOnAxis(ap=idx_all[:, col:col + 1], axis=0)), f"Vpg{bi}",
                            r=["idx_all"], w=[f"Vpg{bi}"])
                        for c in range(6):
                            S.op("pe", lambda e, bi=bi, c=c: e.transpose(
                                out=PSW[0][:, c * 128:(c + 1) * 128], in_=Kpg[bi][:, c * 128:(c + 1) * 128], identity=identf[:]),
                                r=[f"Kpg{bi}", "identf"], w=["psw0"])
                        S.op("act", lambda e, bi=bi: e.activation(
                            out=KTp[bi].rearrange("p c k -> p (c k)"), in_=PSW[0][:, 0:768], func=AF.Copy), r=["psw0"], w=[f"KTp{bi}"])
                        S.op("dve", lambda e, bi=bi: e.tensor_copy(out=Vpb[bi], in_=Vpg[bi]), r=[f"Vpg{bi}"], w=[f"Vpb{bi}"])
                        nk = 128
                        kt_fn = lambda c, bi=bi: KTp[bi][:, c, :]
                        v_fn = lambda c, bi=bi: Vpb[bi][:, c * 128:(c + 1) * 128]
                        ktk, vk = f"KTp{bi}", f"Vpb{bi}"
                    else:
                        nk = 8
                        kt_fn = lambda c, q0=q0: KTnew[:, c, q0:q0 + 8]
                        v_fn = lambda c: VnewS[:, c * 128:(c + 1) * 128]
                        ktk, vk = "KTnew", "VnewS"
                    p, pk = next_ps()
                    for c in range(6):
                        S.op("pe", lambda e, p=p, c=c, kt_fn=kt_fn, nk=nk: e.matmul(
                            p[0:nk, c * 16:(c + 1) * 16], lhsT=kt_fn(c), rhs=Qblk[:, c, :], start=True, stop=True),
                            r=[ktk, "Qblk"], w=[pk])
                    sc = tmpf[1][0:nk, 0:96].rearrange("p (h q) -> p h q", q=8)
                    p3 = p[0:nk, 0:96].rearrange("p (h q) -> p h q", q=8)
                    S.op("dve", lambda e, p3=p3, sc=sc, nk=nk: e.tensor_tensor(out=sc, in0=p3, in1=Ftb[0:nk, :, :], op=ALU.add),
                         r=[pk, "Ftb"], w=["tmpf1"])
                    if not new:
                        S.op("dve", lambda e, sc=sc, sq=sq, n=n: e.tensor_tensor(
                            out=sc, in0=sc, in1=FnT[:, :, n].unsqueeze(2).broadcast_to([128, NH, 8]), op=ALU.add),
                            r=["tmpf1", "FnT"], w=["tmpf1"])
                    else:
                        S.op("dve", lambda e, sc=sc, sq=sq: e.tensor_tensor(
                            out=sc, in0=sc, in1=FnegN[:, sq, :].unsqueeze(2).broadcast_to([8, NH, 8]), op=ALU.add),
                            r=["tmpf1", "FnegN"], w=["tmpf1"])
                        S.op("dve", lambda e, sc=sc: e.tensor_tensor(out=sc, in0=sc, in1=cmask[:, :, :], op=ALU.add),
                             r=["tmpf1", "cmask"], w=["tmpf1"])
                    pt = pT[n % 3]
                    ptk = f"pT{n % 3}"
                    S.op("act", lambda e, pt=pt, nk=nk: e.activation(out=pt[0:nk, 0:96], in_=tmpf[1][0:nk, 0:96], func=AF.Exp),
                         r=["tmpf1"], w=[ptk])
                    for c in range(6):
                        acc = accA[:, c, :] if c < 4 else accB[:, c - 4, :]
                        S.op("pe", lambda e, acc=acc, c=c, v_fn=v_fn, pt=pt, nk=nk, n=n, new=new: e.matmul(
                            acc, lhsT=v_fn(c), rhs=pt[0:nk, 0:96], start=(n == 0 and c in (0, 4)), stop=new,
                            skip_group_check=True), r=[vk, ptk], w=["psw1"])
                    S.op("pe", lambda e, pt=pt, nk=nk, new=new: e.matmul(
                        accB[:, 2, :], lhsT=onesb[0:nk, :], rhs=pt[0:nk, 0:96], start=False, stop=new, skip_group_check=True),
                        r=["onesb", ptk], w=["psw1"])
                S.op("dve", lambda e: e.reciprocal(out=tmpf[2][:, 0:96], in_=accB[:, 2, :]), r=["psw1"], w=["tmpf2"])
                for c in range(6):
                    acc = accA[:, c, :] if c < 4 else accB[:, c - 4, :]
                    for hh in range(2):
                        hb = hh * 64
                        cs = (2 * c + hh) * 8
                        S.op("dve", lambda e, acc=acc, c=c, hb=hb, cs=cs, q0=q0: e.tensor_tensor(
                            out=ymix[hb:hb + 64, c, q0:q0 + 8], in0=acc[hb:hb + 64, cs:cs + 8], in1=tmpf[2][hb:hb + 64, cs:cs + 8],
                            op=ALU.mult), r=["psw1", "tmpf2"], w=[YK[c]])

        for layer in range(2):
            rmsnorm(NS, layer, lambda k: xn[:, k, 0:NS], XK, HK)

            def ev_zs(m, p, pk):
                copy_op(evac_eng(), z[:, m, 0:NS], p[:, 0:NS], [pk], [ZK[m]])
            proj_fm(U_IN[layer], 8, range(8), lambda k: xn[:, k, 0:NS], XK, NS, ev_zs)
            for sq in range(4):
                mem_attention(8, sq * 8,
                              lambda hb, hc, mt, layer=layer, sq=sq: KmTs[hb:hb + 64, sq, layer, hc, mt * 128:(mt + 1) * 128],
                              lambda mt, hd, layer=layer, sq=sq: Vmps[:, sq, layer, mt, hd, :], ["KmTs", "Vmps"])
            if layer == 0:
                s5_layer(NS, 8, 4, lambda ri, j, sg: h0s[:, ri, j, sg:sg + 1], lambda ri, j0, sg: s5so[:, ri, j0:j0 + 6, sg],
                         ["s5so"])
                S.dma("sp", lambda e: e.dma_start(out=o_s5s[:, :, :, :], in_=s5so[:]), "s5s_out", r=["s5so"])
            else:
                sample_fox()
            dense_tail(layer, NS)
            if layer == 0:
                sample_kv()
        enter("yfm")
        rmsnorm(NS, 5, lambda k: yfm[:, k, 0:NS], ["yfm"] * 8, HK)
        enter("xtm")
        for c2 in range(2):
            p, pk = next_ps()
            for cc in range(4):
                c = c2 * 4 + cc
                S.op("pe", lambda e, p=p, c=c, cc=cc: e.transpose(
                    out=p[0:NS, cc * 128:(cc + 1) * 128], in_=yfm[:, c, 0:NS], identity=identf[:]),
                    r=["yfm", "identf"], w=[pk])
            copy_op(evac_eng(), xtm[0:NS, 0, c2 * 512:(c2 + 1) * 512], p[0:NS, :], [pk], ["xtm"])
        S.dma("sp", lambda e: e.dma_start(out=ys_out[:, :], in_=xtm[0:NS, 0, :]), "y_out", r=["xtm"])

        S.dma("sp", lambda e: e.dma_start(out=o_s5[:, :, :], in_=hprev[:]), "s5_out", r=["hprev"])

        S.finish_waits("sp")
        with nc.Block() as block:
            S.emit(block)
    return nc


def _units(W, KG=8):
    K, N = W.shape
    Kc, Mc = K // 128, N // 128
    nkq = (Kc + 7) // 8
    out = np.zeros((Mc, nkq, 128, 8, 128), np.float32)
    W4 = W.reshape(Kc, 128, Mc, 128)
    for kq in range(nkq):
        kn = min(8, Kc - kq * 8)
        out[:, kq, :, :kn, :] = W4[kq * 8:kq * 8 + kn].transpose(2, 1, 0, 3)
    return out.reshape(Mc * nkq, 128, 1024)


def _selden():
    sd = np.zeros((128, 2, 128), np.float32)
    sd[64, 0, 0:64] = 1.0
    sd[0, 1, 64:128] = 1.0
    return sd


def _fm(v, nch):
    return np.ascontiguousarray(np.asarray(v, np.float32).reshape(nch, 128).T)


def prepare_shared(inp):
    f = lambda k: np.asarray(inp[k], np.float32)
    units = []
    for i in range(2):
        units += [_units(f("w_in")[i]), _units(f("w_out")[i]), _units(f("w_up")[i]), _units(f("w_down")[i])]
    units += [_units(f("s5_w_glu")[0]), _units(f("w_kv"))]
    wall = np.concatenate(units, axis=0)
    assert wall.shape[0] == N_UNITS
    gall = np.stack([_fm(f("norm_mix")[0], 8), _fm(f("norm_mix")[1], 8), _fm(f("norm_mlp")[0], 8),
                     _fm(f("norm_mlp")[1], 8), _fm(f("norm_kv"), 8), _fm(f("norm_final"), 8)], axis=1)
    a_re, a_im, ls = f("s5_a_re")[0], f("s5_a_im")[0], f("s5_log_step")[0]
    lse = np.repeat(ls[:, None], 64, axis=1)
    a_b = np.stack([np.broadcast_to(a.reshape(1, 3072), (128, 3072)) for a in (a_re, a_im, lse)]).astype(np.float32)
    sm = lambda a: a.reshape(24, 128).T
    a_s = np.stack([sm(a_re), sm(a_im), sm(lse)], axis=1).astype(np.float32)
    b_re, b_im = f("s5_b_re")[0], f("s5_b_im")[0]
    c_re, c_im = f("s5_c_re")[0], f("s5_c_im")[0]
    bpad = np.zeros((2, 128, 24, 128), np.float32)
    cpad = np.zeros((2, 128, 24, 128), np.float32)
    for g in range(48):
        j, g2, gl = g // 2, g % 2, g % 8
        for ri, (b, c) in enumerate(((b_re, c_re), (b_im, c_im))):
            bpad[ri, gl * 16:(gl + 1) * 16, j, g2 * 64:(g2 + 1) * 64] = b[g].T
            cpad[ri, g2 * 64:(g2 + 1) * 64, j, gl * 16:(gl + 1) * 16] = c[g].T
    tri = np.triu(np.ones((128, 128), np.float32))
    tcount = np.broadcast_to(np.arange(1, 129, dtype=np.float32)[None, :], (128, 128)).copy()
    s_idx = np.arange(128)[:, None, None]
    d_idx = np.arange(4)[None, :, None]
    q_idx = np.arange(512)[None, None, :]
    masks = np.where(128 * d_idx + s_idx <= q_idx, 0.0, -1e30).astype(np.float32).astype(ml_dtypes.bfloat16)
    return {
        "w_mem_kv": np.ascontiguousarray(f("w_mem_kv")), "wall": wall, "gall": np.ascontiguousarray(gall),
        "bglu": _fm(f("s5_b_glu")[0], 6), "s5d": _fm(f("s5_d")[0].reshape(-1), 6),
        "wf": np.ascontiguousarray(f("w_f").reshape(8, 128, 12).transpose(1, 0, 2)),
        "bfb": np.ascontiguousarray(np.broadcast_to(f("b_f")[None, :], (128, 12))),
        "a_b": a_b, "a_s": np.ascontiguousarray(a_s), "bpad": bpad, "cpad": cpad,
        "selden": _selden(), "ident_f": np.eye(128, dtype=np.float32), "tri_f": tri, "tcount": tcount, "masks": masks,
    }


T_PROMPT = 8192
DEBUG = False
COMPACT_DEV = False
DBG_NAMES = []
DBG_OUT = {}


def kernel(**inputs):
    T = T_PROMPT
    shared = prepare_shared(inputs)
    x_prompt = np.asarray(inputs["x_prompt"], np.float32)
    mem_prompt = np.asarray(inputs["mem_prompt"], np.float32)
    x_sample = np.asarray(inputs["x_sample"], np.float32)
    page_table = np.asarray(inputs["page_table"], np.int32)
    cache_k = np.asarray(inputs["cache_k"], np.float32)
    cache_v = np.asarray(inputs["cache_v"], np.float32)
    cache_logf = np.asarray(inputs["cache_logf"], np.float32)
    st_re = np.asarray(inputs["state_s5_re"], np.float32)[0]
    st_im = np.asarray(inputs["state_s5_im"], np.float32)[0]
    cmk = np.asarray(inputs["cache_mem_k"], np.float32).reshape(2, 32, MEM_T, 256)
    cmv = np.asarray(inputs["cache_mem_v"], np.float32).reshape(2, 32, MEM_T, 256)
    npool = cache_k.shape[0]
    nc = build_program(T, 512 if COMPACT_DEV else npool)
    stri = np.triu(np.ones((128, 128), np.float32), k=1)
    ii = np.arange(32)
    btri = ((ii[:, None] // 8 == ii[None, :] // 8) & (ii[:, None] <= ii[None, :])).astype(np.float32)
    esel = np.zeros((128, 4, 32), np.float32)
    for sq in range(4):
        esel[0, sq, sq * 8:(sq + 1) * 8] = 1.0
    qmask = np.zeros((32, 4, NH, 8), np.float32)
    for sq in range(4):
        for q in range(8):
            qmask[sq * 8 + q, sq, :, q] = 1.0
    cmask = np.where(np.arange(8)[:, None, None] <= np.arange(8)[None, None, :], 0.0, -1e30).astype(np.float32)
    cmask = np.ascontiguousarray(np.broadcast_to(cmask, (8, NH, 8)))
    consts = {"iota_i": np.arange(128, dtype=np.int32)[:, None].copy(), "stri_f": stri, "btri32": btri, "esel": esel,
              "qmask": qmask, "cmask": cmask}
    in_maps = []
    for c in range(NCORES):
        s = c // 4
        m = dict(shared)
        m.update(consts)
        m["x"] = np.ascontiguousarray(x_prompt[s, :T])
        j = c % 4
        m["hidx"] = np.ascontiguousarray(((4 * np.arange(4)[None, :] + j) * 128 + np.arange(128)[:, None]).astype(np.int32))
        ohbb = np.zeros((128, 2, 4), np.float32)
        ohbb[:, 0, j] = 1.0
        ohbb[:, 1, j + 1:] = -1e30
        m["ohbb"] = ohbb
        m["mem_prompt"] = np.ascontiguousarray(mem_prompt[s])
        sl = slice(4 * c, 4 * c + 4)
        m["xs"] = np.ascontiguousarray(x_sample[sl].reshape(32, D))
        h0 = np.stack([st_re[sl], st_im[sl]])
        m["h0s"] = np.ascontiguousarray(h0.reshape(2, 4, 24, 2, 64).transpose(3, 4, 0, 2, 1).reshape(128, 2, 24, 4))
        m["cmk"] = np.ascontiguousarray(cmk[:, sl])
        m["cmv"] = np.ascontiguousarray(cmv[:, sl])
        pt = page_table[sl]
        if COMPACT_DEV:
            flat = pt.reshape(-1)
            m["cache_k"] = np.ascontiguousarray(cache_k[flat])
            m["cache_v"] = np.ascontiguousarray(cache_v[flat])
            m["cache_logf"] = np.ascontiguousarray(cache_logf[flat])
            pt = np.arange(512, dtype=np.int32).reshape(4, 128)
        else:
            m["cache_k"], m["cache_v"], m["cache_logf"] = cache_k, cache_v, cache_logf
        m["ptab"] = np.ascontiguousarray(pt.reshape(1, 512))
        m["ptabT"] = np.ascontiguousarray(pt.T)
        in_maps.append(m)
    res = run_bass_kernel_spmd(nc, in_maps, core_ids=list(range(NCORES)))
    R = res.results
    sel = [R[0], R[4]]
    if DEBUG:
        DBG_OUT.clear()
        for i, n in enumerate(DBG_NAMES[0]):
            DBG_OUT[n] = R[0]["dbg"][i]
    nl1 = (T // NT) // 4
    y_prompt = np.zeros((2, T, D), np.float32)
    for c in range(NCORES):
        yc = R[c]["y"].reshape(nl1, NT, D)
        for i in range(nl1):
            t0 = (4 * i + c % 4) * NT
            y_prompt[c // 4, t0:t0 + NT] = yc[i]
    okv = np.stack([r["o_mem_kv"] for r in sel], axis=1)
    p_mem_k = np.ascontiguousarray(okv[..., :256]).reshape(2, 2, 256, 4, 64)
    p_mem_v = np.ascontiguousarray(okv[..., 256:]).reshape(2, 2, 256, 4, 64)
    p_k = np.stack([r["o_k"] for r in sel]).reshape(2, T, NH, 64)
    p_v = np.stack([r["o_v"] for r in sel]).reshape(2, T, NH, 64)
    p_logf = np.stack([r["o_logf"] for r in sel])
    s5 = np.stack([r["o_s5"] for r in sel])
    s5 = s5.reshape(2, 2, 64, 2, 24).transpose(3, 0, 4, 1, 2).reshape(2, 2, 48, 64)
    p_s5_re, p_s5_im = np.ascontiguousarray(s5[0][None]), np.ascontiguousarray(s5[1][None])
    y_sample = np.concatenate([r["ys"].reshape(4, 8, D) for r in R])
    s_k = np.concatenate([r["o_sk"].reshape(4, 8, NH, 64) for r in R])
    s_v = np.concatenate([r["o_sv"].reshape(4, 8, NH, 64) for r in R])
    s_logf = np.concatenate([r["o_slogf"].reshape(4, 8, NH) for r in R])
    ss = np.stack([r["o_s5s"] for r in R])
    ss = ss.reshape(8, 2, 64, 2, 24, 4).transpose(3, 0, 5, 4, 1, 2).reshape(2, 32, 48, 64)
    s_s5_re, s_s5_im = np.ascontiguousarray(ss[0][None]), np.ascontiguousarray(ss[1][None])
    return (y_prompt, y_sample, p_s5_re, p_s5_im, p_mem_k, p_mem_v, p_k, p_v, p_logf,
            s_s5_re, s_s5_im, s_k, s_v, s_logf)
```

```python
import contextlib
import math
import numpy as np
import ml_dtypes
import concourse.bass as bass
import concourse.mybir as mybir
from concourse.bass_utils import run_bass_kernel_spmd

F32 = mybir.dt.float32
BF16 = mybir.dt.bfloat16
I32 = mybir.dt.int32
ALU = mybir.AluOpType
AF = mybir.ActivationFunctionType

NCORES = 8
D = 1024
MEM_T = 256
NT = 512
NH = 12
DMAIN = 768
TWO_PI = 2.0 * math.pi

U_IN = [0, 80]
U_OUT = [8, 88]
U_UP = [16, 96]
U_DOWN = [48, 128]
U_GLU = 160
U_KV = 166
N_UNITS = 178


class Sched:
    def __init__(self, nc, stack):
        self.nc = nc
        self.stack = stack
        self.engs = ["pe", "act", "dve", "pool", "sp"]
        self.ops = {e: [] for e in self.engs}
        self.sem = {e: stack.enter_context(nc.semaphore("sem_" + e)) for e in self.engs}
        self.cnt = {e: 0 for e in self.engs}
        self.epoch = {e: 0 for e in self.engs}
        self.old = []
        self.dsem = {}
        self.lastw = {}
        self.reads = {}
        self.waited = {e: {} for e in self.engs}

    def _collect(self, eng, r, w):
        waits = {}

        def add(t):
            sid, sem, val = t
            if sid not in waits or waits[sid][1] < val:
                waits[sid] = (sem, val)

        for k in list(r) + list(w):
            lw = self.lastw.get(k)
            if lw is not None:
                add(lw)
        for k in w:
            for rd in self.reads.get(k, []):
                add(rd)
        final = []
        for sid, (sem, val) in waits.items():
            if eng == "pe" and sid.startswith("pe#"):
                continue
            if self.waited[eng].get(sid, 0) >= val:
                continue
            self.waited[eng][sid] = val
            final.append((sem, val))
        return final

    def _record(self, t, r, w):
        for k in w:
            self.lastw[k] = t
            self.reads[k] = []
        for k in r:
            lst = self.reads.setdefault(k, [])
            lst.append(t)
            if len(lst) > 24:
                best = {}
                for sid, sem, val in lst:
                    if sid not in best or best[sid][2] < val:
                        best[sid] = (sid, sem, val)
                self.reads[k] = list(best.values())

    def op(self, eng, fn, r=(), w=()):
        final = self._collect(eng, r, w)
        if self.cnt[eng] >= 30000:
            self.old.append((self.sem[eng], self.cnt[eng]))
            self.epoch[eng] += 1
            self.sem[eng] = self.stack.enter_context(self.nc.semaphore(f"sem_{eng}_{self.epoch[eng]}"))
            self.cnt[eng] = 0
        self.cnt[eng] += 1
        v = self.cnt[eng]
        self.ops[eng].append((final, fn, self.sem[eng], 1))
        self._record((f"{eng}#{self.epoch[eng]}", self.sem[eng], v), r, w)

    def dma(self, q, fn, key, r=(), w=()):
        final = self._collect(q, r, w)
        if key not in self.dsem:
            self.dsem[key] = [self.stack.enter_context(self.nc.semaphore("d_" + key)), 0]
        ds = self.dsem[key]
        ds[1] += 16
        self.ops[q].append((final, fn, ds[0], 16))
        self._record(("d_" + key, ds[0], ds[1]), r, w)

    def barrier(self):
        allw = [(sem, val) for (sem, val) in self.dsem.values()] + list(self.old)
        allw += [(self.sem[e], self.cnt[e]) for e in self.engs if self.cnt[e] > 0]
        for e in self.engs:
            self.ops[e].append((list(allw), None, None, 0))

    def finish_waits(self, eng="sp"):
        final = [(sem, val) for (sem, val) in self.dsem.values()] + list(self.old)
        for e in self.engs:
            if e != eng and self.cnt[e] > 0:
                final.append((self.sem[e], self.cnt[e]))
        self.ops[eng].append((final, None, None, 0))

    def emit(self, block):
        table = {"pe": block.tensor, "act": block.scalar, "dve": block.vector,
                 "pool": block.gpsimd, "sp": block.sync}
        for e in self.engs:
            ops = self.ops[e]

            def body(engine, ops=ops):
                for waits, fn, sem, inc in ops:
                    for s, v in waits:
                        engine.wait_ge(s, v)
                    if fn is not None:
                        ins = fn(engine)
                        ins.then_inc(sem, inc)

            table[e](body)


def build_program(T, NPOOL=5120):
    nc = bass.Bass("TRN2", target_bir_lowering=False)
    NTILES = T // NT
    NBLK = T // 128

    def din(name, shape, dt=F32):
        return nc.dram_tensor(name, list(shape), dt, kind="ExternalInput").ap()

    def dout(name, shape, dt=F32):
        return nc.dram_tensor(name, list(shape), dt, kind="ExternalOutput").ap()

    def dscr(name, shape, dt):
        return nc.dram_tensor(name, list(shape), dt, kind="Internal").ap()

    x_in = din("x", [T, D])
    mem_prompt = din("mem_prompt", [MEM_T, D])
    w_mem_kv = din("w_mem_kv", [2, D, 512])
    wall = din("wall", [N_UNITS, 128, 1024])
    gall_d = din("gall", [128, 6, 8])
    bglu_d = din("bglu", [128, 6])
    s5d_d = din("s5d", [128, 6])
    wf_d = din("wf", [128, 8, 12])
    bf_d = din("bfb", [128, 12])
    a_b_d = din("a_b", [3, 128, 3072])
    a_s_d = din("a_s", [128, 3, 24])
    bpad_d = din("bpad", [2, 128, 24, 128])
    cpad_d = din("cpad", [2, 128, 24, 128])
    ident_f = din("ident_f", [128, 128])
    tri_d = din("tri_f", [128, 128])
    tcount_d = din("tcount", [128, 128])
    masks_d = din("masks", [128, 4, 512], BF16)
    selden_d = din("selden", [128, 2, 128])

    NPH = NPOOL
    xs_in = din("xs", [32, D])
    h0s_d = din("h0s", [128, 2, 24, 4])
    cmk_d = din("cmk", [2, 4, MEM_T, 256])
    cmv_d = din("cmv", [2, 4, MEM_T, 256])
    cache_k = din("cache_k", [NPH, 128, NH, 64])
    cache_v = din("cache_v", [NPH, 128, NH, 64])
    cache_logf = din("cache_logf", [NPH, 128, NH])
    ptab_d = din("ptab", [1, 512], I32)
    ptabT_d = din("ptabT", [128, 4], I32)
    iota_d = din("iota_i", [128, 1], I32)
    stri_d = din("stri_f", [128, 128])
    btri_d = din("btri32", [32, 32])
    esel_d = din("esel", [128, 4, 32])
    qmask_d = din("qmask", [32, 4, NH, 8])
    cmask_d = din("cmask", [8, NH, 8])
    ys_out = dout("ys", [32, D])
    o_sk = dout("o_sk", [32, DMAIN])
    o_sv = dout("o_sv", [32, DMAIN])
    o_slogf = dout("o_slogf", [32, NH])
    o_s5s = dout("o_s5s", [128, 2, 24, 4])
    NL1 = NTILES // 4
    hidx_d = din("hidx", [128, 4], I32)
    ohbb_d = din("ohbb", [128, 2, 4])
    y_out = dout("y", [NL1 * NT, D])
    o_mem_kv = dout("o_mem_kv", [2, MEM_T, 512])
    o_k = dout("o_k", [T, DMAIN])
    o_v = dout("o_v", [T, DMAIN])
    o_logf = dout("o_logf", [T, NH])
    o_s5 = dout("o_s5", [128, 2, 24])

    dbg_out = dout("dbg", [16, 128, NT]) if DEBUG else None
    wscr = dscr("wscr", [N_UNITS, 128, 1024], BF16)
    H1 = dscr("H1", [NTILES * 128, 8 * NT], F32)
    Fq = dscr("Fq", [NTILES * 128, 4 * NH], F32)
    KTs = dscr("KTs", [NH, 64, T], BF16)
    Vs = dscr("Vs", [NH, 128, NBLK, 128], BF16)

    with contextlib.ExitStack() as st:
        S = Sched(nc, st)

        def sb(name, shape, dt, stack=None):
            return (stack or st).enter_context(nc.sbuf_tensor("sb_" + name, list(shape), dt))

        def ps(name, shape, dt=F32):
            return st.enter_context(nc.psum_tensor(name, list(shape), dt))

        PS = [ps(f"ps{i}", [128, 512], F32) for i in range(4)]
        PSW = [ps(f"psw{i}", [128, 1024], F32) for i in range(2)]
        psrot = [0]

        def next_ps():
            i = psrot[0] % 4
            psrot[0] += 1
            return PS[i], f"ps{i}"

        evrot = [0]

        def evac_eng():
            evrot[0] += 1
            return "dve" if evrot[0] % 2 else "act"

        def copy_op(eng, out, in_, r, w):
            if eng == "act":
                S.op("act", lambda e: e.activation(out=out, in_=in_, func=AF.Copy), r=r, w=w)
            else:
                S.op(eng, lambda e: e.tensor_copy(out=out, in_=in_), r=r, w=w)

        identf = sb("identf", [128, 128], F32)
        identb = sb("identb", [128, 128], BF16)
        onesb = sb("onesb", [128, 128], BF16)
        onesf = sb("onesf", [128, 128], F32)
        trif = sb("trif", [128, 128], F32)
        masks = sb("masks", [128, 4, 512], BF16)
        selden = sb("selden_sb", [128, 2, 128], F32)
        gall = sb("gall_sb", [128, 6, 8], F32)
        bglu = sb("bglu_sb", [128, 6], F32)
        s5d = sb("s5d_sb", [128, 6], F32)
        wfb = sb("wfb", [128, 8, 12], BF16)
        bfb = sb("bfb_sb", [128, 12], F32)
        Ctab = sb("Ctab", [128, 24, 128], F32)
        Stab = sb("Stab", [128, 24, 128], F32)
        r_s = sb("r_s", [128, 24], F32)
        Bpad = sb("Bpad", [128, 2, 24, 128], BF16)
        Cpad = sb("Cpad", [128, 2, 24, 128], BF16)
        hprev = sb("hprev", [128, 2, 24], F32)
        KmT = sb("KmT", [128, 2, 2, MEM_T], BF16)
        Vmp = sb("Vmp", [128, 2, 2, 4, 128], BF16)
        onespad = sb("onespad", [128, 2, 128], BF16)
        Fneg = sb("Fneg", [128, max(NBLK, 4), NH], F32)
        idx_all = sb("idx_all", [128, 512], I32)
        ptT = sb("ptT", [128, 4], I32)
        h0s = sb("h0s_sb", [128, 2, 24, 4], F32)
        KmTs = sb("KmTs", [128, 4, 2, 2, MEM_T], BF16)
        Vmps = sb("Vmps", [128, 4, 2, 2, 4, 128], BF16)
        strif = sb("strif", [128, 128], F32)
        btri = sb("btri_sb", [32, 32], F32)
        esel = sb("esel_sb", [128, 4, 32], F32)
        qmask = sb("qmask_sb", [32, 4, NH, 8], F32)
        cmask = sb("cmask_sb", [8, NH, 8], F32)
        carry = sb("carry", [128, NH], F32)
        ZF = sb("ZF", [128, 4, NH, 65], BF16)

        def ld(dst, src, key):
            S.dma("sp", lambda e: e.dma_start(out=dst, in_=src), key, w=[key])

        ld(identf[:], ident_f[:, :], "identf")
        ld(trif[:], tri_d[:, :], "trif")
        ld(masks[:], masks_d[:, :, :], "masks")
        ld(selden[:], selden_d[:, :, :], "selden")
        ld(gall[:], gall_d[:, :, :], "gall")
        ld(bglu[:], bglu_d[:, :], "bglu")
        ld(s5d[:], s5d_d[:, :], "s5d")
        ld(bfb[:], bf_d[:, :], "bfb")
        ld(h0s[:], h0s_d[:, :, :, :], "h0s")
        ld(strif[:], stri_d[:, :], "strif")
        ld(btri[:], btri_d[:, :], "btri")
        ld(esel[:], esel_d[:, :, :], "esel")
        ld(qmask[:], qmask_d[:, :, :, :], "qmask")
        ld(cmask[:], cmask_d[:, :, :], "cmask")
        ld(ptT[:], ptabT_d[:, :], "ptT")
        hidx = sb("hidx_sb", [128, 4], I32)
        ohbb = sb("ohbb_sb", [128, 2, 4], F32)
        fq_sb = sb("fq_sb", [128, 4, NH], F32)
        FnB = sb("FnB", [128, 16, NH], F32)
        ld(hidx[:], hidx_d[:, :], "hidx")
        ld(ohbb[:], ohbb_d[:, :, :], "ohbb")
        S.op("dve", lambda e: e.tensor_copy(out=identb[:], in_=identf[:]), r=["identf"], w=["identb"])
        S.op("pool", lambda e: e.memset(onesb[:], 1.0), w=["onesb"])
        S.op("pool", lambda e: e.memset(onesf[:], 1.0), w=["onesf"])
        S.op("pool", lambda e: e.memset(onespad[:], 0.0), w=["onespad"])
        S.op("pool", lambda e: e.memset(onespad[:, 0, 0:64], 1.0), w=["onespad"])
        S.op("pool", lambda e: e.memset(onespad[:, 1, 64:128], 1.0), w=["onespad"])
        S.op("pool", lambda e: e.memset(hprev[:], 0.0), w=["hprev"])
        S.op("pool", lambda e: e.memset(carry[:], 0.0), w=["carry"])
        S.op("pool", lambda e: e.memset(ZF[:], 0.0), w=["ZF0", "ZF1", "ZF2", "ZF3"])
        S.op("pool", lambda e: e.memset(Vmp[:], 0.0), w=["Vmp"])

        with contextlib.ExitStack() as st0:
            wst = [sb(f"wst{i}", [128, 1024], F32, st0) for i in range(3)]
            wbf = [sb(f"wbf{i}", [128, 1024], BF16, st0) for i in range(3)]
            engs3 = ["act", "dve", "pool"]
            for u in range(N_UNITS):
                i = u % 3
                S.dma("sp", lambda e, u=u, i=i: e.dma_start(out=wst[i][:], in_=wall[u]), f"wst{i}", w=[f"wst{i}"])
                copy_op(engs3[i], wbf[i][:], wst[i][:], [f"wst{i}"], [f"wbf{i}"])
                S.dma("sp", lambda e, u=u, i=i: e.dma_start(out=wscr[u], in_=wbf[i][:]), f"wscr_w{i}",
                      r=[f"wbf{i}"], w=[f"wscr{u}"])
            wfst = sb("wfst", [128, 8, 12], F32, st0)
            ld(wfst[:], wf_d[:, :, :], "wfst")
            S.op("dve", lambda e: e.tensor_copy(out=wfb[:], in_=wfst[:]), r=["wfst"], w=["wfb"])

            mem_tm = sb("mem_tm", [128, 2, D], F32, st0)
            memT = sb("memT", [128, 8, MEM_T], BF16, st0)
            for t in range(2):
                S.dma("sp", lambda e, t=t: e.dma_start(out=mem_tm[:, t, :], in_=mem_prompt[t * 128:(t + 1) * 128, :]),
                      "mem_tm", w=[f"mem_tm{t}"])
            for t in range(2):
                for c4 in range(2):
                    p, pk = next_ps()
                    for cc in range(4):
                        c = c4 * 4 + cc
                        S.op("pe", lambda e, p=p, t=t, c=c, cc=cc: e.transpose(
                            out=p[:, cc * 128:(cc + 1) * 128], in_=mem_tm[:, t, c * 128:(c + 1) * 128],
                            identity=identf[:]), r=[f"mem_tm{t}", "identf"], w=[pk])
                    S.op("dve", lambda e, p=p, t=t, c4=c4: e.tensor_copy(
                        out=memT[:, c4 * 4:(c4 + 1) * 4, t * 128:(t + 1) * 128],
                        in_=p[:, :].rearrange("p (c t) -> p c t", c=4)), r=[pk], w=["memT"])
            wmst = sb("wmst", [128, 8, 512], F32, st0)
            wmbf = sb("wmbf", [128, 8, 512], BF16, st0)
            okv = sb("okv", [128, 2, 512], F32, st0)
            for i in range(2):
                S.dma("sp", lambda e, i=i: e.dma_start(
                    out=wmst[:], in_=w_mem_kv[i].rearrange("(c p) n -> p c n", p=128)), "wmst", w=["wmst"])
                S.op("act", lambda e: e.activation(out=wmbf[:], in_=wmst[:], func=AF.Copy), r=["wmst"], w=["wmbf"])
                for t in range(2):
                    p, pk = next_ps()
                    for c in range(8):
                        S.op("pe", lambda e, p=p, t=t, c=c: e.matmul(
                            p[:, :], lhsT=memT[:, c, t * 128:(t + 1) * 128], rhs=wmbf[:, c, :],
                            start=(c == 0), stop=(c == 7)), r=["memT", "wmbf"], w=[pk])
                    S.op("dve", lambda e, p=p, t=t: e.tensor_copy(out=okv[:, t, :], in_=p[:, :]),
                         r=[pk], w=[f"okv{t}"])
                    S.dma("sp", lambda e, i=i, t=t: e.dma_start(
                        out=o_mem_kv[i, t * 128:(t + 1) * 128, :], in_=okv[:, t, :]), "okv_out", r=[f"okv{t}"])
                    for h in range(4):
                        hb = (h % 2) * 64
                        S.op("pool", lambda e, i=i, t=t, h=h, hb=hb: e.tensor_copy(
                            out=Vmp[:, i, t, h, hb:hb + 64], in_=okv[:, t, 256 + h * 64:256 + (h + 1) * 64]),
                            r=[f"okv{t}"], w=["Vmp"])
                for c in range(2):
                    p, pk = next_ps()
                    for k in range(8):
                        S.op("pe", lambda e, p=p, c=c, k=k: e.matmul(
                            p[:, 0:MEM_T], lhsT=wmbf[:, k, c * 128:(c + 1) * 128], rhs=memT[:, k, :],
                            start=(k == 0), stop=(k == 7)), r=["memT", "wmbf"], w=[pk])
                    S.op("act", lambda e, p=p, i=i, c=c: e.activation(out=KmT[:, i, c, :], in_=p[:, 0:MEM_T], func=AF.Copy),
                         r=[pk], w=["KmT"])

            S.op("pool", lambda e: e.memset(Vmps[:], 0.0), w=["Vmps"])
            cm_k = sb("cm_k", [128, 2, 256], F32, st0)
            cm_v = sb("cm_v", [128, 2, 256], F32, st0)
            for sq in range(4):
                for i in range(2):
                    S.dma("sp", lambda e, sq=sq, i=i: e.dma_start(
                        out=cm_k[:], in_=cmk_d[i, sq].rearrange("(t p) f -> p t f", p=128)), "cm_k", w=["cm_k"])
                    S.dma("sp", lambda e, sq=sq, i=i: e.dma_start(
                        out=cm_v[:], in_=cmv_d[i, sq].rearrange("(t p) f -> p t f", p=128)), "cm_v", w=["cm_v"])
                    p, pk = next_ps()
                    for mt in range(2):
                        for c in range(2):
                            S.op("pe", lambda e, p=p, mt=mt, c=c: e.transpose(
                                out=p[:, (c * 2 + mt) * 128:(c * 2 + mt + 1) * 128], in_=cm_k[:, mt, c * 128:(c + 1) * 128],
                                identity=identf[:]), r=["cm_k", "identf"], w=[pk])
                    S.op("act", lambda e, p=p, sq=sq, i=i: e.activation(
                        out=KmTs[:, sq, i, :, :].rearrange("p c m -> p (c m)"), in_=p[:, :], func=AF.Copy), r=[pk], w=["KmTs"])
                    for mt in range(2):
                        for hd4 in range(4):
                            hb = (hd4 % 2) * 64
                            S.op("pool", lambda e, sq=sq, i=i, mt=mt, hd4=hd4, hb=hb: e.tensor_copy(
                                out=Vmps[:, sq, i, mt, hd4, hb:hb + 64], in_=cm_v[:, mt, hd4 * 64:(hd4 + 1) * 64]),
                                r=["cm_v"], w=["Vmps"])
            ptb = sb("ptb", [128, 512], I32, st0)
            ptf = sb("ptf", [128, 512], F32, st0)
            iot = sb("iot", [128, 1], I32, st0)
            iotf = sb("iotf", [128, 1], F32, st0)
            S.dma("sp", lambda e: e.dma_start(out=ptb[:], in_=ptab_d[0:1, :].partition_broadcast(128)), "ptb", w=["ptb"])
            ld(iot[:], iota_d[:, :], "iot")
            S.op("dve", lambda e: e.tensor_copy(out=ptf[:], in_=ptb[:]), r=["ptb"], w=["ptf"])
            S.op("dve", lambda e: e.tensor_copy(out=iotf[:], in_=iot[:]), r=["iot"], w=["iotf"])
            S.op("dve", lambda e: e.tensor_scalar(out=idx_all[:], in0=ptf[:], scalar1=128.0, scalar2=iotf[:, 0:1],
                                                  op0=ALU.mult, op1=ALU.add), r=["ptf", "iotf"], w=["idx_all"])
            S.barrier()
        with contextlib.ExitStack() as st0:
            TB = [sb(f"tb{i}", [128, 3072], F32, st0) for i in range(8)]
            TBi = TB[7][:].bitcast(I32)
            for i in range(3):
                S.dma("sp", lambda e, i=i: e.dma_start(out=TB[i][:], in_=a_b_d[i]), f"tb{i}", w=[f"tb{i}"])
            are, aim, dtb = TB[0], TB[1], TB[2]

            def dv(fn, r, w, eng="dve"):
                S.op(eng, fn, r=r, w=w)

            def sin_reduced(out, in_, ki, kf, keys_r, key_w, ikey, fkey):
                dv(lambda e: e.tensor_scalar(out=ki, in0=in_, scalar1=1.0 / TWO_PI, scalar2=None, op0=ALU.mult),
                   keys_r, [ikey])
                dv(lambda e: e.tensor_copy(out=kf, in_=ki), [ikey], [fkey])
                dv(lambda e: e.scalar_tensor_tensor(out=out, in0=kf, scalar=-TWO_PI, in1=in_, op0=ALU.mult, op1=ALU.add),
                   [fkey] + keys_r, [key_w])
                dv(lambda e: e.tensor_scalar(out=out, in0=out, scalar1=3.141592, scalar2=-3.141592, op0=ALU.min, op1=ALU.max),
                   [key_w], [key_w])
                S.op("act", lambda e: e.activation(out=out, in_=out, func=AF.Sin), r=[key_w], w=[key_w])

            S.op("act", lambda e: e.activation(out=dtb[:], in_=dtb[:], func=AF.Exp), r=["tb2"], w=["tb2"])
            dv(lambda e: e.tensor_tensor(out=TB[3][:], in0=are[:], in1=dtb[:], op=ALU.mult), ["tb0", "tb2"], ["tb3"])
            S.op("act", lambda e: e.activation(out=TB[3][:], in_=TB[3][:], func=AF.Exp), r=["tb3"], w=["tb3"])
            dv(lambda e: e.tensor_tensor(out=TB[4][:], in0=aim[:], in1=dtb[:], op=ALU.mult), ["tb1", "tb2"], ["tb4"])
            sin_reduced(TB[5][:], TB[4][:], TBi, TB[7][:], ["tb4"], "tb5", "tb7", "tb7")
            dv(lambda e: e.tensor_scalar(out=TB[4][:], in0=TB[4][:], scalar1=math.pi / 2, scalar2=None, op0=ALU.add),
               ["tb4"], ["tb4"])
            sin_reduced(TB[6][:], TB[4][:], TBi, TB[7][:], ["tb4"], "tb6", "tb7", "tb7")
            dv(lambda e: e.tensor_tensor(out=TB[6][:], in0=TB[6][:], in1=TB[3][:], op=ALU.mult), ["tb6", "tb3"], ["tb6"])
            dv(lambda e: e.tensor_scalar(out=TB[6][:], in0=TB[6][:], scalar1=-1.0, scalar2=None, op0=ALU.add), ["tb6"], ["tb6"])
            dv(lambda e: e.tensor_tensor(out=TB[5][:], in0=TB[5][:], in1=TB[3][:], op=ALU.mult), ["tb5", "tb3"], ["tb5"])
            dv(lambda e: e.tensor_tensor(out=TB[3][:], in0=are[:], in1=are[:], op=ALU.mult), ["tb0"], ["tb3"])
            dv(lambda e: e.tensor_tensor(out=TB[4][:], in0=aim[:], in1=aim[:], op=ALU.mult), ["tb1"], ["tb4"])
            dv(lambda e: e.tensor_tensor(out=TB[3][:], in0=TB[3][:], in1=TB[4][:], op=ALU.add), ["tb3", "tb4"], ["tb3"])
            dv(lambda e: e.reciprocal(out=TB[3][:], in_=TB[3][:]), ["tb3"], ["tb3"])
            dv(lambda e: e.tensor_tensor(out=TB[4][:], in0=TB[6][:], in1=are[:], op=ALU.mult), ["tb6", "tb0"], ["tb4"])
            dv(lambda e: e.tensor_tensor(out=TB[7][:], in0=TB[5][:], in1=aim[:], op=ALU.mult), ["tb5", "tb1"], ["tb7"])
            dv(lambda e: e.tensor_tensor(out=TB[4][:], in0=TB[4][:], in1=TB[7][:], op=ALU.add), ["tb4", "tb7"], ["tb4"])
            dv(lambda e: e.tensor_tensor(out=TB[4][:], in0=TB[4][:], in1=TB[3][:], op=ALU.mult), ["tb4", "tb3"], ["tb4"])
            dv(lambda e: e.tensor_tensor(out=TB[7][:], in0=TB[5][:], in1=are[:], op=ALU.mult), ["tb5", "tb0"], ["tb7"])
            dv(lambda e: e.tensor_tensor(out=TB[2][:], in0=TB[6][:], in1=aim[:], op=ALU.mult), ["tb6", "tb1"], ["tb2"])
            dv(lambda e: e.tensor_tensor(out=TB[7][:], in0=TB[7][:], in1=TB[2][:], op=ALU.subtract), ["tb7", "tb2"], ["tb7"])
            dv(lambda e: e.tensor_tensor(out=TB[7][:], in0=TB[7][:], in1=TB[3][:], op=ALU.mult), ["tb7", "tb3"], ["tb7"])
            crb, cib = TB[4], TB[7]
            bre, bim = TB[0], TB[1]
            S.dma("sp", lambda e: e.dma_start(out=bre[:], in_=bpad_d[0].rearrange("p j n -> p (j n)")), "tb0", w=["tb0"])
            S.dma("sp", lambda e: e.dma_start(out=bim[:], in_=bpad_d[1].rearrange("p j n -> p (j n)")), "tb1", w=["tb1"])
            dv(lambda e: e.tensor_tensor(out=TB[2][:], in0=crb[:], in1=bre[:], op=ALU.mult), ["tb4", "tb0"], ["tb2"])
            dv(lambda e: e.tensor_tensor(out=TB[3][:], in0=cib[:], in1=bim[:], op=ALU.mult), ["tb7", "tb1"], ["tb3"])
            dv(lambda e: e.tensor_tensor(out=Bpad[:, 0, :, :].rearrange("p j n -> p (j n)"), in0=TB[2][:], in1=TB[3][:],
                                         op=ALU.subtract), ["tb2", "tb3"], ["Bpad0"])
            dv(lambda e: e.tensor_tensor(out=TB[5][:], in0=crb[:], in1=bim[:], op=ALU.mult), ["tb4", "tb1"], ["tb5"])
            dv(lambda e: e.tensor_tensor(out=TB[6][:], in0=cib[:], in1=bre[:], op=ALU.mult), ["tb7", "tb0"], ["tb6"])
            dv(lambda e: e.tensor_tensor(out=Bpad[:, 1, :, :].rearrange("p j n -> p (j n)"), in0=TB[5][:], in1=TB[6][:],
                                         op=ALU.add), ["tb5", "tb6"], ["Bpad1"])
            S.dma("sp", lambda e: e.dma_start(out=TB[2][:], in_=cpad_d[0].rearrange("p j n -> p (j n)")), "tb2",
                  r=["tb2"], w=["tb2"])
            S.dma("sp", lambda e: e.dma_start(out=TB[3][:], in_=cpad_d[1].rearrange("p j n -> p (j n)")), "tb3",
                  r=["tb3"], w=["tb3"])
            dv(lambda e: e.tensor_copy(out=Cpad[:, 0, :, :].rearrange("p j n -> p (j n)"), in_=TB[2][:]), ["tb2"], ["Cpad0"])
            dv(lambda e: e.tensor_scalar(out=Cpad[:, 1, :, :].rearrange("p j n -> p (j n)"), in0=TB[3][:], scalar1=-1.0,
                                         scalar2=None, op0=ALU.mult), ["tb3"], ["Cpad1"])

            a_s = sb("a_s_sb", [128, 3, 24], F32, st0)
            th_s = sb("th_s", [128, 24], F32, st0)
            tcount = sb("tcount_sb", [128, 128], F32, st0)
            ld(a_s[:], a_s_d[:, :, :], "a_s")
            ld(tcount[:], tcount_d[:, :], "tcount")
            S.op("act", lambda e: e.activation(out=a_s[:, 2, :], in_=a_s[:, 2, :], func=AF.Exp), r=["a_s"], w=["a_s"])
            dv(lambda e: e.tensor_tensor(out=r_s[:], in0=a_s[:, 0, :], in1=a_s[:, 2, :], op=ALU.mult), ["a_s"], ["r_s"])
            S.op("act", lambda e: e.activation(out=r_s[:], in_=r_s[:], func=AF.Exp), r=["r_s"], w=["r_s"])
            dv(lambda e: e.tensor_tensor(out=th_s[:], in0=a_s[:, 1, :], in1=a_s[:, 2, :], op=ALU.mult), ["a_s"], ["th_s"])
            ang = TB[0]
            ang3 = ang[:].rearrange("p (j t) -> p j t", j=24)
            for j in range(24):
                dv(lambda e, j=j: e.tensor_scalar(out=ang3[:, j, :], in0=tcount[:], scalar1=th_s[:, j:j + 1], scalar2=None,
                                                  op0=ALU.mult), ["tcount", "th_s", "tb0"], ["tb0"])
            sin_reduced(Stab[:, :, :].rearrange("p j t -> p (j t)"), ang[:], TBi, TB[7][:], ["tb0"], "Stab", "tb7", "tb7")
            dv(lambda e: e.tensor_scalar(out=ang[:], in0=ang[:], scalar1=math.pi / 2, scalar2=None, op0=ALU.add),
               ["tb0"], ["tb0"])
            sin_reduced(Ctab[:, :, :].rearrange("p j t -> p (j t)"), ang[:], TBi, TB[7][:], ["tb0"], "Ctab", "tb7", "tb7")
            S.barrier()
        carry_s = sb("carry_s", [128, 4, NH], F32)
        s5so = sb("s5so", [128, 2, 24, 4], F32)
        Qblk = sb("Qblk", [128, 6, 16], BF16)
        KTnew = sb("KTnew", [128, 6, 32], BF16)
        vnbp = sb("vnbp", [32, DMAIN], BF16)
        Kpb = [sb(f"Kpb{i}", [128, DMAIN], BF16) for i in range(2)]
        FnegN = sb("FnegN", [8, 4, NH], F32)
        Ftb = sb("Ftb", [128, NH, 8], F32)
        h = sb("h", [128, 8, NT], F32)
        xn = sb("xn", [128, 8, NT], BF16)
        z = sb("z", [128, 8, NT], BF16)
        ymix = sb("ymix", [128, 8, NT], BF16)
        hid = sb("hid", [128, 32, NT], BF16)
        wbufs = [sb(f"wb{i}", [128, 8, 128], BF16) for i in range(6)]
        stat = sb("stat", [128, NT], F32)
        tmpf = [sb(f"tmpf{i}", [128, NT], F32) for i in range(3)]
        pT = [sb(f"pT{i}", [128, NT], BF16) for i in range(3)]
        lf = sb("lf", [128, 4, NH], F32)
        hflat = hid[:].rearrange("p a b -> p (a b)")
        hidf = hflat.bitcast(F32)

        def hchunks(a, b_, parts=128):
            return hid[0:parts, a:b_, :].rearrange("p a b -> p (a b)")

        xtm = hidf[:, 0:4096].rearrange("p (a b) -> p a b", a=4)
        yfm = hidf[:, 4096:8192].rearrange("p (k n) -> p k n", k=8)
        W6 = [hidf[:, i * 768:(i + 1) * 768].rearrange("p (j t) -> p j t", j=6) for i in range(4)]
        hre = hchunks(12, 18).rearrange("p (j t) -> p j t", j=24)
        him = hchunks(18, 24).rearrange("p (j t) -> p j t", j=24)
        QT = hid[0:65, 0:12, :]
        Vb = [hchunks(12, 16).rearrange("p (b d) -> p b d", b=16), hchunks(24, 28).rearrange("p (b d) -> p b d", b=16)]
        KTb = [hchunks(16, 20, 65), hchunks(20, 24, 65)]
        kvt = hidf[:, 0:1536]
        vaug = hchunks(6, 18).rearrange("p (s h d) -> p s h d", s=4, h=NH)
        ktt = hid[0:64, 18, :]
        kbf = [hid[:, 19, :], hid[:, 20, :]]
        kst = [hidf[:, 5376 + i * 512:5376 + (i + 1) * 512].rearrange("p (s f) -> p s f", s=4) for i in range(2)]
        Kpg = [hidf[:, 0:768], hidf[:, 768:1536]]
        Vpg = [hidf[:, 1536:2304], hidf[:, 2304:3072]]
        KTp = [hflat[:, 6144:6912].rearrange("p (c k) -> p c k", c=6), hflat[:, 6912:7680].rearrange("p (c k) -> p c k", c=6)]
        Vpb = [hflat[:, 7680:8448], hflat[:, 8448:9216]]
        lfpg = hidf[:, 4608:6144].rearrange("p (s h) -> p s h", h=NH)
        FnT = hidf[:, 6144:7680].rearrange("p (h n) -> p h n", h=NH)
        VnewS = hid[0:8, 30, :].rearrange("p (a b) -> p a b", a=1)[:, 0, :]
        VnewS = hflat[0:8, 15360:16128]
        HIDKEYS = [f"hid{m}" for m in range(32)]
        GROUPS = {
            "satt": ["Kpg0", "Kpg1", "Vpg0", "Vpg1", "KTp0", "KTp1", "Vpb0", "Vpb1", "lfpg", "FnT", "VnewS"],
            "xtm": ["xtm"], "sq": ["sq"], "yfm": ["yfm"], "hid": HIDKEYS,
            "s5": ["w6_0", "w6_1", "w6_2", "w6_3", "hre", "him"],
            "att": ["QT", "KTb0", "KTb1", "Vb0", "Vb1"],
            "kv": [f"kvt{n}" for n in range(12)] + [f"vaug{n}" for n in range(4)] + ["ktt", "kbf0", "kbf1", "kst0", "kst1"],
        }

        def enter(*groups):
            tgt = [k for g in groups for k in GROUPS[g]]
            best = {}
            for g, keys in GROUPS.items():
                if g in groups:
                    continue
                for k in keys:
                    lst = list(S.reads.get(k, []))
                    if S.lastw.get(k) is not None:
                        lst.append(S.lastw[k])
                    for sid, sem, val in lst:
                        if sid not in best or best[sid][2] < val:
                            best[sid] = (sid, sem, val)
            for k in tgt:
                S.reads.setdefault(k, []).extend(best.values())

        wrot = [0]

        def getw(u):
            i = wrot[0] % 6
            wrot[0] += 1
            S.dma("sp", lambda e, u=u, i=i: e.dma_start(out=wbufs[i][:].rearrange("p k n -> p (k n)"), in_=wscr[u]),
                  f"wb{i}", r=[f"wscr{u}"], w=[f"wb{i}"])
            return wbufs[i], f"wb{i}"

        def rmsnorm(N, gidx, out_fn, out_keys, hkeys):
            enter("sq")
            sq = hid[:, 0:8, 0:N]
            for k in range(8):
                S.op("act", lambda e, k=k: e.activation(out=sq[:, k, :], in_=h[:, k, 0:N], func=AF.Square),
                     r=[hkeys[k]], w=["sq"])
            p, pk = next_ps()
            for k in range(8):
                S.op("pe", lambda e, p=p, k=k: e.matmul(p[:, 0:N], lhsT=onesb[:], rhs=sq[:, k, :],
                                                         start=(k == 0), stop=(k == 7)), r=["sq", "onesb"], w=[pk])
            S.op("act", lambda e, p=p: e.activation(out=stat[:, 0:N], in_=p[:, 0:N], func=AF.Sqrt, bias=1e-6,
                                                    scale=1.0 / D), r=[pk], w=["stat"])
            S.op("dve", lambda e: e.reciprocal(out=stat[:, 0:N], in_=stat[:, 0:N]), r=["stat"], w=["stat"])
            for k in range(8):
                S.op("dve", lambda e, k=k: e.scalar_tensor_tensor(
                    out=out_fn(k), in0=h[:, k, 0:N], scalar=gall[:, gidx, k:k + 1], in1=stat[:, 0:N],
                    op0=ALU.mult, op1=ALU.mult), r=[hkeys[k], "stat", "gall"], w=[out_keys[k]])

        dbgbuf = sb("dbgbuf", [128, NT], F32) if DEBUG else None
        dbg_names = []

        def dump(name, ap, keys, n=NT):
            if not DEBUG or len(dbg_names) >= 16:
                return
            i = len(dbg_names)
            dbg_names.append(name)
            S.op("dve", lambda e: e.tensor_copy(out=dbgbuf[:, 0:n], in_=ap), r=keys, w=["dbgbuf"])
            S.dma("sp", lambda e: e.dma_start(out=dbg_out[i, :, 0:n], in_=dbgbuf[:, 0:n]), "dbg", r=["dbgbuf"])

        DBG_NAMES.clear()
        DBG_NAMES.append(dbg_names)
        HK = [f"h{k}" for k in range(8)]
        XK = [f"xn{k}" for k in range(8)]
        ZK = [f"z{k}" for k in range(8)]
        YK = [f"ym{k}" for k in range(8)]

        def proj_fm(ubase, Kc, m_list, src_fn, src_keys, N, evac):
            nkq = (Kc + 7) // 8
            for m in m_list:
                p, pk = next_ps()
                for kq in range(nkq):
                    wb, wk = getw(ubase + m * nkq + kq)
                    kn = min(8, Kc - kq * 8)
                    for kk in range(kn):
                        k = kq * 8 + kk
                        S.op("pe", lambda e, p=p, wb=wb, kk=kk, k=k: e.matmul(
                            p[:, 0:N], lhsT=wb[:, kk, :], rhs=src_fn(k), start=(k == 0), stop=(k == Kc - 1)),
                            r=[wk, src_keys[k]], w=[pk])
                evac(m, p, pk)

        def mem_attention(N, q0, km_fn, vm_fn, kkeys):
            for hc in range(2):
                pn, pnk = PSW[0][:, 0:512], "psw0"
                pd, pdk = PSW[1][:, 0:512], "psw1"
                first = True
                cnt = 0
                for hh in range(2):
                    hd = hc * 2 + hh
                    hb = hh * 64
                    for mt in range(2):
                        p, pk = next_ps()
                        S.op("pe", lambda e, p=p, hb=hb, mt=mt, hc=hc: e.matmul(
                            p[:, 0:N], lhsT=km_fn(hb, hc, mt),
                            rhs=z[hb:hb + 64, 6 + hc, q0:q0 + N], start=True, stop=True), r=kkeys + [ZK[6 + hc]], w=[pk])
                        pt = pT[cnt % 3]
                        ptk = f"pT{cnt % 3}"
                        S.op("act", lambda e, p=p, pt=pt: e.activation(out=pt[:, 0:N], in_=p[:, 0:N], func=AF.Exp,
                                                                       scale=0.125), r=[pk], w=[ptk])
                        last = (hh == 1 and mt == 1)
                        S.op("pe", lambda e, pn=pn, pt=pt, mt=mt, hd=hd, first=first, last=last: e.matmul(
                            pn[:, 0:N], lhsT=vm_fn(mt, hd), rhs=pt[:, 0:N], start=first, stop=last),
                            r=kkeys + [ptk], w=[pnk])
                        S.op("pe", lambda e, pd=pd, pt=pt, hh=hh, first=first, last=last: e.matmul(
                            pd[:, 0:N], lhsT=onespad[:, hh, :], rhs=pt[:, 0:N], start=first, stop=last),
                            r=["onespad", ptk], w=[pdk])
                        first = False
                        cnt += 1
                S.op("dve", lambda e, pd=pd: e.reciprocal(out=tmpf[0][:, 0:N], in_=pd[:, 0:N]), r=[pdk], w=["tmpf0"])
                S.op("dve", lambda e, pn=pn, hc=hc: e.tensor_tensor(out=ymix[:, 6 + hc, q0:q0 + N], in0=pn[:, 0:N],
                                                                    in1=tmpf[0][:, 0:N], op=ALU.mult),
                     r=[pnk, "tmpf0"], w=[YK[6 + hc]])

        s5_dumped = []

        def s5_layer(N, L, nseg, init_fn, fin_fn, skeys):
            enter("s5")
            CH = nseg * L
            zgk = [f"zg{c}" for c in range(6)]
            for ch in range(N // CH):
                c0 = ch * CH
                for grp in range(4):
                    j0 = grp * 6
                    bre, bim = PSW[0], PSW[1]
                    for jj in range(6):
                        j = j0 + jj
                        S.op("pe", lambda e, j=j, jj=jj, c0=c0: e.matmul(
                            bre[:, jj * CH:(jj + 1) * CH], lhsT=Bpad[:, 0, j, :], rhs=z[:, j // 4, c0:c0 + CH],
                            start=True, stop=True), r=["Bpad0", ZK[j // 4]], w=["psw0"])
                        S.op("pe", lambda e, j=j, jj=jj, c0=c0: e.matmul(
                            bim[:, jj * CH:(jj + 1) * CH], lhsT=Bpad[:, 1, j, :], rhs=z[:, j // 4, c0:c0 + CH],
                            start=True, stop=True), r=["Bpad1", ZK[j // 4]], w=["psw1"])
                    br3 = bre[:, 0:6 * CH].rearrange("p (j t) -> p j t", j=6)
                    bi3 = bim[:, 0:6 * CH].rearrange("p (j t) -> p j t", j=6)
                    w0, w1, w2, w3 = [w[:, :, 0:CH] for w in W6]
                    Cv = Ctab[:, j0:j0 + 6, 0:L]
                    Sv = Stab[:, j0:j0 + 6, 0:L]
                    SG = [(sg * L, (sg + 1) * L) for sg in range(nseg)]
                    for (s0, s1) in SG:
                        S.op("dve", lambda e, Cv=Cv, s0=s0, s1=s1: e.tensor_tensor(out=w0[:, :, s0:s1], in0=br3[:, :, s0:s1], in1=Cv, op=ALU.mult), r=["psw0", "Ctab"], w=["w6_0"])
                        S.op("dve", lambda e, Sv=Sv, s0=s0, s1=s1: e.tensor_tensor(out=w1[:, :, s0:s1], in0=bi3[:, :, s0:s1], in1=Sv, op=ALU.mult), r=["psw1", "Stab"], w=["w6_1"])
                        S.op("dve", lambda e, Cv=Cv, s0=s0, s1=s1: e.tensor_tensor(out=w2[:, :, s0:s1], in0=bi3[:, :, s0:s1], in1=Cv, op=ALU.mult), r=["psw1", "Ctab"], w=["w6_2"])
                        S.op("dve", lambda e, Sv=Sv, s0=s0, s1=s1: e.tensor_tensor(out=w3[:, :, s0:s1], in0=br3[:, :, s0:s1], in1=Sv, op=ALU.mult), r=["psw0", "Stab"], w=["w6_3"])
                    S.op("pool", lambda e: e.tensor_tensor(out=w0, in0=w0, in1=w1, op=ALU.add), r=["w6_0", "w6_1"], w=["w6_0"])
                    S.op("pool", lambda e: e.tensor_tensor(out=w2, in0=w2, in1=w3, op=ALU.subtract), r=["w6_2", "w6_3"], w=["w6_2"])
                    if False:
                        dump("bu_re", bre[:, 0:128], ["psw0"], 128)
                        dump("ctab", Ctab[:, 0, :], ["Ctab"], 128)
                        dump("stab", Stab[:, 0, :], ["Stab"], 128)
                        dump("gr", w0[:, 0, :], ["w6_0"], 128)
                        dump("bu_im", bim[:, 0:128], ["psw1"], 128)
                        dump("biS", w1[:, 0, :], ["w6_1"], 128)
                        dump("gi", w2[:, 0, :], ["w6_2"], 128)
                        dump("r_s", r_s[:, :], ["r_s"], 24)
                    for jj in range(6):
                        j = j0 + jj
                        for sg, (s0, s1) in enumerate(SG):
                            S.op("dve", lambda e, j=j, jj=jj, s0=s0, s1=s1, sg=sg: e.tensor_tensor_scan(
                                out=w0[:, jj, s0:s1], data0=r_s[:, j:j + 1].broadcast_to([128, L]), data1=w0[:, jj, s0:s1],
                                initial=init_fn(0, j, sg), op0=ALU.mult, op1=ALU.add), r=["w6_0", "r_s"] + skeys, w=["w6_0"])
                            S.op("dve", lambda e, j=j, jj=jj, s0=s0, s1=s1, sg=sg: e.tensor_tensor_scan(
                                out=w2[:, jj, s0:s1], data0=r_s[:, j:j + 1].broadcast_to([128, L]), data1=w2[:, jj, s0:s1],
                                initial=init_fn(1, j, sg), op0=ALU.mult, op1=ALU.add), r=["w6_2", "r_s"] + skeys, w=["w6_2"])
                    if False:
                        dump("sr", w0[:, 0, :], ["w6_0"], 128)
                    hre_o, him_o = hre[:, j0:j0 + 6, 0:CH], him[:, j0:j0 + 6, 0:CH]
                    for (s0, s1) in SG:
                        S.op("dve", lambda e, Cv=Cv, s0=s0, s1=s1: e.tensor_tensor(out=w1[:, :, s0:s1], in0=w0[:, :, s0:s1], in1=Cv, op=ALU.mult), r=["w6_0", "Ctab"], w=["w6_1"])
                        S.op("pool", lambda e, Sv=Sv, s0=s0, s1=s1: e.tensor_tensor(out=w3[:, :, s0:s1], in0=w2[:, :, s0:s1], in1=Sv, op=ALU.mult), r=["w6_2", "Stab"], w=["w6_3"])
                    S.op("pool", lambda e, o=hre_o: e.tensor_tensor(out=o, in0=w1, in1=w3, op=ALU.subtract),
                         r=["w6_1", "w6_3"], w=["hre"])
                    for sg, (s0, s1) in enumerate(SG):
                        S.op("dve", lambda e, o=fin_fn(0, j0, sg), s1=s1: e.tensor_tensor(out=o, in0=w1[:, :, s1 - 1], in1=w3[:, :, s1 - 1], op=ALU.subtract),
                             r=["w6_1", "w6_3"], w=skeys)
                    for (s0, s1) in SG:
                        S.op("dve", lambda e, Sv=Sv, s0=s0, s1=s1: e.tensor_tensor(out=w1[:, :, s0:s1], in0=w0[:, :, s0:s1], in1=Sv, op=ALU.mult), r=["w6_0", "Stab"], w=["w6_1"])
                        S.op("pool", lambda e, Cv=Cv, s0=s0, s1=s1: e.tensor_tensor(out=w3[:, :, s0:s1], in0=w2[:, :, s0:s1], in1=Cv, op=ALU.mult), r=["w6_2", "Ctab"], w=["w6_3"])
                    S.op("pool", lambda e, o=him_o: e.tensor_tensor(out=o, in0=w1, in1=w3, op=ALU.add),
                         r=["w6_1", "w6_3"], w=["him"])
                    for sg, (s0, s1) in enumerate(SG):
                        S.op("dve", lambda e, o=fin_fn(1, j0, sg), s1=s1: e.tensor_tensor(out=o, in0=w1[:, :, s1 - 1], in1=w3[:, :, s1 - 1], op=ALU.add),
                             r=["w6_1", "w6_3"], w=skeys)
                if False:
                    dump("hre", hre[:, 0, :], ["hre"], 128)
                    s5_dumped.append(1)
                for c in range(6):
                    p, pk = next_ps()
                    n = 0
                    for j in range(4 * c, 4 * c + 4):
                        S.op("pe", lambda e, p=p, j=j, n=n: e.matmul(p[:, 0:CH], lhsT=Cpad[:, 0, j, :], rhs=hre[:, j, 0:CH],
                                                                    start=(n == 0), stop=False), r=["Cpad0", "hre"], w=[pk])
                        S.op("pe", lambda e, p=p, j=j, n=n: e.matmul(p[:, 0:CH], lhsT=Cpad[:, 1, j, :], rhs=him[:, j, 0:CH],
                                                                    start=False, stop=(n == 3)), r=["Cpad1", "him"], w=[pk])
                        n += 1
                    t0, t1 = tmpf[0][:, 0:CH], tmpf[1][:, 0:CH]
                    S.op("dve", lambda e, p=p, c=c, c0=c0: e.scalar_tensor_tensor(
                        out=t0, in0=z[:, c, c0:c0 + CH], scalar=s5d[:, c:c + 1], in1=p[:, 0:CH], op0=ALU.mult, op1=ALU.add),
                        r=[pk, ZK[c], "s5d"], w=["tmpf0"])
                    S.op("pool", lambda e: e.tensor_tensor(out=t1, in0=t0, in1=t0, op=ALU.mult), r=["tmpf0"], w=["tmpf1"])
                    S.op("pool", lambda e: e.tensor_scalar(out=t1, in0=t1, scalar1=0.044715, scalar2=1.0, op0=ALU.mult,
                                                           op1=ALU.add), r=["tmpf1"], w=["tmpf1"])
                    S.op("pool", lambda e: e.tensor_tensor(out=t1, in0=t1, in1=t0, op=ALU.mult), r=["tmpf1", "tmpf0"], w=["tmpf1"])
                    S.op("act", lambda e: e.activation(out=t1, in_=t1, func=AF.Sigmoid, scale=2.0 * math.sqrt(2.0 / math.pi)),
                         r=["tmpf1"], w=["tmpf1"])
                    S.op("dve", lambda e, c=c, c0=c0: e.tensor_tensor(out=xn[:, c, c0:c0 + CH], in0=t0, in1=t1, op=ALU.mult),
                         r=["tmpf0", "tmpf1"], w=[XK[c]])
            def ev(m, p, pk):
                S.op("act", lambda e: e.activation(out=tmpf[2][:, 0:N], in_=p[:, 0:N], func=AF.Sigmoid, bias=bglu[:, m:m + 1],
                                                   scale=1.0), r=[pk, "bglu"], w=["tmpf2"])
                S.op("dve", lambda e: e.tensor_tensor(out=ymix[:, m, 0:N], in0=xn[:, m, 0:N], in1=tmpf[2][:, 0:N], op=ALU.mult),
                     r=["tmpf2", XK[m]], w=[YK[m]])
            proj_fm(U_GLU, 6, range(6), lambda k: xn[:, k, 0:N], XK, N, ev)

        def fox_layer(i1):
            N = NT
            enter("att")
            S.dma("pool", lambda e: e.indirect_dma_start(
                out=fq_sb[:].rearrange("p s h -> p (s h)"), out_offset=None, in_=Fq,
                in_offset=bass.IndirectOffsetOnAxis(ap=hidx[:, i1:i1 + 1], axis=0)), "fq_g",
                r=["hidx"] + [f"Fq{tt}" for tt in range(NTILES)], w=["fq_sb"])
            for sbk in range(4):
                S.op("dve", lambda e, sbk=sbk: e.tensor_copy(out=ZF[:, sbk, :, 64], in_=fq_sb[:, sbk, :]), r=["fq_sb"],
                     w=[f"ZF{sbk}"])
            for jj in range(4):
                S.op("dve", lambda e, jj=jj: e.tensor_scalar(
                    out=FnB[:, jj * 4:(jj + 1) * 4, :], in0=Fneg[:, 16 * i1 + jj * 4:16 * i1 + (jj + 1) * 4, :],
                    scalar1=ohbb[:, 1, jj:jj + 1], scalar2=None, op0=ALU.add), r=["Fneg", "ohbb"], w=["FnB"])
            for i in range(2):
                S.op("pool", lambda e, i=i: e.memset(KTb[i][64:65, :], 1.0), w=[f"KTb{i}"])
            for hd in range(NH):
                c, hb = hd // 2, (hd % 2) * 64
                if hb == 0:
                    S.op("act", lambda e, hd=hd, c=c: e.activation(out=QT[0:64, hd, :], in_=z[0:64, c, :], func=AF.Copy,
                                                                   scale=0.125), r=[ZK[c]], w=["QT"])
                else:
                    S.dma("sp", lambda e, hd=hd, c=c: e.dma_start(out=QT[0:64, hd, :], in_=z[64:128, c, :]), "QTmv",
                          r=[ZK[c]], w=["QT"])
                    S.op("act", lambda e, hd=hd: e.activation(out=QT[0:64, hd, :], in_=QT[0:64, hd, :], func=AF.Copy,
                                                              scale=0.125), r=["QT"], w=["QT"])
            for hd in range(NH):
                p, pk = next_ps()
                for sbk in range(4):
                    S.op("pe", lambda e, p=p, sbk=sbk, hd=hd: e.matmul(
                        p[0:65, sbk * 128:(sbk + 1) * 128], lhsT=ZF[:, sbk, hd, :], rhs=identb[:], start=True, stop=True),
                        r=[f"ZF{sbk}", "identb"], w=[pk])
                S.op("dve", lambda e, p=p, hd=hd: e.tensor_copy(out=QT[64:65, hd, :], in_=p[64:65, :]), r=[pk], w=["QT"])
            nkb = 16 * i1 + 16
            for hd in range(NH):
                po, pok = PSW[0][:, 0:512], "psw0"
                npieces = (nkb + 15) // 16
                cnt = 0
                hh = hd % 2
                hb = hh * 64
                for pc in range(npieces):
                    kb0 = pc * 16
                    nb = min(16, nkb - kb0)
                    bi = (hd * npieces + pc) % 2
                    tiles_needed = sorted(set((kb0 + i) // 4 for i in range(nb)))
                    S.dma("sp", lambda e, hd=hd, kb0=kb0, nb=nb, bi=bi: e.dma_start(
                        out=KTb[bi][0:64, 0:nb * 128], in_=KTs[hd, :, kb0 * 128:(kb0 + nb) * 128]), f"KTb{bi}",
                        r=[f"KTs{tt}" for tt in tiles_needed], w=[f"KTb{bi}"])
                    S.dma("act", lambda e, hd=hd, kb0=kb0, nb=nb, bi=bi: e.dma_start(
                        out=Vb[bi][:, 0:nb, :], in_=Vs[hd, :, kb0:kb0 + nb, :]), f"Vb{bi}",
                        r=[f"Vs{tt}" for tt in tiles_needed], w=[f"Vb{bi}"])
                    for i in range(nb):
                        kb = kb0 + i
                        zone = kb - 16 * i1
                        q0 = 0
                        p, pk = next_ps()
                        S.op("pe", lambda e, p=p, bi=bi, i=i, hd=hd: e.matmul(
                            p[:, 0:N], lhsT=KTb[bi][:, i * 128:(i + 1) * 128], rhs=QT[:, hd, 0:N], start=True, stop=True),
                            r=[f"KTb{bi}", "QT"], w=[pk])
                        pt = pT[cnt % 3]
                        ptk = f"pT{cnt % 3}"
                        cnt += 1
                        if zone >= 0:
                            jj, d = zone // 4, zone % 4
                            S.op("dve", lambda e, p=p, d=d, jj=jj: e.scalar_tensor_tensor(
                                out=tmpf[2][:, 0:N], in0=masks[:, d, 0:N], scalar=ohbb[:, 0, jj:jj + 1], in1=p[:, 0:N],
                                op0=ALU.mult, op1=ALU.add), r=[pk, "masks", "ohbb"], w=["tmpf2"])
                            S.op("act", lambda e, pt=pt, zone=zone, hd=hd: e.activation(
                                out=pt[:, 0:N], in_=tmpf[2][:, 0:N], func=AF.Exp, bias=FnB[:, zone, hd:hd + 1], scale=1.0),
                                r=["tmpf2", "FnB"], w=[ptk])
                        else:
                            S.op("act", lambda e, p=p, pt=pt, kb=kb, hd=hd: e.activation(
                                out=pt[:, 0:N], in_=p[:, 0:N], func=AF.Exp, bias=Fneg[:, kb, hd:hd + 1], scale=1.0),
                                r=[pk, "Fneg"], w=[ptk])
                        S.op("pe", lambda e, po=po, pt=pt, bi=bi, i=i, kb=kb, q0=q0: e.matmul(
                            po[:, q0:N], lhsT=Vb[bi][:, i, :], rhs=pt[:, q0:N], start=(kb == 0), stop=(kb == nkb - 1)),
                            r=[ptk, f"Vb{bi}"], w=[pok])
                osb = tmpf[0]
                S.op("dve", lambda e, po=po: e.tensor_copy(out=osb[:, 0:N], in_=po[:, 0:N]), r=[pok], w=["tmpf0"])
                pd, pdk = PSW[1][:, 0:512], "psw1"
                S.op("pe", lambda e, pd=pd, hh=hh: e.matmul(pd[:, 0:N], lhsT=selden[:, hh, :], rhs=osb[:, 0:N], start=True, stop=True),
                     r=["selden", "tmpf0"], w=[pdk])
                S.op("dve", lambda e, pd=pd, hb=hb: e.reciprocal(out=tmpf[1][hb:hb + 64, 0:N], in_=pd[hb:hb + 64, 0:N]),
                     r=[pdk], w=["tmpf1"])
                S.op("pool", lambda e, hb=hb, hd=hd: e.tensor_tensor(
                    out=ymix[hb:hb + 64, hd // 2, 0:N], in0=osb[hb:hb + 64, 0:N], in1=tmpf[1][hb:hb + 64, 0:N], op=ALU.mult),
                    r=["tmpf0", "tmpf1"], w=[YK[hd // 2]])

        def kv_stage(t, N):
            rmsnorm(N, 4, lambda k: xn[:, k, 0:N], XK, HK)
            enter("kv")
            S.op("pool", lambda e: e.memset(vaug[:, :, :, :], 0.0), w=[f"vaug{n}" for n in range(4)])
            va7 = vaug[:, :, :, :].rearrange("p s (c two) d -> p s c two d", two=2)
            S.op("pool", lambda e: e.memset(va7[:, :, :, 0, 64:65], 1.0), w=[f"vaug{n}" for n in range(4)])
            S.op("pool", lambda e: e.memset(va7[:, :, :, 1, 0:1], 1.0), w=[f"vaug{n}" for n in range(4)])
            for m in range(12):
                p, pk = next_ps()
                wb, wk = getw(U_KV + m)
                for k in range(8):
                    S.op("pe", lambda e, p=p, wb=wb, k=k: e.matmul(
                        p[:, 0:N], lhsT=wb[:, k, :], rhs=xn[:, k, 0:N], start=(k == 0), stop=(k == 7)),
                        r=[wk, XK[k]], w=[pk])
                fi = m % 2
                fm = tmpf[fi]
                S.op("dve", lambda e, p=p, fm=fm: e.tensor_copy(out=fm[:, 0:N], in_=p[:, 0:N]), r=[pk], w=[f"tmpf{fi}"])
                if m < 6:
                    kb_ = kbf[fi]
                    S.op("act", lambda e, fm=fm, kb_=kb_: e.activation(out=kb_[:, 0:N], in_=fm[:, 0:N], func=AF.Copy),
                         r=[f"tmpf{fi}"], w=[f"kbf{fi}"])
                    for hh in range(2):
                        S.dma("sp", lambda e, m=m, hh=hh, kb_=kb_: e.dma_start(
                            out=KTs[2 * m + hh, :, t * NT:t * NT + N], in_=kb_[hh * 64:(hh + 1) * 64, 0:N]), "ktt_out",
                            r=[f"kbf{fi}"], w=[f"KTs{t}"])
                p2, pk2 = next_ps()
                for sbk in range(4):
                    S.op("pe", lambda e, p2=p2, fm=fm, sbk=sbk: e.transpose(
                        out=p2[:, sbk * 128:(sbk + 1) * 128], in_=fm[:, sbk * 128:(sbk + 1) * 128], identity=identf[:]),
                        r=[f"tmpf{fi}", "identf"], w=[pk2])
                st = kst[fi]
                copy_op("act" if m % 2 else "dve", st[:, :, :].rearrange("p s f -> p (s f)"), p2[:, :], [pk2], [f"kst{fi}"])
                dst = o_k if m < 6 else o_v
                mm = m % 6
                S.dma("sp", lambda e, dst=dst, mm=mm, st=st: e.dma_start(
                    out=dst[t * NT:t * NT + N, mm * 128:(mm + 1) * 128].rearrange("(s p) f -> p s f", p=128), in_=st[:, :, :]),
                    "okv_out2", r=[f"kst{fi}"])
                if m >= 6:
                    c = m - 6
                    S.op("pool", lambda e, st=st, c=c: e.tensor_copy(out=va7[:, :, c, 0, 0:64], in_=st[:, :, 0:64]),
                         r=[f"kst{fi}"], w=[f"vaug{n}" for n in range(4)])
                    S.op("pool", lambda e, st=st, c=c: e.tensor_copy(out=va7[:, :, c, 1, 64:128], in_=st[:, :, 64:128]),
                         r=[f"kst{fi}"], w=[f"vaug{n}" for n in range(4)])
            for sbk in range(N // 128):
                tok0 = t * NT + sbk * 128
                blk = tok0 // 128
                S.dma("sp", lambda e, sbk=sbk, blk=blk: e.dma_start(
                    out=Vs[:, :, blk, :].rearrange("h p d -> p h d"), in_=vaug[:, sbk, :, :]), "vs_out",
                    r=[f"vaug{sbk}"], w=[f"Vs{t}"])
                p, pk = next_ps()
                for k in range(8):
                    S.op("pe", lambda e, p=p, k=k, sbk=sbk: e.matmul(
                        p[:, 0:NH], lhsT=xn[:, k, sbk * 128:(sbk + 1) * 128], rhs=wfb[:, k, :], start=(k == 0), stop=(k == 7)),
                        r=["wfb", XK[k]], w=[pk])
                lfs = lf[:, sbk, :]
                lk = f"lf{sbk}"
                S.op("dve", lambda e, p=p, lfs=lfs: e.tensor_tensor(out=lfs, in0=p[:, 0:NH], in1=bfb[:], op=ALU.add),
                     r=[pk, "bfb"], w=[lk])
                S.op("act", lambda e, lfs=lfs: e.activation(out=lfs, in_=lfs, func=AF.Exp, scale=-1.0), r=[lk], w=[lk])
                S.op("act", lambda e, lfs=lfs: e.activation(out=lfs, in_=lfs, func=AF.Ln, bias=1.0, scale=1.0), r=[lk], w=[lk])
                S.op("dve", lambda e, lfs=lfs: e.tensor_scalar(out=lfs, in0=lfs, scalar1=-1.0, scalar2=None, op0=ALU.mult),
                     r=[lk], w=[lk])
                S.dma("sp", lambda e, tok0=tok0, lfs=lfs: e.dma_start(out=o_logf[tok0:tok0 + 128, :], in_=lfs), "olf_out", r=[lk])
                p, pk = next_ps()
                S.op("pe", lambda e, p=p, lfs=lfs: e.matmul(p[:, 0:NH], lhsT=trif[:], rhs=lfs, start=True, stop=True),
                     r=["trif", lk], w=[pk])
                p2, pk2 = next_ps()
                S.op("pe", lambda e, p2=p2, lfs=lfs: e.matmul(p2[:, 0:NH], lhsT=onesf[:], rhs=lfs, start=True, stop=True),
                     r=["onesf", lk], w=[pk2])
                S.op("dve", lambda e, p=p: e.tensor_tensor(out=tmpf[0][:, 0:NH], in0=p[:, 0:NH], in1=carry[:], op=ALU.add),
                     r=[pk, "carry"], w=["tmpf0"])
                S.op("dve", lambda e, blk=blk: e.tensor_scalar(out=Fneg[:, blk, :], in0=tmpf[0][:, 0:NH], scalar1=-1.0,
                                                                scalar2=None, op0=ALU.mult), r=["tmpf0"], w=["Fneg"])
                S.op("dve", lambda e, sbk=sbk: e.tensor_copy(out=fq_sb[:, sbk, :], in_=tmpf[0][:, 0:NH]), r=["tmpf0"],
                     w=["fq_sb"])
                S.op("dve", lambda e, p2=p2: e.tensor_tensor(out=carry[:], in0=carry[:], in1=p2[:, 0:NH], op=ALU.add),
                     r=[pk2, "carry"], w=["carry"])

        def dense_tail(layer, N):
            def ev_out(m, p, pk):
                S.op("dve", lambda e: e.tensor_tensor(out=h[:, m, 0:N], in0=p[:, 0:N], in1=h[:, m, 0:N], op=ALU.add),
                     r=[pk, HK[m]], w=[HK[m]])
            proj_fm(U_OUT[layer], 8, range(8), lambda k: ymix[:, k, 0:N], YK, N, ev_out)
            rmsnorm(N, 2 + layer, lambda k: xn[:, k, 0:N], XK, HK)
            enter("hid")

            def ev_up(m, p, pk):
                i = m % 2
                S.op("act", lambda e: e.activation(out=tmpf[i][:, 0:N], in_=p[:, 0:N], func=AF.Relu), r=[pk], w=[f"tmpf{i}"])
                S.op("pool", lambda e: e.tensor_tensor(out=hid[:, m, 0:N], in0=tmpf[i][:, 0:N], in1=tmpf[i][:, 0:N], op=ALU.mult),
                     r=[f"tmpf{i}"], w=[HIDKEYS[m]])
            proj_fm(U_UP[layer], 8, range(32), lambda k: xn[:, k, 0:N], XK, N, ev_up)
            proj_fm(U_DOWN[layer], 32, range(8), lambda k: hid[:, k, 0:N], HIDKEYS, N, ev_out)

        def in_proj(layer, N):
            rmsnorm(N, layer, lambda k: xn[:, k, 0:N], XK, HK)

            def ev_z(m, p, pk):
                copy_op(evac_eng(), z[:, m, 0:N], p[:, 0:N], [pk], [ZK[m]])
            proj_fm(U_IN[layer], 8, range(8), lambda k: xn[:, k, 0:N], XK, N, ev_z)
            mem_attention(N, 0, lambda hb, hc, mt: KmT[hb:hb + 64, layer, hc, mt * 128:(mt + 1) * 128],
                          lambda mt, hd: Vmp[:, layer, mt, hd, :], ["KmT", "Vmp"])

        for t in range(NTILES):
            enter("xtm")
            for sbk in range(4):
                S.dma("sp", lambda e, sbk=sbk, t=t: e.dma_start(out=xtm[:, sbk, :], in_=x_in[t * NT + sbk * 128:t * NT + (sbk + 1) * 128, :]),
                      "xtm", w=["xtm"])
            for c in range(8):
                p, pk = next_ps()
                for sbk in range(4):
                    S.op("pe", lambda e, p=p, sbk=sbk, c=c: e.transpose(
                        out=p[:, sbk * 128:(sbk + 1) * 128], in_=xtm[:, sbk, c * 128:(c + 1) * 128], identity=identf[:]),
                        r=["xtm", "identf"], w=[pk])
                copy_op(evac_eng(), h[:, c, :], p[:, :], [pk], [HK[c]])
            in_proj(0, NT)
            s5_layer(NT, 128, 1, lambda ri, j, sg: hprev[:, ri, j:j + 1], lambda ri, j0, sg: hprev[:, ri, j0:j0 + 6], ["hprev"])
            dense_tail(0, NT)
            S.dma("sp", lambda e, t=t: e.dma_start(out=H1[t * 128:(t + 1) * 128, :], in_=h[:].rearrange("p k n -> p (k n)")),
                  "h1_out", r=HK, w=[f"H1_{t}"])
            kv_stage(t, NT)
            S.dma("sp", lambda e, t=t: e.dma_start(out=Fq[t * 128:(t + 1) * 128, :], in_=fq_sb[:].rearrange("p s h -> p (s h)")),
                  "fq_out", r=["fq_sb"], w=[f"Fq{t}"])

        for i1 in range(NL1):
            S.dma("pool", lambda e, i1=i1: e.indirect_dma_start(
                out=h[:].rearrange("p k n -> p (k n)"), out_offset=None, in_=H1,
                in_offset=bass.IndirectOffsetOnAxis(ap=hidx[:, i1:i1 + 1], axis=0)), "h1_g",
                r=["hidx"] + [f"H1_{tt}" for tt in range(NTILES)], w=HK)
            in_proj(1, NT)
            fox_layer(i1)
            dense_tail(1, NT)
            enter("yfm")
            rmsnorm(NT, 5, lambda k: yfm[:, k, 0:NT], ["yfm"] * 8, HK)
            enter("xtm")
            for sbk in range(4):
                for c2 in range(2):
                    p, pk = next_ps()
                    for cc in range(4):
                        c = c2 * 4 + cc
                        S.op("pe", lambda e, p=p, sbk=sbk, c=c, cc=cc: e.transpose(
                            out=p[:, cc * 128:(cc + 1) * 128], in_=yfm[:, c, sbk * 128:(sbk + 1) * 128], identity=identf[:]),
                            r=["yfm", "identf"], w=[pk])
                    copy_op(evac_eng(), xtm[:, sbk, c2 * 512:(c2 + 1) * 512], p[:, :], [pk], ["xtm"])
                S.dma("sp", lambda e, sbk=sbk, i1=i1: e.dma_start(
                    out=y_out[i1 * NT + sbk * 128:i1 * NT + (sbk + 1) * 128, :], in_=xtm[:, sbk, :]), "y_out", r=["xtm"])

        NS = 32
        enter("xtm")
        S.dma("sp", lambda e: e.dma_start(out=xtm[0:NS, 0, :], in_=xs_in[:, :]), "xtm", w=["xtm"])
        for c2 in range(2):
            p, pk = next_ps()
            for cc in range(4):
                c = c2 * 4 + cc
                S.op("pe", lambda e, p=p, c=c, cc=cc: e.transpose(
                    out=p[:, cc * NS:(cc + 1) * NS], in_=xtm[0:NS, 0, c * 128:(c + 1) * 128], identity=identf[0:NS, 0:NS]),
                    r=["xtm", "identf"], w=[pk])
            copy_op(evac_eng(), h[:, c2 * 4:(c2 + 1) * 4, 0:NS], p[:, 0:4 * NS].rearrange("p (c n) -> p c n", c=4), [pk],
                    HK[c2 * 4:(c2 + 1) * 4])

        def sample_kv():
            N = NS
            rmsnorm(N, 4, lambda k: xn[:, k, 0:N], XK, HK)
            enter("kv")
            for nch in range(12):
                p, pk = next_ps()
                wb, wk = getw(U_KV + nch)
                for k in range(8):
                    S.op("pe", lambda e, p=p, wb=wb, k=k: e.matmul(
                        p[0:N, 0:128], lhsT=xn[:, k, 0:N], rhs=wb[:, k, :], start=(k == 0), stop=(k == 7)),
                        r=[wk, XK[k]], w=[pk])
                copy_op(evac_eng(), kvt[0:N, nch * 128:(nch + 1) * 128], p[0:N, 0:128], [pk], [f"kvt{nch}"])
                if nch < 6:
                    p2, pk2 = next_ps()
                    for k in range(8):
                        S.op("pe", lambda e, p2=p2, wb=wb, k=k: e.matmul(
                            p2[:, 0:N], lhsT=wb[:, k, :], rhs=xn[:, k, 0:N], start=(k == 0), stop=(k == 7)),
                            r=[wk, XK[k]], w=[pk2])
                    copy_op(evac_eng(), KTnew[:, nch, :], p2[:, 0:N], [pk2], ["KTnew"])
            kkeys = [f"kvt{n}" for n in range(6)]
            vkeys = [f"kvt{n}" for n in range(6, 12)]
            S.dma("sp", lambda e: e.dma_start(out=o_sk[:, :], in_=kvt[0:N, 0:768]), "ok_out", r=kkeys)
            S.dma("sp", lambda e: e.dma_start(out=o_sv[:, :], in_=kvt[0:N, 768:1536]), "ov_out", r=vkeys)
            S.op("pool", lambda e: e.tensor_copy(out=vnbp[:, :], in_=kvt[0:N, 768:1536]), r=vkeys, w=["vnbp"])
            p, pk = next_ps()
            for k in range(8):
                S.op("pe", lambda e, p=p, k=k: e.matmul(p[0:N, 0:NH], lhsT=xn[:, k, 0:N], rhs=wfb[:, k, :],
                                                         start=(k == 0), stop=(k == 7)), r=["wfb", XK[k]], w=[pk])
            lfs = lf[0:N, 0, :]
            S.op("dve", lambda e, p=p: e.tensor_tensor(out=lfs, in0=p[0:N, 0:NH], in1=bfb[0:N, :], op=ALU.add),
                 r=[pk, "bfb"], w=["lf0"])
            S.op("act", lambda e: e.activation(out=lfs, in_=lfs, func=AF.Exp, scale=-1.0), r=["lf0"], w=["lf0"])
            S.op("act", lambda e: e.activation(out=lfs, in_=lfs, func=AF.Ln, bias=1.0, scale=1.0), r=["lf0"], w=["lf0"])
            S.op("dve", lambda e: e.tensor_scalar(out=lfs, in0=lfs, scalar1=-1.0, scalar2=None, op0=ALU.mult), r=["lf0"], w=["lf0"])
            S.dma("sp", lambda e: e.dma_start(out=o_slogf[:, :], in_=lfs), "olf_out", r=["lf0"])

        def sample_past_logf(sq):
            cl2 = cache_logf.rearrange("n s h -> n (s h)")
            if True:
                S.dma("pool", lambda e, sq=sq: e.indirect_dma_start(
                    out=lfpg.rearrange("p s h -> p (s h)"), out_offset=None, in_=cl2,
                    in_offset=bass.IndirectOffsetOnAxis(ap=ptT[:, sq:sq + 1], axis=0)), "lfpg", r=["ptT"], w=["lfpg"])
                for hd in range(NH):
                    S.op("dve", lambda e, hd=hd: e.tensor_tensor_scan(
                        out=lfpg[:, :, hd], data0=onesf[:, :], data1=lfpg[:, :, hd], initial=0.0, op0=ALU.mult, op1=ALU.add),
                        r=["lfpg", "onesf"], w=["lfpg"])
                S.op("dve", lambda e: e.tensor_copy(out=tmpf[0][:, 0:NH], in_=lfpg[:, 127, :]), r=["lfpg"], w=["tmpf0"])
                p, pk = next_ps()
                S.op("pe", lambda e, p=p: e.matmul(p[:, 0:NH], lhsT=strif[:], rhs=tmpf[0][:, 0:NH], start=True, stop=True),
                     r=["strif", "tmpf0"], w=[pk])
                p2, pk2 = next_ps()
                S.op("pe", lambda e, p2=p2: e.matmul(p2[:, 0:NH], lhsT=onesf[:], rhs=tmpf[0][:, 0:NH], start=True, stop=True),
                     r=["onesf", "tmpf0"], w=[pk2])
                S.op("dve", lambda e, p2=p2, sq=sq: e.tensor_copy(out=carry_s[:, sq, :], in_=p2[:, 0:NH]), r=[pk2], w=["carry_s"])
                S.op("dve", lambda e, p=p: e.tensor_copy(out=tmpf[1][:, 0:NH], in_=p[:, 0:NH]), r=[pk], w=["tmpf1"])
                S.op("dve", lambda e: e.tensor_tensor(
                    out=lfpg, in0=lfpg, in1=tmpf[1][:, 0:NH].unsqueeze(1).broadcast_to([128, 128, NH]), op=ALU.add),
                    r=["lfpg", "tmpf1"], w=["lfpg"])
                for h4 in range(3):
                    p, pk = next_ps()
                    for hh in range(4):
                        hd = h4 * 4 + hh
                        S.op("pe", lambda e, p=p, hh=hh, hd=hd: e.transpose(
                            out=p[:, hh * 128:(hh + 1) * 128], in_=lfpg[:, :, hd], identity=identf[:]),
                            r=["lfpg", "identf"], w=[pk])
                    S.op("dve", lambda e, p=p, h4=h4, sq=sq: e.tensor_scalar(
                        out=FnT[:, h4 * 4:(h4 + 1) * 4, :], in0=p[:, :].rearrange("p (h n) -> p h n", h=4),
                        scalar1=-1.0, scalar2=None, op0=ALU.mult), r=[pk], w=["FnT"])

        def sample_fnew(sq):
            N = NS
            p, pk = next_ps()
            S.op("pe", lambda e, p=p: e.matmul(p[0:N, 0:NH], lhsT=btri[:, :], rhs=lf[0:N, 0, :], start=True, stop=False),
                 r=["btri", "lf0"], w=[pk])
            S.op("pe", lambda e, p=p, sq=sq: e.matmul(p[0:N, 0:NH], lhsT=esel[:, sq, :], rhs=carry_s[:, sq, :],
                                                       start=False, stop=True), r=["esel", "carry_s"], w=[pk])
            fnew = tmpf[0][0:N, 0:NH]
            S.op("dve", lambda e, p=p: e.tensor_copy(out=fnew, in_=p[0:N, 0:NH]), r=[pk], w=["tmpf0"])
            S.op("dve", lambda e: e.tensor_scalar(out=tmpf[1][0:N, 0:NH], in0=fnew, scalar1=-1.0, scalar2=None, op0=ALU.mult),
                 r=["tmpf0"], w=["tmpf1"])
            S.dma("sp", lambda e, sq=sq: e.dma_start(out=FnegN[:, sq, :], in_=tmpf[1][sq * 8:(sq + 1) * 8, 0:NH]), "fnegn",
                  r=["tmpf1"], w=["FnegN"])

        def sample_fox():
            N = NS
            enter("satt")
            ck2 = cache_k.rearrange("n s h d -> (n s) (h d)")
            cv2 = cache_v.rearrange("n s h d -> (n s) (h d)")
            accA = PSW[1][:, 0:384].rearrange("p (c n) -> p c n", c=4)
            accB = PSW[1][:, 512:512 + 288].rearrange("p (c n) -> p c n", c=3)
            S.op("pool", lambda e: e.memset(Qblk[:], 0.0), w=["Qblk"])
            for sq in range(4):
                q0 = sq * 8
                sample_past_logf(sq)
                sample_fnew(sq)
                S.dma("sp", lambda e, sq=sq: e.dma_start(out=VnewS, in_=vnbp[sq * 8:(sq + 1) * 8, :]), "vnew",
                      r=["vnbp"], w=["VnewS"])
                for hh in range(2):
                    hb = hh * 64
                    S.op("act", lambda e, hb=hb, hh=hh, q0=q0: e.activation(
                        out=Qblk[hb:hb + 64, :, hh * 8:(hh + 1) * 8], in_=z[hb:hb + 64, 0:6, q0:q0 + 8], func=AF.Copy, scale=0.125),
                        r=ZK[0:6], w=["Qblk"])
                xq = tmpf[2][0:N, 0:96].rearrange("p (h q) -> p h q", q=8)
                S.op("dve", lambda e, sq=sq: e.tensor_tensor(
                    out=xq, in0=qmask[:, sq, :, :], in1=tmpf[0][0:N, 0:NH].unsqueeze(2).broadcast_to([N, NH, 8]), op=ALU.mult),
                    r=["qmask", "tmpf0"], w=["tmpf2"])
                p, pk = next_ps()
                S.op("pe", lambda e, p=p: e.matmul(p[:, 0:96], lhsT=onesf[0:N, :], rhs=tmpf[2][0:N, 0:96], start=True, stop=True),
                     r=["onesf", "tmpf2"], w=[pk])
                S.op("dve", lambda e, p=p: e.tensor_copy(out=Ftb[:].rearrange("p h q -> p (h q)"), in_=p[:, 0:96]), r=[pk], w=["Ftb"])
                for n in range(129):
                    new = (n == 128)
                    bi = n % 2
                    if not new:
                        col = sq * 128 + n
                        S.dma("pool", lambda e, bi=bi, col=col: e.indirect_dma_start(
                            out=Kpg[bi], out_offset=None, in_=ck2,
                            in_offset=bass.IndirectOffsetOnAxis(ap=idx_all[:, col:col + 1], axis=0)), f"Kpg{bi}",
                            r=["idx_all"], w=[f"Kpg{bi}"])
                        S.dma("pool", lambda e, bi=bi, col=col: e.indirect_dma_start(
                            out=Vpg[bi], out_offset=None, in_=cv2,
                            in_offset=bass.IndirectOffsetOnAxis(ap=idx_all[:, col:col + 1], axis=0)), f"Vpg{bi}",
                            r=["idx_all"], w=[f"Vpg{bi}"])
                        S.op("dve", lambda e, bi=bi: e.tensor_copy(out=Kpb[bi][:, :], in_=Kpg[bi]), r=[f"Kpg{bi}"], w=[f"Kpb{bi}"])
                        ptb_ = PSW[0][:].bitcast(BF16)
                        for c in range(6):
                            S.op("pe", lambda e, bi=bi, c=c, ptb_=ptb_: e.transpose(
                                out=ptb_[:, c * 128:(c + 1) * 128], in_=Kpb[bi][:, c * 128:(c + 1) * 128], identity=identb[:]),
                                r=[f"Kpb{bi}", "identb"], w=["psw0"])
                        S.op("act", lambda e, bi=bi, ptb_=ptb_: e.activation(
                            out=KTp[bi].rearrange("p c k -> p (c k)"), in_=ptb_[:, 0:768], func=AF.Copy), r=["psw0"], w=[f"KTp{bi}"])
                        S.op("dve", lambda e, bi=bi: e.tensor_copy(out=Vpb[bi], in_=Vpg[bi]), r=[f"Vpg{bi}"], w=[f"Vpb{bi}"])
                        nk = 128
                        kt_fn = lambda c, bi=bi: KTp[bi][:, c, :]
                        v_fn = lambda c, bi=bi: Vpb[bi][:, c * 128:(c + 1) * 128]
                        ktk, vk = f"KTp{bi}", f"Vpb{bi}"
                    else:
                        nk = 8
                        kt_fn = lambda c, q0=q0: KTnew[:, c, q0:q0 + 8]
                        v_fn = lambda c: VnewS[:, c * 128:(c + 1) * 128]
                        ktk, vk = "KTnew", "VnewS"
                    p, pk = next_ps()
                    for c in range(6):
                        S.op("pe", lambda e, p=p, c=c, kt_fn=kt_fn, nk=nk: e.matmul(
                            p[0:nk, c * 16:(c + 1) * 16], lhsT=kt_fn(c), rhs=Qblk[:, c, :], start=True, stop=True),
                            r=[ktk, "Qblk"], w=[pk])
                    sc = tmpf[1][0:nk, 0:96].rearrange("p (h q) -> p h q", q=8)
                    p3 = p[0:nk, 0:96].rearrange("p (h q) -> p h q", q=8)
                    S.op("dve", lambda e, p3=p3, sc=sc, nk=nk: e.tensor_tensor(out=sc, in0=p3, in1=Ftb[0:nk, :, :], op=ALU.add),
                         r=[pk, "Ftb"], w=["tmpf1"])
                    if not new:
                        S.op("dve", lambda e, sc=sc, sq=sq, n=n: e.tensor_tensor(
                            out=sc, in0=sc, in1=FnT[:, :, n].unsqueeze(2).broadcast_to([128, NH, 8]), op=ALU.add),
                            r=["tmpf1", "FnT"], w=["tmpf1"])
                    else:
                        S.op("dve", lambda e, sc=sc, sq=sq: e.tensor_tensor(
                            out=sc, in0=sc, in1=FnegN[:, sq, :].unsqueeze(2).broadcast_to([8, NH, 8]), op=ALU.add),
                            r=["tmpf1", "FnegN"], w=["tmpf1"])
                        S.op("dve", lambda e, sc=sc: e.tensor_tensor(out=sc, in0=sc, in1=cmask[:, :, :], op=ALU.add),
                             r=["tmpf1", "cmask"], w=["tmpf1"])
                    pt = pT[n % 3]
                    ptk = f"pT{n % 3}"
                    S.op("act", lambda e, pt=pt, nk=nk: e.activation(out=pt[0:nk, 0:96], in_=tmpf[1][0:nk, 0:96], func=AF.Exp),
                         r=["tmpf1"], w=[ptk])
                    for c in range(6):
                        acc = accA[:, c, :] if c < 4 else accB[:, c - 4, :]
                        S.op("pe", lambda e, acc=acc, c=c, v_fn=v_fn, pt=pt, nk=nk, n=n, new=new: e.matmul(
                            acc, lhsT=v_fn(c), rhs=pt[0:nk, 0:96], start=(n == 0 and c in (0, 4)), stop=new,
                            skip_group_check=True), r=[vk, ptk], w=["psw1"])
                    S.op("pe", lambda e, pt=pt, nk=nk, new=new: e.matmul(
                        accB[:, 2, :], lhsT=onesb[0:nk, :], rhs=pt[0:nk, 0:96], start=False, stop=new, skip_group_check=True),
                        r=["onesb", ptk], w=["psw1"])
                S.op("dve", lambda e: e.reciprocal(out=tmpf[2][:, 0:96], in_=accB[:, 2, :]), r=["psw1"], w=["tmpf2"])
                for c in range(6):
                    acc = accA[:, c, :] if c < 4 else accB[:, c - 4, :]
                    for hh in range(2):
                        hb = hh * 64
                        cs = (2 * c + hh) * 8
                        S.op("dve", lambda e, acc=acc, c=c, hb=hb, cs=cs, q0=q0: e.tensor_tensor(
                            out=ymix[hb:hb + 64, c, q0:q0 + 8], in0=acc[hb:hb + 64, cs:cs + 8], in1=tmpf[2][hb:hb + 64, cs:cs + 8],
                            op=ALU.mult), r=["psw1", "tmpf2"], w=[YK[c]])

        for layer in range(2):
            rmsnorm(NS, layer, lambda k: xn[:, k, 0:NS], XK, HK)

            def ev_zs(m, p, pk):
                copy_op(evac_eng(), z[:, m, 0:NS], p[:, 0:NS], [pk], [ZK[m]])
            proj_fm(U_IN[layer], 8, range(8), lambda k: xn[:, k, 0:NS], XK, NS, ev_zs)
            for sq in range(4):
                mem_attention(8, sq * 8,
                              lambda hb, hc, mt, layer=layer, sq=sq: KmTs[hb:hb + 64, sq, layer, hc, mt * 128:(mt + 1) * 128],
                              lambda mt, hd, layer=layer, sq=sq: Vmps[:, sq, layer, mt, hd, :], ["KmTs", "Vmps"])
            if layer == 0:
                s5_layer(NS, 8, 4, lambda ri, j, sg: h0s[:, ri, j, sg:sg + 1], lambda ri, j0, sg: s5so[:, ri, j0:j0 + 6, sg],
                         ["s5so"])
                S.dma("sp", lambda e: e.dma_start(out=o_s5s[:, :, :, :], in_=s5so[:]), "s5s_out", r=["s5so"])
            else:
                sample_fox()
            dense_tail(layer, NS)
            if layer == 0:
                sample_kv()
        enter("yfm")
        rmsnorm(NS, 5, lambda k: yfm[:, k, 0:NS], ["yfm"] * 8, HK)
        enter("xtm")
        for c2 in range(2):
            p, pk = next_ps()
            for cc in range(4):
                c = c2 * 4 + cc
                S.op("pe", lambda e, p=p, c=c, cc=cc: e.transpose(
                    out=p[0:NS, cc * 128:(cc + 1) * 128], in_=yfm[:, c, 0:NS], identity=identf[:]),
                    r=["yfm", "identf"], w=[pk])
            copy_op(evac_eng(), xtm[0:NS, 0, c2 * 512:(c2 + 1) * 512], p[0:NS, :], [pk], ["xtm"])
        S.dma("sp", lambda e: e.dma_start(out=ys_out[:, :], in_=xtm[0:NS, 0, :]), "y_out", r=["xtm"])

        S.dma("sp", lambda e: e.dma_start(out=o_s5[:, :, :], in_=hprev[:]), "s5_out", r=["hprev"])

        S.finish_waits("sp")
        with nc.Block() as block:
            S.emit(block)
    return nc


def _units(W, KG=8):
    K, N = W.shape
    Kc, Mc = K // 128, N // 128
    nkq = (Kc + 7) // 8
    out = np.zeros((Mc, nkq, 128, 8, 128), np.float32)
    W4 = W.reshape(Kc, 128, Mc, 128)
    for kq in range(nkq):
        kn = min(8, Kc - kq * 8)
        out[:, kq, :, :kn, :] = W4[kq * 8:kq * 8 + kn].transpose(2, 1, 0, 3)
    return out.reshape(Mc * nkq, 128, 1024)


def _selden():
    sd = np.zeros((128, 2, 128), np.float32)
    sd[64, 0, 0:64] = 1.0
    sd[0, 1, 64:128] = 1.0
    return sd


def _fm(v, nch):
    return np.ascontiguousarray(np.asarray(v, np.float32).reshape(nch, 128).T)


def prepare_shared(inp):
    f = lambda k: np.asarray(inp[k], np.float32)
    units = []
    for i in range(2):
        units += [_units(f("w_in")[i]), _units(f("w_out")[i]), _units(f("w_up")[i]), _units(f("w_down")[i])]
    units += [_units(f("s5_w_glu")[0]), _units(f("w_kv"))]
    wall = np.concatenate(units, axis=0)
    assert wall.shape[0] == N_UNITS
    gall = np.stack([_fm(f("norm_mix")[0], 8), _fm(f("norm_mix")[1], 8), _fm(f("norm_mlp")[0], 8),
                     _fm(f("norm_mlp")[1], 8), _fm(f("norm_kv"), 8), _fm(f("norm_final"), 8)], axis=1)
    a_re, a_im, ls = f("s5_a_re")[0], f("s5_a_im")[0], f("s5_log_step")[0]
    lse = np.repeat(ls[:, None], 64, axis=1)
    a_b = np.stack([np.broadcast_to(a.reshape(1, 3072), (128, 3072)) for a in (a_re, a_im, lse)]).astype(np.float32)
    sm = lambda a: a.reshape(24, 128).T
    a_s = np.stack([sm(a_re), sm(a_im), sm(lse)], axis=1).astype(np.float32)
    b_re, b_im = f("s5_b_re")[0], f("s5_b_im")[0]
    c_re, c_im = f("s5_c_re")[0], f("s5_c_im")[0]
    bpad = np.zeros((2, 128, 24, 128), np.float32)
    cpad = np.zeros((2, 128, 24, 128), np.float32)
    for g in range(48):
        j, g2, gl = g // 2, g % 2, g % 8
        for ri, (b, c) in enumerate(((b_re, c_re), (b_im, c_im))):
            bpad[ri, gl * 16:(gl + 1) * 16, j, g2 * 64:(g2 + 1) * 64] = b[g].T
            cpad[ri, g2 * 64:(g2 + 1) * 64, j, gl * 16:(gl + 1) * 16] = c[g].T
    tri = np.triu(np.ones((128, 128), np.float32))
    tcount = np.broadcast_to(np.arange(1, 129, dtype=np.float32)[None, :], (128, 128)).copy()
    s_idx = np.arange(128)[:, None, None]
    d_idx = np.arange(4)[None, :, None]
    q_idx = np.arange(512)[None, None, :]
    masks = np.where(128 * d_idx + s_idx <= q_idx, 0.0, -1e30).astype(np.float32).astype(ml_dtypes.bfloat16)
    return {
        "w_mem_kv": np.ascontiguousarray(f("w_mem_kv")), "wall": wall, "gall": np.ascontiguousarray(gall),
        "bglu": _fm(f("s5_b_glu")[0], 6), "s5d": _fm(f("s5_d")[0].reshape(-1), 6),
        "wf": np.ascontiguousarray(f("w_f").reshape(8, 128, 12).transpose(1, 0, 2)),
        "bfb": np.ascontiguousarray(np.broadcast_to(f("b_f")[None, :], (128, 12))),
        "a_b": a_b, "a_s": np.ascontiguousarray(a_s), "bpad": bpad, "cpad": cpad,
        "selden": _selden(), "ident_f": np.eye(128, dtype=np.float32), "tri_f": tri, "tcount": tcount, "masks": masks,
    }


T_PROMPT = 8192
DEBUG = False
COMPACT_DEV = False
DBG_NAMES = []
DBG_OUT = {}


def kernel(**inputs):
    T = T_PROMPT
    shared = prepare_shared(inputs)
    x_prompt = np.asarray(inputs["x_prompt"], np.float32)
    mem_prompt = np.asarray(inputs["mem_prompt"], np.float32)
    x_sample = np.asarray(inputs["x_sample"], np.float32)
    page_table = np.asarray(inputs["page_table"], np.int32)
    cache_k = np.asarray(inputs["cache_k"], np.float32)
    cache_v = np.asarray(inputs["cache_v"], np.float32)
    cache_logf = np.asarray(inputs["cache_logf"], np.float32)
    st_re = np.asarray(inputs["state_s5_re"], np.float32)[0]
    st_im = np.asarray(inputs["state_s5_im"], np.float32)[0]
    cmk = np.asarray(inputs["cache_mem_k"], np.float32).reshape(2, 32, MEM_T, 256)
    cmv = np.asarray(inputs["cache_mem_v"], np.float32).reshape(2, 32, MEM_T, 256)
    npool = cache_k.shape[0]
    nc = build_program(T, 512 if COMPACT_DEV else npool)
    stri = np.triu(np.ones((128, 128), np.float32), k=1)
    ii = np.arange(32)
    btri = ((ii[:, None] // 8 == ii[None, :] // 8) & (ii[:, None] <= ii[None, :])).astype(np.float32)
    esel = np.zeros((128, 4, 32), np.float32)
    for sq in range(4):
        esel[0, sq, sq * 8:(sq + 1) * 8] = 1.0
    qmask = np.zeros((32, 4, NH, 8), np.float32)
    for sq in range(4):
        for q in range(8):
            qmask[sq * 8 + q, sq, :, q] = 1.0
    cmask = np.where(np.arange(8)[:, None, None] <= np.arange(8)[None, None, :], 0.0, -1e30).astype(np.float32)
    cmask = np.ascontiguousarray(np.broadcast_to(cmask, (8, NH, 8)))
    consts = {"iota_i": np.arange(128, dtype=np.int32)[:, None].copy(), "stri_f": stri, "btri32": btri, "esel": esel,
              "qmask": qmask, "cmask": cmask}
    in_maps = []
    for c in range(NCORES):
        s = c // 4
        m = dict(shared)
        m.update(consts)
        m["x"] = np.ascontiguousarray(x_prompt[s, :T])
        j = c % 4
        m["hidx"] = np.ascontiguousarray(((4 * np.arange(4)[None, :] + j) * 128 + np.arange(128)[:, None]).astype(np.int32))
        ohbb = np.zeros((128, 2, 4), np.float32)
        ohbb[:, 0, j] = 1.0
        ohbb[:, 1, j + 1:] = -1e30
        m["ohbb"] = ohbb
        m["mem_prompt"] = np.ascontiguousarray(mem_prompt[s])
        sl = slice(4 * c, 4 * c + 4)
        m["xs"] = np.ascontiguousarray(x_sample[sl].reshape(32, D))
        h0 = np.stack([st_re[sl], st_im[sl]])
        m["h0s"] = np.ascontiguousarray(h0.reshape(2, 4, 24, 2, 64).transpose(3, 4, 0, 2, 1).reshape(128, 2, 24, 4))
        m["cmk"] = np.ascontiguousarray(cmk[:, sl])
        m["cmv"] = np.ascontiguousarray(cmv[:, sl])
        pt = page_table[sl]
        if COMPACT_DEV:
            flat = pt.reshape(-1)
            m["cache_k"] = np.ascontiguousarray(cache_k[flat])
            m["cache_v"] = np.ascontiguousarray(cache_v[flat])
            m["cache_logf"] = np.ascontiguousarray(cache_logf[flat])
            pt = np.arange(512, dtype=np.int32).reshape(4, 128)
        else:
            m["cache_k"], m["cache_v"], m["cache_logf"] = cache_k, cache_v, cache_logf
        m["ptab"] = np.ascontiguousarray(pt.reshape(1, 512))
        m["ptabT"] = np.ascontiguousarray(pt.T)
        in_maps.append(m)
    res = run_bass_kernel_spmd(nc, in_maps, core_ids=list(range(NCORES)))
    R = res.results
    sel = [R[0], R[4]]
    if DEBUG:
        DBG_OUT.clear()
        for i, n in enumerate(DBG_NAMES[0]):
            DBG_OUT[n] = R[0]["dbg"][i]
    nl1 = (T // NT) // 4
    y_prompt = np.zeros((2, T, D), np.float32)
    for c in range(NCORES):
        yc = R[c]["y"].reshape(nl1, NT, D)
        for i in range(nl1):
            t0 = (4 * i + c % 4) * NT
            y_prompt[c // 4, t0:t0 + NT] = yc[i]
    okv = np.stack([r["o_mem_kv"] for r in sel], axis=1)
    p_mem_k = np.ascontiguousarray(okv[..., :256]).reshape(2, 2, 256, 4, 64)
    p_mem_v = np.ascontiguousarray(okv[..., 256:]).reshape(2, 2, 256, 4, 64)
    p_k = np.stack([r["o_k"] for r in sel]).reshape(2, T, NH, 64)
    p_v = np.stack([r["o_v"] for r in sel]).reshape(2, T, NH, 64)
    p_logf = np.stack([r["o_logf"] for r in sel])
    s5 = np.stack([r["o_s5"] for r in sel])
    s5 = s5.reshape(2, 2, 64, 2, 24).transpose(3, 0, 4, 1, 2).reshape(2, 2, 48, 64)
    p_s5_re, p_s5_im = np.ascontiguousarray(s5[0][None]), np.ascontiguousarray(s5[1][None])
    y_sample = np.concatenate([r["ys"].reshape(4, 8, D) for r in R])
    s_k = np.concatenate([r["o_sk"].reshape(4, 8, NH, 64) for r in R])
    s_v = np.concatenate([r["o_sv"].reshape(4, 8, NH, 64) for r in R])
    s_logf = np.concatenate([r["o_slogf"].reshape(4, 8, NH) for r in R])
    ss = np.stack([r["o_s5s"] for r in R])
    ss = ss.reshape(8, 2, 64, 2, 24, 4).transpose(3, 0, 5, 4, 1, 2).reshape(2, 32, 48, 64)
    s_s5_re, s_s5_im = np.ascontiguousarray(ss[0][None]), np.ascontiguousarray(ss[1][None])
    return (y_prompt, y_sample, p_s5_re, p_s5_im, p_mem_k, p_mem_v, p_k, p_v, p_logf,
            s_s5_re, s_s5_im, s_k, s_v, s_logf)
```

```python
import contextlib
import math
import numpy as np
import ml_dtypes
import concourse.bass as bass
import concourse.mybir as mybir
from concourse.bass_utils import run_bass_kernel_spmd

F32 = mybir.dt.float32
BF16 = mybir.dt.bfloat16
I32 = mybir.dt.int32
ALU = mybir.AluOpType
AF = mybir.ActivationFunctionType

NCORES = 8
D = 1024
MEM_T = 256
NT = 512
NH = 12
DMAIN = 768
TWO_PI = 2.0 * math.pi

U_IN = [0, 80]
U_OUT = [8, 88]
U_UP = [16, 96]
U_DOWN = [48, 128]
U_GLU = 160
U_KV = 166
N_UNITS = 178


class Sched:
    def __init__(self, nc, stack):
        self.nc = nc
        self.stack = stack
        self.engs = ["pe", "act", "dve", "pool", "sp"]
        self.ops = {e: [] for e in self.engs}
        self.sem = {e: stack.enter_context(nc.semaphore("sem_" + e)) for e in self.engs}
        self.cnt = {e: 0 for e in self.engs}
        self.epoch = {e: 0 for e in self.engs}
        self.old = []
        self.dsem = {}
        self.lastw = {}
        self.reads = {}
        self.waited = {e: {} for e in self.engs}

    def _collect(self, eng, r, w):
        waits = {}

        def add(t):
            sid, sem, val = t
            if sid not in waits or waits[sid][1] < val:
                waits[sid] = (sem, val)

        for k in list(r) + list(w):
            lw = self.lastw.get(k)
            if lw is not None:
                add(lw)
        for k in w:
            for rd in self.reads.get(k, []):
                add(rd)
        final = []
        for sid, (sem, val) in waits.items():
            if eng == "pe" and sid.startswith("pe#"):
                continue
            if self.waited[eng].get(sid, 0) >= val:
                continue
            self.waited[eng][sid] = val
            final.append((sem, val))
        return final

    def _record(self, t, r, w):
        for k in w:
            self.lastw[k] = t
            self.reads[k] = []
        for k in r:
            lst = self.reads.setdefault(k, [])
            lst.append(t)
            if len(lst) > 24:
                best = {}
                for sid, sem, val in lst:
                    if sid not in best or best[sid][2] < val:
                        best[sid] = (sid, sem, val)
                self.reads[k] = list(best.values())

    def op(self, eng, fn, r=(), w=()):
        final = self._collect(eng, r, w)
        if self.cnt[eng] >= 30000:
            self.old.append((self.sem[eng], self.cnt[eng]))
            self.epoch[eng] += 1
            self.sem[eng] = self.stack.enter_context(self.nc.semaphore(f"sem_{eng}_{self.epoch[eng]}"))
            self.cnt[eng] = 0
        self.cnt[eng] += 1
        v = self.cnt[eng]
        self.ops[eng].append((final, fn, self.sem[eng], 1))
        self._record((f"{eng}#{self.epoch[eng]}", self.sem[eng], v), r, w)

    def dma(self, q, fn, key, r=(), w=()):
        final = self._collect(q, r, w)
        if key not in self.dsem:
            self.dsem[key] = [self.stack.enter_context(self.nc.semaphore("d_" + key)), 0]
        ds = self.dsem[key]
        ds[1] += 16
        self.ops[q].append((final, fn, ds[0], 16))
        self._record(("d_" + key, ds[0], ds[1]), r, w)

    def barrier(self):
        allw = [(sem, val) for (sem, val) in self.dsem.values()] + list(self.old)
        allw += [(self.sem[e], self.cnt[e]) for e in self.engs if self.cnt[e] > 0]
        for e in self.engs:
            self.ops[e].append((list(allw), None, None, 0))

    def finish_waits(self, eng="sp"):
        final = [(sem, val) for (sem, val) in self.dsem.values()] + list(self.old)
        for e in self.engs:
            if e != eng and self.cnt[e] > 0:
                final.append((self.sem[e], self.cnt[e]))
        self.ops[eng].append((final, None, None, 0))

    def emit(self, block):
        table = {"pe": block.tensor, "act": block.scalar, "dve": block.vector,
                 "pool": block.gpsimd, "sp": block.sync}
        for e in self.engs:
            ops = self.ops[e]

            def body(engine, ops=ops):
                for waits, fn, sem, inc in ops:
                    for s, v in waits:
                        engine.wait_ge(s, v)
                    if fn is not None:
                        ins = fn(engine)
                        ins.then_inc(sem, inc)

            table[e](body)


def build_program(T, NPOOL=5120):
    nc = bass.Bass("TRN2", target_bir_lowering=False)
    NTILES = T // NT
    NBLK = T // 128

    def din(name, shape, dt=F32):
        return nc.dram_tensor(name, list(shape), dt, kind="ExternalInput").ap()

    def dout(name, shape, dt=F32):
        return nc.dram_tensor(name, list(shape), dt, kind="ExternalOutput").ap()

    def dscr(name, shape, dt):
        return nc.dram_tensor(name, list(shape), dt, kind="Internal").ap()

    x_in = din("x", [T, D])
    mem_prompt = din("mem_prompt", [MEM_T, D])
    w_mem_kv = din("w_mem_kv", [2, D, 512])
    wall = din("wall", [N_UNITS, 128, 1024])
    gall_d = din("gall", [128, 6, 8])
    bglu_d = din("bglu", [128, 6])
    s5d_d = din("s5d", [128, 6])
    wf_d = din("wf", [128, 8, 12])
    bf_d = din("bfb", [128, 12])
    a_b_d = din("a_b", [3, 128, 3072])
    a_s_d = din("a_s", [128, 3, 24])
    bpad_d = din("bpad", [2, 128, 24, 128])
    cpad_d = din("cpad", [2, 128, 24, 128])
    ident_f = din("ident_f", [128, 128])
    tri_d = din("tri_f", [128, 128])
    tcount_d = din("tcount", [128, 128])
    masks_d = din("masks", [128, 4, 512], BF16)
    selden_d = din("selden", [128, 2, 128])

    NPH = NPOOL
    xs_in = din("xs", [32, D])
    h0s_d = din("h0s", [128, 2, 24, 4])
    cmk_d = din("cmk", [2, 4, MEM_T, 256])
    cmv_d = din("cmv", [2, 4, MEM_T, 256])
    cache_k = din("cache_k", [NPH, 128, NH, 64])
    cache_v = din("cache_v", [NPH, 128, NH, 64])
    cache_logf = din("cache_logf", [NPH, 128, NH])
    ptab_d = din("ptab", [1, 512], I32)
    ptabT_d = din("ptabT", [128, 4], I32)
    iota_d = din("iota_i", [128, 1], I32)
    stri_d = din("stri_f", [128, 128])
    btri_d = din("btri32", [32, 32])
    esel_d = din("esel", [128, 4, 32])
    qmask_d = din("qmask", [32, 4, NH, 8])
    cmask_d = din("cmask", [8, NH, 8])
    ys_out = dout("ys", [32, D])
    o_sk = dout("o_sk", [32, DMAIN])
    o_sv = dout("o_sv", [32, DMAIN])
    o_slogf = dout("o_slogf", [32, NH])
    o_s5s = dout("o_s5s", [128, 2, 24, 4])
    NL1 = NTILES // 4
    hidx_d = din("hidx", [128, 4], I32)
    ohbb_d = din("ohbb", [128, 2, 4])
    y_out = dout("y", [NL1 * NT, D])
    o_mem_kv = dout("o_mem_kv", [2, MEM_T, 512])
    o_k = dout("o_k", [T, DMAIN])
    o_v = dout("o_v", [T, DMAIN])
    o_logf = dout("o_logf", [T, NH])
    o_s5 = dout("o_s5", [128, 2, 24])

    dbg_out = dout("dbg", [16, 128, NT]) if DEBUG else None
    wscr = dscr("wscr", [N_UNITS, 128, 1024], BF16)
    H1 = dscr("H1", [NTILES * 128, 8 * NT], F32)
    Fq = dscr("Fq", [NTILES * 128, 4 * NH], F32)
    KTs = dscr("KTs", [NH, 64, T], BF16)
    Vs = dscr("Vs", [NH, 128, NBLK, 128], BF16)

    with contextlib.ExitStack() as st:
        S = Sched(nc, st)

        def sb(name, shape, dt, stack=None):
            return (stack or st).enter_context(nc.sbuf_tensor("sb_" + name, list(shape), dt))

        def ps(name, shape, dt=F32):
            return st.enter_context(nc.psum_tensor(name, list(shape), dt))

        PS = [ps(f"ps{i}", [128, 512], F32) for i in range(4)]
        PSW = [ps(f"psw{i}", [128, 1024], F32) for i in range(2)]
        psrot = [0]

        def next_ps():
            i = psrot[0] % 4
            psrot[0] += 1
            return PS[i], f"ps{i}"

        evrot = [0]

        def evac_eng():
            evrot[0] += 1
            return "dve" if evrot[0] % 2 else "act"

        def copy_op(eng, out, in_, r, w):
            if eng == "act":
                S.op("act", lambda e: e.activation(out=out, in_=in_, func=AF.Copy), r=r, w=w)
            else:
                S.op(eng, lambda e: e.tensor_copy(out=out, in_=in_), r=r, w=w)

        identf = sb("identf", [128, 128], F32)
        identb = sb("identb", [128, 128], BF16)
        onesb = sb("onesb", [128, 128], BF16)
        onesf = sb("onesf", [128, 128], F32)
        trif = sb("trif", [128, 128], F32)
        masks = sb("masks", [128, 4, 512], BF16)
        selden = sb("selden_sb", [128, 2, 128], F32)
        gall = sb("gall_sb", [128, 6, 8], F32)
        bglu = sb("bglu_sb", [128, 6], F32)
        s5d = sb("s5d_sb", [128, 6], F32)
        wfb = sb("wfb", [128, 8, 12], BF16)
        bfb = sb("bfb_sb", [128, 12], F32)
        Ctab = sb("Ctab", [128, 24, 128], F32)
        Stab = sb("Stab", [128, 24, 128], F32)
        r_s = sb("r_s", [128, 24], F32)
        Bpad = sb("Bpad", [128, 2, 24, 128], BF16)
        Cpad = sb("Cpad", [128, 2, 24, 128], BF16)
        hprev = sb("hprev", [128, 2, 24], F32)
        KmT = sb("KmT", [128, 2, 2, MEM_T], BF16)
        Vmp = sb("Vmp", [128, 2, 2, 4, 128], BF16)
        onespad = sb("onespad", [128, 2, 128], BF16)
        Fneg = sb("Fneg", [128, max(NBLK, 4), NH], F32)
        idx_all = sb("idx_all", [128, 512], I32)
        ptT = sb("ptT", [128, 4], I32)
        h0s = sb("h0s_sb", [128, 2, 24, 4], F32)
        KmTs = sb("KmTs", [128, 4, 2, 2, MEM_T], BF16)
        Vmps = sb("Vmps", [128, 4, 2, 2, 4, 128], BF16)
        strif = sb("strif", [128, 128], F32)
        btri = sb("btri_sb", [32, 32], F32)
        esel = sb("esel_sb", [128, 4, 32], F32)
        qmask = sb("qmask_sb", [32, 4, NH, 8], F32)
        cmask = sb("cmask_sb", [8, NH, 8], F32)
        carry = sb("carry", [128, NH], F32)
        ZF = sb("ZF", [128, 4, NH, 65], BF16)

        def ld(dst, src, key):
            S.dma("sp", lambda e: e.dma_start(out=dst, in_=src), key, w=[key])

        ld(identf[:], ident_f[:, :], "identf")
        ld(trif[:], tri_d[:, :], "trif")
        ld(masks[:], masks_d[:, :, :], "masks")
        ld(selden[:], selden_d[:, :, :], "selden")
        ld(gall[:], gall_d[:, :, :], "gall")
        ld(bglu[:], bglu_d[:, :], "bglu")
        ld(s5d[:], s5d_d[:, :], "s5d")
        ld(bfb[:], bf_d[:, :], "bfb")
        ld(h0s[:], h0s_d[:, :, :, :], "h0s")
        ld(strif[:], stri_d[:, :], "strif")
        ld(btri[:], btri_d[:, :], "btri")
        ld(esel[:], esel_d[:, :, :], "esel")
        ld(qmask[:], qmask_d[:, :, :, :], "qmask")
        ld(cmask[:], cmask_d[:, :, :], "cmask")
        ld(ptT[:], ptabT_d[:, :], "ptT")
        hidx = sb("hidx_sb", [128, 4], I32)
        ohbb = sb("ohbb_sb", [128, 2, 4], F32)
        fq_sb = sb("fq_sb", [128, 4, NH], F32)
        FnB = sb("FnB", [128, 16, NH], F32)
        ld(hidx[:], hidx_d[:, :], "hidx")
        ld(ohbb[:], ohbb_d[:, :, :], "ohbb")
        S.op("dve", lambda e: e.tensor_copy(out=identb[:], in_=identf[:]), r=["identf"], w=["identb"])
        S.op("pool", lambda e: e.memset(onesb[:], 1.0), w=["onesb"])
        S.op("pool", lambda e: e.memset(onesf[:], 1.0), w=["onesf"])
        S.op("pool", lambda e: e.memset(onespad[:], 0.0), w=["onespad"])
        S.op("pool", lambda e: e.memset(onespad[:, 0, 0:64], 1.0), w=["onespad"])
        S.op("pool", lambda e: e.memset(onespad[:, 1, 64:128], 1.0), w=["onespad"])
        S.op("pool", lambda e: e.memset(hprev[:], 0.0), w=["hprev"])
        S.op("pool", lambda e: e.memset(carry[:], 0.0), w=["carry"])
        S.op("pool", lambda e: e.memset(ZF[:], 0.0), w=["ZF0", "ZF1", "ZF2", "ZF3"])
        S.op("pool", lambda e: e.memset(Vmp[:], 0.0), w=["Vmp"])

        with contextlib.ExitStack() as st0:
            wst = [sb(f"wst{i}", [128, 1024], F32, st0) for i in range(3)]
            wbf = [sb(f"wbf{i}", [128, 1024], BF16, st0) for i in range(3)]
            engs3 = ["act", "dve", "pool"]
            for u in range(N_UNITS):
                i = u % 3
                S.dma("sp", lambda e, u=u, i=i: e.dma_start(out=wst[i][:], in_=wall[u]), f"wst{i}", w=[f"wst{i}"])
                copy_op(engs3[i], wbf[i][:], wst[i][:], [f"wst{i}"], [f"wbf{i}"])
                S.dma("sp", lambda e, u=u, i=i: e.dma_start(out=wscr[u], in_=wbf[i][:]), f"wscr_w{i}",
                      r=[f"wbf{i}"], w=[f"wscr{u}"])
            wfst = sb("wfst", [128, 8, 12], F32, st0)
            ld(wfst[:], wf_d[:, :, :], "wfst")
            S.op("dve", lambda e: e.tensor_copy(out=wfb[:], in_=wfst[:]), r=["wfst"], w=["wfb"])

            mem_tm = sb("mem_tm", [128, 2, D], F32, st0)
            memT = sb("memT", [128, 8, MEM_T], BF16, st0)
            for t in range(2):
                S.dma("sp", lambda e, t=t: e.dma_start(out=mem_tm[:, t, :], in_=mem_prompt[t * 128:(t + 1) * 128, :]),
                      "mem_tm", w=[f"mem_tm{t}"])
            for t in range(2):
                for c4 in range(2):
                    p, pk = next_ps()
                    for cc in range(4):
                        c = c4 * 4 + cc
                        S.op("pe", lambda e, p=p, t=t, c=c, cc=cc: e.transpose(
                            out=p[:, cc * 128:(cc + 1) * 128], in_=mem_tm[:, t, c * 128:(c + 1) * 128],
                            identity=identf[:]), r=[f"mem_tm{t}", "identf"], w=[pk])
                    S.op("dve", lambda e, p=p, t=t, c4=c4: e.tensor_copy(
                        out=memT[:, c4 * 4:(c4 + 1) * 4, t * 128:(t + 1) * 128],
                        in_=p[:, :].rearrange("p (c t) -> p c t", c=4)), r=[pk], w=["memT"])
            wmst = sb("wmst", [128, 8, 512], F32, st0)
            wmbf = sb("wmbf", [128, 8, 512], BF16, st0)
            okv = sb("okv", [128, 2, 512], F32, st0)
            for i in range(2):
                S.dma("sp", lambda e, i=i: e.dma_start(
                    out=wmst[:], in_=w_mem_kv[i].rearrange("(c p) n -> p c n", p=128)), "wmst", w=["wmst"])
                S.op("act", lambda e: e.activation(out=wmbf[:], in_=wmst[:], func=AF.Copy), r=["wmst"], w=["wmbf"])
                for t in range(2):
                    p, pk = next_ps()
                    for c in range(8):
                        S.op("pe", lambda e, p=p, t=t, c=c: e.matmul(
                            p[:, :], lhsT=memT[:, c, t * 128:(t + 1) * 128], rhs=wmbf[:, c, :],
                            start=(c == 0), stop=(c == 7)), r=["memT", "wmbf"], w=[pk])
                    S.op("dve", lambda e, p=p, t=t: e.tensor_copy(out=okv[:, t, :], in_=p[:, :]),
                         r=[pk], w=[f"okv{t}"])
                    S.dma("sp", lambda e, i=i, t=t: e.dma_start(
                        out=o_mem_kv[i, t * 128:(t + 1) * 128, :], in_=okv[:, t, :]), "okv_out", r=[f"okv{t}"])
                    for h in range(4):
                        hb = (h % 2) * 64
                        S.op("pool", lambda e, i=i, t=t, h=h, hb=hb: e.tensor_copy(
                            out=Vmp[:, i, t, h, hb:hb + 64], in_=okv[:, t, 256 + h * 64:256 + (h + 1) * 64]),
                            r=[f"okv{t}"], w=["Vmp"])
                for c in range(2):
                    p, pk = next_ps()
                    for k in range(8):
                        S.op("pe", lambda e, p=p, c=c, k=k: e.matmul(
                            p[:, 0:MEM_T], lhsT=wmbf[:, k, c * 128:(c + 1) * 128], rhs=memT[:, k, :],
                            start=(k == 0), stop=(k == 7)), r=["memT", "wmbf"], w=[pk])
                    S.op("act", lambda e, p=p, i=i, c=c: e.activation(out=KmT[:, i, c, :], in_=p[:, 0:MEM_T], func=AF.Copy),
                         r=[pk], w=["KmT"])

            S.op("pool", lambda e: e.memset(Vmps[:], 0.0), w=["Vmps"])
            cm_k = sb("cm_k", [128, 2, 256], F32, st0)
            cm_v = sb("cm_v", [128, 2, 256], F32, st0)
            for sq in range(4):
                for i in range(2):
                    S.dma("sp", lambda e, sq=sq, i=i: e.dma_start(
                        out=cm_k[:], in_=cmk_d[i, sq].rearrange("(t p) f -> p t f", p=128)), "cm_k", w=["cm_k"])
                    S.dma("sp", lambda e, sq=sq, i=i: e.dma_start(
                        out=cm_v[:], in_=cmv_d[i, sq].rearrange("(t p) f -> p t f", p=128)), "cm_v", w=["cm_v"])
                    p, pk = next_ps()
                    for mt in range(2):
                        for c in range(2):
                            S.op("pe", lambda e, p=p, mt=mt, c=c: e.transpose(
                                out=p[:, (c * 2 + mt) * 128:(c * 2 + mt + 1) * 128], in_=cm_k[:, mt, c * 128:(c + 1) * 128],
                                identity=identf[:]), r=["cm_k", "identf"], w=[pk])
                    S.op("act", lambda e, p=p, sq=sq, i=i: e.activation(
                        out=KmTs[:, sq, i, :, :].rearrange("p c m -> p (c m)"), in_=p[:, :], func=AF.Copy), r=[pk], w=["KmTs"])
                    for mt in range(2):
                        for hd4 in range(4):
                            hb = (hd4 % 2) * 64
                            S.op("pool", lambda e, sq=sq, i=i, mt=mt, hd4=hd4, hb=hb: e.tensor_copy(
                                out=Vmps[:, sq, i, mt, hd4, hb:hb + 64], in_=cm_v[:, mt, hd4 * 64:(hd4 + 1) * 64]),
                                r=["cm_v"], w=["Vmps"])
            ptb = sb("ptb", [128, 512], I32, st0)
            ptf = sb("ptf", [128, 512], F32, st0)
            iot = sb("iot", [128, 1], I32, st0)
            iotf = sb("iotf", [128, 1], F32, st0)
            S.dma("sp", lambda e: e.dma_start(out=ptb[:], in_=ptab_d[0:1, :].partition_broadcast(128)), "ptb", w=["ptb"])
            ld(iot[:], iota_d[:, :], "iot")
            S.op("dve", lambda e: e.tensor_copy(out=ptf[:], in_=ptb[:]), r=["ptb"], w=["ptf"])
            S.op("dve", lambda e: e.tensor_copy(out=iotf[:], in_=iot[:]), r=["iot"], w=["iotf"])
            S.op("dve", lambda e: e.tensor_scalar(out=idx_all[:], in0=ptf[:], scalar1=128.0, scalar2=iotf[:, 0:1],
                                                  op0=ALU.mult, op1=ALU.add), r=["ptf", "iotf"], w=["idx_all"])
            S.barrier()
        with contextlib.ExitStack() as st0:
            TB = [sb(f"tb{i}", [128, 3072], F32, st0) for i in range(8)]
            TBi = TB[7][:].bitcast(I32)
            for i in range(3):
                S.dma("sp", lambda e, i=i: e.dma_start(out=TB[i][:], in_=a_b_d[i]), f"tb{i}", w=[f"tb{i}"])
            are, aim, dtb = TB[0], TB[1], TB[2]

            def dv(fn, r, w, eng="dve"):
                S.op(eng, fn, r=r, w=w)

            def sin_reduced(out, in_, ki, kf, keys_r, key_w, ikey, fkey):
                dv(lambda e: e.tensor_scalar(out=ki, in0=in_, scalar1=1.0 / TWO_PI, scalar2=None, op0=ALU.mult),
                   keys_r, [ikey])
                dv(lambda e: e.tensor_copy(out=kf, in_=ki), [ikey], [fkey])
                dv(lambda e: e.scalar_tensor_tensor(out=out, in0=kf, scalar=-TWO_PI, in1=in_, op0=ALU.mult, op1=ALU.add),
                   [fkey] + keys_r, [key_w])
                dv(lambda e: e.tensor_scalar(out=out, in0=out, scalar1=3.141592, scalar2=-3.141592, op0=ALU.min, op1=ALU.max),
                   [key_w], [key_w])
                S.op("act", lambda e: e.activation(out=out, in_=out, func=AF.Sin), r=[key_w], w=[key_w])

            S.op("act", lambda e: e.activation(out=dtb[:], in_=dtb[:], func=AF.Exp), r=["tb2"], w=["tb2"])
            dv(lambda e: e.tensor_tensor(out=TB[3][:], in0=are[:], in1=dtb[:], op=ALU.mult), ["tb0", "tb2"], ["tb3"])
            S.op("act", lambda e: e.activation(out=TB[3][:], in_=TB[3][:], func=AF.Exp), r=["tb3"], w=["tb3"])
            dv(lambda e: e.tensor_tensor(out=TB[4][:], in0=aim[:], in1=dtb[:], op=ALU.mult), ["tb1", "tb2"], ["tb4"])
            sin_reduced(TB[5][:], TB[4][:], TBi, TB[7][:], ["tb4"], "tb5", "tb7", "tb7")
            dv(lambda e: e.tensor_scalar(out=TB[4][:], in0=TB[4][:], scalar1=math.pi / 2, scalar2=None, op0=ALU.add),
               ["tb4"], ["tb4"])
            sin_reduced(TB[6][:], TB[4][:], TBi, TB[7][:], ["tb4"], "tb6", "tb7", "tb7")
            dv(lambda e: e.tensor_tensor(out=TB[6][:], in0=TB[6][:], in1=TB[3][:], op=ALU.mult), ["tb6", "tb3"], ["tb6"])
            dv(lambda e: e.tensor_scalar(out=TB[6][:], in0=TB[6][:], scalar1=-1.0, scalar2=None, op0=ALU.add), ["tb6"], ["tb6"])
            dv(lambda e: e.tensor_tensor(out=TB[5][:], in0=TB[5][:], in1=TB[3][:], op=ALU.mult), ["tb5", "tb3"], ["tb5"])
            dv(lambda e: e.tensor_tensor(out=TB[3][:], in0=are[:], in1=are[:], op=ALU.mult), ["tb0"], ["tb3"])
            dv(lambda e: e.tensor_tensor(out=TB[4][:], in0=aim[:], in1=aim[:], op=ALU.mult), ["tb1"], ["tb4"])
            dv(lambda e: e.tensor_tensor(out=TB[3][:], in0=TB[3][:], in1=TB[4][:], op=ALU.add), ["tb3", "tb4"], ["tb3"])
            dv(lambda e: e.reciprocal(out=TB[3][:], in_=TB[3][:]), ["tb3"], ["tb3"])
            dv(lambda e: e.tensor_tensor(out=TB[4][:], in0=TB[6][:], in1=are[:], op=ALU.mult), ["tb6", "tb0"], ["tb4"])
            dv(lambda e: e.tensor_tensor(out=TB[7][:], in0=TB[5][:], in1=aim[:], op=ALU.mult), ["tb5", "tb1"], ["tb7"])
            dv(lambda e: e.tensor_tensor(out=TB[4][:], in0=TB[4][:], in1=TB[7][:], op=ALU.add), ["tb4", "tb7"], ["tb4"])
            dv(lambda e: e.tensor_tensor(out=TB[4][:], in0=TB[4][:], in1=TB[3][:], op=ALU.mult), ["tb4", "tb3"], ["tb4"])
            dv(lambda e: e.tensor_tensor(out=TB[7][:], in0=TB[5][:], in1=are[:], op=ALU.mult), ["tb5", "tb0"], ["tb7"])
            dv(lambda e: e.tensor_tensor(out=TB[2][:], in0=TB[6][:], in1=aim[:], op=ALU.mult), ["tb6", "tb1"], ["tb2"])
            dv(lambda e: e.tensor_tensor(out=TB[7][:], in0=TB[7][:], in1=TB[2][:], op=ALU.subtract), ["tb7", "tb2"], ["tb7"])
            dv(lambda e: e.tensor_tensor(out=TB[7][:], in0=TB[7][:], in1=TB[3][:], op=ALU.mult), ["tb7", "tb3"], ["tb7"])
            crb, cib = TB[4], TB[7]
            bre, bim = TB[0], TB[1]
            S.dma("sp", lambda e: e.dma_start(out=bre[:], in_=bpad_d[0].rearrange("p j n -> p (j n)")), "tb0", w=["tb0"])
            S.dma("sp", lambda e: e.dma_start(out=bim[:], in_=bpad_d[1].rearrange("p j n -> p (j n)")), "tb1", w=["tb1"])
            dv(lambda e: e.tensor_tensor(out=TB[2][:], in0=crb[:], in1=bre[:], op=ALU.mult), ["tb4", "tb0"], ["tb2"])
            dv(lambda e: e.tensor_tensor(out=TB[3][:], in0=cib[:], in1=bim[:], op=ALU.mult), ["tb7", "tb1"], ["tb3"])
            dv(lambda e: e.tensor_tensor(out=Bpad[:, 0, :, :].rearrange("p j n -> p (j n)"), in0=TB[2][:], in1=TB[3][:],
                                         op=ALU.subtract), ["tb2", "tb3"], ["Bpad0"])
            dv(lambda e: e.tensor_tensor(out=TB[5][:], in0=crb[:], in1=bim[:], op=ALU.mult), ["tb4", "tb1"], ["tb5"])
            dv(lambda e: e.tensor_tensor(out=TB[6][:], in0=cib[:], in1=bre[:], op=ALU.mult), ["tb7", "tb0"], ["tb6"])
            dv(lambda e: e.tensor_tensor(out=Bpad[:, 1, :, :].rearrange("p j n -> p (j n)"), in0=TB[5][:], in1=TB[6][:],
                                         op=ALU.add), ["tb5", "tb6"], ["Bpad1"])
            S.dma("sp", lambda e: e.dma_start(out=TB[2][:], in_=cpad_d[0].rearrange("p j n -> p (j n)")), "tb2",
                  r=["tb2"], w=["tb2"])
            S.dma("sp", lambda e: e.dma_start(out=TB[3][:], in_=cpad_d[1].rearrange("p j n -> p (j n)")), "tb3",
                  r=["tb3"], w=["tb3"])
            dv(lambda e: e.tensor_copy(out=Cpad[:, 0, :, :].rearrange("p j n -> p (j n)"), in_=TB[2][:]), ["tb2"], ["Cpad0"])
            dv(lambda e: e.tensor_scalar(out=Cpad[:, 1, :, :].rearrange("p j n -> p (j n)"), in0=TB[3][:], scalar1=-1.0,
                                         scalar2=None, op0=ALU.mult), ["tb3"], ["Cpad1"])

            a_s = sb("a_s_sb", [128, 3, 24], F32, st0)
            th_s = sb("th_s", [128, 24], F32, st0)
            tcount = sb("tcount_sb", [128, 128], F32, st0)
            ld(a_s[:], a_s_d[:, :, :], "a_s")
            ld(tcount[:], tcount_d[:, :], "tcount")
            S.op("act", lambda e: e.activation(out=a_s[:, 2, :], in_=a_s[:, 2, :], func=AF.Exp), r=["a_s"], w=["a_s"])
            dv(lambda e: e.tensor_tensor(out=r_s[:], in0=a_s[:, 0, :], in1=a_s[:, 2, :], op=ALU.mult), ["a_s"], ["r_s"])
            S.op("act", lambda e: e.activation(out=r_s[:], in_=r_s[:], func=AF.Exp), r=["r_s"], w=["r_s"])
            dv(lambda e: e.tensor_tensor(out=th_s[:], in0=a_s[:, 1, :], in1=a_s[:, 2, :], op=ALU.mult), ["a_s"], ["th_s"])
            ang = TB[0]
            ang3 = ang[:].rearrange("p (j t) -> p j t", j=24)
            for j in range(24):
                dv(lambda e, j=j: e.tensor_scalar(out=ang3[:, j, :], in0=tcount[:], scalar1=th_s[:, j:j + 1], scalar2=None,
                                                  op0=ALU.mult), ["tcount", "th_s", "tb0"], ["tb0"])
            sin_reduced(Stab[:, :, :].rearrange("p j t -> p (j t)"), ang[:], TBi, TB[7][:], ["tb0"], "Stab", "tb7", "tb7")
            dv(lambda e: e.tensor_scalar(out=ang[:], in0=ang[:], scalar1=math.pi / 2, scalar2=None, op0=ALU.add),
               ["tb0"], ["tb0"])
            sin_reduced(Ctab[:, :, :].rearrange("p j t -> p (j t)"), ang[:], TBi, TB[7][:], ["tb0"], "Ctab", "tb7", "tb7")
            S.barrier()
        carry_s = sb("carry_s", [128, 4, NH], F32)
        s5so = sb("s5so", [128, 2, 24, 4], F32)
        Qblk = sb("Qblk", [128, 6, 16], BF16)
        KTnew = sb("KTnew", [128, 6, 32], BF16)
        vnbp = sb("vnbp", [32, DMAIN], BF16)
        Kpb = [sb(f"Kpb{i}", [128, DMAIN], BF16) for i in range(2)]
        FnegN = sb("FnegN", [8, 4, NH], F32)
        Ftb = sb("Ftb", [128, NH, 8], F32)
        h = sb("h", [128, 8, NT], F32)
        xn = sb("xn", [128, 8, NT], BF16)
        z = sb("z", [128, 8, NT], BF16)
        ymix = sb("ymix", [128, 8, NT], BF16)
        hid = sb("hid", [128, 32, NT], BF16)
        wbufs = [sb(f"wb{i}", [128, 8, 128], BF16) for i in range(6)]
        stat = sb("stat", [128, NT], F32)
        tmpf = [sb(f"tmpf{i}", [128, NT], F32) for i in range(3)]
        pT = [sb(f"pT{i}", [128, NT], BF16) for i in range(3)]
        lf = sb("lf", [128, 4, NH], F32)
        hflat = hid[:].rearrange("p a b -> p (a b)")
        hidf = hflat.bitcast(F32)

        def hchunks(a, b_, parts=128):
            return hid[0:parts, a:b_, :].rearrange("p a b -> p (a b)")

        xtm = hidf[:, 0:4096].rearrange("p (a b) -> p a b", a=4)
        yfm = hidf[:, 4096:8192].rearrange("p (k n) -> p k n", k=8)
        W6 = [hidf[:, i * 768:(i + 1) * 768].rearrange("p (j t) -> p j t", j=6) for i in range(4)]
        hre = hchunks(12, 18).rearrange("p (j t) -> p j t", j=24)
        him = hchunks(18, 24).rearrange("p (j t) -> p j t", j=24)
        QT = hid[0:65, 0:12, :]
        Vb = [hchunks(12, 16).rearrange("p (b d) -> p b d", b=16), hchunks(24, 28).rearrange("p (b d) -> p b d", b=16)]
        KTb = [hchunks(16, 20, 65), hchunks(20, 24, 65)]
        kvt = hidf[:, 0:1536]
        vaug = hchunks(6, 18).rearrange("p (s h d) -> p s h d", s=4, h=NH)
        ktt = hid[0:64, 18, :]
        kbf = [hid[:, 19, :], hid[:, 20, :]]
        kst = [hidf[:, 5376 + i * 512:5376 + (i + 1) * 512].rearrange("p (s f) -> p s f", s=4) for i in range(2)]
        Kpg = [hidf[:, 0:768], hidf[:, 768:1536]]
        Vpg = [hidf[:, 1536:2304], hidf[:, 2304:3072]]
        KTp = [hflat[:, 6144:6912].rearrange("p (c k) -> p c k", c=6), hflat[:, 6912:7680].rearrange("p (c k) -> p c k", c=6)]
        Vpb = [hflat[:, 7680:8448], hflat[:, 8448:9216]]
        lfpg = hidf[:, 4608:6144].rearrange("p (s h) -> p s h", h=NH)
        FnT = hidf[:, 6144:7680].rearrange("p (h n) -> p h n", h=NH)
        VnewS = hid[0:8, 30, :].rearrange("p (a b) -> p a b", a=1)[:, 0, :]
        VnewS = hflat[0:8, 15360:16128]
        HIDKEYS = [f"hid{m}" for m in range(32)]
        GROUPS = {
            "satt": ["Kpg0", "Kpg1", "Vpg0", "Vpg1", "KTp0", "KTp1", "Vpb0", "Vpb1", "lfpg", "FnT", "VnewS"],
            "xtm": ["xtm"], "sq": ["sq"], "yfm": ["yfm"], "hid": HIDKEYS,
            "s5": ["w6_0", "w6_1", "w6_2", "w6_3", "hre", "him"],
            "att": ["QT", "KTb0", "KTb1", "Vb0", "Vb1"],
            "kv": [f"kvt{n}" for n in range(12)] + [f"vaug{n}" for n in range(4)] + ["ktt", "kbf0", "kbf1", "kst0", "kst1"],
        }

        def enter(*groups):
            tgt = [k for g in groups for k in GROUPS[g]]
            best = {}
            for g, keys in GROUPS.items():
                if g in groups:
                    continue
                for k in keys:
                    lst = list(S.reads.get(k, []))
                    if S.lastw.get(k) is not None:
                        lst.append(S.lastw[k])
                    for sid, sem, val in lst:
                        if sid not in best or best[sid][2] < val:
                            best[sid] = (sid, sem, val)
            for k in tgt:
                S.reads.setdefault(k, []).extend(best.values())

        wrot = [0]

        def getw(u):
            i = wrot[0] % 6
            wrot[0] += 1
            S.dma("sp", lambda e, u=u, i=i: e.dma_start(out=wbufs[i][:].rearrange("p k n -> p (k n)"), in_=wscr[u]),
                  f"wb{i}", r=[f"wscr{u}"], w=[f"wb{i}"])
            return wbufs[i], f"wb{i}"

        def rmsnorm(N, gidx, out_fn, out_keys, hkeys):
            enter("sq")
            sq = hid[:, 0:8, 0:N]
            for k in range(8):
                S.op("act", lambda e, k=k: e.activation(out=sq[:, k, :], in_=h[:, k, 0:N], func=AF.Square),
                     r=[hkeys[k]], w=["sq"])
            p, pk = next_ps()
            for k in range(8):
                S.op("pe", lambda e, p=p, k=k: e.matmul(p[:, 0:N], lhsT=onesb[:], rhs=sq[:, k, :],
                                                         start=(k == 0), stop=(k == 7)), r=["sq", "onesb"], w=[pk])
            S.op("act", lambda e, p=p: e.activation(out=stat[:, 0:N], in_=p[:, 0:N], func=AF.Sqrt, bias=1e-6,
                                                    scale=1.0 / D), r=[pk], w=["stat"])
            S.op("dve", lambda e: e.reciprocal(out=stat[:, 0:N], in_=stat[:, 0:N]), r=["stat"], w=["stat"])
            for k in range(8):
                S.op("dve", lambda e, k=k: e.scalar_tensor_tensor(
                    out=out_fn(k), in0=h[:, k, 0:N], scalar=gall[:, gidx, k:k + 1], in1=stat[:, 0:N],
                    op0=ALU.mult, op1=ALU.mult), r=[hkeys[k], "stat", "gall"], w=[out_keys[k]])

        dbgbuf = sb("dbgbuf", [128, NT], F32) if DEBUG else None
        dbg_names = []

        def dump(name, ap, keys, n=NT):
            if not DEBUG or len(dbg_names) >= 16:
                return
            i = len(dbg_names)
            dbg_names.append(name)
            S.op("dve", lambda e: e.tensor_copy(out=dbgbuf[:, 0:n], in_=ap), r=keys, w=["dbgbuf"])
            S.dma("sp", lambda e: e.dma_start(out=dbg_out[i, :, 0:n], in_=dbgbuf[:, 0:n]), "dbg", r=["dbgbuf"])

        DBG_NAMES.clear()
        DBG_NAMES.append(dbg_names)
        HK = [f"h{k}" for k in range(8)]
        XK = [f"xn{k}" for k in range(8)]
        ZK = [f"z{k}" for k in range(8)]
        YK = [f"ym{k}" for k in range(8)]

        def proj_fm(ubase, Kc, m_list, src_fn, src_keys, N, evac):
            nkq = (Kc + 7) // 8
            for m in m_list:
                p, pk = next_ps()
                for kq in range(nkq):
                    wb, wk = getw(ubase + m * nkq + kq)
                    kn = min(8, Kc - kq * 8)
                    for kk in range(kn):
                        k = kq * 8 + kk
                        S.op("pe", lambda e, p=p, wb=wb, kk=kk, k=k: e.matmul(
                            p[:, 0:N], lhsT=wb[:, kk, :], rhs=src_fn(k), start=(k == 0), stop=(k == Kc - 1)),
                            r=[wk, src_keys[k]], w=[pk])
                evac(m, p, pk)

        def mem_attention(N, q0, km_fn, vm_fn, kkeys):
            for hc in range(2):
                pn, pnk = PSW[0][:, 0:512], "psw0"
                pd, pdk = PSW[1][:, 0:512], "psw1"
                first = True
                cnt = 0
                for hh in range(2):
                    hd = hc * 2 + hh
                    hb = hh * 64
                    for mt in range(2):
                        p, pk = next_ps()
                        S.op("pe", lambda e, p=p, hb=hb, mt=mt, hc=hc: e.matmul(
                            p[:, 0:N], lhsT=km_fn(hb, hc, mt),
                            rhs=z[hb:hb + 64, 6 + hc, q0:q0 + N], start=True, stop=True), r=kkeys + [ZK[6 + hc]], w=[pk])
                        pt = pT[cnt % 3]
                        ptk = f"pT{cnt % 3}"
                        S.op("act", lambda e, p=p, pt=pt: e.activation(out=pt[:, 0:N], in_=p[:, 0:N], func=AF.Exp,
                                                                       scale=0.125), r=[pk], w=[ptk])
                        last = (hh == 1 and mt == 1)
                        S.op("pe", lambda e, pn=pn, pt=pt, mt=mt, hd=hd, first=first, last=last: e.matmul(
                            pn[:, 0:N], lhsT=vm_fn(mt, hd), rhs=pt[:, 0:N], start=first, stop=last),
                            r=kkeys + [ptk], w=[pnk])
                        S.op("pe", lambda e, pd=pd, pt=pt, hh=hh, first=first, last=last: e.matmul(
                            pd[:, 0:N], lhsT=onespad[:, hh, :], rhs=pt[:, 0:N], start=first, stop=last),
                            r=["onespad", ptk], w=[pdk])
                        first = False
                        cnt += 1
                S.op("dve", lambda e, pd=pd: e.reciprocal(out=tmpf[0][:, 0:N], in_=pd[:, 0:N]), r=[pdk], w=["tmpf0"])
                S.op("dve", lambda e, pn=pn, hc=hc: e.tensor_tensor(out=ymix[:, 6 + hc, q0:q0 + N], in0=pn[:, 0:N],
                                                                    in1=tmpf[0][:, 0:N], op=ALU.mult),
                     r=[pnk, "tmpf0"], w=[YK[6 + hc]])

        s5_dumped = []

        def s5_layer(N, L, nseg, init_fn, fin_fn, skeys):
            enter("s5")
            CH = nseg * L
            zgk = [f"zg{c}" for c in range(6)]
            for ch in range(N // CH):
                c0 = ch * CH
                for grp in range(4):
                    j0 = grp * 6
                    bre, bim = PSW[0], PSW[1]
                    for jj in range(6):
                        j = j0 + jj
                        S.op("pe", lambda e, j=j, jj=jj, c0=c0: e.matmul(
                            bre[:, jj * CH:(jj + 1) * CH], lhsT=Bpad[:, 0, j, :], rhs=z[:, j // 4, c0:c0 + CH],
                            start=True, stop=True), r=["Bpad0", ZK[j // 4]], w=["psw0"])
                        S.op("pe", lambda e, j=j, jj=jj, c0=c0: e.matmul(
                            bim[:, jj * CH:(jj + 1) * CH], lhsT=Bpad[:, 1, j, :], rhs=z[:, j // 4, c0:c0 + CH],
                            start=True, stop=True), r=["Bpad1", ZK[j // 4]], w=["psw1"])
                    br3 = bre[:, 0:6 * CH].rearrange("p (j t) -> p j t", j=6)
                    bi3 = bim[:, 0:6 * CH].rearrange("p (j t) -> p j t", j=6)
                    w0, w1, w2, w3 = [w[:, :, 0:CH] for w in W6]
                    Cv = Ctab[:, j0:j0 + 6, 0:L]
                    Sv = Stab[:, j0:j0 + 6, 0:L]
                    SG = [(sg * L, (sg + 1) * L) for sg in range(nseg)]
                    for (s0, s1) in SG:
                        S.op("dve", lambda e, Cv=Cv, s0=s0, s1=s1: e.tensor_tensor(out=w0[:, :, s0:s1], in0=br3[:, :, s0:s1], in1=Cv, op=ALU.mult), r=["psw0", "Ctab"], w=["w6_0"])
                        S.op("dve", lambda e, Sv=Sv, s0=s0, s1=s1: e.tensor_tensor(out=w1[:, :, s0:s1], in0=bi3[:, :, s0:s1], in1=Sv, op=ALU.mult), r=["psw1", "Stab"], w=["w6_1"])
                        S.op("dve", lambda e, Cv=Cv, s0=s0, s1=s1: e.tensor_tensor(out=w2[:, :, s0:s1], in0=bi3[:, :, s0:s1], in1=Cv, op=ALU.mult), r=["psw1", "Ctab"], w=["w6_2"])
                        S.op("dve", lambda e, Sv=Sv, s0=s0, s1=s1: e.tensor_tensor(out=w3[:, :, s0:s1], in0=br3[:, :, s0:s1], in1=Sv, op=ALU.mult), r=["psw0", "Stab"], w=["w6_3"])
                    S.op("pool", lambda e: e.tensor_tensor(out=w0, in0=w0, in1=w1, op=ALU.add), r=["w6_0", "w6_1"], w=["w6_0"])
                    S.op("pool", lambda e: e.tensor_tensor(out=w2, in0=w2, in1=w3, op=ALU.subtract), r=["w6_2", "w6_3"], w=["w6_2"])
                    if False:
                        dump("bu_re", bre[:, 0:128], ["psw0"], 128)
                        dump("ctab", Ctab[:, 0, :], ["Ctab"], 128)
                        dump("stab", Stab[:, 0, :], ["Stab"], 128)
                        dump("gr", w0[:, 0, :], ["w6_0"], 128)
                        dump("bu_im", bim[:, 0:128], ["psw1"], 128)
                        dump("biS", w1[:, 0, :], ["w6_1"], 128)
                        dump("gi", w2[:, 0, :], ["w6_2"], 128)
                        dump("r_s", r_s[:, :], ["r_s"], 24)
                    for jj in range(6):
                        j = j0 + jj
                        for sg, (s0, s1) in enumerate(SG):
                            S.op("dve", lambda e, j=j, jj=jj, s0=s0, s1=s1, sg=sg: e.tensor_tensor_scan(
                                out=w0[:, jj, s0:s1], data0=r_s[:, j:j + 1].broadcast_to([128, L]), data1=w0[:, jj, s0:s1],
                                initial=init_fn(0, j, sg), op0=ALU.mult, op1=ALU.add), r=["w6_0", "r_s"] + skeys, w=["w6_0"])
                            S.op("dve", lambda e, j=j, jj=jj, s0=s0, s1=s1, sg=sg: e.tensor_tensor_scan(
                                out=w2[:, jj, s0:s1], data0=r_s[:, j:j + 1].broadcast_to([128, L]), data1=w2[:, jj, s0:s1],
                                initial=init_fn(1, j, sg), op0=ALU.mult, op1=ALU.add), r=["w6_2", "r_s"] + skeys, w=["w6_2"])
                    if False:
                        dump("sr", w0[:, 0, :], ["w6_0"], 128)
                    hre_o, him_o = hre[:, j0:j0 + 6, 0:CH], him[:, j0:j0 + 6, 0:CH]
                    for (s0, s1) in SG:
                        S.op("dve", lambda e, Cv=Cv, s0=s0, s1=s1: e.tensor_tensor(out=w1[:, :, s0:s1], in0=w0[:, :, s0:s1], in1=Cv, op=ALU.mult), r=["w6_0", "Ctab"], w=["w6_1"])
                        S.op("pool", lambda e, Sv=Sv, s0=s0, s1=s1: e.tensor_tensor(out=w3[:, :, s0:s1], in0=w2[:, :, s0:s1], in1=Sv, op=ALU.mult), r=["w6_2", "Stab"], w=["w6_3"])
                    S.op("pool", lambda e, o=hre_o: e.tensor_tensor(out=o, in0=w1, in1=w3, op=ALU.subtract),
                         r=["w6_1", "w6_3"], w=["hre"])
                    for sg, (s0, s1) in enumerate(SG):
                        S.op("dve", lambda e, o=fin_fn(0, j0, sg), s1=s1: e.tensor_tensor(out=o, in0=w1[:, :, s1 - 1], in1=w3[:, :, s1 - 1], op=ALU.subtract),
                             r=["w6_1", "w6_3"], w=skeys)
                    for (s0, s1) in SG:
                        S.op("dve", lambda e, Sv=Sv, s0=s0, s1=s1: e.tensor_tensor(out=w1[:, :, s0:s1], in0=w0[:, :, s0:s1], in1=Sv, op=ALU.mult), r=["w6_0", "Stab"], w=["w6_1"])
                        S.op("pool", lambda e, Cv=Cv, s0=s0, s1=s1: e.tensor_tensor(out=w3[:, :, s0:s1], in0=w2[:, :, s0:s1], in1=Cv, op=ALU.mult), r=["w6_2", "Ctab"], w=["w6_3"])
                    S.op("pool", lambda e, o=him_o: e.tensor_tensor(out=o, in0=w1, in1=w3, op=ALU.add),
                         r=["w6_1", "w6_3"], w=["him"])
                    for sg, (s0, s1) in enumerate(SG):
                        S.op("dve", lambda e, o=fin_fn(1, j0, sg), s1=s1: e.tensor_tensor(out=o, in0=w1[:, :, s1 - 1], in1=w3[:, :, s1 - 1], op=ALU.add),
                             r=["w6_1", "w6_3"], w=skeys)
                if False:
                    dump("hre", hre[:, 0, :], ["hre"], 128)
                    s5_dumped.append(1)
                for c in range(6):
                    p, pk = next_ps()
                    n = 0
                    for j in range(4 * c, 4 * c + 4):
                        S.op("pe", lambda e, p=p, j=j, n=n: e.matmul(p[:, 0:CH], lhsT=Cpad[:, 0, j, :], rhs=hre[:, j, 0:CH],
                                                                    start=(n == 0), stop=False), r=["Cpad0", "hre"], w=[pk])
                        S.op("pe", lambda e, p=p, j=j, n=n: e.matmul(p[:, 0:CH], lhsT=Cpad[:, 1, j, :], rhs=him[:, j, 0:CH],
                                                                    start=False, stop=(n == 3)), r=["Cpad1", "him"], w=[pk])
                        n += 1
                    t0, t1 = tmpf[0][:, 0:CH], tmpf[1][:, 0:CH]
                    S.op("dve", lambda e, p=p, c=c, c0=c0: e.scalar_tensor_tensor(
                        out=t0, in0=z[:, c, c0:c0 + CH], scalar=s5d[:, c:c + 1], in1=p[:, 0:CH], op0=ALU.mult, op1=ALU.add),
                        r=[pk, ZK[c], "s5d"], w=["tmpf0"])
                    S.op("pool", lambda e: e.tensor_tensor(out=t1, in0=t0, in1=t0, op=ALU.mult), r=["tmpf0"], w=["tmpf1"])
                    S.op("pool", lambda e: e.tensor_scalar(out=t1, in0=t1, scalar1=0.044715, scalar2=1.0, op0=ALU.mult,
                                                           op1=ALU.add), r=["tmpf1"], w=["tmpf1"])
                    S.op("pool", lambda e: e.tensor_tensor(out=t1, in0=t1, in1=t0, op=ALU.mult), r=["tmpf1", "tmpf0"], w=["tmpf1"])
                    S.op("act", lambda e: e.activation(out=t1, in_=t1, func=AF.Sigmoid, scale=2.0 * math.sqrt(2.0 / math.pi)),
                         r=["tmpf1"], w=["tmpf1"])
                    S.op("dve", lambda e, c=c, c0=c0: e.tensor_tensor(out=xn[:, c, c0:c0 + CH], in0=t0, in1=t1, op=ALU.mult),
                         r=["tmpf0", "tmpf1"], w=[XK[c]])
            def ev(m, p, pk):
                S.op("act", lambda e: e.activation(out=tmpf[2][:, 0:N], in_=p[:, 0:N], func=AF.Sigmoid, bias=bglu[:, m:m + 1],
                                                   scale=1.0), r=[pk, "bglu"], w=["tmpf2"])
                S.op("dve", lambda e: e.tensor_tensor(out=ymix[:, m, 0:N], in0=xn[:, m, 0:N], in1=tmpf[2][:, 0:N], op=ALU.mult),
                     r=["tmpf2", XK[m]], w=[YK[m]])
            proj_fm(U_GLU, 6, range(6), lambda k: xn[:, k, 0:N], XK, N, ev)

        def fox_layer(i1):
            N = NT
            enter("att")
            S.dma("pool", lambda e: e.indirect_dma_start(
                out=fq_sb[:].rearrange("p s h -> p (s h)"), out_offset=None, in_=Fq,
                in_offset=bass.IndirectOffsetOnAxis(ap=hidx[:, i1:i1 + 1], axis=0)), "fq_g",
                r=["hidx"] + [f"Fq{tt}" for tt in range(NTILES)], w=["fq_sb"])
            for sbk in range(4):
                S.op("dve", lambda e, sbk=sbk: e.tensor_copy(out=ZF[:, sbk, :, 64], in_=fq_sb[:, sbk, :]), r=["fq_sb"],
                     w=[f"ZF{sbk}"])
            for jj in range(4):
                S.op("dve", lambda e, jj=jj: e.tensor_scalar(
                    out=FnB[:, jj * 4:(jj + 1) * 4, :], in0=Fneg[:, 16 * i1 + jj * 4:16 * i1 + (jj + 1) * 4, :],
                    scalar1=ohbb[:, 1, jj:jj + 1], scalar2=None, op0=ALU.add), r=["Fneg", "ohbb"], w=["FnB"])
            for i in range(2):
                S.op("pool", lambda e, i=i: e.memset(KTb[i][64:65, :], 1.0), w=[f"KTb{i}"])
            for hd in range(NH):
                c, hb = hd // 2, (hd % 2) * 64
                if hb == 0:
                    S.op("act", lambda e, hd=hd, c=c: e.activation(out=QT[0:64, hd, :], in_=z[0:64, c, :], func=AF.Copy,
                                                                   scale=0.125), r=[ZK[c]], w=["QT"])
                else:
                    S.dma("sp", lambda e, hd=hd, c=c: e.dma_start(out=QT[0:64, hd, :], in_=z[64:128, c, :]), "QTmv",
                          r=[ZK[c]], w=["QT"])
                    S.op("act", lambda e, hd=hd: e.activation(out=QT[0:64, hd, :], in_=QT[0:64, hd, :], func=AF.Copy,
                                                              scale=0.125), r=["QT"], w=["QT"])
            for hd in range(NH):
                p, pk = next_ps()
                for sbk in range(4):
                    S.op("pe", lambda e, p=p, sbk=sbk, hd=hd: e.matmul(
                        p[0:65, sbk * 128:(sbk + 1) * 128], lhsT=ZF[:, sbk, hd, :], rhs=identb[:], start=True, stop=True),
                        r=[f"ZF{sbk}", "identb"], w=[pk])
                S.op("dve", lambda e, p=p, hd=hd: e.tensor_copy(out=QT[64:65, hd, :], in_=p[64:65, :]), r=[pk], w=["QT"])
            nkb = 16 * i1 + 16
            for hd in range(NH):
                po, pok = PSW[0][:, 0:512], "psw0"
                npieces = (nkb + 15) // 16
                cnt = 0
                hh = hd % 2
                hb = hh * 64
                pending = None

                def emit_pv(pt, ptk, bi, i, kb, po=po, pok=pok):
                    S.op("pe", lambda e: e.matmul(
                        po[:, 0:N], lhsT=Vb[bi][:, i, :], rhs=pt[:, 0:N], start=(kb == 0), stop=(kb == nkb - 1)),
                        r=[ptk, f"Vb{bi}"], w=[pok])
                for pc in range(npieces):
                    kb0 = pc * 16
                    nb = min(16, nkb - kb0)
                    bi = (hd * npieces + pc) % 2
                    tiles_needed = sorted(set((kb0 + i) // 4 for i in range(nb)))
                    S.dma("sp", lambda e, hd=hd, kb0=kb0, nb=nb, bi=bi: e.dma_start(
                        out=KTb[bi][0:64, 0:nb * 128], in_=KTs[hd, :, kb0 * 128:(kb0 + nb) * 128]), f"KTb{bi}",
                        r=[f"KTs{tt}" for tt in tiles_needed], w=[f"KTb{bi}"])
                    S.dma("act", lambda e, hd=hd, kb0=kb0, nb=nb, bi=bi: e.dma_start(
                        out=Vb[bi][:, 0:nb, :], in_=Vs[hd, :, kb0:kb0 + nb, :]), f"Vb{bi}",
                        r=[f"Vs{tt}" for tt in tiles_needed], w=[f"Vb{bi}"])
                    for i in range(nb):
                        kb = kb0 + i
                        zone = kb - 16 * i1
                        q0 = 0
                        p, pk = next_ps()
                        S.op("pe", lambda e, p=p, bi=bi, i=i, hd=hd: e.matmul(
                            p[:, 0:N], lhsT=KTb[bi][:, i * 128:(i + 1) * 128], rhs=QT[:, hd, 0:N], start=True, stop=True),
                            r=[f"KTb{bi}", "QT"], w=[pk])
                        pt = pT[cnt % 3]
                        ptk = f"pT{cnt % 3}"
                        cnt += 1
                        if zone >= 0:
                            jj, d = zone // 4, zone % 4
                            S.op("dve", lambda e, p=p, d=d, jj=jj: e.scalar_tensor_tensor(
                                out=tmpf[2][:, 0:N], in0=masks[:, d, 0:N], scalar=ohbb[:, 0, jj:jj + 1], in1=p[:, 0:N],
                                op0=ALU.mult, op1=ALU.add), r=[pk, "masks", "ohbb"], w=["tmpf2"])
                            S.op("act", lambda e, pt=pt, zone=zone, hd=hd: e.activation(
                                out=pt[:, 0:N], in_=tmpf[2][:, 0:N], func=AF.Exp, bias=FnB[:, zone, hd:hd + 1], scale=1.0),
                                r=["tmpf2", "FnB"], w=[ptk])
                        else:
                            S.op("act", lambda e, p=p, pt=pt, kb=kb, hd=hd: e.activation(
                                out=pt[:, 0:N], in_=p[:, 0:N], func=AF.Exp, bias=Fneg[:, kb, hd:hd + 1], scale=1.0),
                                r=[pk, "Fneg"], w=[ptk])
                        if pending is not None:
                            emit_pv(*pending)
                        pending = (pt, ptk, bi, i, kb)
                emit_pv(*pending)
                osb = tmpf[0]
                S.op("dve", lambda e, po=po: e.tensor_copy(out=osb[:, 0:N], in_=po[:, 0:N]), r=[pok], w=["tmpf0"])
                pd, pdk = PSW[1][:, 0:512], "psw1"
                S.op("pe", lambda e, pd=pd, hh=hh: e.matmul(pd[:, 0:N], lhsT=selden[:, hh, :], rhs=osb[:, 0:N], start=True, stop=True),
                     r=["selden", "tmpf0"], w=[pdk])
                S.op("dve", lambda e, pd=pd, hb=hb: e.reciprocal(out=tmpf[1][hb:hb + 64, 0:N], in_=pd[hb:hb + 64, 0:N]),
                     r=[pdk], w=["tmpf1"])
                S.op("pool", lambda e, hb=hb, hd=hd: e.tensor_tensor(
                    out=ymix[hb:hb + 64, hd // 2, 0:N], in0=osb[hb:hb + 64, 0:N], in1=tmpf[1][hb:hb + 64, 0:N], op=ALU.mult),
                    r=["tmpf0", "tmpf1"], w=[YK[hd // 2]])

        def kv_stage(t, N):
            rmsnorm(N, 4, lambda k: xn[:, k, 0:N], XK, HK)
            enter("kv")
            S.op("pool", lambda e: e.memset(vaug[:, :, :, :], 0.0), w=[f"vaug{n}" for n in range(4)])
            va7 = vaug[:, :, :, :].rearrange("p s (c two) d -> p s c two d", two=2)
            S.op("pool", lambda e: e.memset(va7[:, :, :, 0, 64:65], 1.0), w=[f"vaug{n}" for n in range(4)])
            S.op("pool", lambda e: e.memset(va7[:, :, :, 1, 0:1], 1.0), w=[f"vaug{n}" for n in range(4)])
            for m in range(12):
                p, pk = next_ps()
                wb, wk = getw(U_KV + m)
                for k in range(8):
                    S.op("pe", lambda e, p=p, wb=wb, k=k: e.matmul(
                        p[:, 0:N], lhsT=wb[:, k, :], rhs=xn[:, k, 0:N], start=(k == 0), stop=(k == 7)),
                        r=[wk, XK[k]], w=[pk])
                fi = m % 2
                fm = tmpf[fi]
                S.op("dve", lambda e, p=p, fm=fm: e.tensor_copy(out=fm[:, 0:N], in_=p[:, 0:N]), r=[pk], w=[f"tmpf{fi}"])
                if m < 6:
                    kb_ = kbf[fi]
                    S.op("act", lambda e, fm=fm, kb_=kb_: e.activation(out=kb_[:, 0:N], in_=fm[:, 0:N], func=AF.Copy),
                         r=[f"tmpf{fi}"], w=[f"kbf{fi}"])
                    for hh in range(2):
                        S.dma("sp", lambda e, m=m, hh=hh, kb_=kb_: e.dma_start(
                            out=KTs[2 * m + hh, :, t * NT:t * NT + N], in_=kb_[hh * 64:(hh + 1) * 64, 0:N]), "ktt_out",
                            r=[f"kbf{fi}"], w=[f"KTs{t}"])
                p2, pk2 = next_ps()
                for sbk in range(4):
                    S.op("pe", lambda e, p2=p2, fm=fm, sbk=sbk: e.transpose(
                        out=p2[:, sbk * 128:(sbk + 1) * 128], in_=fm[:, sbk * 128:(sbk + 1) * 128], identity=identf[:]),
                        r=[f"tmpf{fi}", "identf"], w=[pk2])
                st = kst[fi]
                copy_op("act" if m % 2 else "dve", st[:, :, :].rearrange("p s f -> p (s f)"), p2[:, :], [pk2], [f"kst{fi}"])
                dst = o_k if m < 6 else o_v
                mm = m % 6
                S.dma("sp", lambda e, dst=dst, mm=mm, st=st: e.dma_start(
                    out=dst[t * NT:t * NT + N, mm * 128:(mm + 1) * 128].rearrange("(s p) f -> p s f", p=128), in_=st[:, :, :]),
                    "okv_out2", r=[f"kst{fi}"])
                if m >= 6:
                    c = m - 6
                    S.op("pool", lambda e, st=st, c=c: e.tensor_copy(out=va7[:, :, c, 0, 0:64], in_=st[:, :, 0:64]),
                         r=[f"kst{fi}"], w=[f"vaug{n}" for n in range(4)])
                    S.op("pool", lambda e, st=st, c=c: e.tensor_copy(out=va7[:, :, c, 1, 64:128], in_=st[:, :, 64:128]),
                         r=[f"kst{fi}"], w=[f"vaug{n}" for n in range(4)])
            for sbk in range(N // 128):
                tok0 = t * NT + sbk * 128
                blk = tok0 // 128
                S.dma("sp", lambda e, sbk=sbk, blk=blk: e.dma_start(
                    out=Vs[:, :, blk, :].rearrange("h p d -> p h d"), in_=vaug[:, sbk, :, :]), "vs_out",
                    r=[f"vaug{sbk}"], w=[f"Vs{t}"])
                p, pk = next_ps()
                for k in range(8):
                    S.op("pe", lambda e, p=p, k=k, sbk=sbk: e.matmul(
                        p[:, 0:NH], lhsT=xn[:, k, sbk * 128:(sbk + 1) * 128], rhs=wfb[:, k, :], start=(k == 0), stop=(k == 7)),
                        r=["wfb", XK[k]], w=[pk])
                lfs = lf[:, sbk, :]
                lk = f"lf{sbk}"
                S.op("dve", lambda e, p=p, lfs=lfs: e.tensor_tensor(out=lfs, in0=p[:, 0:NH], in1=bfb[:], op=ALU.add),
                     r=[pk, "bfb"], w=[lk])
                S.op("act", lambda e, lfs=lfs: e.activation(out=lfs, in_=lfs, func=AF.Exp, scale=-1.0), r=[lk], w=[lk])
                S.op("act", lambda e, lfs=lfs: e.activation(out=lfs, in_=lfs, func=AF.Ln, bias=1.0, scale=1.0), r=[lk], w=[lk])
                S.op("dve", lambda e, lfs=lfs: e.tensor_scalar(out=lfs, in0=lfs, scalar1=-1.0, scalar2=None, op0=ALU.mult),
                     r=[lk], w=[lk])
                S.dma("sp", lambda e, tok0=tok0, lfs=lfs: e.dma_start(out=o_logf[tok0:tok0 + 128, :], in_=lfs), "olf_out", r=[lk])
                p, pk = next_ps()
                S.op("pe", lambda e, p=p, lfs=lfs: e.matmul(p[:, 0:NH], lhsT=trif[:], rhs=lfs, start=True, stop=True),
                     r=["trif", lk], w=[pk])
                p2, pk2 = next_ps()
                S.op("pe", lambda e, p2=p2, lfs=lfs: e.matmul(p2[:, 0:NH], lhsT=onesf[:], rhs=lfs, start=True, stop=True),
                     r=["onesf", lk], w=[pk2])
                S.op("dve", lambda e, p=p: e.tensor_tensor(out=tmpf[0][:, 0:NH], in0=p[:, 0:NH], in1=carry[:], op=ALU.add),
                     r=[pk, "carry"], w=["tmpf0"])
                S.op("dve", lambda e, blk=blk: e.tensor_scalar(out=Fneg[:, blk, :], in0=tmpf[0][:, 0:NH], scalar1=-1.0,
                                                                scalar2=None, op0=ALU.mult), r=["tmpf0"], w=["Fneg"])
                S.op("dve", lambda e, sbk=sbk: e.tensor_copy(out=fq_sb[:, sbk, :], in_=tmpf[0][:, 0:NH]), r=["tmpf0"],
                     w=["fq_sb"])
                S.op("dve", lambda e, p2=p2: e.tensor_tensor(out=carry[:], in0=carry[:], in1=p2[:, 0:NH], op=ALU.add),
                     r=[pk2, "carry"], w=["carry"])

        def dense_tail(layer, N):
            def ev_out(m, p, pk):
                S.op("dve", lambda e: e.tensor_tensor(out=h[:, m, 0:N], in0=p[:, 0:N], in1=h[:, m, 0:N], op=ALU.add),
                     r=[pk, HK[m]], w=[HK[m]])
            proj_fm(U_OUT[layer], 8, range(8), lambda k: ymix[:, k, 0:N], YK, N, ev_out)
            rmsnorm(N, 2 + layer, lambda k: xn[:, k, 0:N], XK, HK)
            enter("hid")

            def ev_up(m, p, pk):
                i = m % 2
                S.op("act", lambda e: e.activation(out=tmpf[i][:, 0:N], in_=p[:, 0:N], func=AF.Relu), r=[pk], w=[f"tmpf{i}"])
                S.op("pool", lambda e: e.tensor_tensor(out=hid[:, m, 0:N], in0=tmpf[i][:, 0:N], in1=tmpf[i][:, 0:N], op=ALU.mult),
                     r=[f"tmpf{i}"], w=[HIDKEYS[m]])
            proj_fm(U_UP[layer], 8, range(32), lambda k: xn[:, k, 0:N], XK, N, ev_up)
            proj_fm(U_DOWN[layer], 32, range(8), lambda k: hid[:, k, 0:N], HIDKEYS, N, ev_out)

        def in_proj(layer, N):
            rmsnorm(N, layer, lambda k: xn[:, k, 0:N], XK, HK)

            def ev_z(m, p, pk):
                copy_op(evac_eng(), z[:, m, 0:N], p[:, 0:N], [pk], [ZK[m]])
            proj_fm(U_IN[layer], 8, range(8), lambda k: xn[:, k, 0:N], XK, N, ev_z)
            mem_attention(N, 0, lambda hb, hc, mt: KmT[hb:hb + 64, layer, hc, mt * 128:(mt + 1) * 128],
                          lambda mt, hd: Vmp[:, layer, mt, hd, :], ["KmT", "Vmp"])

        for t in range(NTILES):
            enter("xtm")
            for sbk in range(4):
                S.dma("sp", lambda e, sbk=sbk, t=t: e.dma_start(out=xtm[:, sbk, :], in_=x_in[t * NT + sbk * 128:t * NT + (sbk + 1) * 128, :]),
                      "xtm", w=["xtm"])
            for c in range(8):
                p, pk = next_ps()
                for sbk in range(4):
                    S.op("pe", lambda e, p=p, sbk=sbk, c=c: e.transpose(
                        out=p[:, sbk * 128:(sbk + 1) * 128], in_=xtm[:, sbk, c * 128:(c + 1) * 128], identity=identf[:]),
                        r=["xtm", "identf"], w=[pk])
                copy_op(evac_eng(), h[:, c, :], p[:, :], [pk], [HK[c]])
            in_proj(0, NT)
            s5_layer(NT, 128, 1, lambda ri, j, sg: hprev[:, ri, j:j + 1], lambda ri, j0, sg: hprev[:, ri, j0:j0 + 6], ["hprev"])
            dense_tail(0, NT)
            S.dma("sp", lambda e, t=t: e.dma_start(out=H1[t * 128:(t + 1) * 128, :], in_=h[:].rearrange("p k n -> p (k n)")),
                  "h1_out", r=HK, w=[f"H1_{t}"])
            kv_stage(t, NT)
            S.dma("sp", lambda e, t=t: e.dma_start(out=Fq[t * 128:(t + 1) * 128, :], in_=fq_sb[:].rearrange("p s h -> p (s h)")),
                  "fq_out", r=["fq_sb"], w=[f"Fq{t}"])

        for i1 in range(NL1):
            S.dma("pool", lambda e, i1=i1: e.indirect_dma_start(
                out=h[:].rearrange("p k n -> p (k n)"), out_offset=None, in_=H1,
                in_offset=bass.IndirectOffsetOnAxis(ap=hidx[:, i1:i1 + 1], axis=0)), "h1_g",
                r=["hidx"] + [f"H1_{tt}" for tt in range(NTILES)], w=HK)
            in_proj(1, NT)
            fox_layer(i1)
            dense_tail(1, NT)
            enter("yfm")
            rmsnorm(NT, 5, lambda k: yfm[:, k, 0:NT], ["yfm"] * 8, HK)
            enter("xtm")
            for sbk in range(4):
                for c2 in range(2):
                    p, pk = next_ps()
                    for cc in range(4):
                        c = c2 * 4 + cc
                        S.op("pe", lambda e, p=p, sbk=sbk, c=c, cc=cc: e.transpose(
                            out=p[:, cc * 128:(cc + 1) * 128], in_=yfm[:, c, sbk * 128:(sbk + 1) * 128], identity=identf[:]),
                            r=["yfm", "identf"], w=[pk])
                    copy_op(evac_eng(), xtm[:, sbk, c2 * 512:(c2 + 1) * 512], p[:, :], [pk], ["xtm"])
                S.dma("sp", lambda e, sbk=sbk, i1=i1: e.dma_start(
                    out=y_out[i1 * NT + sbk * 128:i1 * NT + (sbk + 1) * 128, :], in_=xtm[:, sbk, :]), "y_out", r=["xtm"])

        NS = 32
        enter("xtm")
        S.dma("sp", lambda e: e.dma_start(out=xtm[0:NS, 0, :], in_=xs_in[:, :]), "xtm", w=["xtm"])
        for c2 in range(2):
            p, pk = next_ps()
            for cc in range(4):
                c = c2 * 4 + cc
                S.op("pe", lambda e, p=p, c=c, cc=cc: e.transpose(
                    out=p[:, cc * NS:(cc + 1) * NS], in_=xtm[0:NS, 0, c * 128:(c + 1) * 128], identity=identf[0:NS, 0:NS]),
                    r=["xtm", "identf"], w=[pk])
            copy_op(evac_eng(), h[:, c2 * 4:(c2 + 1) * 4, 0:NS], p[:, 0:4 * NS].rearrange("p (c n) -> p c n", c=4), [pk],
                    HK[c2 * 4:(c2 + 1) * 4])

        def sample_kv():
            N = NS
            rmsnorm(N, 4, lambda k: xn[:, k, 0:N], XK, HK)
            enter("kv")
            for nch in range(12):
                p, pk = next_ps()
                wb, wk = getw(U_KV + nch)
                for k in range(8):
                    S.op("pe", lambda e, p=p, wb=wb, k=k: e.matmul(
                        p[0:N, 0:128], lhsT=xn[:, k, 0:N], rhs=wb[:, k, :], start=(k == 0), stop=(k == 7)),
                        r=[wk, XK[k]], w=[pk])
                copy_op(evac_eng(), kvt[0:N, nch * 128:(nch + 1) * 128], p[0:N, 0:128], [pk], [f"kvt{nch}"])
                if nch < 6:
                    p2, pk2 = next_ps()
                    for k in range(8):
                        S.op("pe", lambda e, p2=p2, wb=wb, k=k: e.matmul(
                            p2[:, 0:N], lhsT=wb[:, k, :], rhs=xn[:, k, 0:N], start=(k == 0), stop=(k == 7)),
                            r=[wk, XK[k]], w=[pk2])
                    copy_op(evac_eng(), KTnew[:, nch, :], p2[:, 0:N], [pk2], ["KTnew"])
            kkeys = [f"kvt{n}" for n in range(6)]
            vkeys = [f"kvt{n}" for n in range(6, 12)]
            S.dma("sp", lambda e: e.dma_start(out=o_sk[:, :], in_=kvt[0:N, 0:768]), "ok_out", r=kkeys)
            S.dma("sp", lambda e: e.dma_start(out=o_sv[:, :], in_=kvt[0:N, 768:1536]), "ov_out", r=vkeys)
            S.op("pool", lambda e: e.tensor_copy(out=vnbp[:, :], in_=kvt[0:N, 768:1536]), r=vkeys, w=["vnbp"])
            p, pk = next_ps()
            for k in range(8):
                S.op("pe", lambda e, p=p, k=k: e.matmul(p[0:N, 0:NH], lhsT=xn[:, k, 0:N], rhs=wfb[:, k, :],
                                                         start=(k == 0), stop=(k == 7)), r=["wfb", XK[k]], w=[pk])
            lfs = lf[0:N, 0, :]
            S.op("dve", lambda e, p=p: e.tensor_tensor(out=lfs, in0=p[0:N, 0:NH], in1=bfb[0:N, :], op=ALU.add),
                 r=[pk, "bfb"], w=["lf0"])
            S.op("act", lambda e: e.activation(out=lfs, in_=lfs, func=AF.Exp, scale=-1.0), r=["lf0"], w=["lf0"])
            S.op("act", lambda e: e.activation(out=lfs, in_=lfs, func=AF.Ln, bias=1.0, scale=1.0), r=["lf0"], w=["lf0"])
            S.op("dve", lambda e: e.tensor_scalar(out=lfs, in0=lfs, scalar1=-1.0, scalar2=None, op0=ALU.mult), r=["lf0"], w=["lf0"])
            S.dma("sp", lambda e: e.dma_start(out=o_slogf[:, :], in_=lfs), "olf_out", r=["lf0"])

        def sample_past_logf(sq):
            cl2 = cache_logf.rearrange("n s h -> n (s h)")
            if True:
                S.dma("pool", lambda e, sq=sq: e.indirect_dma_start(
                    out=lfpg.rearrange("p s h -> p (s h)"), out_offset=None, in_=cl2,
                    in_offset=bass.IndirectOffsetOnAxis(ap=ptT[:, sq:sq + 1], axis=0)), "lfpg", r=["ptT"], w=["lfpg"])
                for hd in range(NH):
                    S.op("dve", lambda e, hd=hd: e.tensor_tensor_scan(
                        out=lfpg[:, :, hd], data0=onesf[:, :], data1=lfpg[:, :, hd], initial=0.0, op0=ALU.mult, op1=ALU.add),
                        r=["lfpg", "onesf"], w=["lfpg"])
                S.op("dve", lambda e: e.tensor_copy(out=tmpf[0][:, 0:NH], in_=lfpg[:, 127, :]), r=["lfpg"], w=["tmpf0"])
                p, pk = next_ps()
                S.op("pe", lambda e, p=p: e.matmul(p[:, 0:NH], lhsT=strif[:], rhs=tmpf[0][:, 0:NH], start=True, stop=True),
                     r=["strif", "tmpf0"], w=[pk])
                p2, pk2 = next_ps()
                S.op("pe", lambda e, p2=p2: e.matmul(p2[:, 0:NH], lhsT=onesf[:], rhs=tmpf[0][:, 0:NH], start=True, stop=True),
                     r=["onesf", "tmpf0"], w=[pk2])
                S.op("dve", lambda e, p2=p2, sq=sq: e.tensor_copy(out=carry_s[:, sq, :], in_=p2[:, 0:NH]), r=[pk2], w=["carry_s"])
                S.op("dve", lambda e, p=p: e.tensor_copy(out=tmpf[1][:, 0:NH], in_=p[:, 0:NH]), r=[pk], w=["tmpf1"])
                S.op("dve", lambda e: e.tensor_tensor(
                    out=lfpg, in0=lfpg, in1=tmpf[1][:, 0:NH].unsqueeze(1).broadcast_to([128, 128, NH]), op=ALU.add),
                    r=["lfpg", "tmpf1"], w=["lfpg"])
                for h4 in range(3):
                    p, pk = next_ps()
                    for hh in range(4):
                        hd = h4 * 4 + hh
                        S.op("pe", lambda e, p=p, hh=hh, hd=hd: e.transpose(
                            out=p[:, hh * 128:(hh + 1) * 128], in_=lfpg[:, :, hd], identity=identf[:]),
                            r=["lfpg", "identf"], w=[pk])
                    S.op("dve", lambda e, p=p, h4=h4, sq=sq: e.tensor_scalar(
                        out=FnT[:, h4 * 4:(h4 + 1) * 4, :], in0=p[:, :].rearrange("p (h n) -> p h n", h=4),
                        scalar1=-1.0, scalar2=None, op0=ALU.mult), r=[pk], w=["FnT"])

        def sample_fnew(sq):
            N = NS
            p, pk = next_ps()
            S.op("pe", lambda e, p=p: e.matmul(p[0:N, 0:NH], lhsT=btri[:, :], rhs=lf[0:N, 0, :], start=True, stop=False),
                 r=["btri", "lf0"], w=[pk])
            S.op("pe", lambda e, p=p, sq=sq: e.matmul(p[0:N, 0:NH], lhsT=esel[:, sq, :], rhs=carry_s[:, sq, :],
                                                       start=False, stop=True), r=["esel", "carry_s"], w=[pk])
            fnew = tmpf[0][0:N, 0:NH]
            S.op("dve", lambda e, p=p: e.tensor_copy(out=fnew, in_=p[0:N, 0:NH]), r=[pk], w=["tmpf0"])
            S.op("dve", lambda e: e.tensor_scalar(out=tmpf[1][0:N, 0:NH], in0=fnew, scalar1=-1.0, scalar2=None, op0=ALU.mult),
                 r=["tmpf0"], w=["tmpf1"])
            S.dma("sp", lambda e, sq=sq: e.dma_start(out=FnegN[:, sq, :], in_=tmpf[1][sq * 8:(sq + 1) * 8, 0:NH]), "fnegn",
                  r=["tmpf1"], w=["FnegN"])

        def sample_fox():
            N = NS
            enter("satt")
            ck2 = cache_k.rearrange("n s h d -> (n s) (h d)")
            cv2 = cache_v.rearrange("n s h d -> (n s) (h d)")
            accA = PSW[1][:, 0:384].rearrange("p (c n) -> p c n", c=4)
            accB = PSW[1][:, 512:512 + 288].rearrange("p (c n) -> p c n", c=3)
            S.op("pool", lambda e: e.memset(Qblk[:], 0.0), w=["Qblk"])
            for sq in range(4):
                q0 = sq * 8
                sample_past_logf(sq)
                sample_fnew(sq)
                S.dma("sp", lambda e, sq=sq: e.dma_start(out=VnewS, in_=vnbp[sq * 8:(sq + 1) * 8, :]), "vnew",
                      r=["vnbp"], w=["VnewS"])
                for hh in range(2):
                    hb = hh * 64
                    S.op("act", lambda e, hb=hb, hh=hh, q0=q0: e.activation(
                        out=Qblk[hb:hb + 64, :, hh * 8:(hh + 1) * 8], in_=z[hb:hb + 64, 0:6, q0:q0 + 8], func=AF.Copy, scale=0.125),
                        r=ZK[0:6], w=["Qblk"])
                xq = tmpf[2][0:N, 0:96].rearrange("p (h q) -> p h q", q=8)
                S.op("dve", lambda e, sq=sq: e.tensor_tensor(
                    out=xq, in0=qmask[:, sq, :, :], in1=tmpf[0][0:N, 0:NH].unsqueeze(2).broadcast_to([N, NH, 8]), op=ALU.mult),
                    r=["qmask", "tmpf0"], w=["tmpf2"])
                p, pk = next_ps()
                S.op("pe", lambda e, p=p: e.matmul(p[:, 0:96], lhsT=onesf[0:N, :], rhs=tmpf[2][0:N, 0:96], start=True, stop=True),
                     r=["onesf", "tmpf2"], w=[pk])
                S.op("dve", lambda e, p=p: e.tensor_copy(out=Ftb[:].rearrange("p h q -> p (h q)"), in_=p[:, 0:96]), r=[pk], w=["Ftb"])
                for n in range(129):
                    new = (n == 128)
                    bi = n % 2
                    if not new:
                        col = sq * 128 + n
                        S.dma("pool", lambda e, bi=bi, col=col: e.indirect_dma_start(
                            out=Kpg[bi], out_offset=None, in_=ck2,
                            in_offset=bass.IndirectOffsetOnAxis(ap=idx_all[:, col:col + 1], axis=0)), f"Kpg{bi}",
                            r=["idx_all"], w=[f"Kpg{bi}"])
                        S.dma("pool", lambda e, bi=bi, col=col: e.indirect_dma_start(
                            out=Vpg[bi], out_offset=None, in_=cv2,
                            in_offset=bass.IndirectOffsetOnAxis(ap=idx_all[:, col:col + 1], axis=0)), f"Vpg{bi}",
                            r=["idx_all"], w=[f"Vpg{bi}"])
                        S.op("dve", lambda e, bi=bi: e.tensor_copy(out=Kpb[bi][:, :], in_=Kpg[bi]), r=[f"Kpg{bi}"], w=[f"Kpb{bi}"])
                        ptb_ = PSW[0][:].bitcast(BF16)
                        for c in range(6):
                            S.op("pe", lambda e, bi=bi, c=c, ptb_=ptb_: e.transpose(
                                out=ptb_[:, c * 128:(c + 1) * 128], in_=Kpb[bi][:, c * 128:(c + 1) * 128], identity=identb[:]),
                                r=[f"Kpb{bi}", "identb"], w=["psw0"])
                        S.op("act", lambda e, bi=bi, ptb_=ptb_: e.activation(
                            out=KTp[bi].rearrange("p c k -> p (c k)"), in_=ptb_[:, 0:768], func=AF.Copy), r=["psw0"], w=[f"KTp{bi}"])
                        S.op("dve", lambda e, bi=bi: e.tensor_copy(out=Vpb[bi], in_=Vpg[bi]), r=[f"Vpg{bi}"], w=[f"Vpb{bi}"])
                        nk = 128
                        kt_fn = lambda c, bi=bi: KTp[bi][:, c, :]
                        v_fn = lambda c, bi=bi: Vpb[bi][:, c * 128:(c + 1) * 128]
                        ktk, vk = f"KTp{bi}", f"Vpb{bi}"
                    else:
                        nk = 8
                        kt_fn = lambda c, q0=q0: KTnew[:, c, q0:q0 + 8]
                        v_fn = lambda c: VnewS[:, c * 128:(c + 1) * 128]
                        ktk, vk = "KTnew", "VnewS"
                    p, pk = next_ps()
                    for c in range(6):
                        S.op("pe", lambda e, p=p, c=c, kt_fn=kt_fn, nk=nk: e.matmul(
                            p[0:nk, c * 16:(c + 1) * 16], lhsT=kt_fn(c), rhs=Qblk[:, c, :], start=True, stop=True),
                            r=[ktk, "Qblk"], w=[pk])
                    sc = tmpf[1][0:nk, 0:96].rearrange("p (h q) -> p h q", q=8)
                    p3 = p[0:nk, 0:96].rearrange("p (h q) -> p h q", q=8)
                    S.op("dve", lambda e, p3=p3, sc=sc, nk=nk: e.tensor_tensor(out=sc, in0=p3, in1=Ftb[0:nk, :, :], op=ALU.add),
                         r=[pk, "Ftb"], w=["tmpf1"])
                    if not new:
                        S.op("dve", lambda e, sc=sc, sq=sq, n=n: e.tensor_tensor(
                            out=sc, in0=sc, in1=FnT[:, :, n].unsqueeze(2).broadcast_to([128, NH, 8]), op=ALU.add),
                            r=["tmpf1", "FnT"], w=["tmpf1"])
                    else:
                        S.op("dve", lambda e, sc=sc, sq=sq: e.tensor_tensor(
                            out=sc, in0=sc, in1=FnegN[:, sq, :].unsqueeze(2).broadcast_to([8, NH, 8]), op=ALU.add),
                            r=["tmpf1", "FnegN"], w=["tmpf1"])
                        S.op("dve", lambda e, sc=sc: e.tensor_tensor(out=sc, in0=sc, in1=cmask[:, :, :], op=ALU.add),
                             r=["tmpf1", "cmask"], w=["tmpf1"])
                    pt = pT[n % 3]
                    ptk = f"pT{n % 3}"
                    S.op("act", lambda e, pt=pt, nk=nk: e.activation(out=pt[0:nk, 0:96], in_=tmpf[1][0:nk, 0:96], func=AF.Exp),
                         r=["tmpf1"], w=[ptk])
                    for c in range(6):
                        acc = accA[:, c, :] if c < 4 else accB[:, c - 4, :]
                        S.op("pe", lambda e, acc=acc, c=c, v_fn=v_fn, pt=pt, nk=nk, n=n, new=new: e.matmul(
                            acc, lhsT=v_fn(c), rhs=pt[0:nk, 0:96], start=(n == 0 and c in (0, 4)), stop=new,
                            skip_group_check=True), r=[vk, ptk], w=["psw1"])
                    S.op("pe", lambda e, pt=pt, nk=nk, new=new: e.matmul(
                        accB[:, 2, :], lhsT=onesb[0:nk, :], rhs=pt[0:nk, 0:96], start=False, stop=new, skip_group_check=True),
                        r=["onesb", ptk], w=["psw1"])
                S.op("dve", lambda e: e.reciprocal(out=tmpf[2][:, 0:96], in_=accB[:, 2, :]), r=["psw1"], w=["tmpf2"])
                for c in range(6):
                    acc = accA[:, c, :] if c < 4 else accB[:, c - 4, :]
                    for hh in range(2):
                        hb = hh * 64
                        cs = (2 * c + hh) * 8
                        S.op("dve", lambda e, acc=acc, c=c, hb=hb, cs=cs, q0=q0: e.tensor_tensor(
                            out=ymix[hb:hb + 64, c, q0:q0 + 8], in0=acc[hb:hb + 64, cs:cs + 8], in1=tmpf[2][hb:hb + 64, cs:cs + 8],
                            op=ALU.mult), r=["psw1", "tmpf2"], w=[YK[c]])

        for layer in range(2):
            rmsnorm(NS, layer, lambda k: xn[:, k, 0:NS], XK, HK)

            def ev_zs(m, p, pk):
                copy_op(evac_eng(), z[:, m, 0:NS], p[:, 0:NS], [pk], [ZK[m]])
            proj_fm(U_IN[layer], 8, range(8), lambda k: xn[:, k, 0:NS], XK, NS, ev_zs)
            for sq in range(4):
                mem_attention(8, sq * 8,
                              lambda hb, hc, mt, layer=layer, sq=sq: KmTs[hb:hb + 64, sq, layer, hc, mt * 128:(mt + 1) * 128],
                              lambda mt, hd, layer=layer, sq=sq: Vmps[:, sq, layer, mt, hd, :], ["KmTs", "Vmps"])
            if layer == 0:
                s5_layer(NS, 8, 4, lambda ri, j, sg: h0s[:, ri, j, sg:sg + 1], lambda ri, j0, sg: s5so[:, ri, j0:j0 + 6, sg],
                         ["s5so"])
                S.dma("sp", lambda e: e.dma_start(out=o_s5s[:, :, :, :], in_=s5so[:]), "s5s_out", r=["s5so"])
            else:
                sample_fox()
            dense_tail(layer, NS)
            if layer == 0:
                sample_kv()
        enter("yfm")
        rmsnorm(NS, 5, lambda k: yfm[:, k, 0:NS], ["yfm"] * 8, HK)
        enter("xtm")
        for c2 in range(2):
            p, pk = next_ps()
            for cc in range(4):
                c = c2 * 4 + cc
                S.op("pe", lambda e, p=p, c=c, cc=cc: e.transpose(
                    out=p[0:NS, cc * 128:(cc + 1) * 128], in_=yfm[:, c, 0:NS], identity=identf[:]),
                    r=["yfm", "identf"], w=[pk])
            copy_op(evac_eng(), xtm[0:NS, 0, c2 * 512:(c2 + 1) * 512], p[0:NS, :], [pk], ["xtm"])
        S.dma("sp", lambda e: e.dma_start(out=ys_out[:, :], in_=xtm[0:NS, 0, :]), "y_out", r=["xtm"])

        S.dma("sp", lambda e: e.dma_start(out=o_s5[:, :, :], in_=hprev[:]), "s5_out", r=["hprev"])

        S.finish_waits("sp")
        with nc.Block() as block:
            S.emit(block)
    return nc


def _units(W, KG=8):
    K, N = W.shape
    Kc, Mc = K // 128, N // 128
    nkq = (Kc + 7) // 8
    out = np.zeros((Mc, nkq, 128, 8, 128), np.float32)
    W4 = W.reshape(Kc, 128, Mc, 128)
    for kq in range(nkq):
        kn = min(8, Kc - kq * 8)
        out[:, kq, :, :kn, :] = W4[kq * 8:kq * 8 + kn].transpose(2, 1, 0, 3)
    return out.reshape(Mc * nkq, 128, 1024)


def _selden():
    sd = np.zeros((128, 2, 128), np.float32)
    sd[64, 0, 0:64] = 1.0
    sd[0, 1, 64:128] = 1.0
    return sd


def _fm(v, nch):
    return np.ascontiguousarray(np.asarray(v, np.float32).reshape(nch, 128).T)


def prepare_shared(inp):
    f = lambda k: np.asarray(inp[k], np.float32)
    units = []
    for i in range(2):
        units += [_units(f("w_in")[i]), _units(f("w_out")[i]), _units(f("w_up")[i]), _units(f("w_down")[i])]
    units += [_units(f("s5_w_glu")[0]), _units(f("w_kv"))]
    wall = np.concatenate(units, axis=0)
    assert wall.shape[0] == N_UNITS
    gall = np.stack([_fm(f("norm_mix")[0], 8), _fm(f("norm_mix")[1], 8), _fm(f("norm_mlp")[0], 8),
                     _fm(f("norm_mlp")[1], 8), _fm(f("norm_kv"), 8), _fm(f("norm_final"), 8)], axis=1)
    a_re, a_im, ls = f("s5_a_re")[0], f("s5_a_im")[0], f("s5_log_step")[0]
    lse = np.repeat(ls[:, None], 64, axis=1)
    a_b = np.stack([np.broadcast_to(a.reshape(1, 3072), (128, 3072)) for a in (a_re, a_im, lse)]).astype(np.float32)
    sm = lambda a: a.reshape(24, 128).T
    a_s = np.stack([sm(a_re), sm(a_im), sm(lse)], axis=1).astype(np.float32)
    b_re, b_im = f("s5_b_re")[0], f("s5_b_im")[0]
    c_re, c_im = f("s5_c_re")[0], f("s5_c_im")[0]
    bpad = np.zeros((2, 128, 24, 128), np.float32)
    cpad = np.zeros((2, 128, 24, 128), np.float32)
    for g in range(48):
        j, g2, gl = g // 2, g % 2, g % 8
        for ri, (b, c) in enumerate(((b_re, c_re), (b_im, c_im))):
            bpad[ri, gl * 16:(gl + 1) * 16, j, g2 * 64:(g2 + 1) * 64] = b[g].T
            cpad[ri, g2 * 64:(g2 + 1) * 64, j, gl * 16:(gl + 1) * 16] = c[g].T
    tri = np.triu(np.ones((128, 128), np.float32))
    tcount = np.broadcast_to(np.arange(1, 129, dtype=np.float32)[None, :], (128, 128)).copy()
    s_idx = np.arange(128)[:, None, None]
    d_idx = np.arange(4)[None, :, None]
    q_idx = np.arange(512)[None, None, :]
    masks = np.where(128 * d_idx + s_idx <= q_idx, 0.0, -1e30).astype(np.float32).astype(ml_dtypes.bfloat16)
    return {
        "w_mem_kv": np.ascontiguousarray(f("w_mem_kv")), "wall": wall, "gall": np.ascontiguousarray(gall),
        "bglu": _fm(f("s5_b_glu")[0], 6), "s5d": _fm(f("s5_d")[0].reshape(-1), 6),
        "wf": np.ascontiguousarray(f("w_f").reshape(8, 128, 12).transpose(1, 0, 2)),
        "bfb": np.ascontiguousarray(np.broadcast_to(f("b_f")[None, :], (128, 12))),
        "a_b": a_b, "a_s": np.ascontiguousarray(a_s), "bpad": bpad, "cpad": cpad,
        "selden": _selden(), "ident_f": np.eye(128, dtype=np.float32), "tri_f": tri, "tcount": tcount, "masks": masks,
    }


T_PROMPT = 8192
DEBUG = False
COMPACT_DEV = False
DBG_NAMES = []
DBG_OUT = {}


def kernel(**inputs):
    T = T_PROMPT
    shared = prepare_shared(inputs)
    x_prompt = np.asarray(inputs["x_prompt"], np.float32)
    mem_prompt = np.asarray(inputs["mem_prompt"], np.float32)
    x_sample = np.asarray(inputs["x_sample"], np.float32)
    page_table = np.asarray(inputs["page_table"], np.int32)
    cache_k = np.asarray(inputs["cache_k"], np.float32)
    cache_v = np.asarray(inputs["cache_v"], np.float32)
    cache_logf = np.asarray(inputs["cache_logf"], np.float32)
    st_re = np.asarray(inputs["state_s5_re"], np.float32)[0]
    st_im = np.asarray(inputs["state_s5_im"], np.float32)[0]
    cmk = np.asarray(inputs["cache_mem_k"], np.float32).reshape(2, 32, MEM_T, 256)
    cmv = np.asarray(inputs["cache_mem_v"], np.float32).reshape(2, 32, MEM_T, 256)
    npool = cache_k.shape[0]
    nc = build_program(T, 512 if COMPACT_DEV else npool)
    stri = np.triu(np.ones((128, 128), np.float32), k=1)
    ii = np.arange(32)
    btri = ((ii[:, None] // 8 == ii[None, :] // 8) & (ii[:, None] <= ii[None, :])).astype(np.float32)
    esel = np.zeros((128, 4, 32), np.float32)
    for sq in range(4):
        esel[0, sq, sq * 8:(sq + 1) * 8] = 1.0
    qmask = np.zeros((32, 4, NH, 8), np.float32)
    for sq in range(4):
        for q in range(8):
            qmask[sq * 8 + q, sq, :, q] = 1.0
    cmask = np.where(np.arange(8)[:, None, None] <= np.arange(8)[None, None, :], 0.0, -1e30).astype(np.float32)
    cmask = np.ascontiguousarray(np.broadcast_to(cmask, (8, NH, 8)))
    consts = {"iota_i": np.arange(128, dtype=np.int32)[:, None].copy(), "stri_f": stri, "btri32": btri, "esel": esel,
              "qmask": qmask, "cmask": cmask}
    in_maps = []
    for c in range(NCORES):
        s = c // 4
        m = dict(shared)
        m.update(consts)
        m["x"] = np.ascontiguousarray(x_prompt[s, :T])
        j = c % 4
        m["hidx"] = np.ascontiguousarray(((4 * np.arange(4)[None, :] + j) * 128 + np.arange(128)[:, None]).astype(np.int32))
        ohbb = np.zeros((128, 2, 4), np.float32)
        ohbb[:, 0, j] = 1.0
        ohbb[:, 1, j + 1:] = -1e30
        m["ohbb"] = ohbb
        m["mem_prompt"] = np.ascontiguousarray(mem_prompt[s])
        sl = slice(4 * c, 4 * c + 4)
        m["xs"] = np.ascontiguousarray(x_sample[sl].reshape(32, D))
        h0 = np.stack([st_re[sl], st_im[sl]])
        m["h0s"] = np.ascontiguousarray(h0.reshape(2, 4, 24, 2, 64).transpose(3, 4, 0, 2, 1).reshape(128, 2, 24, 4))
        m["cmk"] = np.ascontiguousarray(cmk[:, sl])
        m["cmv"] = np.ascontiguousarray(cmv[:, sl])
        pt = page_table[sl]
        if COMPACT_DEV:
            flat = pt.reshape(-1)
            m["cache_k"] = np.ascontiguousarray(cache_k[flat])
            m["cache_v"] = np.ascontiguousarray(cache_v[flat])
            m["cache_logf"] = np.ascontiguousarray(cache_logf[flat])
            pt = np.arange(512, dtype=np.int32).reshape(4, 128)
        else:
            m["cache_k"], m["cache_v"], m["cache_logf"] = cache_k, cache_v, cache_logf
        m["ptab"] = np.ascontiguousarray(pt.reshape(1, 512))
        m["ptabT"] = np.ascontiguousarray(pt.T)
        in_maps.append(m)
    res = run_bass_kernel_spmd(nc, in_maps, core_ids=list(range(NCORES)))
    R = res.results
    sel = [R[0], R[4]]
    if DEBUG:
        DBG_OUT.clear()
        for i, n in enumerate(DBG_NAMES[0]):
            DBG_OUT[n] = R[0]["dbg"][i]
    nl1 = (T // NT) // 4
    y_prompt = np.zeros((2, T, D), np.float32)
    for c in range(NCORES):
        yc = R[c]["y"].reshape(nl1, NT, D)
        for i in range(nl1):
            t0 = (4 * i + c % 4) * NT
            y_prompt[c // 4, t0:t0 + NT] = yc[i]
    okv = np.stack([r["o_mem_kv"] for r in sel], axis=1)
    p_mem_k = np.ascontiguousarray(okv[..., :256]).reshape(2, 2, 256, 4, 64)
    p_mem_v = np.ascontiguousarray(okv[..., 256:]).reshape(2, 2, 256, 4, 64)
    p_k = np.stack([r["o_k"] for r in sel]).reshape(2, T, NH, 64)
    p_v = np.stack([r["o_v"] for r in sel]).reshape(2, T, NH, 64)
    p_logf = np.stack([r["o_logf"] for r in sel])
    s5 = np.stack([r["o_s5"] for r in sel])
    s5 = s5.reshape(2, 2, 64, 2, 24).transpose(3, 0, 4, 1, 2).reshape(2, 2, 48, 64)
    p_s5_re, p_s5_im = np.ascontiguousarray(s5[0][None]), np.ascontiguousarray(s5[1][None])
    y_sample = np.concatenate([r["ys"].reshape(4, 8, D) for r in R])
    s_k = np.concatenate([r["o_sk"].reshape(4, 8, NH, 64) for r in R])
    s_v = np.concatenate([r["o_sv"].reshape(4, 8, NH, 64) for r in R])
    s_logf = np.concatenate([r["o_slogf"].reshape(4, 8, NH) for r in R])
    ss = np.stack([r["o_s5s"] for r in R])
    ss = ss.reshape(8, 2, 64, 2, 24, 4).transpose(3, 0, 5, 4, 1, 2).reshape(2, 32, 48, 64)
    s_s5_re, s_s5_im = np.ascontiguousarray(ss[0][None]), np.ascontiguousarray(ss[1][None])
    return (y_prompt, y_sample, p_s5_re, p_s5_im, p_mem_k, p_mem_v, p_k, p_v, p_logf,
            s_s5_re, s_s5_im, s_k, s_v, s_logf)
```

```python
import contextlib
import math
import numpy as np
import ml_dtypes
import concourse.bass as bass
import concourse.mybir as mybir
from concourse.bass_utils import run_bass_kernel_spmd

F32 = mybir.dt.float32
BF16 = mybir.dt.bfloat16
I32 = mybir.dt.int32
ALU = mybir.AluOpType
AF = mybir.ActivationFunctionType

NCORES = 8
D = 1024
MEM_T = 256
NT = 512
NH = 12
DMAIN = 768
TWO_PI = 2.0 * math.pi

U_IN = [0, 80]
U_OUT = [8, 88]
U_UP = [16, 96]
U_DOWN = [48, 128]
U_GLU = 160
U_KV = 166
N_UNITS = 178


class Sched:
    def __init__(self, nc, stack):
        self.nc = nc
        self.stack = stack
        self.engs = ["pe", "act", "dve", "pool", "sp"]
        self.ops = {e: [] for e in self.engs}
        self.sem = {e: stack.enter_context(nc.semaphore("sem_" + e)) for e in self.engs}
        self.cnt = {e: 0 for e in self.engs}
        self.epoch = {e: 0 for e in self.engs}
        self.old = []
        self.dsem = {}
        self.lastw = {}
        self.reads = {}
        self.waited = {e: {} for e in self.engs}

    def _collect(self, eng, r, w):
        waits = {}

        def add(t):
            sid, sem, val = t
            if sid not in waits or waits[sid][1] < val:
                waits[sid] = (sem, val)

        for k in list(r) + list(w):
            lw = self.lastw.get(k)
            if lw is not None:
                add(lw)
        for k in w:
            for rd in self.reads.get(k, []):
                add(rd)
        final = []
        for sid, (sem, val) in waits.items():
            if eng == "pe" and sid.startswith("pe#"):
                continue
            if self.waited[eng].get(sid, 0) >= val:
                continue
            self.waited[eng][sid] = val
            final.append((sem, val))
        return final

    def _record(self, t, r, w):
        for k in w:
            self.lastw[k] = t
            self.reads[k] = []
        for k in r:
            lst = self.reads.setdefault(k, [])
            lst.append(t)
            if len(lst) > 24:
                best = {}
                for sid, sem, val in lst:
                    if sid not in best or best[sid][2] < val:
                        best[sid] = (sid, sem, val)
                self.reads[k] = list(best.values())

    def op(self, eng, fn, r=(), w=()):
        final = self._collect(eng, r, w)
        if self.cnt[eng] >= 30000:
            self.old.append((self.sem[eng], self.cnt[eng]))
            self.epoch[eng] += 1
            self.sem[eng] = self.stack.enter_context(self.nc.semaphore(f"sem_{eng}_{self.epoch[eng]}"))
            self.cnt[eng] = 0
        self.cnt[eng] += 1
        v = self.cnt[eng]
        self.ops[eng].append((final, fn, self.sem[eng], 1))
        self._record((f"{eng}#{self.epoch[eng]}", self.sem[eng], v), r, w)

    def dma(self, q, fn, key, r=(), w=()):
        final = self._collect(q, r, w)
        if key not in self.dsem:
            self.dsem[key] = [self.stack.enter_context(self.nc.semaphore("d_" + key)), 0]
        ds = self.dsem[key]
        ds[1] += 16
        self.ops[q].append((final, fn, ds[0], 16))
        self._record(("d_" + key, ds[0], ds[1]), r, w)

    def barrier(self):
        allw = [(sem, val) for (sem, val) in self.dsem.values()] + list(self.old)
        allw += [(self.sem[e], self.cnt[e]) for e in self.engs if self.cnt[e] > 0]
        for e in self.engs:
            self.ops[e].append((list(allw), None, None, 0))

    def finish_waits(self, eng="sp"):
        final = [(sem, val) for (sem, val) in self.dsem.values()] + list(self.old)
        for e in self.engs:
            if e != eng and self.cnt[e] > 0:
                final.append((self.sem[e], self.cnt[e]))
        self.ops[eng].append((final, None, None, 0))

    def emit(self, block):
        table = {"pe": block.tensor, "act": block.scalar, "dve": block.vector,
                 "pool": block.gpsimd, "sp": block.sync}
        for e in self.engs:
            ops = self.ops[e]

            def body(engine, ops=ops):
                for waits, fn, sem, inc in ops:
                    for s, v in waits:
                        engine.wait_ge(s, v)
                    if fn is not None:
                        ins = fn(engine)
                        ins.then_inc(sem, inc)

            table[e](body)


def build_program(T, NPOOL=5120):
    nc = bass.Bass("TRN2", target_bir_lowering=False)
    NTILES = T // NT
    NBLK = T // 128

    def din(name, shape, dt=F32):
        return nc.dram_tensor(name, list(shape), dt, kind="ExternalInput").ap()

    def dout(name, shape, dt=F32):
        return nc.dram_tensor(name, list(shape), dt, kind="ExternalOutput").ap()

    def dscr(name, shape, dt):
        return nc.dram_tensor(name, list(shape), dt, kind="Internal").ap()

    x_in = din("x", [T, D])
    mem_prompt = din("mem_prompt", [MEM_T, D])
    w_mem_kv = din("w_mem_kv", [2, D, 512])
    wall = din("wall", [N_UNITS, 128, 1024])
    gall_d = din("gall", [128, 6, 8])
    bglu_d = din("bglu", [128, 6])
    s5d_d = din("s5d", [128, 6])
    wf_d = din("wf", [128, 8, 12])
    bf_d = din("bfb", [128, 12])
    a_b_d = din("a_b", [3, 128, 3072])
    a_s_d = din("a_s", [128, 3, 24])
    bpad_d = din("bpad", [2, 128, 24, 128])
    cpad_d = din("cpad", [2, 128, 24, 128])
    ident_f = din("ident_f", [128, 128])
    tri_d = din("tri_f", [128, 128])
    tcount_d = din("tcount", [128, 128])
    masks_d = din("masks", [128, 4, 512], BF16)
    selden_d = din("selden", [128, 2, 128])

    NPH = NPOOL
    xs_in = din("xs", [32, D])
    h0s_d = din("h0s", [128, 2, 24, 4])
    cmk_d = din("cmk", [2, 4, MEM_T, 256])
    cmv_d = din("cmv", [2, 4, MEM_T, 256])
    cache_k = din("cache_k", [NPH, 128, NH, 64])
    cache_v = din("cache_v", [NPH, 128, NH, 64])
    cache_logf = din("cache_logf", [NPH, 128, NH])
    ptab_d = din("ptab", [1, 512], I32)
    ptabT_d = din("ptabT", [128, 4], I32)
    iota_d = din("iota_i", [128, 1], I32)
    stri_d = din("stri_f", [128, 128])
    btri_d = din("btri32", [32, 32])
    esel_d = din("esel", [128, 4, 32])
    qmask_d = din("qmask", [32, 4, NH, 8])
    cmask_d = din("cmask", [8, NH, 8])
    ys_out = dout("ys", [32, D])
    o_sk = dout("o_sk", [32, DMAIN])
    o_sv = dout("o_sv", [32, DMAIN])
    o_slogf = dout("o_slogf", [32, NH])
    o_s5s = dout("o_s5s", [128, 2, 24, 4])
    NL1 = NTILES // 4
    hidx_d = din("hidx", [128, 4], I32)
    ohbb_d = din("ohbb", [128, 2, 4])
    y_out = dout("y", [NL1 * NT, D])
    o_mem_kv = dout("o_mem_kv", [2, MEM_T, 512])
    o_k = dout("o_k", [T, DMAIN])
    o_v = dout("o_v", [T, DMAIN])
    o_logf = dout("o_logf", [T, NH])
    o_s5 = dout("o_s5", [128, 2, 24])

    dbg_out = dout("dbg", [16, 128, NT]) if DEBUG else None
    wscr = dscr("wscr", [N_UNITS, 128, 1024], BF16)
    H1 = dscr("H1", [NTILES * 128, 8 * NT], F32)
    Fq = dscr("Fq", [NTILES * 128, 4 * NH], F32)
    KTs = dscr("KTs", [NH, 64, T], BF16)
    Vs = dscr("Vs", [NH, 128, NBLK, 128], BF16)

    with contextlib.ExitStack() as st:
        S = Sched(nc, st)

        def sb(name, shape, dt, stack=None):
            return (stack or st).enter_context(nc.sbuf_tensor("sb_" + name, list(shape), dt))

        def ps(name, shape, dt=F32):
            return st.enter_context(nc.psum_tensor(name, list(shape), dt))

        PS = [ps(f"ps{i}", [128, 512], F32) for i in range(4)]
        PSW = [ps(f"psw{i}", [128, 1024], F32) for i in range(2)]
        psrot = [0]

        def next_ps():
            i = psrot[0] % 4
            psrot[0] += 1
            return PS[i], f"ps{i}"

        evrot = [0]

        def evac_eng():
            evrot[0] += 1
            return "dve" if evrot[0] % 2 else "act"

        def copy_op(eng, out, in_, r, w):
            if eng == "act":
                S.op("act", lambda e: e.activation(out=out, in_=in_, func=AF.Copy), r=r, w=w)
            else:
                S.op(eng, lambda e: e.tensor_copy(out=out, in_=in_), r=r, w=w)

        identf = sb("identf", [128, 128], F32)
        identb = sb("identb", [128, 128], BF16)
        onesb = sb("onesb", [128, 128], BF16)
        onesf = sb("onesf", [128, 128], F32)
        trif = sb("trif", [128, 128], F32)
        masks = sb("masks", [128, 4, 512], BF16)
        selden = sb("selden_sb", [128, 2, 128], F32)
        gall = sb("gall_sb", [128, 6, 8], F32)
        bglu = sb("bglu_sb", [128, 6], F32)
        s5d = sb("s5d_sb", [128, 6], F32)
        wfb = sb("wfb", [128, 8, 12], BF16)
        bfb = sb("bfb_sb", [128, 12], F32)
        Ctab = sb("Ctab", [128, 24, 128], F32)
        Stab = sb("Stab", [128, 24, 128], F32)
        r_s = sb("r_s", [128, 24], F32)
        Bpad = sb("Bpad", [128, 2, 24, 128], BF16)
        Cpad = sb("Cpad", [128, 2, 24, 128], BF16)
        hprev = sb("hprev", [128, 2, 24], F32)
        KmT = sb("KmT", [128, 2, 2, MEM_T], BF16)
        Vmp = sb("Vmp", [128, 2, 2, 4, 128], BF16)
        onespad = sb("onespad", [128, 2, 128], BF16)
        Fneg = sb("Fneg", [128, max(NBLK, 4), NH], F32)
        idx_all = sb("idx_all", [128, 512], I32)
        ptT = sb("ptT", [128, 4], I32)
        h0s = sb("h0s_sb", [128, 2, 24, 4], F32)
        KmTs = sb("KmTs", [128, 4, 2, 2, MEM_T], BF16)
        Vmps = sb("Vmps", [128, 4, 2, 2, 4, 128], BF16)
        strif = sb("strif", [128, 128], F32)
        btri = sb("btri_sb", [32, 32], F32)
        esel = sb("esel_sb", [128, 4, 32], F32)
        qmask = sb("qmask_sb", [32, 4, NH, 8], F32)
        cmask = sb("cmask_sb", [8, NH, 8], F32)
        carry = sb("carry", [128, NH], F32)
        ZF = sb("ZF", [128, 4, NH, 65], BF16)

        def ld(dst, src, key):
            S.dma("sp", lambda e: e.dma_start(out=dst, in_=src), key, w=[key])

        ld(identf[:], ident_f[:, :], "identf")
        ld(trif[:], tri_d[:, :], "trif")
        ld(masks[:], masks_d[:, :, :], "masks")
        ld(selden[:], selden_d[:, :, :], "selden")
        ld(gall[:], gall_d[:, :, :], "gall")
        ld(bglu[:], bglu_d[:, :], "bglu")
        ld(s5d[:], s5d_d[:, :], "s5d")
        ld(bfb[:], bf_d[:, :], "bfb")
        ld(h0s[:], h0s_d[:, :, :, :], "h0s")
        ld(strif[:], stri_d[:, :], "strif")
        ld(btri[:], btri_d[:, :], "btri")
        ld(esel[:], esel_d[:, :, :], "esel")
        ld(qmask[:], qmask_d[:, :, :, :], "qmask")
        ld(cmask[:], cmask_d[:, :, :], "cmask")
        ld(ptT[:], ptabT_d[:, :], "ptT")
        hidx = sb("hidx_sb", [128, 4], I32)
        ohbb = sb("ohbb_sb", [128, 2, 4], F32)
        fq_sb = sb("fq_sb", [128, 4, NH], F32)
        FnB = sb("FnB", [128, 16, NH], F32)
        ld(hidx[:], hidx_d[:, :], "hidx")
        ld(ohbb[:], ohbb_d[:, :, :], "ohbb")
        S.op("dve", lambda e: e.tensor_copy(out=identb[:], in_=identf[:]), r=["identf"], w=["identb"])
        S.op("pool", lambda e: e.memset(onesb[:], 1.0), w=["onesb"])
        S.op("pool", lambda e: e.memset(onesf[:], 1.0), w=["onesf"])
        S.op("pool", lambda e: e.memset(onespad[:], 0.0), w=["onespad"])
        S.op("pool", lambda e: e.memset(onespad[:, 0, 0:64], 1.0), w=["onespad"])
        S.op("pool", lambda e: e.memset(onespad[:, 1, 64:128], 1.0), w=["onespad"])
        S.op("pool", lambda e: e.memset(hprev[:], 0.0), w=["hprev"])
        S.op("pool", lambda e: e.memset(carry[:], 0.0), w=["carry"])
        S.op("pool", lambda e: e.memset(ZF[:], 0.0), w=["ZF0", "ZF1", "ZF2", "ZF3"])
        S.op("pool", lambda e: e.memset(Vmp[:], 0.0), w=["Vmp"])

        with contextlib.ExitStack() as st0:
            wst = [sb(f"wst{i}", [128, 1024], F32, st0) for i in range(3)]
            wbf = [sb(f"wbf{i}", [128, 1024], BF16, st0) for i in range(3)]
            engs3 = ["act", "dve", "pool"]
            for u in range(N_UNITS):
                i = u % 3
                S.dma("sp", lambda e, u=u, i=i: e.dma_start(out=wst[i][:], in_=wall[u]), f"wst{i}", w=[f"wst{i}"])
                copy_op(engs3[i], wbf[i][:], wst[i][:], [f"wst{i}"], [f"wbf{i}"])
                S.dma("sp", lambda e, u=u, i=i: e.dma_start(out=wscr[u], in_=wbf[i][:]), f"wscr_w{i}",
                      r=[f"wbf{i}"], w=[f"wscr{u}"])
            wfst = sb("wfst", [128, 8, 12], F32, st0)
            ld(wfst[:], wf_d[:, :, :], "wfst")
            S.op("dve", lambda e: e.tensor_copy(out=wfb[:], in_=wfst[:]), r=["wfst"], w=["wfb"])

            mem_tm = sb("mem_tm", [128, 2, D], F32, st0)
            memT = sb("memT", [128, 8, MEM_T], BF16, st0)
            for t in range(2):
                S.dma("sp", lambda e, t=t: e.dma_start(out=mem_tm[:, t, :], in_=mem_prompt[t * 128:(t + 1) * 128, :]),
                      "mem_tm", w=[f"mem_tm{t}"])
            for t in range(2):
                for c4 in range(2):
                    p, pk = next_ps()
                    for cc in range(4):
                        c = c4 * 4 + cc
                        S.op("pe", lambda e, p=p, t=t, c=c, cc=cc: e.transpose(
                            out=p[:, cc * 128:(cc + 1) * 128], in_=mem_tm[:, t, c * 128:(c + 1) * 128],
                            identity=identf[:]), r=[f"mem_tm{t}", "identf"], w=[pk])
                    S.op("dve", lambda e, p=p, t=t, c4=c4: e.tensor_copy(
                        out=memT[:, c4 * 4:(c4 + 1) * 4, t * 128:(t + 1) * 128],
                        in_=p[:, :].rearrange("p (c t) -> p c t", c=4)), r=[pk], w=["memT"])
            wmst = sb("wmst", [128, 8, 512], F32, st0)
            wmbf = sb("wmbf", [128, 8, 512], BF16, st0)
            okv = sb("okv", [128, 2, 512], F32, st0)
            for i in range(2):
                S.dma("sp", lambda e, i=i: e.dma_start(
                    out=wmst[:], in_=w_mem_kv[i].rearrange("(c p) n -> p c n", p=128)), "wmst", w=["wmst"])
                S.op("act", lambda e: e.activation(out=wmbf[:], in_=wmst[:], func=AF.Copy), r=["wmst"], w=["wmbf"])
                for t in range(2):
                    p, pk = next_ps()
                    for c in range(8):
                        S.op("pe", lambda e, p=p, t=t, c=c: e.matmul(
                            p[:, :], lhsT=memT[:, c, t * 128:(t + 1) * 128], rhs=wmbf[:, c, :],
                            start=(c == 0), stop=(c == 7)), r=["memT", "wmbf"], w=[pk])
                    S.op("dve", lambda e, p=p, t=t: e.tensor_copy(out=okv[:, t, :], in_=p[:, :]),
                         r=[pk], w=[f"okv{t}"])
                    S.dma("sp", lambda e, i=i, t=t: e.dma_start(
                        out=o_mem_kv[i, t * 128:(t + 1) * 128, :], in_=okv[:, t, :]), "okv_out", r=[f"okv{t}"])
                    for h in range(4):
                        hb = (h % 2) * 64
                        S.op("pool", lambda e, i=i, t=t, h=h, hb=hb: e.tensor_copy(
                            out=Vmp[:, i, t, h, hb:hb + 64], in_=okv[:, t, 256 + h * 64:256 + (h + 1) * 64]),
                            r=[f"okv{t}"], w=["Vmp"])
                for c in range(2):
                    p, pk = next_ps()
                    for k in range(8):
                        S.op("pe", lambda e, p=p, c=c, k=k: e.matmul(
                            p[:, 0:MEM_T], lhsT=wmbf[:, k, c * 128:(c + 1) * 128], rhs=memT[:, k, :],
                            start=(k == 0), stop=(k == 7)), r=["memT", "wmbf"], w=[pk])
                    S.op("act", lambda e, p=p, i=i, c=c: e.activation(out=KmT[:, i, c, :], in_=p[:, 0:MEM_T], func=AF.Copy),
                         r=[pk], w=["KmT"])

            S.op("pool", lambda e: e.memset(Vmps[:], 0.0), w=["Vmps"])
            cm_k = sb("cm_k", [128, 2, 256], F32, st0)
            cm_v = sb("cm_v", [128, 2, 256], F32, st0)
            for sq in range(4):
                for i in range(2):
                    S.dma("sp", lambda e, sq=sq, i=i: e.dma_start(
                        out=cm_k[:], in_=cmk_d[i, sq].rearrange("(t p) f -> p t f", p=128)), "cm_k", w=["cm_k"])
                    S.dma("sp", lambda e, sq=sq, i=i: e.dma_start(
                        out=cm_v[:], in_=cmv_d[i, sq].rearrange("(t p) f -> p t f", p=128)), "cm_v", w=["cm_v"])
                    p, pk = next_ps()
                    for mt in range(2):
                        for c in range(2):
                            S.op("pe", lambda e, p=p, mt=mt, c=c: e.transpose(
                                out=p[:, (c * 2 + mt) * 128:(c * 2 + mt + 1) * 128], in_=cm_k[:, mt, c * 128:(c + 1) * 128],
                                identity=identf[:]), r=["cm_k", "identf"], w=[pk])
                    S.op("act", lambda e, p=p, sq=sq, i=i: e.activation(
                        out=KmTs[:, sq, i, :, :].rearrange("p c m -> p (c m)"), in_=p[:, :], func=AF.Copy), r=[pk], w=["KmTs"])
                    for mt in range(2):
                        for hd4 in range(4):
                            hb = (hd4 % 2) * 64
                            S.op("pool", lambda e, sq=sq, i=i, mt=mt, hd4=hd4, hb=hb: e.tensor_copy(
                                out=Vmps[:, sq, i, mt, hd4, hb:hb + 64], in_=cm_v[:, mt, hd4 * 64:(hd4 + 1) * 64]),
                                r=["cm_v"], w=["Vmps"])
            ptb = sb("ptb", [128, 512], I32, st0)
            ptf = sb("ptf", [128, 512], F32, st0)
            iot = sb("iot", [128, 1], I32, st0)
            iotf = sb("iotf", [128, 1], F32, st0)
            S.dma("sp", lambda e: e.dma_start(out=ptb[:], in_=ptab_d[0:1, :].partition_broadcast(128)), "ptb", w=["ptb"])
            ld(iot[:], iota_d[:, :], "iot")
            S.op("dve", lambda e: e.tensor_copy(out=ptf[:], in_=ptb[:]), r=["ptb"], w=["ptf"])
            S.op("dve", lambda e: e.tensor_copy(out=iotf[:], in_=iot[:]), r=["iot"], w=["iotf"])
            S.op("dve", lambda e: e.tensor_scalar(out=idx_all[:], in0=ptf[:], scalar1=128.0, scalar2=iotf[:, 0:1],
                                                  op0=ALU.mult, op1=ALU.add), r=["ptf", "iotf"], w=["idx_all"])
            S.barrier()
        with contextlib.ExitStack() as st0:
            TB = [sb(f"tb{i}", [128, 3072], F32, st0) for i in range(8)]
            TBi = TB[7][:].bitcast(I32)
            for i in range(3):
                S.dma("sp", lambda e, i=i: e.dma_start(out=TB[i][:], in_=a_b_d[i]), f"tb{i}", w=[f"tb{i}"])
            are, aim, dtb = TB[0], TB[1], TB[2]

            def dv(fn, r, w, eng="dve"):
                S.op(eng, fn, r=r, w=w)

            def sin_reduced(out, in_, ki, kf, keys_r, key_w, ikey, fkey):
                dv(lambda e: e.tensor_scalar(out=ki, in0=in_, scalar1=1.0 / TWO_PI, scalar2=None, op0=ALU.mult),
                   keys_r, [ikey])
                dv(lambda e: e.tensor_copy(out=kf, in_=ki), [ikey], [fkey])
                dv(lambda e: e.scalar_tensor_tensor(out=out, in0=kf, scalar=-TWO_PI, in1=in_, op0=ALU.mult, op1=ALU.add),
                   [fkey] + keys_r, [key_w])
                dv(lambda e: e.tensor_scalar(out=out, in0=out, scalar1=3.141592, scalar2=-3.141592, op0=ALU.min, op1=ALU.max),
                   [key_w], [key_w])
                S.op("act", lambda e: e.activation(out=out, in_=out, func=AF.Sin), r=[key_w], w=[key_w])

            S.op("act", lambda e: e.activation(out=dtb[:], in_=dtb[:], func=AF.Exp), r=["tb2"], w=["tb2"])
            dv(lambda e: e.tensor_tensor(out=TB[3][:], in0=are[:], in1=dtb[:], op=ALU.mult), ["tb0", "tb2"], ["tb3"])
            S.op("act", lambda e: e.activation(out=TB[3][:], in_=TB[3][:], func=AF.Exp), r=["tb3"], w=["tb3"])
            dv(lambda e: e.tensor_tensor(out=TB[4][:], in0=aim[:], in1=dtb[:], op=ALU.mult), ["tb1", "tb2"], ["tb4"])
            sin_reduced(TB[5][:], TB[4][:], TBi, TB[7][:], ["tb4"], "tb5", "tb7", "tb7")
            dv(lambda e: e.tensor_scalar(out=TB[4][:], in0=TB[4][:], scalar1=math.pi / 2, scalar2=None, op0=ALU.add),
               ["tb4"], ["tb4"])
            sin_reduced(TB[6][:], TB[4][:], TBi, TB[7][:], ["tb4"], "tb6", "tb7", "tb7")
            dv(lambda e: e.tensor_tensor(out=TB[6][:], in0=TB[6][:], in1=TB[3][:], op=ALU.mult), ["tb6", "tb3"], ["tb6"])
            dv(lambda e: e.tensor_scalar(out=TB[6][:], in0=TB[6][:], scalar1=-1.0, scalar2=None, op0=ALU.add), ["tb6"], ["tb6"])
            dv(lambda e: e.tensor_tensor(out=TB[5][:], in0=TB[5][:], in1=TB[3][:], op=ALU.mult), ["tb5", "tb3"], ["tb5"])
            dv(lambda e: e.tensor_tensor(out=TB[3][:], in0=are[:], in1=are[:], op=ALU.mult), ["tb0"], ["tb3"])
            dv(lambda e: e.tensor_tensor(out=TB[4][:], in0=aim[:], in1=aim[:], op=ALU.mult), ["tb1"], ["tb4"])
            dv(lambda e: e.tensor_tensor(out=TB[3][:], in0=TB[3][:], in1=TB[4][:], op=ALU.add), ["tb3", "tb4"], ["tb3"])
            dv(lambda e: e.reciprocal(out=TB[3][:], in_=TB[3][:]), ["tb3"], ["tb3"])
            dv(lambda e: e.tensor_tensor(out=TB[4][:], in0=TB[6][:], in1=are[:], op=ALU.mult), ["tb6", "tb0"], ["tb4"])
            dv(lambda e: e.tensor_tensor(out=TB[7][:], in0=TB[5][:], in1=aim[:], op=ALU.mult), ["tb5", "tb1"], ["tb7"])
            dv(lambda e: e.tensor_tensor(out=TB[4][:], in0=TB[4][:], in1=TB[7][:], op=ALU.add), ["tb4", "tb7"], ["tb4"])
            dv(lambda e: e.tensor_tensor(out=TB[4][:], in0=TB[4][:], in1=TB[3][:], op=ALU.mult), ["tb4", "tb3"], ["tb4"])
            dv(lambda e: e.tensor_tensor(out=TB[7][:], in0=TB[5][:], in1=are[:], op=ALU.mult), ["tb5", "tb0"], ["tb7"])
            dv(lambda e: e.tensor_tensor(out=TB[2][:], in0=TB[6][:], in1=aim[:], op=ALU.mult), ["tb6", "tb1"], ["tb2"])
            dv(lambda e: e.tensor_tensor(out=TB[7][:], in0=TB[7][:], in1=TB[2][:], op=ALU.subtract), ["tb7", "tb2"], ["tb7"])
            dv(lambda e: e.tensor_tensor(out=TB[7][:], in0=TB[7][:], in1=TB[3][:], op=ALU.mult), ["tb7", "tb3"], ["tb7"])
            crb, cib = TB[4], TB[7]
            bre, bim = TB[0], TB[1]
            S.dma("sp", lambda e: e.dma_start(out=bre[:], in_=bpad_d[0].rearrange("p j n -> p (j n)")), "tb0", w=["tb0"])
            S.dma("sp", lambda e: e.dma_start(out=bim[:], in_=bpad_d[1].rearrange("p j n -> p (j n)")), "tb1", w=["tb1"])
            dv(lambda e: e.tensor_tensor(out=TB[2][:], in0=crb[:], in1=bre[:], op=ALU.mult), ["tb4", "tb0"], ["tb2"])
            dv(lambda e: e.tensor_tensor(out=TB[3][:], in0=cib[:], in1=bim[:], op=ALU.mult), ["tb7", "tb1"], ["tb3"])
            dv(lambda e: e.tensor_tensor(out=Bpad[:, 0, :, :].rearrange("p j n -> p (j n)"), in0=TB[2][:], in1=TB[3][:],
                                         op=ALU.subtract), ["tb2", "tb3"], ["Bpad0"])
            dv(lambda e: e.tensor_tensor(out=TB[5][:], in0=crb[:], in1=bim[:], op=ALU.mult), ["tb4", "tb1"], ["tb5"])
            dv(lambda e: e.tensor_tensor(out=TB[6][:], in0=cib[:], in1=bre[:], op=ALU.mult), ["tb7", "tb0"], ["tb6"])
            dv(lambda e: e.tensor_tensor(out=Bpad[:, 1, :, :].rearrange("p j n -> p (j n)"), in0=TB[5][:], in1=TB[6][:],
                                         op=ALU.add), ["tb5", "tb6"], ["Bpad1"])
            S.dma("sp", lambda e: e.dma_start(out=TB[2][:], in_=cpad_d[0].rearrange("p j n -> p (j n)")), "tb2",
                  r=["tb2"], w=["tb2"])
            S.dma("sp", lambda e: e.dma_start(out=TB[3][:], in_=cpad_d[1].rearrange("p j n -> p (j n)")), "tb3",
                  r=["tb3"], w=["tb3"])
            dv(lambda e: e.tensor_copy(out=Cpad[:, 0, :, :].rearrange("p j n -> p (j n)"), in_=TB[2][:]), ["tb2"], ["Cpad0"])
            dv(lambda e: e.tensor_scalar(out=Cpad[:, 1, :, :].rearrange("p j n -> p (j n)"), in0=TB[3][:], scalar1=-1.0,
                                         scalar2=None, op0=ALU.mult), ["tb3"], ["Cpad1"])

            a_s = sb("a_s_sb", [128, 3, 24], F32, st0)
            th_s = sb("th_s", [128, 24], F32, st0)
            tcount = sb("tcount_sb", [128, 128], F32, st0)
            ld(a_s[:], a_s_d[:, :, :], "a_s")
            ld(tcount[:], tcount_d[:, :], "tcount")
            S.op("act", lambda e: e.activation(out=a_s[:, 2, :], in_=a_s[:, 2, :], func=AF.Exp), r=["a_s"], w=["a_s"])
            dv(lambda e: e.tensor_tensor(out=r_s[:], in0=a_s[:, 0, :], in1=a_s[:, 2, :], op=ALU.mult), ["a_s"], ["r_s"])
            S.op("act", lambda e: e.activation(out=r_s[:], in_=r_s[:], func=AF.Exp), r=["r_s"], w=["r_s"])
            dv(lambda e: e.tensor_tensor(out=th_s[:], in0=a_s[:, 1, :], in1=a_s[:, 2, :], op=ALU.mult), ["a_s"], ["th_s"])
            ang = TB[0]
            ang3 = ang[:].rearrange("p (j t) -> p j t", j=24)
            for j in range(24):
                dv(lambda e, j=j: e.tensor_scalar(out=ang3[:, j, :], in0=tcount[:], scalar1=th_s[:, j:j + 1], scalar2=None,
                                                  op0=ALU.mult), ["tcount", "th_s", "tb0"], ["tb0"])
            sin_reduced(Stab[:, :, :].rearrange("p j t -> p (j t)"), ang[:], TBi, TB[7][:], ["tb0"], "Stab", "tb7", "tb7")
            dv(lambda e: e.tensor_scalar(out=ang[:], in0=ang[:], scalar1=math.pi / 2, scalar2=None, op0=ALU.add),
               ["tb0"], ["tb0"])
            sin_reduced(Ctab[:, :, :].rearrange("p j t -> p (j t)"), ang[:], TBi, TB[7][:], ["tb0"], "Ctab", "tb7", "tb7")
            S.barrier()
        carry_s = sb("carry_s", [128, 4, NH], F32)
        s5so = sb("s5so", [128, 2, 24, 4], F32)
        Qblk = sb("Qblk", [128, 6, 16], BF16)
        KTnew = sb("KTnew", [128, 6, 32], BF16)
        vnbp = sb("vnbp", [32, DMAIN], BF16)
        Kpb = [sb(f"Kpb{i}", [128, DMAIN], BF16) for i in range(2)]
        FnegN = sb("FnegN", [8, 4, NH], F32)
        Ftb = sb("Ftb", [128, NH, 8], F32)
        h = sb("h", [128, 8, NT], F32)
        xn = sb("xn", [128, 8, NT], BF16)
        z = sb("z", [128, 8, NT], BF16)
        ymix = sb("ymix", [128, 8, NT], BF16)
        hid = sb("hid", [128, 32, NT], BF16)
        wbufs = [sb(f"wb{i}", [128, 8, 128], BF16) for i in range(6)]
        stat = sb("stat", [128, NT], F32)
        tmpf = [sb(f"tmpf{i}", [128, NT], F32) for i in range(3)]
        pT = [sb(f"pT{i}", [128, NT], BF16) for i in range(3)]
        lf = sb("lf", [128, 4, NH], F32)
        hflat = hid[:].rearrange("p a b -> p (a b)")
        hidf = hflat.bitcast(F32)

        def hchunks(a, b_, parts=128):
            return hid[0:parts, a:b_, :].rearrange("p a b -> p (a b)")

        xtm = hidf[:, 0:4096].rearrange("p (a b) -> p a b", a=4)
        yfm = hidf[:, 4096:8192].rearrange("p (k n) -> p k n", k=8)
        W6 = [hidf[:, i * 768:(i + 1) * 768].rearrange("p (j t) -> p j t", j=6) for i in range(4)]
        hre = hchunks(12, 18).rearrange("p (j t) -> p j t", j=24)
        him = hchunks(18, 24).rearrange("p (j t) -> p j t", j=24)
        QT = hid[0:65, 0:12, :]
        Vb = [hchunks(12, 16).rearrange("p (b d) -> p b d", b=16), hchunks(24, 28).rearrange("p (b d) -> p b d", b=16)]
        KTb = [hchunks(16, 20, 65), hchunks(20, 24, 65)]
        kvt = hidf[:, 0:1536]
        vaug = hchunks(6, 18).rearrange("p (s h d) -> p s h d", s=4, h=NH)
        ktt = hid[0:64, 18, :]
        kbf = [hid[:, 19, :], hid[:, 20, :]]
        kst = [hidf[:, 5376 + i * 512:5376 + (i + 1) * 512].rearrange("p (s f) -> p s f", s=4) for i in range(2)]
        Kpg = [hidf[:, 0:768], hidf[:, 768:1536]]
        Vpg = [hidf[:, 1536:2304], hidf[:, 2304:3072]]
        KTp = [hflat[:, 6144:6912].rearrange("p (c k) -> p c k", c=6), hflat[:, 6912:7680].rearrange("p (c k) -> p c k", c=6)]
        Vpb = [hflat[:, 7680:8448], hflat[:, 8448:9216]]
        lfpg = hidf[:, 4608:6144].rearrange("p (s h) -> p s h", h=NH)
        FnT = hidf[:, 6144:7680].rearrange("p (h n) -> p h n", h=NH)
        VnewS = hid[0:8, 30, :].rearrange("p (a b) -> p a b", a=1)[:, 0, :]
        VnewS = hflat[0:8, 15360:16128]
        HIDKEYS = [f"hid{m}" for m in range(32)]
        GROUPS = {
            "satt": ["Kpg0", "Kpg1", "Vpg0", "Vpg1", "KTp0", "KTp1", "Vpb0", "Vpb1", "lfpg", "FnT", "VnewS"],
            "xtm": ["xtm"], "sq": ["sq"], "yfm": ["yfm"], "hid": HIDKEYS,
            "s5": ["w6_0", "w6_1", "w6_2", "w6_3", "hre", "him"],
            "att": ["QT", "KTb0", "KTb1", "Vb0", "Vb1"],
            "kv": [f"kvt{n}" for n in range(12)] + [f"vaug{n}" for n in range(4)] + ["ktt", "kbf0", "kbf1", "kst0", "kst1"],
        }

        def enter(*groups):
            tgt = [k for g in groups for k in GROUPS[g]]
            best = {}
            for g, keys in GROUPS.items():
                if g in groups:
                    continue
                for k in keys:
                    lst = list(S.reads.get(k, []))
                    if S.lastw.get(k) is not None:
                        lst.append(S.lastw[k])
                    for sid, sem, val in lst:
                        if sid not in best or best[sid][2] < val:
                            best[sid] = (sid, sem, val)
            for k in tgt:
                S.reads.setdefault(k, []).extend(best.values())

        wrot = [0]

        def getw(u):
            i = wrot[0] % 6
            wrot[0] += 1
            S.dma("sp", lambda e, u=u, i=i: e.dma_start(out=wbufs[i][:].rearrange("p k n -> p (k n)"), in_=wscr[u]),
                  f"wb{i}", r=[f"wscr{u}"], w=[f"wb{i}"])
            return wbufs[i], f"wb{i}"

        def rmsnorm(N, gidx, out_fn, out_keys, hkeys):
            enter("sq")
            sq = hid[:, 0:8, 0:N]
            for k in range(8):
                S.op("act", lambda e, k=k: e.activation(out=sq[:, k, :], in_=h[:, k, 0:N], func=AF.Square),
                     r=[hkeys[k]], w=["sq"])
            p, pk = next_ps()
            for k in range(8):
                S.op("pe", lambda e, p=p, k=k: e.matmul(p[:, 0:N], lhsT=onesb[:], rhs=sq[:, k, :],
                                                         start=(k == 0), stop=(k == 7)), r=["sq", "onesb"], w=[pk])
            S.op("act", lambda e, p=p: e.activation(out=stat[:, 0:N], in_=p[:, 0:N], func=AF.Sqrt, bias=1e-6,
                                                    scale=1.0 / D), r=[pk], w=["stat"])
            S.op("dve", lambda e: e.reciprocal(out=stat[:, 0:N], in_=stat[:, 0:N]), r=["stat"], w=["stat"])
            for k in range(8):
                S.op("dve", lambda e, k=k: e.scalar_tensor_tensor(
                    out=out_fn(k), in0=h[:, k, 0:N], scalar=gall[:, gidx, k:k + 1], in1=stat[:, 0:N],
                    op0=ALU.mult, op1=ALU.mult), r=[hkeys[k], "stat", "gall"], w=[out_keys[k]])

        dbgbuf = sb("dbgbuf", [128, NT], F32) if DEBUG else None
        dbg_names = []

        def dump(name, ap, keys, n=NT):
            if not DEBUG or len(dbg_names) >= 16:
                return
            i = len(dbg_names)
            dbg_names.append(name)
            S.op("dve", lambda e: e.tensor_copy(out=dbgbuf[:, 0:n], in_=ap), r=keys, w=["dbgbuf"])
            S.dma("sp", lambda e: e.dma_start(out=dbg_out[i, :, 0:n], in_=dbgbuf[:, 0:n]), "dbg", r=["dbgbuf"])

        DBG_NAMES.clear()
        DBG_NAMES.append(dbg_names)
        HK = [f"h{k}" for k in range(8)]
        XK = [f"xn{k}" for k in range(8)]
        ZK = [f"z{k}" for k in range(8)]
        YK = [f"ym{k}" for k in range(8)]

        def proj_fm(ubase, Kc, m_list, src_fn, src_keys, N, evac):
            nkq = (Kc + 7) // 8
            for m in m_list:
                p, pk = next_ps()
                for kq in range(nkq):
                    wb, wk = getw(ubase + m * nkq + kq)
                    kn = min(8, Kc - kq * 8)
                    for kk in range(kn):
                        k = kq * 8 + kk
                        S.op("pe", lambda e, p=p, wb=wb, kk=kk, k=k: e.matmul(
                            p[:, 0:N], lhsT=wb[:, kk, :], rhs=src_fn(k), start=(k == 0), stop=(k == Kc - 1)),
                            r=[wk, src_keys[k]], w=[pk])
                evac(m, p, pk)

        def mem_attention(N, q0, km_fn, vm_fn, kkeys):
            for hc in range(2):
                pn, pnk = PSW[0][:, 0:512], "psw0"
                pd, pdk = PSW[1][:, 0:512], "psw1"
                first = True
                cnt = 0
                for hh in range(2):
                    hd = hc * 2 + hh
                    hb = hh * 64
                    for mt in range(2):
                        p, pk = next_ps()
                        S.op("pe", lambda e, p=p, hb=hb, mt=mt, hc=hc: e.matmul(
                            p[:, 0:N], lhsT=km_fn(hb, hc, mt),
                            rhs=z[hb:hb + 64, 6 + hc, q0:q0 + N], start=True, stop=True), r=kkeys + [ZK[6 + hc]], w=[pk])
                        pt = pT[cnt % 3]
                        ptk = f"pT{cnt % 3}"
                        S.op("act", lambda e, p=p, pt=pt: e.activation(out=pt[:, 0:N], in_=p[:, 0:N], func=AF.Exp,
                                                                       scale=0.125), r=[pk], w=[ptk])
                        last = (hh == 1 and mt == 1)
                        S.op("pe", lambda e, pn=pn, pt=pt, mt=mt, hd=hd, first=first, last=last: e.matmul(
                            pn[:, 0:N], lhsT=vm_fn(mt, hd), rhs=pt[:, 0:N], start=first, stop=last),
                            r=kkeys + [ptk], w=[pnk])
                        S.op("pe", lambda e, pd=pd, pt=pt, hh=hh, first=first, last=last: e.matmul(
                            pd[:, 0:N], lhsT=onespad[:, hh, :], rhs=pt[:, 0:N], start=first, stop=last),
                            r=["onespad", ptk], w=[pdk])
                        first = False
                        cnt += 1
                S.op("dve", lambda e, pd=pd: e.reciprocal(out=tmpf[0][:, 0:N], in_=pd[:, 0:N]), r=[pdk], w=["tmpf0"])
                S.op("dve", lambda e, pn=pn, hc=hc: e.tensor_tensor(out=ymix[:, 6 + hc, q0:q0 + N], in0=pn[:, 0:N],
                                                                    in1=tmpf[0][:, 0:N], op=ALU.mult),
                     r=[pnk, "tmpf0"], w=[YK[6 + hc]])

        s5_dumped = []

        def s5_layer(N, L, nseg, init_fn, fin_fn, skeys):
            enter("s5")
            CH = nseg * L
            zgk = [f"zg{c}" for c in range(6)]
            for ch in range(N // CH):
                c0 = ch * CH
                for grp in range(4):
                    j0 = grp * 6
                    bre, bim = PSW[0], PSW[1]
                    for jj in range(6):
                        j = j0 + jj
                        S.op("pe", lambda e, j=j, jj=jj, c0=c0: e.matmul(
                            bre[:, jj * CH:(jj + 1) * CH], lhsT=Bpad[:, 0, j, :], rhs=z[:, j // 4, c0:c0 + CH],
                            start=True, stop=True), r=["Bpad0", ZK[j // 4]], w=["psw0"])
                        S.op("pe", lambda e, j=j, jj=jj, c0=c0: e.matmul(
                            bim[:, jj * CH:(jj + 1) * CH], lhsT=Bpad[:, 1, j, :], rhs=z[:, j // 4, c0:c0 + CH],
                            start=True, stop=True), r=["Bpad1", ZK[j // 4]], w=["psw1"])
                    br3 = bre[:, 0:6 * CH].rearrange("p (j t) -> p j t", j=6)
                    bi3 = bim[:, 0:6 * CH].rearrange("p (j t) -> p j t", j=6)
                    w0, w1, w2, w3 = [w[:, :, 0:CH] for w in W6]
                    Cv = Ctab[:, j0:j0 + 6, 0:L]
                    Sv = Stab[:, j0:j0 + 6, 0:L]
                    SG = [(sg * L, (sg + 1) * L) for sg in range(nseg)]
                    for (s0, s1) in SG:
                        S.op("dve", lambda e, Cv=Cv, s0=s0, s1=s1: e.tensor_tensor(out=w0[:, :, s0:s1], in0=br3[:, :, s0:s1], in1=Cv, op=ALU.mult), r=["psw0", "Ctab"], w=["w6_0"])
                        S.op("dve", lambda e, Sv=Sv, s0=s0, s1=s1: e.tensor_tensor(out=w1[:, :, s0:s1], in0=bi3[:, :, s0:s1], in1=Sv, op=ALU.mult), r=["psw1", "Stab"], w=["w6_1"])
                        S.op("dve", lambda e, Cv=Cv, s0=s0, s1=s1: e.tensor_tensor(out=w2[:, :, s0:s1], in0=bi3[:, :, s0:s1], in1=Cv, op=ALU.mult), r=["psw1", "Ctab"], w=["w6_2"])
                        S.op("dve", lambda e, Sv=Sv, s0=s0, s1=s1: e.tensor_tensor(out=w3[:, :, s0:s1], in0=br3[:, :, s0:s1], in1=Sv, op=ALU.mult), r=["psw0", "Stab"], w=["w6_3"])
                    S.op("pool", lambda e: e.tensor_tensor(out=w0, in0=w0, in1=w1, op=ALU.add), r=["w6_0", "w6_1"], w=["w6_0"])
                    S.op("pool", lambda e: e.tensor_tensor(out=w2, in0=w2, in1=w3, op=ALU.subtract), r=["w6_2", "w6_3"], w=["w6_2"])
                    if False:
                        dump("bu_re", bre[:, 0:128], ["psw0"], 128)
                        dump("ctab", Ctab[:, 0, :], ["Ctab"], 128)
                        dump("stab", Stab[:, 0, :], ["Stab"], 128)
                        dump("gr", w0[:, 0, :], ["w6_0"], 128)
                        dump("bu_im", bim[:, 0:128], ["psw1"], 128)
                        dump("biS", w1[:, 0, :], ["w6_1"], 128)
                        dump("gi", w2[:, 0, :], ["w6_2"], 128)
                        dump("r_s", r_s[:, :], ["r_s"], 24)
                    for jj in range(6):
                        j = j0 + jj
                        for sg, (s0, s1) in enumerate(SG):
                            S.op("dve", lambda e, j=j, jj=jj, s0=s0, s1=s1, sg=sg: e.tensor_tensor_scan(
                                out=w0[:, jj, s0:s1], data0=r_s[:, j:j + 1].broadcast_to([128, L]), data1=w0[:, jj, s0:s1],
                                initial=init_fn(0, j, sg), op0=ALU.mult, op1=ALU.add), r=["w6_0", "r_s"] + skeys, w=["w6_0"])
                            S.op("dve", lambda e, j=j, jj=jj, s0=s0, s1=s1, sg=sg: e.tensor_tensor_scan(
                                out=w2[:, jj, s0:s1], data0=r_s[:, j:j + 1].broadcast_to([128, L]), data1=w2[:, jj, s0:s1],
                                initial=init_fn(1, j, sg), op0=ALU.mult, op1=ALU.add), r=["w6_2", "r_s"] + skeys, w=["w6_2"])
                    if False:
                        dump("sr", w0[:, 0, :], ["w6_0"], 128)
                    hre_o, him_o = hre[:, j0:j0 + 6, 0:CH], him[:, j0:j0 + 6, 0:CH]
                    for (s0, s1) in SG:
                        S.op("dve", lambda e, Cv=Cv, s0=s0, s1=s1: e.tensor_tensor(out=w1[:, :, s0:s1], in0=w0[:, :, s0:s1], in1=Cv, op=ALU.mult), r=["w6_0", "Ctab"], w=["w6_1"])
                        S.op("pool", lambda e, Sv=Sv, s0=s0, s1=s1: e.tensor_tensor(out=w3[:, :, s0:s1], in0=w2[:, :, s0:s1], in1=Sv, op=ALU.mult), r=["w6_2", "Stab"], w=["w6_3"])
                    S.op("pool", lambda e, o=hre_o: e.tensor_tensor(out=o, in0=w1, in1=w3, op=ALU.subtract),
                         r=["w6_1", "w6_3"], w=["hre"])
                    for sg, (s0, s1) in enumerate(SG):
                        S.op("dve", lambda e, o=fin_fn(0, j0, sg), s1=s1: e.tensor_tensor(out=o, in0=w1[:, :, s1 - 1], in1=w3[:, :, s1 - 1], op=ALU.subtract),
                             r=["w6_1", "w6_3"], w=skeys)
                    for (s0, s1) in SG:
                        S.op("dve", lambda e, Sv=Sv, s0=s0, s1=s1: e.tensor_tensor(out=w1[:, :, s0:s1], in0=w0[:, :, s0:s1], in1=Sv, op=ALU.mult), r=["w6_0", "Stab"], w=["w6_1"])
                        S.op("pool", lambda e, Cv=Cv, s0=s0, s1=s1: e.tensor_tensor(out=w3[:, :, s0:s1], in0=w2[:, :, s0:s1], in1=Cv, op=ALU.mult), r=["w6_2", "Ctab"], w=["w6_3"])
                    S.op("pool", lambda e, o=him_o: e.tensor_tensor(out=o, in0=w1, in1=w3, op=ALU.add),
                         r=["w6_1", "w6_3"], w=["him"])
                    for sg, (s0, s1) in enumerate(SG):
                        S.op("dve", lambda e, o=fin_fn(1, j0, sg), s1=s1: e.tensor_tensor(out=o, in0=w1[:, :, s1 - 1], in1=w3[:, :, s1 - 1], op=ALU.add),
                             r=["w6_1", "w6_3"], w=skeys)
                if False:
                    dump("hre", hre[:, 0, :], ["hre"], 128)
                    s5_dumped.append(1)
                for c in range(6):
                    p, pk = next_ps()
                    n = 0
                    for j in range(4 * c, 4 * c + 4):
                        S.op("pe", lambda e, p=p, j=j, n=n: e.matmul(p[:, 0:CH], lhsT=Cpad[:, 0, j, :], rhs=hre[:, j, 0:CH],
                                                                    start=(n == 0), stop=False), r=["Cpad0", "hre"], w=[pk])
                        S.op("pe", lambda e, p=p, j=j, n=n: e.matmul(p[:, 0:CH], lhsT=Cpad[:, 1, j, :], rhs=him[:, j, 0:CH],
                                                                    start=False, stop=(n == 3)), r=["Cpad1", "him"], w=[pk])
                        n += 1
                    t0, t1 = tmpf[0][:, 0:CH], tmpf[1][:, 0:CH]
                    S.op("dve", lambda e, p=p, c=c, c0=c0: e.scalar_tensor_tensor(
                        out=t0, in0=z[:, c, c0:c0 + CH], scalar=s5d[:, c:c + 1], in1=p[:, 0:CH], op0=ALU.mult, op1=ALU.add),
                        r=[pk, ZK[c], "s5d"], w=["tmpf0"])
                    S.op("pool", lambda e: e.tensor_tensor(out=t1, in0=t0, in1=t0, op=ALU.mult), r=["tmpf0"], w=["tmpf1"])
                    S.op("pool", lambda e: e.tensor_scalar(out=t1, in0=t1, scalar1=0.044715, scalar2=1.0, op0=ALU.mult,
                                                           op1=ALU.add), r=["tmpf1"], w=["tmpf1"])
                    S.op("pool", lambda e: e.tensor_tensor(out=t1, in0=t1, in1=t0, op=ALU.mult), r=["tmpf1", "tmpf0"], w=["tmpf1"])
                    S.op("act", lambda e: e.activation(out=t1, in_=t1, func=AF.Sigmoid, scale=2.0 * math.sqrt(2.0 / math.pi)),
                         r=["tmpf1"], w=["tmpf1"])
                    S.op("dve", lambda e, c=c, c0=c0: e.tensor_tensor(out=xn[:, c, c0:c0 + CH], in0=t0, in1=t1, op=ALU.mult),
                         r=["tmpf0", "tmpf1"], w=[XK[c]])
            def ev(m, p, pk):
                S.op("act", lambda e: e.activation(out=tmpf[2][:, 0:N], in_=p[:, 0:N], func=AF.Sigmoid, bias=bglu[:, m:m + 1],
                                                   scale=1.0), r=[pk, "bglu"], w=["tmpf2"])
                S.op("dve", lambda e: e.tensor_tensor(out=ymix[:, m, 0:N], in0=xn[:, m, 0:N], in1=tmpf[2][:, 0:N], op=ALU.mult),
                     r=["tmpf2", XK[m]], w=[YK[m]])
            proj_fm(U_GLU, 6, range(6), lambda k: xn[:, k, 0:N], XK, N, ev)

        def fox_layer(i1):
            N = NT
            enter("att")
            S.dma("pool", lambda e: e.indirect_dma_start(
                out=fq_sb[:].rearrange("p s h -> p (s h)"), out_offset=None, in_=Fq,
                in_offset=bass.IndirectOffsetOnAxis(ap=hidx[:, i1:i1 + 1], axis=0)), "fq_g",
                r=["hidx"] + [f"Fq{tt}" for tt in range(NTILES)], w=["fq_sb"])
            for sbk in range(4):
                S.op("dve", lambda e, sbk=sbk: e.tensor_copy(out=ZF[:, sbk, :, 64], in_=fq_sb[:, sbk, :]), r=["fq_sb"],
                     w=[f"ZF{sbk}"])
            for jj in range(4):
                S.op("dve", lambda e, jj=jj: e.tensor_scalar(
                    out=FnB[:, jj * 4:(jj + 1) * 4, :], in0=Fneg[:, 16 * i1 + jj * 4:16 * i1 + (jj + 1) * 4, :],
                    scalar1=ohbb[:, 1, jj:jj + 1], scalar2=None, op0=ALU.add), r=["Fneg", "ohbb"], w=["FnB"])
            for i in range(2):
                S.op("pool", lambda e, i=i: e.memset(KTb[i][64:65, :], 1.0), w=[f"KTb{i}"])
            for hd in range(NH):
                c, hb = hd // 2, (hd % 2) * 64
                if hb == 0:
                    S.op("act", lambda e, hd=hd, c=c: e.activation(out=QT[0:64, hd, :], in_=z[0:64, c, :], func=AF.Copy,
                                                                   scale=0.125), r=[ZK[c]], w=["QT"])
                else:
                    S.dma("sp", lambda e, hd=hd, c=c: e.dma_start(out=QT[0:64, hd, :], in_=z[64:128, c, :]), "QTmv",
                          r=[ZK[c]], w=["QT"])
                    S.op("act", lambda e, hd=hd: e.activation(out=QT[0:64, hd, :], in_=QT[0:64, hd, :], func=AF.Copy,
                                                              scale=0.125), r=["QT"], w=["QT"])
            for hd in range(NH):
                p, pk = next_ps()
                for sbk in range(4):
                    S.op("pe", lambda e, p=p, sbk=sbk, hd=hd: e.matmul(
                        p[0:65, sbk * 128:(sbk + 1) * 128], lhsT=ZF[:, sbk, hd, :], rhs=identb[:], start=True, stop=True),
                        r=[f"ZF{sbk}", "identb"], w=[pk])
                S.op("dve", lambda e, p=p, hd=hd: e.tensor_copy(out=QT[64:65, hd, :], in_=p[64:65, :]), r=[pk], w=["QT"])
            nkb = 16 * i1 + 16
            for hd in range(NH):
                po, pok = PSW[0][:, 0:512], "psw0"
                npieces = (nkb + 15) // 16
                cnt = 0
                hh = hd % 2
                hb = hh * 64
                pending = None

                def emit_pv(pt, ptk, bi, i, kb, po=po, pok=pok):
                    S.op("pe", lambda e: e.matmul(
                        po[:, 0:N], lhsT=Vb[bi][:, i, :], rhs=pt[:, 0:N], start=(kb == 0), stop=(kb == nkb - 1)),
                        r=[ptk, f"Vb{bi}"], w=[pok])
                for pc in range(npieces):
                    kb0 = pc * 16
                    nb = min(16, nkb - kb0)
                    bi = (hd * npieces + pc) % 2
                    tiles_needed = sorted(set((kb0 + i) // 4 for i in range(nb)))
                    S.dma("sp", lambda e, hd=hd, kb0=kb0, nb=nb, bi=bi: e.dma_start(
                        out=KTb[bi][0:64, 0:nb * 128], in_=KTs[hd, :, kb0 * 128:(kb0 + nb) * 128]), f"KTb{bi}",
                        r=[f"KTs{tt}" for tt in tiles_needed], w=[f"KTb{bi}"])
                    S.dma("act", lambda e, hd=hd, kb0=kb0, nb=nb, bi=bi: e.dma_start(
                        out=Vb[bi][:, 0:nb, :], in_=Vs[hd, :, kb0:kb0 + nb, :]), f"Vb{bi}",
                        r=[f"Vs{tt}" for tt in tiles_needed], w=[f"Vb{bi}"])
                    for i in range(nb):
                        kb = kb0 + i
                        zone = kb - 16 * i1
                        q0 = 0
                        p, pk = next_ps()
                        S.op("pe", lambda e, p=p, bi=bi, i=i, hd=hd: e.matmul(
                            p[:, 0:N], lhsT=KTb[bi][:, i * 128:(i + 1) * 128], rhs=QT[:, hd, 0:N], start=True, stop=True),
                            r=[f"KTb{bi}", "QT"], w=[pk])
                        pt = pT[cnt % 3]
                        ptk = f"pT{cnt % 3}"
                        cnt += 1
                        if zone >= 0:
                            jj, d = zone // 4, zone % 4
                            S.op("dve", lambda e, p=p, d=d, jj=jj: e.scalar_tensor_tensor(
                                out=tmpf[2][:, 0:N], in0=masks[:, d, 0:N], scalar=ohbb[:, 0, jj:jj + 1], in1=p[:, 0:N],
                                op0=ALU.mult, op1=ALU.add), r=[pk, "masks", "ohbb"], w=["tmpf2"])
                            S.op("act", lambda e, pt=pt, zone=zone, hd=hd: e.activation(
                                out=pt[:, 0:N], in_=tmpf[2][:, 0:N], func=AF.Exp, bias=FnB[:, zone, hd:hd + 1], scale=1.0),
                                r=["tmpf2", "FnB"], w=[ptk])
                        else:
                            S.op("act", lambda e, p=p, pt=pt, kb=kb, hd=hd: e.activation(
                                out=pt[:, 0:N], in_=p[:, 0:N], func=AF.Exp, bias=Fneg[:, kb, hd:hd + 1], scale=1.0),
                                r=[pk, "Fneg"], w=[ptk])
                        if pending is not None:
                            emit_pv(*pending)
                        pending = (pt, ptk, bi, i, kb)
                emit_pv(*pending)
                osb = tmpf[0]
                S.op("dve", lambda e, po=po: e.tensor_copy(out=osb[:, 0:N], in_=po[:, 0:N]), r=[pok], w=["tmpf0"])
                pd, pdk = PSW[1][:, 0:512], "psw1"
                S.op("pe", lambda e, pd=pd, hh=hh: e.matmul(pd[:, 0:N], lhsT=selden[:, hh, :], rhs=osb[:, 0:N], start=True, stop=True),
                     r=["selden", "tmpf0"], w=[pdk])
                S.op("dve", lambda e, pd=pd, hb=hb: e.reciprocal(out=tmpf[1][hb:hb + 64, 0:N], in_=pd[hb:hb + 64, 0:N]),
                     r=[pdk], w=["tmpf1"])
                S.op("pool", lambda e, hb=hb, hd=hd: e.tensor_tensor(
                    out=ymix[hb:hb + 64, hd // 2, 0:N], in0=osb[hb:hb + 64, 0:N], in1=tmpf[1][hb:hb + 64, 0:N], op=ALU.mult),
                    r=["tmpf0", "tmpf1"], w=[YK[hd // 2]])

        def kv_stage(t, N):
            rmsnorm(N, 4, lambda k: xn[:, k, 0:N], XK, HK)
            enter("kv")
            S.op("pool", lambda e: e.memset(vaug[:, :, :, :], 0.0), w=[f"vaug{n}" for n in range(4)])
            va7 = vaug[:, :, :, :].rearrange("p s (c two) d -> p s c two d", two=2)
            S.op("pool", lambda e: e.memset(va7[:, :, :, 0, 64:65], 1.0), w=[f"vaug{n}" for n in range(4)])
            S.op("pool", lambda e: e.memset(va7[:, :, :, 1, 0:1], 1.0), w=[f"vaug{n}" for n in range(4)])
            nxt = getw(U_KV + 0)
            for m in range(12):
                p, pk = next_ps()
                wb, wk = nxt
                for k in range(8):
                    S.op("pe", lambda e, p=p, wb=wb, k=k: e.matmul(
                        p[:, 0:N], lhsT=wb[:, k, :], rhs=xn[:, k, 0:N], start=(k == 0), stop=(k == 7)),
                        r=[wk, XK[k]], w=[pk])
                if m + 1 < 12:
                    nxt = getw(U_KV + m + 1)
                fi = m % 2
                fm = tmpf[fi]
                S.op("dve", lambda e, p=p, fm=fm: e.tensor_copy(out=fm[:, 0:N], in_=p[:, 0:N]), r=[pk], w=[f"tmpf{fi}"])
                if m < 6:
                    kb_ = kbf[fi]
                    S.op("act", lambda e, fm=fm, kb_=kb_: e.activation(out=kb_[:, 0:N], in_=fm[:, 0:N], func=AF.Copy),
                         r=[f"tmpf{fi}"], w=[f"kbf{fi}"])
                    for hh in range(2):
                        S.dma("sp", lambda e, m=m, hh=hh, kb_=kb_: e.dma_start(
                            out=KTs[2 * m + hh, :, t * NT:t * NT + N], in_=kb_[hh * 64:(hh + 1) * 64, 0:N]), "ktt_out",
                            r=[f"kbf{fi}"], w=[f"KTs{t}"])
                p2, pk2 = next_ps()
                for sbk in range(4):
                    S.op("pe", lambda e, p2=p2, fm=fm, sbk=sbk: e.transpose(
                        out=p2[:, sbk * 128:(sbk + 1) * 128], in_=fm[:, sbk * 128:(sbk + 1) * 128], identity=identf[:]),
                        r=[f"tmpf{fi}", "identf"], w=[pk2])
                st = kst[fi]
                copy_op("act" if m % 2 else "dve", st[:, :, :].rearrange("p s f -> p (s f)"), p2[:, :], [pk2], [f"kst{fi}"])
                dst = o_k if m < 6 else o_v
                mm = m % 6
                S.dma("sp", lambda e, dst=dst, mm=mm, st=st: e.dma_start(
                    out=dst[t * NT:t * NT + N, mm * 128:(mm + 1) * 128].rearrange("(s p) f -> p s f", p=128), in_=st[:, :, :]),
                    "okv_out2", r=[f"kst{fi}"])
                if m >= 6:
                    c = m - 6
                    S.op("pool", lambda e, st=st, c=c: e.tensor_copy(out=va7[:, :, c, 0, 0:64], in_=st[:, :, 0:64]),
                         r=[f"kst{fi}"], w=[f"vaug{n}" for n in range(4)])
                    S.op("pool", lambda e, st=st, c=c: e.tensor_copy(out=va7[:, :, c, 1, 64:128], in_=st[:, :, 64:128]),
                         r=[f"kst{fi}"], w=[f"vaug{n}" for n in range(4)])
            for sbk in range(N // 128):
                tok0 = t * NT + sbk * 128
                blk = tok0 // 128
                S.dma("sp", lambda e, sbk=sbk, blk=blk: e.dma_start(
                    out=Vs[:, :, blk, :].rearrange("h p d -> p h d"), in_=vaug[:, sbk, :, :]), "vs_out",
                    r=[f"vaug{sbk}"], w=[f"Vs{t}"])
                p, pk = next_ps()
                for k in range(8):
                    S.op("pe", lambda e, p=p, k=k, sbk=sbk: e.matmul(
                        p[:, 0:NH], lhsT=xn[:, k, sbk * 128:(sbk + 1) * 128], rhs=wfb[:, k, :], start=(k == 0), stop=(k == 7)),
                        r=["wfb", XK[k]], w=[pk])
                lfs = lf[:, sbk, :]
                lk = f"lf{sbk}"
                S.op("dve", lambda e, p=p, lfs=lfs: e.tensor_tensor(out=lfs, in0=p[:, 0:NH], in1=bfb[:], op=ALU.add),
                     r=[pk, "bfb"], w=[lk])
                S.op("act", lambda e, lfs=lfs: e.activation(out=lfs, in_=lfs, func=AF.Exp, scale=-1.0), r=[lk], w=[lk])
                S.op("act", lambda e, lfs=lfs: e.activation(out=lfs, in_=lfs, func=AF.Ln, bias=1.0, scale=1.0), r=[lk], w=[lk])
                S.op("dve", lambda e, lfs=lfs: e.tensor_scalar(out=lfs, in0=lfs, scalar1=-1.0, scalar2=None, op0=ALU.mult),
                     r=[lk], w=[lk])
                S.dma("sp", lambda e, tok0=tok0, lfs=lfs: e.dma_start(out=o_logf[tok0:tok0 + 128, :], in_=lfs), "olf_out", r=[lk])
                p, pk = next_ps()
                S.op("pe", lambda e, p=p, lfs=lfs: e.matmul(p[:, 0:NH], lhsT=trif[:], rhs=lfs, start=True, stop=True),
                     r=["trif", lk], w=[pk])
                p2, pk2 = next_ps()
                S.op("pe", lambda e, p2=p2, lfs=lfs: e.matmul(p2[:, 0:NH], lhsT=onesf[:], rhs=lfs, start=True, stop=True),
                     r=["onesf", lk], w=[pk2])
                S.op("dve", lambda e, p=p: e.tensor_tensor(out=tmpf[0][:, 0:NH], in0=p[:, 0:NH], in1=carry[:], op=ALU.add),
                     r=[pk, "carry"], w=["tmpf0"])
                S.op("dve", lambda e, blk=blk: e.tensor_scalar(out=Fneg[:, blk, :], in0=tmpf[0][:, 0:NH], scalar1=-1.0,
                                                                scalar2=None, op0=ALU.mult), r=["tmpf0"], w=["Fneg"])
                S.op("dve", lambda e, sbk=sbk: e.tensor_copy(out=fq_sb[:, sbk, :], in_=tmpf[0][:, 0:NH]), r=["tmpf0"],
                     w=["fq_sb"])
                S.op("dve", lambda e, p2=p2: e.tensor_tensor(out=carry[:], in0=carry[:], in1=p2[:, 0:NH], op=ALU.add),
                     r=[pk2, "carry"], w=["carry"])

        def dense_tail(layer, N):
            def ev_out(m, p, pk):
                S.op("dve", lambda e: e.tensor_tensor(out=h[:, m, 0:N], in0=p[:, 0:N], in1=h[:, m, 0:N], op=ALU.add),
                     r=[pk, HK[m]], w=[HK[m]])
            proj_fm(U_OUT[layer], 8, range(8), lambda k: ymix[:, k, 0:N], YK, N, ev_out)
            rmsnorm(N, 2 + layer, lambda k: xn[:, k, 0:N], XK, HK)
            enter("hid")

            def ev_up(m, p, pk):
                i = m % 2
                S.op("act", lambda e: e.activation(out=tmpf[i][:, 0:N], in_=p[:, 0:N], func=AF.Relu), r=[pk], w=[f"tmpf{i}"])
                S.op("pool", lambda e: e.tensor_tensor(out=hid[:, m, 0:N], in0=tmpf[i][:, 0:N], in1=tmpf[i][:, 0:N], op=ALU.mult),
                     r=[f"tmpf{i}"], w=[HIDKEYS[m]])
            proj_fm(U_UP[layer], 8, range(32), lambda k: xn[:, k, 0:N], XK, N, ev_up)
            proj_fm(U_DOWN[layer], 32, range(8), lambda k: hid[:, k, 0:N], HIDKEYS, N, ev_out)

        def in_proj(layer, N):
            rmsnorm(N, layer, lambda k: xn[:, k, 0:N], XK, HK)

            def ev_z(m, p, pk):
                copy_op(evac_eng(), z[:, m, 0:N], p[:, 0:N], [pk], [ZK[m]])
            proj_fm(U_IN[layer], 8, range(8), lambda k: xn[:, k, 0:N], XK, N, ev_z)
            mem_attention(N, 0, lambda hb, hc, mt: KmT[hb:hb + 64, layer, hc, mt * 128:(mt + 1) * 128],
                          lambda mt, hd: Vmp[:, layer, mt, hd, :], ["KmT", "Vmp"])

        for t in range(NTILES):
            enter("xtm")
            for sbk in range(4):
                S.dma("sp", lambda e, sbk=sbk, t=t: e.dma_start(out=xtm[:, sbk, :], in_=x_in[t * NT + sbk * 128:t * NT + (sbk + 1) * 128, :]),
                      "xtm", w=["xtm"])
            for c in range(8):
                p, pk = next_ps()
                for sbk in range(4):
                    S.op("pe", lambda e, p=p, sbk=sbk, c=c: e.transpose(
                        out=p[:, sbk * 128:(sbk + 1) * 128], in_=xtm[:, sbk, c * 128:(c + 1) * 128], identity=identf[:]),
                        r=["xtm", "identf"], w=[pk])
                copy_op(evac_eng(), h[:, c, :], p[:, :], [pk], [HK[c]])
            in_proj(0, NT)
            s5_layer(NT, 128, 1, lambda ri, j, sg: hprev[:, ri, j:j + 1], lambda ri, j0, sg: hprev[:, ri, j0:j0 + 6], ["hprev"])
            dense_tail(0, NT)
            S.dma("sp", lambda e, t=t: e.dma_start(out=H1[t * 128:(t + 1) * 128, :], in_=h[:].rearrange("p k n -> p (k n)")),
                  "h1_out", r=HK, w=[f"H1_{t}"])
            kv_stage(t, NT)
            S.dma("sp", lambda e, t=t: e.dma_start(out=Fq[t * 128:(t + 1) * 128, :], in_=fq_sb[:].rearrange("p s h -> p (s h)")),
                  "fq_out", r=["fq_sb"], w=[f"Fq{t}"])

        for i1 in range(NL1):
            S.dma("pool", lambda e, i1=i1: e.indirect_dma_start(
                out=h[:].rearrange("p k n -> p (k n)"), out_offset=None, in_=H1,
                in_offset=bass.IndirectOffsetOnAxis(ap=hidx[:, i1:i1 + 1], axis=0)), "h1_g",
                r=["hidx"] + [f"H1_{tt}" for tt in range(NTILES)], w=HK)
            in_proj(1, NT)
            fox_layer(i1)
            dense_tail(1, NT)
            enter("yfm")
            rmsnorm(NT, 5, lambda k: yfm[:, k, 0:NT], ["yfm"] * 8, HK)
            enter("xtm")
            for sbk in range(4):
                for c2 in range(2):
                    p, pk = next_ps()
                    for cc in range(4):
                        c = c2 * 4 + cc
                        S.op("pe", lambda e, p=p, sbk=sbk, c=c, cc=cc: e.transpose(
                            out=p[:, cc * 128:(cc + 1) * 128], in_=yfm[:, c, sbk * 128:(sbk + 1) * 128], identity=identf[:]),
                            r=["yfm", "identf"], w=[pk])
                    copy_op(evac_eng(), xtm[:, sbk, c2 * 512:(c2 + 1) * 512], p[:, :], [pk], ["xtm"])
                S.dma("sp", lambda e, sbk=sbk, i1=i1: e.dma_start(
                    out=y_out[i1 * NT + sbk * 128:i1 * NT + (sbk + 1) * 128, :], in_=xtm[:, sbk, :]), "y_out", r=["xtm"])

        NS = 32
        enter("xtm")
        S.dma("sp", lambda e: e.dma_start(out=xtm[0:NS, 0, :], in_=xs_in[:, :]), "xtm", w=["xtm"])
        for c2 in range(2):
            p, pk = next_ps()
            for cc in range(4):
                c = c2 * 4 + cc
                S.op("pe", lambda e, p=p, c=c, cc=cc: e.transpose(
                    out=p[:, cc * NS:(cc + 1) * NS], in_=xtm[0:NS, 0, c * 128:(c + 1) * 128], identity=identf[0:NS, 0:NS]),
                    r=["xtm", "identf"], w=[pk])
            copy_op(evac_eng(), h[:, c2 * 4:(c2 + 1) * 4, 0:NS], p[:, 0:4 * NS].rearrange("p (c n) -> p c n", c=4), [pk],
                    HK[c2 * 4:(c2 + 1) * 4])

        def sample_kv():
            N = NS
            rmsnorm(N, 4, lambda k: xn[:, k, 0:N], XK, HK)
            enter("kv")
            for nch in range(12):
                p, pk = next_ps()
                wb, wk = getw(U_KV + nch)
                for k in range(8):
                    S.op("pe", lambda e, p=p, wb=wb, k=k: e.matmul(
                        p[0:N, 0:128], lhsT=xn[:, k, 0:N], rhs=wb[:, k, :], start=(k == 0), stop=(k == 7)),
                        r=[wk, XK[k]], w=[pk])
                copy_op(evac_eng(), kvt[0:N, nch * 128:(nch + 1) * 128], p[0:N, 0:128], [pk], [f"kvt{nch}"])
                if nch < 6:
                    p2, pk2 = next_ps()
                    for k in range(8):
                        S.op("pe", lambda e, p2=p2, wb=wb, k=k: e.matmul(
                            p2[:, 0:N], lhsT=wb[:, k, :], rhs=xn[:, k, 0:N], start=(k == 0), stop=(k == 7)),
                            r=[wk, XK[k]], w=[pk2])
                    copy_op(evac_eng(), KTnew[:, nch, :], p2[:, 0:N], [pk2], ["KTnew"])
            kkeys = [f"kvt{n}" for n in range(6)]
            vkeys = [f"kvt{n}" for n in range(6, 12)]
            S.dma("sp", lambda e: e.dma_start(out=o_sk[:, :], in_=kvt[0:N, 0:768]), "ok_out", r=kkeys)
            S.dma("sp", lambda e: e.dma_start(out=o_sv[:, :], in_=kvt[0:N, 768:1536]), "ov_out", r=vkeys)
            S.op("pool", lambda e: e.tensor_copy(out=vnbp[:, :], in_=kvt[0:N, 768:1536]), r=vkeys, w=["vnbp"])
            p, pk = next_ps()
            for k in range(8):
                S.op("pe", lambda e, p=p, k=k: e.matmul(p[0:N, 0:NH], lhsT=xn[:, k, 0:N], rhs=wfb[:, k, :],
                                                         start=(k == 0), stop=(k == 7)), r=["wfb", XK[k]], w=[pk])
            lfs = lf[0:N, 0, :]
            S.op("dve", lambda e, p=p: e.tensor_tensor(out=lfs, in0=p[0:N, 0:NH], in1=bfb[0:N, :], op=ALU.add),
                 r=[pk, "bfb"], w=["lf0"])
            S.op("act", lambda e: e.activation(out=lfs, in_=lfs, func=AF.Exp, scale=-1.0), r=["lf0"], w=["lf0"])
            S.op("act", lambda e: e.activation(out=lfs, in_=lfs, func=AF.Ln, bias=1.0, scale=1.0), r=["lf0"], w=["lf0"])
            S.op("dve", lambda e: e.tensor_scalar(out=lfs, in0=lfs, scalar1=-1.0, scalar2=None, op0=ALU.mult), r=["lf0"], w=["lf0"])
            S.dma("sp", lambda e: e.dma_start(out=o_slogf[:, :], in_=lfs), "olf_out", r=["lf0"])

        def sample_past_logf(sq):
            cl2 = cache_logf.rearrange("n s h -> n (s h)")
            if True:
                S.dma("pool", lambda e, sq=sq: e.indirect_dma_start(
                    out=lfpg.rearrange("p s h -> p (s h)"), out_offset=None, in_=cl2,
                    in_offset=bass.IndirectOffsetOnAxis(ap=ptT[:, sq:sq + 1], axis=0)), "lfpg", r=["ptT"], w=["lfpg"])
                for hd in range(NH):
                    S.op("dve", lambda e, hd=hd: e.tensor_tensor_scan(
                        out=lfpg[:, :, hd], data0=onesf[:, :], data1=lfpg[:, :, hd], initial=0.0, op0=ALU.mult, op1=ALU.add),
                        r=["lfpg", "onesf"], w=["lfpg"])
                S.op("dve", lambda e: e.tensor_copy(out=tmpf[0][:, 0:NH], in_=lfpg[:, 127, :]), r=["lfpg"], w=["tmpf0"])
                p, pk = next_ps()
                S.op("pe", lambda e, p=p: e.matmul(p[:, 0:NH], lhsT=strif[:], rhs=tmpf[0][:, 0:NH], start=True, stop=True),
                     r=["strif", "tmpf0"], w=[pk])
                p2, pk2 = next_ps()
                S.op("pe", lambda e, p2=p2: e.matmul(p2[:, 0:NH], lhsT=onesf[:], rhs=tmpf[0][:, 0:NH], start=True, stop=True),
                     r=["onesf", "tmpf0"], w=[pk2])
                S.op("dve", lambda e, p2=p2, sq=sq: e.tensor_copy(out=carry_s[:, sq, :], in_=p2[:, 0:NH]), r=[pk2], w=["carry_s"])
                S.op("dve", lambda e, p=p: e.tensor_copy(out=tmpf[1][:, 0:NH], in_=p[:, 0:NH]), r=[pk], w=["tmpf1"])
                S.op("dve", lambda e: e.tensor_tensor(
                    out=lfpg, in0=lfpg, in1=tmpf[1][:, 0:NH].unsqueeze(1).broadcast_to([128, 128, NH]), op=ALU.add),
                    r=["lfpg", "tmpf1"], w=["lfpg"])
                for h4 in range(3):
                    p, pk = next_ps()
                    for hh in range(4):
                        hd = h4 * 4 + hh
                        S.op("pe", lambda e, p=p, hh=hh, hd=hd: e.transpose(
                            out=p[:, hh * 128:(hh + 1) * 128], in_=lfpg[:, :, hd], identity=identf[:]),
                            r=["lfpg", "identf"], w=[pk])
                    S.op("dve", lambda e, p=p, h4=h4, sq=sq: e.tensor_scalar(
                        out=FnT[:, h4 * 4:(h4 + 1) * 4, :], in0=p[:, :].rearrange("p (h n) -> p h n", h=4),
                        scalar1=-1.0, scalar2=None, op0=ALU.mult), r=[pk], w=["FnT"])

        def sample_fnew(sq):
            N = NS
            p, pk = next_ps()
            S.op("pe", lambda e, p=p: e.matmul(p[0:N, 0:NH], lhsT=btri[:, :], rhs=lf[0:N, 0, :], start=True, stop=False),
                 r=["btri", "lf0"], w=[pk])
            S.op("pe", lambda e, p=p, sq=sq: e.matmul(p[0:N, 0:NH], lhsT=esel[:, sq, :], rhs=carry_s[:, sq, :],
                                                       start=False, stop=True), r=["esel", "carry_s"], w=[pk])
            fnew = tmpf[0][0:N, 0:NH]
            S.op("dve", lambda e, p=p: e.tensor_copy(out=fnew, in_=p[0:N, 0:NH]), r=[pk], w=["tmpf0"])
            S.op("dve", lambda e: e.tensor_scalar(out=tmpf[1][0:N, 0:NH], in0=fnew, scalar1=-1.0, scalar2=None, op0=ALU.mult),
                 r=["tmpf0"], w=["tmpf1"])
            S.dma("sp", lambda e, sq=sq: e.dma_start(out=FnegN[:, sq, :], in_=tmpf[1][sq * 8:(sq + 1) * 8, 0:NH]), "fnegn",
                  r=["tmpf1"], w=["FnegN"])

        def sample_fox():
            N = NS
            enter("satt")
            ck2 = cache_k.rearrange("n s h d -> (n s) (h d)")
            cv2 = cache_v.rearrange("n s h d -> (n s) (h d)")
            accA = PSW[1][:, 0:384].rearrange("p (c n) -> p c n", c=4)
            accB = PSW[1][:, 512:512 + 288].rearrange("p (c n) -> p c n", c=3)
            S.op("pool", lambda e: e.memset(Qblk[:], 0.0), w=["Qblk"])
            for sq in range(4):
                q0 = sq * 8
                sample_past_logf(sq)
                sample_fnew(sq)
                S.dma("sp", lambda e, sq=sq: e.dma_start(out=VnewS, in_=vnbp[sq * 8:(sq + 1) * 8, :]), "vnew",
                      r=["vnbp"], w=["VnewS"])
                for hh in range(2):
                    hb = hh * 64
                    S.op("act", lambda e, hb=hb, hh=hh, q0=q0: e.activation(
                        out=Qblk[hb:hb + 64, :, hh * 8:(hh + 1) * 8], in_=z[hb:hb + 64, 0:6, q0:q0 + 8], func=AF.Copy, scale=0.125),
                        r=ZK[0:6], w=["Qblk"])
                xq = tmpf[2][0:N, 0:96].rearrange("p (h q) -> p h q", q=8)
                S.op("dve", lambda e, sq=sq: e.tensor_tensor(
                    out=xq, in0=qmask[:, sq, :, :], in1=tmpf[0][0:N, 0:NH].unsqueeze(2).broadcast_to([N, NH, 8]), op=ALU.mult),
                    r=["qmask", "tmpf0"], w=["tmpf2"])
                p, pk = next_ps()
                S.op("pe", lambda e, p=p: e.matmul(p[:, 0:96], lhsT=onesf[0:N, :], rhs=tmpf[2][0:N, 0:96], start=True, stop=True),
                     r=["onesf", "tmpf2"], w=[pk])
                S.op("dve", lambda e, p=p: e.tensor_copy(out=Ftb[:].rearrange("p h q -> p (h q)"), in_=p[:, 0:96]), r=[pk], w=["Ftb"])
                for n in range(129):
                    new = (n == 128)
                    bi = n % 2
                    if not new:
                        col = sq * 128 + n
                        S.dma("pool", lambda e, bi=bi, col=col: e.indirect_dma_start(
                            out=Kpg[bi], out_offset=None, in_=ck2,
                            in_offset=bass.IndirectOffsetOnAxis(ap=idx_all[:, col:col + 1], axis=0)), f"Kpg{bi}",
                            r=["idx_all"], w=[f"Kpg{bi}"])
                        S.dma("pool", lambda e, bi=bi, col=col: e.indirect_dma_start(
                            out=Vpg[bi], out_offset=None, in_=cv2,
                            in_offset=bass.IndirectOffsetOnAxis(ap=idx_all[:, col:col + 1], axis=0)), f"Vpg{bi}",
                            r=["idx_all"], w=[f"Vpg{bi}"])
                        S.op("dve", lambda e, bi=bi: e.tensor_copy(out=Kpb[bi][:, :], in_=Kpg[bi]), r=[f"Kpg{bi}"], w=[f"Kpb{bi}"])
                        ptb_ = PSW[0][:].bitcast(BF16)
                        for c in range(6):
                            S.op("pe", lambda e, bi=bi, c=c, ptb_=ptb_: e.transpose(
                                out=ptb_[:, c * 128:(c + 1) * 128], in_=Kpb[bi][:, c * 128:(c + 1) * 128], identity=identb[:]),
                                r=[f"Kpb{bi}", "identb"], w=["psw0"])
                        S.op("act", lambda e, bi=bi, ptb_=ptb_: e.activation(
                            out=KTp[bi].rearrange("p c k -> p (c k)"), in_=ptb_[:, 0:768], func=AF.Copy), r=["psw0"], w=[f"KTp{bi}"])
                        S.op("dve", lambda e, bi=bi: e.tensor_copy(out=Vpb[bi], in_=Vpg[bi]), r=[f"Vpg{bi}"], w=[f"Vpb{bi}"])
                        nk = 128
                        kt_fn = lambda c, bi=bi: KTp[bi][:, c, :]
                        v_fn = lambda c, bi=bi: Vpb[bi][:, c * 128:(c + 1) * 128]
                        ktk, vk = f"KTp{bi}", f"Vpb{bi}"
                    else:
                        nk = 8
                        kt_fn = lambda c, q0=q0: KTnew[:, c, q0:q0 + 8]
                        v_fn = lambda c: VnewS[:, c * 128:(c + 1) * 128]
                        ktk, vk = "KTnew", "VnewS"
                    p, pk = next_ps()
                    for c in range(6):
                        S.op("pe", lambda e, p=p, c=c, kt_fn=kt_fn, nk=nk: e.matmul(
                            p[0:nk, c * 16:(c + 1) * 16], lhsT=kt_fn(c), rhs=Qblk[:, c, :], start=True, stop=True),
                            r=[ktk, "Qblk"], w=[pk])
                    sc = tmpf[1][0:nk, 0:96].rearrange("p (h q) -> p h q", q=8)
                    p3 = p[0:nk, 0:96].rearrange("p (h q) -> p h q", q=8)
                    S.op("dve", lambda e, p3=p3, sc=sc, nk=nk: e.tensor_tensor(out=sc, in0=p3, in1=Ftb[0:nk, :, :], op=ALU.add),
                         r=[pk, "Ftb"], w=["tmpf1"])
                    if not new:
                        S.op("dve", lambda e, sc=sc, sq=sq, n=n: e.tensor_tensor(
                            out=sc, in0=sc, in1=FnT[:, :, n].unsqueeze(2).broadcast_to([128, NH, 8]), op=ALU.add),
                            r=["tmpf1", "FnT"], w=["tmpf1"])
                    else:
                        S.op("dve", lambda e, sc=sc, sq=sq: e.tensor_tensor(
                            out=sc, in0=sc, in1=FnegN[:, sq, :].unsqueeze(2).broadcast_to([8, NH, 8]), op=ALU.add),
                            r=["tmpf1", "FnegN"], w=["tmpf1"])
                        S.op("dve", lambda e, sc=sc: e.tensor_tensor(out=sc, in0=sc, in1=cmask[:, :, :], op=ALU.add),
                             r=["tmpf1", "cmask"], w=["tmpf1"])
                    pt = pT[n % 3]
                    ptk = f"pT{n % 3}"
                    S.op("act", lambda e, pt=pt, nk=nk: e.activation(out=pt[0:nk, 0:96], in_=tmpf[1][0:nk, 0:96], func=AF.Exp),
                         r=["tmpf1"], w=[ptk])
                    for c in range(6):
                        acc = accA[:, c, :] if c < 4 else accB[:, c - 4, :]
                        S.op("pe", lambda e, acc=acc, c=c, v_fn=v_fn, pt=pt, nk=nk, n=n, new=new: e.matmul(
                            acc, lhsT=v_fn(c), rhs=pt[0:nk, 0:96], start=(n == 0 and c in (0, 4)), stop=new,
                            skip_group_check=True), r=[vk, ptk], w=["psw1"])
                    S.op("pe", lambda e, pt=pt, nk=nk, new=new: e.matmul(
                        accB[:, 2, :], lhsT=onesb[0:nk, :], rhs=pt[0:nk, 0:96], start=False, stop=new, skip_group_check=True),
                        r=["onesb", ptk], w=["psw1"])
                S.op("dve", lambda e: e.reciprocal(out=tmpf[2][:, 0:96], in_=accB[:, 2, :]), r=["psw1"], w=["tmpf2"])
                for c in range(6):
                    acc = accA[:, c, :] if c < 4 else accB[:, c - 4, :]
                    for hh in range(2):
                        hb = hh * 64
                        cs = (2 * c + hh) * 8
                        S.op("dve", lambda e, acc=acc, c=c, hb=hb, cs=cs, q0=q0: e.tensor_tensor(
                            out=ymix[hb:hb + 64, c, q0:q0 + 8], in0=acc[hb:hb + 64, cs:cs + 8], in1=tmpf[2][hb:hb + 64, cs:cs + 8],
                            op=ALU.mult), r=["psw1", "tmpf2"], w=[YK[c]])

        for layer in range(2):
            rmsnorm(NS, layer, lambda k: xn[:, k, 0:NS], XK, HK)

            def ev_zs(m, p, pk):
                copy_op(evac_eng(), z[:, m, 0:NS], p[:, 0:NS], [pk], [ZK[m]])
            proj_fm(U_IN[layer], 8, range(8), lambda k: xn[:, k, 0:NS], XK, NS, ev_zs)
            for sq in range(4):
                mem_attention(8, sq * 8,
                              lambda hb, hc, mt, layer=layer, sq=sq: KmTs[hb:hb + 64, sq, layer, hc, mt * 128:(mt + 1) * 128],
                              lambda mt, hd, layer=layer, sq=sq: Vmps[:, sq, layer, mt, hd, :], ["KmTs", "Vmps"])
            if layer == 0:
                s5_layer(NS, 8, 4, lambda ri, j, sg: h0s[:, ri, j, sg:sg + 1], lambda ri, j0, sg: s5so[:, ri, j0:j0 + 6, sg],
                         ["s5so"])
                S.dma("sp", lambda e: e.dma_start(out=o_s5s[:, :, :, :], in_=s5so[:]), "s5s_out", r=["s5so"])
            else:
                sample_fox()
            dense_tail(layer, NS)
            if layer == 0:
                sample_kv()
        enter("yfm")
        rmsnorm(NS, 5, lambda k: yfm[:, k, 0:NS], ["yfm"] * 8, HK)
        enter("xtm")
        for c2 in range(2):
            p, pk = next_ps()
            for cc in range(4):
                c = c2 * 4 + cc
                S.op("pe", lambda e, p=p, c=c, cc=cc: e.transpose(
                    out=p[0:NS, cc * 128:(cc + 1) * 128], in_=yfm[:, c, 0:NS], identity=identf[:]),
                    r=["yfm", "identf"], w=[pk])
            copy_op(evac_eng(), xtm[0:NS, 0, c2 * 512:(c2 + 1) * 512], p[0:NS, :], [pk], ["xtm"])
        S.dma("sp", lambda e: e.dma_start(out=ys_out[:, :], in_=xtm[0:NS, 0, :]), "y_out", r=["xtm"])

        S.dma("sp", lambda e: e.dma_start(out=o_s5[:, :, :], in_=hprev[:]), "s5_out", r=["hprev"])

        S.finish_waits("sp")
        with nc.Block() as block:
            S.emit(block)
    return nc


def _units(W, KG=8):
    K, N = W.shape
    Kc, Mc = K // 128, N // 128
    nkq = (Kc + 7) // 8
    out = np.zeros((Mc, nkq, 128, 8, 128), np.float32)
    W4 = W.reshape(Kc, 128, Mc, 128)
    for kq in range(nkq):
        kn = min(8, Kc - kq * 8)
        out[:, kq, :, :kn, :] = W4[kq * 8:kq * 8 + kn].transpose(2, 1, 0, 3)
    return out.reshape(Mc * nkq, 128, 1024)


def _selden():
    sd = np.zeros((128, 2, 128), np.float32)
    sd[64, 0, 0:64] = 1.0
    sd[0, 1, 64:128] = 1.0
    return sd


def _fm(v, nch):
    return np.ascontiguousarray(np.asarray(v, np.float32).reshape(nch, 128).T)


def prepare_shared(inp):
    f = lambda k: np.asarray(inp[k], np.float32)
    units = []
    for i in range(2):
        units += [_units(f("w_in")[i]), _units(f("w_out")[i]), _units(f("w_up")[i]), _units(f("w_down")[i])]
    units += [_units(f("s5_w_glu")[0]), _units(f("w_kv"))]
    wall = np.concatenate(units, axis=0)
    assert wall.shape[0] == N_UNITS
    gall = np.stack([_fm(f("norm_mix")[0], 8), _fm(f("norm_mix")[1], 8), _fm(f("norm_mlp")[0], 8),
                     _fm(f("norm_mlp")[1], 8), _fm(f("norm_kv"), 8), _fm(f("norm_final"), 8)], axis=1)
    a_re, a_im, ls = f("s5_a_re")[0], f("s5_a_im")[0], f("s5_log_step")[0]
    lse = np.repeat(ls[:, None], 64, axis=1)
    a_b = np.stack([np.broadcast_to(a.reshape(1, 3072), (128, 3072)) for a in (a_re, a_im, lse)]).astype(np.float32)
    sm = lambda a: a.reshape(24, 128).T
    a_s = np.stack([sm(a_re), sm(a_im), sm(lse)], axis=1).astype(np.float32)
    b_re, b_im = f("s5_b_re")[0], f("s5_b_im")[0]
    c_re, c_im = f("s5_c_re")[0], f("s5_c_im")[0]
    bpad = np.zeros((2, 128, 24, 128), np.float32)
    cpad = np.zeros((2, 128, 24, 128), np.float32)
    for g in range(48):
        j, g2, gl = g // 2, g % 2, g % 8
        for ri, (b, c) in enumerate(((b_re, c_re), (b_im, c_im))):
            bpad[ri, gl * 16:(gl + 1) * 16, j, g2 * 64:(g2 + 1) * 64] = b[g].T
            cpad[ri, g2 * 64:(g2 + 1) * 64, j, gl * 16:(gl + 1) * 16] = c[g].T
    tri = np.triu(np.ones((128, 128), np.float32))
    tcount = np.broadcast_to(np.arange(1, 129, dtype=np.float32)[None, :], (128, 128)).copy()
    s_idx = np.arange(128)[:, None, None]
    d_idx = np.arange(4)[None, :, None]
    q_idx = np.arange(512)[None, None, :]
    masks = np.where(128 * d_idx + s_idx <= q_idx, 0.0, -1e30).astype(np.float32).astype(ml_dtypes.bfloat16)
    return {
        "w_mem_kv": np.ascontiguousarray(f("w_mem_kv")), "wall": wall, "gall": np.ascontiguousarray(gall),
        "bglu": _fm(f("s5_b_glu")[0], 6), "s5d": _fm(f("s5_d")[0].reshape(-1), 6),
        "wf": np.ascontiguousarray(f("w_f").reshape(8, 128, 12).transpose(1, 0, 2)),
        "bfb": np.ascontiguousarray(np.broadcast_to(f("b_f")[None, :], (128, 12))),
        "a_b": a_b, "a_s": np.ascontiguousarray(a_s), "bpad": bpad, "cpad": cpad,
        "selden": _selden(), "ident_f": np.eye(128, dtype=np.float32), "tri_f": tri, "tcount": tcount, "masks": masks,
    }


T_PROMPT = 8192
DEBUG = False
COMPACT_DEV = False
DBG_NAMES = []
DBG_OUT = {}


def kernel(**inputs):
    T = T_PROMPT
    shared = prepare_shared(inputs)
    x_prompt = np.asarray(inputs["x_prompt"], np.float32)
    mem_prompt = np.asarray(inputs["mem_prompt"], np.float32)
    x_sample = np.asarray(inputs["x_sample"], np.float32)
    page_table = np.asarray(inputs["page_table"], np.int32)
    cache_k = np.asarray(inputs["cache_k"], np.float32)
    cache_v = np.asarray(inputs["cache_v"], np.float32)
    cache_logf = np.asarray(inputs["cache_logf"], np.float32)
    st_re = np.asarray(inputs["state_s5_re"], np.float32)[0]
    st_im = np.asarray(inputs["state_s5_im"], np.float32)[0]
    cmk = np.asarray(inputs["cache_mem_k"], np.float32).reshape(2, 32, MEM_T, 256)
    cmv = np.asarray(inputs["cache_mem_v"], np.float32).reshape(2, 32, MEM_T, 256)
    npool = cache_k.shape[0]
    nc = build_program(T, 512 if COMPACT_DEV else npool)
    stri = np.triu(np.ones((128, 128), np.float32), k=1)
    ii = np.arange(32)
    btri = ((ii[:, None] // 8 == ii[None, :] // 8) & (ii[:, None] <= ii[None, :])).astype(np.float32)
    esel = np.zeros((128, 4, 32), np.float32)
    for sq in range(4):
        esel[0, sq, sq * 8:(sq + 1) * 8] = 1.0
    qmask = np.zeros((32, 4, NH, 8), np.float32)
    for sq in range(4):
        for q in range(8):
            qmask[sq * 8 + q, sq, :, q] = 1.0
    cmask = np.where(np.arange(8)[:, None, None] <= np.arange(8)[None, None, :], 0.0, -1e30).astype(np.float32)
    cmask = np.ascontiguousarray(np.broadcast_to(cmask, (8, NH, 8)))
    consts = {"iota_i": np.arange(128, dtype=np.int32)[:, None].copy(), "stri_f": stri, "btri32": btri, "esel": esel,
              "qmask": qmask, "cmask": cmask}
    in_maps = []
    for c in range(NCORES):
        s = c // 4
        m = dict(shared)
        m.update(consts)
        m["x"] = np.ascontiguousarray(x_prompt[s, :T])
        j = c % 4
        m["hidx"] = np.ascontiguousarray(((4 * np.arange(4)[None, :] + j) * 128 + np.arange(128)[:, None]).astype(np.int32))
        ohbb = np.zeros((128, 2, 4), np.float32)
        ohbb[:, 0, j] = 1.0
        ohbb[:, 1, j + 1:] = -1e30
        m["ohbb"] = ohbb
        m["mem_prompt"] = np.ascontiguousarray(mem_prompt[s])
        sl = slice(4 * c, 4 * c + 4)
        m["xs"] = np.ascontiguousarray(x_sample[sl].reshape(32, D))
        h0 = np.stack([st_re[sl], st_im[sl]])
        m["h0s"] = np.ascontiguousarray(h0.reshape(2, 4, 24, 2, 64).transpose(3, 4, 0, 2, 1).reshape(128, 2, 24, 4))
        m["cmk"] = np.ascontiguousarray(cmk[:, sl])
        m["cmv"] = np.ascontiguousarray(cmv[:, sl])
        pt = page_table[sl]
        if COMPACT_DEV:
            flat = pt.reshape(-1)
            m["cache_k"] = np.ascontiguousarray(cache_k[flat])
            m["cache_v"] = np.ascontiguousarray(cache_v[flat])
            m["cache_logf"] = np.ascontiguousarray(cache_logf[flat])
            pt = np.arange(512, dtype=np.int32).reshape(4, 128)
        else:
            m["cache_k"], m["cache_v"], m["cache_logf"] = cache_k, cache_v, cache_logf
        m["ptab"] = np.ascontiguousarray(pt.reshape(1, 512))
        m["ptabT"] = np.ascontiguousarray(pt.T)
        in_maps.append(m)
    res = run_bass_kernel_spmd(nc, in_maps, core_ids=list(range(NCORES)))
    R = res.results
    sel = [R[0], R[4]]
    if DEBUG:
        DBG_OUT.clear()
        for i, n in enumerate(DBG_NAMES[0]):
            DBG_OUT[n] = R[0]["dbg"][i]
    nl1 = (T // NT) // 4
    y_prompt = np.zeros((2, T, D), np.float32)
    for c in range(NCORES):
        yc = R[c]["y"].reshape(nl1, NT, D)
        for i in range(nl1):
            t0 = (4 * i + c % 4) * NT
            y_prompt[c // 4, t0:t0 + NT] = yc[i]
    okv = np.stack([r["o_mem_kv"] for r in sel], axis=1)
    p_mem_k = np.ascontiguousarray(okv[..., :256]).reshape(2, 2, 256, 4, 64)
    p_mem_v = np.ascontiguousarray(okv[..., 256:]).reshape(2, 2, 256, 4, 64)
    p_k = np.stack([r["o_k"] for r in sel]).reshape(2, T, NH, 64)
    p_v = np.stack([r["o_v"] for r in sel]).reshape(2, T, NH, 64)
    p_logf = np.stack([r["o_logf"] for r in sel])
    s5 = np.stack([r["o_s5"] for r in sel])
    s5 = s5.reshape(2, 2, 64, 2, 24).transpose(3, 0, 4, 1, 2).reshape(2, 2, 48, 64)
    p_s5_re, p_s5_im = np.ascontiguousarray(s5[0][None]), np.ascontiguousarray(s5[1][None])
    y_sample = np.concatenate([r["ys"].reshape(4, 8, D) for r in R])
    s_k = np.concatenate([r["o_sk"].reshape(4, 8, NH, 64) for r in R])
    s_v = np.concatenate([r["o_sv"].reshape(4, 8, NH, 64) for r in R])
    s_logf = np.concatenate([r["o_slogf"].reshape(4, 8, NH) for r in R])
    ss = np.stack([r["o_s5s"] for r in R])
    ss = ss.reshape(8, 2, 64, 2, 24, 4).transpose(3, 0, 5, 4, 1, 2).reshape(2, 32, 48, 64)
    s_s5_re, s_s5_im = np.ascontiguousarray(ss[0][None]), np.ascontiguousarray(ss[1][None])
    return (y_prompt, y_sample, p_s5_re, p_s5_im, p_mem_k, p_mem_v, p_k, p_v, p_logf,
            s_s5_re, s_s5_im, s_k, s_v, s_logf)
```
